# Optimizing a Trainium2 kernel written in Bass

```python
import math
import jax, jax.numpy as jnp
from jax import lax
import numpy as np

D_MODEL = 1024
BATCH = 8
SEQ = 2048
DEPTH = 2
DEC_BATCH = 128
DEC_SEQ = 8
PAST_LEN = 16384
PAGE_SIZE = 128

MIX_WIDTH = D_MODEL
DN_HEADS = 4
DN_WIDTH = MIX_WIDTH // 2
DN_HEAD_DIM = DN_WIDTH // DN_HEADS
DN_CONV = 4
DN_CHUNK = 64
SSM_WIDTH = MIX_WIDTH // 4
SSM_GROUP = 16
SSM_GROUPS = SSM_WIDTH // SSM_GROUP
SSM_STATE = 64
HG_WIDTH = MIX_WIDTH - DN_WIDTH - SSM_WIDTH
HG_HEADS = 4
HG_HEAD_DIM = HG_WIDTH // HG_HEADS
HG_CHUNK = 32
FF_DIM = 2816
FF_CONV = 3
PLE_DIM = 256
EPS = 1e-6

IN_SIZES = (3 * DN_WIDTH, DN_WIDTH, DN_HEADS, DN_HEADS, SSM_WIDTH, HG_WIDTH, HG_WIDTH, HG_WIDTH, HG_WIDTH)
IN_COLS = sum(IN_SIZES)
IN_SPLITS = tuple(int(s) for s in np.cumsum(IN_SIZES)[:-1])

kernel_name = 'hybrid_deltanet_s5_hgrn2_step'


def rmsnorm(x, g):
    xf = x.astype(jnp.float32)
    y = xf * lax.rsqrt(jnp.mean(xf * xf, axis=-1, keepdims=True) + EPS)
    return (y * g.astype(jnp.float32)).astype(x.dtype)


def l2norm(x):
    xf = x.astype(jnp.float32)
    return xf * lax.rsqrt(jnp.sum(xf * xf, axis=-1, keepdims=True) + EPS)


def causal_dwconv(x, buf, w):
    width = w.shape[0]
    t = x.shape[1]
    xp = jnp.concatenate([buf.astype(x.dtype), x], axis=1)
    y = xp[:, 0:t] * w[0]
    for j in range(1, width):
        y = y + xp[:, j:j + t] * w[j]
    return y, xp[:, t:]


def to_chunks(a, c, n):
    bsz, t = a.shape[:2]
    a = jnp.pad(a.astype(jnp.float32), [(0, 0), (0, n * c - t)] + [(0, 0)] * (a.ndim - 2))
    a = a.reshape((bsz, n, c) + a.shape[2:])
    return jnp.transpose(a, (1, 0, 3, 2) + tuple(range(4, a.ndim)))


def from_chunks(o, t):
    n, bsz, h, c, d = o.shape
    return jnp.transpose(o, (1, 0, 3, 2, 4)).reshape(bsz, n * c, h, d)[:, :t]


def gated_delta_chunked(q, k, v, g, beta, s0):
    t = q.shape[1]
    dv = v.shape[-1]
    c = min(DN_CHUNK, t)
    n = -(-t // c)
    q, k, v, g, beta = [to_chunks(a, c, n) for a in (q, k, v, g, beta)]
    causal = jnp.tril(jnp.ones((c, c), dtype=bool))
    strict = jnp.tril(jnp.ones((c, c), dtype=bool), -1)
    gc = jnp.cumsum(g, axis=-1)
    decay = jnp.exp(jnp.where(causal, gc[..., :, None] - gc[..., None, :], -jnp.inf))
    m = jnp.where(strict, beta[..., :, None] * jnp.einsum('nbhtk,nbhsk->nbhts', k, k) * decay, 0.0)
    rhs = jnp.concatenate([beta[..., None] * v, (beta * jnp.exp(gc))[..., None] * k], axis=-1)
    uw = lax.linalg.triangular_solve(m + jnp.eye(c, dtype=m.dtype), rhs, left_side=True, lower=True)
    u, w = uw[..., :dv], uw[..., dv:]
    qk = jnp.einsum('nbhtk,nbhsk->nbhts', q, k) * decay
    q_dec = q * jnp.exp(gc)[..., None]
    k_dec = k * jnp.exp(gc[..., -1:] - gc)[..., None]
    g_last = jnp.exp(gc[..., -1])[..., None, None]

    def step(s, xs):
        qk_i, q_i, k_i, u_i, w_i, gl_i = xs
        v_new = u_i - jnp.einsum('bhtk,bhkv->bhtv', w_i, s)
        o = jnp.einsum('bhtk,bhkv->bhtv', q_i, s) + jnp.einsum('bhts,bhsv->bhtv', qk_i, v_new)
        s = s * gl_i + jnp.einsum('bhtk,bhtv->bhkv', k_i, v_new)
        return s, o

    s_fin, o = lax.scan(step, s0.astype(jnp.float32), (qk, q_dec, k_dec, u, w, g_last))
    return from_chunks(o, t), s_fin


def hgrn2_chunked(q, k, v, logf, s0):
    t = q.shape[1]
    c = min(HG_CHUNK, t)
    n = -(-t // c)
    q, k, v, logf = [to_chunks(a, c, n) for a in (q, k, v, logf)]
    causal = jnp.tril(jnp.ones((c, c), dtype=bool))[..., None]
    b = jnp.cumsum(logf, axis=-2)

    def step(s, xs):
        q_i, k_i, v_i, b_i = xs
        decay = jnp.exp(jnp.where(causal, b_i[..., :, None, :] - b_i[..., None, :, :], -jnp.inf))
        a = jnp.einsum('bhtk,bhsk,bhtsk->bhts', q_i, k_i, decay)
        o = jnp.einsum('bhtk,bhkv->bhtv', q_i * jnp.exp(b_i), s) + jnp.einsum('bhts,bhsv->bhtv', a, v_i)
        s = s * jnp.exp(b_i[..., -1, :])[..., None] + jnp.einsum('bhtk,bhtv->bhkv', k_i * jnp.exp(b_i[..., -1:, :] - b_i), v_i)
        return s, o

    s_fin, o = lax.scan(step, s0.astype(jnp.float32), (q, k, v, b))
    return from_chunks(o, t), s_fin


def s5_ssm(u, x0_re, x0_im, lam_re, lam_im, log_step, b_re, b_im, c_re, c_im, d_skip):
    f32 = jnp.float32
    bsz, t, _ = u.shape
    uf = u.astype(f32).reshape(bsz, t, SSM_GROUPS, SSM_GROUP)
    lam = lax.complex(lam_re.astype(f32), lam_im.astype(f32))
    delta = jnp.exp(log_step.astype(f32))[:, None]
    lam_bar = jnp.exp(lam * delta)
    b_bar = ((lam_bar - 1.0) / lam)[..., None] * lax.complex(b_re.astype(f32), b_im.astype(f32))
    c_mat = lax.complex(c_re.astype(f32), c_im.astype(f32))
    bu = jnp.einsum('gph,btgh->btgp', b_bar, uf.astype(jnp.complex64))
    x0 = lax.complex(x0_re.astype(f32), x0_im.astype(f32))
    bu = bu.at[:, 0].add(lam_bar * x0)
    a = jnp.broadcast_to(lam_bar, bu.shape)

    def combine(e1, e2):
        a1, b1 = e1
        a2, b2 = e2
        return a1 * a2, a2 * b1 + b2

    _, xs = lax.associative_scan(combine, (a, bu), axis=1)
    y = jnp.einsum('ghp,btgp->btgh', c_mat, xs).real + d_skip.astype(f32).reshape(SSM_GROUPS, SSM_GROUP) * uf
    x_last = xs[:, -1]
    return y.reshape(bsz, t, SSM_WIDTH), jnp.real(x_last), jnp.imag(x_last)


def trunk(x, p, conv_qkv, delta, ssm_re, ssm_im, hgrn, conv_ffn, weights):
    (norm_mix, w_in, dn_conv_w, dn_a_log, dn_dt_bias, dn_norm, ssm_lam_re, ssm_lam_im, ssm_log_step,
     ssm_b_re, ssm_b_im, ssm_c_re, ssm_c_im, ssm_d, ssm_glu_w, ssm_glu_b, hg_lower, hg_norm, w_out,
     norm_ffn, ffn_w_up, ffn_conv_w, ffn_w_down, norm_ple, ple_w_gate, ple_w_proj, norm_final) = weights
    f32 = jnp.float32
    bsz, t, _ = x.shape
    lb_p = jax.nn.softmax(hg_lower.astype(f32), axis=0)
    lower_bounds = jnp.cumsum(lb_p, axis=0) - lb_p[0]
    hshape = (bsz, t, HG_HEADS, HG_HEAD_DIM)
    h = x
    n_conv_qkv, n_delta, n_ssm_re, n_ssm_im, n_hgrn, n_conv_ffn = [], [], [], [], [], []
    for i in range(DEPTH):
        hn = rmsnorm(h, norm_mix[i])
        z = hn @ w_in[i]
        z_qkv, z_gate, z_beta, z_a, z_u, z_hq, z_hf, z_hi, z_hg = jnp.split(z, IN_SPLITS, axis=-1)
        qkv, buf_qkv = causal_dwconv(z_qkv, conv_qkv[i], dn_conv_w[i])
        qkv = jax.nn.silu(qkv).reshape(bsz, t, 3, DN_HEADS, DN_HEAD_DIM)
        q = l2norm(qkv[:, :, 0]) * DN_HEAD_DIM ** -0.5
        k = l2norm(qkv[:, :, 1])
        v = qkv[:, :, 2]
        beta = jax.nn.sigmoid(z_beta.astype(f32))
        g = -jnp.exp(dn_a_log[i].astype(f32)) * jax.nn.softplus(z_a.astype(f32) + dn_dt_bias[i].astype(f32))
        o_a, s_delta = gated_delta_chunked(q, k, v, g, beta, delta[i])
        o_a = rmsnorm(o_a, dn_norm[i]) * jax.nn.silu(z_gate.astype(f32).reshape(bsz, t, DN_HEADS, DN_HEAD_DIM))
        o_a = o_a.reshape(bsz, t, DN_WIDTH).astype(x.dtype)
        y_b, x_re, x_im = s5_ssm(z_u, ssm_re[i], ssm_im[i], ssm_lam_re[i], ssm_lam_im[i], ssm_log_step[i],
                                 ssm_b_re[i], ssm_b_im[i], ssm_c_re[i], ssm_c_im[i], ssm_d[i])
        y_b = jax.nn.gelu(y_b)
        o_b = (y_b * jax.nn.sigmoid(y_b @ ssm_glu_w[i].astype(f32) + ssm_glu_b[i].astype(f32))).astype(x.dtype)
        lb = lower_bounds[i]
        f = lb + (1.0 - lb) * jax.nn.sigmoid(z_hf.astype(f32))
        hq = jax.nn.silu(z_hq.astype(f32)).reshape(hshape)
        hk = (1.0 - f).reshape(hshape)
        hv = z_hi.astype(f32).reshape(hshape)
        o_c, s_hg = hgrn2_chunked(hq, hk, hv, jnp.log(f).reshape(hshape), hgrn[i])
        o_c = rmsnorm(o_c, hg_norm[i]) * jax.nn.silu(z_hg.astype(f32).reshape(hshape))
        o_c = o_c.reshape(bsz, t, HG_WIDTH).astype(x.dtype)
        h = h + jnp.concatenate([o_a, o_b, o_c], axis=-1) @ w_out[i]
        hn = rmsnorm(h, norm_ffn[i])
        up, buf_ffn = causal_dwconv(hn @ ffn_w_up[i], conv_ffn[i], ffn_conv_w[i])
        a_ff, b_ff = jnp.split(up, 2, axis=-1)
        h = h + (jax.nn.silu(a_ff) * b_ff) @ ffn_w_down[i]
        gate = jax.nn.sigmoid(rmsnorm(h, norm_ple[i]) @ ple_w_gate[i])
        h = h + gate * (p[i] @ ple_w_proj[i])
        n_conv_qkv.append(buf_qkv)
        n_delta.append(s_delta)
        n_ssm_re.append(x_re)
        n_ssm_im.append(x_im)
        n_hgrn.append(s_hg)
        n_conv_ffn.append(buf_ffn)
    return (rmsnorm(h, norm_final), jnp.stack(n_conv_qkv), jnp.stack(n_delta), jnp.stack(n_ssm_re),
            jnp.stack(n_ssm_im), jnp.stack(n_hgrn), jnp.stack(n_conv_ffn))


def setup_inputs(seed: int = 0) -> dict:
    key = jax.random.key(seed)
    ks = iter(jax.random.split(key, 48))
    f32 = jnp.float32

    def nrm(shape, scale):
        return jax.random.normal(next(ks), shape, f32) * scale

    def gain(shape):
        return 1.0 + nrm(shape, 0.02)

    def unif(shape, lo, hi):
        return jax.random.uniform(next(ks), shape, f32, minval=lo, maxval=hi)

    dt = jnp.exp(unif((DEPTH, DN_HEADS), math.log(1e-3), math.log(1e-1)))
    return {
        'x_prompt': nrm((BATCH, SEQ, D_MODEL), 1.0),
        'x_sample': nrm((DEC_BATCH, DEC_SEQ, D_MODEL), 1.0),
        'p_prompt': nrm((DEPTH, BATCH, SEQ, PLE_DIM), 1.0),
        'p_sample': nrm((DEPTH, DEC_BATCH, DEC_SEQ, PLE_DIM), 1.0),
        'state_conv_qkv': nrm((DEPTH, DEC_BATCH, DN_CONV - 1, 3 * DN_WIDTH), 1.0),
        'state_delta': nrm((DEPTH, DEC_BATCH, DN_HEADS, DN_HEAD_DIM, DN_HEAD_DIM), 0.1),
        'state_ssm_re': nrm((DEPTH, DEC_BATCH, SSM_GROUPS, SSM_STATE), 0.5),
        'state_ssm_im': nrm((DEPTH, DEC_BATCH, SSM_GROUPS, SSM_STATE), 0.5),
        'state_hgrn': nrm((DEPTH, DEC_BATCH, HG_HEADS, HG_HEAD_DIM, HG_HEAD_DIM), 0.5),
        'state_conv_ffn': nrm((DEPTH, DEC_BATCH, FF_CONV - 1, 2 * FF_DIM), 1.0),
        'norm_mix': gain((DEPTH, D_MODEL)),
        'w_in': nrm((DEPTH, D_MODEL, IN_COLS), D_MODEL ** -0.5),
        'dn_conv_w': nrm((DEPTH, DN_CONV, 3 * DN_WIDTH), DN_CONV ** -0.5),
        'dn_a_log': jnp.log(unif((DEPTH, DN_HEADS), 1.0, 16.0)),
        'dn_dt_bias': dt + jnp.log(-jnp.expm1(-dt)),
        'dn_norm': gain((DEPTH, DN_HEAD_DIM)),
        'ssm_lam_re': -0.5 + nrm((DEPTH, SSM_GROUPS, SSM_STATE), 0.01),
        'ssm_lam_im': jnp.pi * jnp.arange(SSM_STATE, dtype=f32) + nrm((DEPTH, SSM_GROUPS, SSM_STATE), 0.01),
        'ssm_log_step': unif((DEPTH, SSM_GROUPS), math.log(1e-3), math.log(1e-1)),
        'ssm_b_re': nrm((DEPTH, SSM_GROUPS, SSM_STATE, SSM_GROUP), (2 * SSM_GROUP) ** -0.5),
        'ssm_b_im': nrm((DEPTH, SSM_GROUPS, SSM_STATE, SSM_GROUP), (2 * SSM_GROUP) ** -0.5),
        'ssm_c_re': nrm((DEPTH, SSM_GROUPS, SSM_GROUP, SSM_STATE), SSM_STATE ** -0.5),
        'ssm_c_im': nrm((DEPTH, SSM_GROUPS, SSM_GROUP, SSM_STATE), SSM_STATE ** -0.5),
        'ssm_d': nrm((DEPTH, SSM_WIDTH), 1.0),
        'ssm_glu_w': nrm((DEPTH, SSM_WIDTH, SSM_WIDTH), SSM_WIDTH ** -0.5),
        'ssm_glu_b': nrm((DEPTH, SSM_WIDTH), 0.01),
        'hg_lower': nrm((DEPTH, HG_WIDTH), 1.0),
        'hg_norm': gain((DEPTH, HG_HEAD_DIM)),
        'w_out': nrm((DEPTH, MIX_WIDTH, D_MODEL), MIX_WIDTH ** -0.5),
        'norm_ffn': gain((DEPTH, D_MODEL)),
        'ffn_w_up': nrm((DEPTH, D_MODEL, 2 * FF_DIM), D_MODEL ** -0.5),
        'ffn_conv_w': nrm((DEPTH, FF_CONV, 2 * FF_DIM), FF_CONV ** -0.5),
        'ffn_w_down': nrm((DEPTH, FF_DIM, D_MODEL), FF_DIM ** -0.5),
        'norm_ple': gain((DEPTH, D_MODEL)),
        'ple_w_gate': nrm((DEPTH, D_MODEL, D_MODEL), D_MODEL ** -0.5),
        'ple_w_proj': nrm((DEPTH, PLE_DIM, D_MODEL), PLE_DIM ** -0.5),
        'norm_final': gain((D_MODEL,)),
    }


def reference(x_prompt, x_sample, p_prompt, p_sample, state_conv_qkv, state_delta, state_ssm_re, state_ssm_im,
              state_hgrn, state_conv_ffn, norm_mix, w_in, dn_conv_w, dn_a_log, dn_dt_bias, dn_norm, ssm_lam_re,
              ssm_lam_im, ssm_log_step, ssm_b_re, ssm_b_im, ssm_c_re, ssm_c_im, ssm_d, ssm_glu_w, ssm_glu_b,
              hg_lower, hg_norm, w_out, norm_ffn, ffn_w_up, ffn_conv_w, ffn_w_down, norm_ple, ple_w_gate,
              ple_w_proj, norm_final):
    weights = (norm_mix, w_in, dn_conv_w, dn_a_log, dn_dt_bias, dn_norm, ssm_lam_re, ssm_lam_im, ssm_log_step,
               ssm_b_re, ssm_b_im, ssm_c_re, ssm_c_im, ssm_d, ssm_glu_w, ssm_glu_b, hg_lower, hg_norm, w_out,
               norm_ffn, ffn_w_up, ffn_conv_w, ffn_w_down, norm_ple, ple_w_gate, ple_w_proj, norm_final)
    bp = x_prompt.shape[0]
    dt = x_prompt.dtype
    z_conv_qkv = jnp.zeros((DEPTH, bp, DN_CONV - 1, 3 * DN_WIDTH), dt)
    z_delta = jnp.zeros((DEPTH, bp, DN_HEADS, DN_HEAD_DIM, DN_HEAD_DIM), dt)
    z_ssm = jnp.zeros((DEPTH, bp, SSM_GROUPS, SSM_STATE), dt)
    z_hgrn = jnp.zeros((DEPTH, bp, HG_HEADS, HG_HEAD_DIM, HG_HEAD_DIM), dt)
    z_conv_ffn = jnp.zeros((DEPTH, bp, FF_CONV - 1, 2 * FF_DIM), dt)
    y_prompt, p_conv_qkv, p_delta, p_ssm_re, p_ssm_im, p_hgrn, p_conv_ffn = trunk(
        x_prompt, p_prompt, z_conv_qkv, z_delta, z_ssm, z_ssm, z_hgrn, z_conv_ffn, weights)
    y_sample, s_conv_qkv, s_delta, s_ssm_re, s_ssm_im, s_hgrn, s_conv_ffn = trunk(
        x_sample, p_sample, state_conv_qkv, state_delta, state_ssm_re, state_ssm_im, state_hgrn, state_conv_ffn,
        weights)
    return (y_prompt, y_sample, p_conv_qkv, p_delta, p_ssm_re, p_ssm_im, p_hgrn, p_conv_ffn,
            s_conv_qkv, s_delta, s_ssm_re, s_ssm_im, s_hgrn, s_conv_ffn)
```

```python
import math
from contextlib import ExitStack
import numpy as np
import concourse.bass as bass
import concourse.mybir as mybir
from concourse.bass_utils import run_bass_kernel_spmd

F32 = mybir.dt.float32
BF16 = mybir.dt.bfloat16
I32 = mybir.dt.int32
ALU = mybir.AluOpType
AF = mybir.ActivationFunctionType
EPS = 1e-6
import os
NSUB = int(os.environ.get('KNSUB', '17'))
NRING = 4
SEM_EPOCH = 4000
NEPOCH = {'pe': 10, 'dve': 5, 'act': 3, 'pool': 1}
KSUB = int(os.environ.get('KSUB', '9'))
KSTAGE = int(os.environ.get('KSTAGE', '9'))
TWO_PI = 2.0 * math.pi


class Buf:
    __slots__ = ('name', 'w', 'r', 'excl')

    def __init__(self, name, excl=False):
        self.name = name
        self.w = None
        self.r = {}
        self.excl = excl


class TB:
    def __init__(self, t, name, bufs=None):
        self.t = t
        self.bs = bufs if bufs is not None else [Buf(name)]

    def __getitem__(self, k):
        return self.t[k]


class Sched:
    def __init__(self, sems):
        self.sems = sems
        self.cnt = {k: 0 for k in sems}
        self.prog = {'pe': [], 'dve': [], 'act': [], 'pool': [], 'sp': []}
        self.waited = {e: {} for e in self.prog}
        self.dma_rr = {'sp': 0, 'pool': 0, 'act': 0}
        self.epoch = {}
        self.last_pe = None
        self.dma_names = {q: sorted(n for n in sems if n.startswith('d_%s_' % q)) for q in ('sp', 'pool', 'act')}

    def _need(self, eng, s, v, force=False):
        if eng == 'pe' and s.startswith('pe_') and not force:
            return
        if self.waited[eng].get(s, 0) < v:
            self.prog[eng].append(('w', s, v))
            self.waited[eng][s] = v

    def _deps(self, eng, reads, writes):
        for b in reads:
            if b.w is not None:
                self._need(eng, *b.w)
        for b in writes:
            if b.w is not None:
                self._need(eng, *b.w)
            for s, v in b.r.items():
                self._need(eng, s, v)

    def _done(self, tok, reads, writes):
        for b in writes:
            b.w = tok
            b.r = {}
        for b in reads:
            if b.r.get(tok[0], 0) < tok[1]:
                b.r[tok[0]] = tok[1]

    def op(self, eng, fn, reads=(), writes=(), pe_sync=False):
        if pe_sync and self.last_pe is not None:
            self._need('pe', self.last_pe[0], self.last_pe[1], force=True)
        reads = [b for x in reads for b in x.bs]
        writes = [b for x in writes for b in x.bs]
        writes = writes + [b for b in reads if b.excl and b not in writes]
        reads = [b for b in reads if not b.excl]
        self._deps(eng, reads, writes)
        ep = self.epoch.setdefault(eng, 0)
        s = '%s_%d' % (eng, ep)
        if self.cnt[s] >= SEM_EPOCH:
            ep += 1
            self.epoch[eng] = ep
            s = '%s_%d' % (eng, ep)
        self.cnt[s] += 1
        tok = (s, self.cnt[s])
        self.prog[eng].append(('o', fn, s, 1))
        if eng == 'pe':
            self.last_pe = tok
        self._done(tok, reads, writes)

    def dma(self, q, fn, reads=(), writes=()):
        reads = [b for x in reads for b in x.bs]
        writes = [b for x in writes for b in x.bs]
        names = self.dma_names[q]
        s = names[self.dma_rr[q] % len(names)]
        self.dma_rr[q] += 1
        if self.cnt[s] > 0:
            self._need(q, s, self.cnt[s])
        self._deps(q, reads, writes)
        self.cnt[s] += 16
        tok = (s, self.cnt[s])
        self.prog[q].append(('o', fn, s, 16))
        self._done(tok, reads, writes)

    def finish(self):
        for q in ('sp', 'pool', 'act'):
            for s in self.dma_names[q]:
                if self.cnt[s] > 0:
                    self._need('sp', s, self.cnt[s])
        for s in self.sems:
            if not s.startswith('d_') and self.cnt[s] > 0:
                self._need('sp', s, self.cnt[s])

    def emit(self, nc, block):
        sems = self.sems

        def replay(engobj, prog):
            for it in prog:
                if it[0] == 'w':
                    engobj.wait_ge(sems[it[1]], it[2])
                else:
                    it[1](engobj).then_inc(sems[it[2]], it[3])

        @block.tensor
        def _(e):
            replay(e, self.prog['pe'])

        @block.vector
        def _(e):
            replay(e, self.prog['dve'])

        @block.scalar
        def _(e):
            replay(e, self.prog['act'])

        @block.gpsimd
        def _(e):
            replay(e, self.prog['pool'])

        @block.sync
        def _(e):
            replay(e, self.prog['sp'])


def _consts():
    p = np.arange(128)
    c = {}
    c['ident'] = np.eye(128)
    for nm, bs in (('P', 128), ('S', 8), ('H', 32)):
        same = (p[:, None] // bs) == (p[None, :] // bs)
        c['cT' + nm] = ((p[:, None] <= p[None, :]) & same)
        c['st' + nm] = ((p[None, :] < p[:, None]) & same)
        c['sm' + nm] = same
        nb = 128 // bs
        sel = np.zeros((128, 16))
        sel[p, p // bs] = 1.0
        c['sel' + nm] = sel
    c['iota'] = np.tile(np.arange(128)[None, :], (128, 1))
    c['pcol'] = np.tile(p[:, None], (1, 2))
    c['pcol'][:, 1] = p % 8
    names = ['ident', 'cTP', 'cTS', 'cTH', 'stP', 'stS', 'smP', 'smS', 'smH', 'selP', 'selS', 'selH', 'iota', 'pcol']
    offs = {}
    cols = []
    o = 0
    for n in names:
        a = c[n].astype(np.float32)
        offs[n] = (o, a.shape[1])
        o += a.shape[1]
        cols.append(a)
    return np.concatenate(cols, axis=1), offs


CONSTS, COFF = _consts()
NCONST = CONSTS.shape[1]
PP = {}
_o = 0
for _n, _w in (('nmix', 8), ('nffn', 8), ('nple', 8), ('dcw', 48), ('fcw', 132), ('lre', 8), ('lim', 8), ('lst', 8),
               ('ssd', 2), ('glub', 2), ('hl0', 2), ('hl1', 2)):
    PP[_n] = (_o, _w)
    _o += _w
NPP = _o
BC = {}
_o = 0
for _n, _w in (('dnn', 128), ('hgn', 64), ('alog', 4), ('dtb', 4), ('hl0', 256), ('hl1', 256),
               ('lre', 1024), ('lim', 1024), ('lst', 1024), ('nfin', 1024)):
    BC[_n] = (_o, _w)
    _o += _w
NBC = _o
NBCS = 712


def build():
    nc = bass.Bass("TRN2", target_bir_lowering=False)

    def din(name, shape):
        return nc.dram_tensor(name, list(shape), F32, kind="ExternalInput").ap()

    def dout(name, shape):
        return nc.dram_tensor(name, list(shape), F32, kind="ExternalOutput").ap()

    xin = din("xin", [NSUB, 128, 1024])
    pin = din("pin", [2, NSUB, 128, 256])
    consts_d = din("consts", [128, NCONST])
    pp_d = din("pp", [2, 128, NPP])
    bc_d = din("bc", [2, NBC])
    w_in = din("w_in", [2, 1024, 3336])
    w_out = din("w_out", [2, 1024, 1024])
    w_up = din("w_up", [2, 1024, 5632])
    w_down = din("w_down", [2, 2816, 1024])
    w_pg = din("w_pg", [2, 1024, 1024])
    w_pp = din("w_pp", [2, 256, 1024])
    w_glu = din("w_glu", [2, 256, 256])
    bfull = din("bfull", [2, 2, 128, 2, 512])
    cfull = din("cfull", [2, 2, 128, 8, 32])
    s_cq = din("s_cq", [2, 128, 12, 16, 3])
    s_dl = din("s_dl", [2, 16, 4, 128, 128])
    s_ss = din("s_ss", [2, 2, 128, 8, 16])
    s_hg = din("s_hg", [2, 16, 4, 64, 64])
    s_cf = din("s_cf", [2, 128, 11, 4, 16, 2])
    y_d = dout("y", [NSUB, 128, 1024])
    o_pcq = dout("o_pcq", [2, 128, 12, 3])
    o_pdl = dout("o_pdl", [2, 4, 128, 128])
    o_pss = dout("o_pss", [2, 2, 128, 8])
    o_phg = dout("o_phg", [2, 128, 2, 64])
    o_pcf = dout("o_pcf", [2, 128, 11, 4, 2])
    o_scq = dout("o_scq", [2, 128, 12, 16, 3])
    o_sdl = dout("o_sdl", [2, 16, 4, 128, 128])
    o_sss = dout("o_sss", [2, 2, 128, 8, 16])
    o_shg = dout("o_shg", [2, 128, 2, 16, 64])
    o_scf = dout("o_scf", [2, 128, 11, 4, 16, 2])

    es = ExitStack()
    with es:
        def sb(name, shape, dt=F32):
            return TB(es.enter_context(nc.sbuf_tensor(name, list(shape), dt)), name)

        def ps(name, shape, dt=F32):
            return TB(es.enter_context(nc.psum_tensor(name, list(shape), dt)), name, bufs=[Buf(name, excl=True)])

        sems = {}
        for n in ['%s_%d' % (e_, i_) for e_ in NEPOCH for i_ in range(NEPOCH[e_])] + ['d_sp_%d' % i for i in range(8)] + ['d_pool_%d' % i for i in range(6)] + ['d_act_%d' % i for i in range(2)]:
            sems[n] = es.enter_context(nc.semaphore(n))
        S = Sched(sems)

        def V(fn, r=(), w=()):
            S.op('dve', fn, r, w)

        def A(fn, r=(), w=()):
            S.op('act', fn, r, w)

        def mm(out, lhsT, rhs, r, w, start=True, stop=True, skip=False, sync=False):
            S.op('pe', lambda e: e.matmul(out, lhsT=lhsT, rhs=rhs, start=start, stop=stop, skip_group_check=skip), r, w, pe_sync=sync)

        def tr(out, in_, ident, r, w):
            S.op('pe', lambda e: e.transpose(out=out, in_=in_, identity=ident), r, w)

        def vtt(out, a, b, op, r, w):
            V(lambda e: e.tensor_tensor(out=out, in0=a, in1=b, op=op), r, w)

        def vts(out, a, s1, op0, r, w, s2=None, op1=None):
            if op1 is None:
                V(lambda e: e.tensor_scalar(out=out, in0=a, scalar1=s1, scalar2=None, op0=op0), r, w)
            else:
                V(lambda e: e.tensor_scalar(out=out, in0=a, scalar1=s1, scalar2=s2, op0=op0, op1=op1), r, w)

        def vstt(out, a, sc, b, op0, op1, r, w):
            V(lambda e: e.scalar_tensor_tensor(out=out, in0=a, scalar=sc, in1=b, op0=op0, op1=op1), r, w)

        def vcopy(out, a, r, w):
            V(lambda e: e.tensor_copy(out=out, in_=a), r, w)

        def acopy(out, a, r, w):
            A(lambda e: e.copy(out=out, in_=a), r, w)

        def act(out, a, func, r, w, bias=None, scale=None, accum=None):
            kw = {}
            if bias is not None:
                kw['bias'] = bias
            if scale is not None:
                kw['scale'] = scale
            if accum is not None:
                kw['accum_out'] = accum
            A(lambda e: e.activation(out=out, in_=a, func=func, **kw), r, w)

        def dma(q, out, in_, r=(), w=()):
            S.dma(q, lambda e: e.dma_start(out=out, in_=in_), reads=r, writes=w)

        WB = {}

        def wreg(key, kc, n, parts):
            d = nc.dram_tensor("wb_" + "_".join(str(k) for k in key), [128, kc * n], BF16, kind="Internal").ap()
            tb = TB(None, "wb" + str(key))
            d3 = d.rearrange("p (k n) -> p k n", k=kc)
            for (src, c0, ncol) in parts:
                S.dma('pool', lambda e, d3=d3, src=src, c0=c0, ncol=ncol: e.dma_start(out=d3[:, :, c0:c0 + ncol], in_=src.rearrange("(k p) n -> p k n", p=128)), writes=[tb])
            WB[key] = (d3, tb, kc, n)
        for l in range(2):
            W = w_in[l]
            for g in range(3):
                wreg(('in', l, g), 8, 512, [(W[:, g * 512:(g + 1) * 512], 0, 512)])
            wreg(('in', l, 3), 8, 520, [(W[:, 1536:2056], 0, 520)])
            wreg(('in', l, 4), 8, 512, [(W[:, 2056:2568], 0, 512)])
            wreg(('in', l, 5), 8, 512, [(W[:, 2568:3080], 0, 512)])
            wreg(('in', l, 6), 8, 256, [(W[:, 3080:3336], 0, 256)])
            for half in range(2):
                wreg(('out', l, half), 8, 512, [(w_out[l][:, half * 512:(half + 1) * 512], 0, 512)])
            for blk in range(11):
                wreg(('up', l, blk), 8, 512, [(w_up[l][:, blk * 256:(blk + 1) * 256], 0, 256),
                                              (w_up[l][:, 2816 + blk * 256:2816 + (blk + 1) * 256], 256, 256)])
            for half in range(2):
                for q4 in range(4):
                    n_c = 6 if q4 < 3 else 4
                    c0 = q4 * 6
                    wreg(('down', l, half, q4), n_c, 512, [(w_down[l][c0 * 128:(c0 + n_c) * 128, half * 512:(half + 1) * 512], 0, 512)])
                wreg(('pg', l, half), 8, 512, [(w_pg[l][:, half * 512:(half + 1) * 512], 0, 512)])
                wreg(('pp', l, half), 2, 512, [(w_pp[l][:, half * 512:(half + 1) * 512], 0, 512)])

        cst = sb("cst", [128, NCONST])
        cstb = sb("cstb", [128, NCONST], BF16)
        dma('sp', cst[:], consts_d, w=[cst])
        vcopy(cstb[:], cst[:], [cst], [cstb])

        def C(n, bf=False, cols=None):
            o, wd = COFF[n]
            if cols is not None:
                wd = cols
            return (cstb if bf else cst)[:, o:o + wd]

        identf = C('ident')
        identb = C('ident', True)
        onesb = C('smP', True)
        onesf = C('smP')
        ppt = [sb("pp%d" % l, [128, NPP]) for l in range(2)]
        bcs = [sb("bcs%d" % l, [128, NBCS]) for l in range(2)]
        nfin = sb("nfin", [128, 1024])
        dma('sp', nfin[:], bc_d[0, BC['nfin'][0]:BC['nfin'][0] + 1024].partition_broadcast(128), w=[nfin])
        for l in range(2):
            dma('sp', ppt[l][:], pp_d[l], w=[ppt[l]])
            dma('sp', bcs[l][:], bc_d[l, 0:NBCS].partition_broadcast(128), w=[bcs[l]])

        def pp(l, n, a=0, b=None):
            o, wd = PP[n]
            return ppt[l][:, o + a:o + (wd if b is None else b)]

        def bcv(l, n, a=0, b=None):
            o, wd = BC[n]
            return bcs[l][:, o + a:o + (wd if b is None else b)]

        glw = [sb("glw%d" % l, [128, 2, 256], BF16) for l in range(2)]
        bft = [[sb("bf%d%d" % (l, r), [128, 2, 512], BF16) for r in range(2)] for l in range(2)]
        cft = [[sb("cf%d%d" % (l, r), [128, 8, 32], BF16) for r in range(2)] for l in range(2)]
        cftf = sb("cftf", [128, 8, 32])
        diagD = [sb("diagD%d" % l, [128, 2, 128], BF16) for l in range(2)]
        negA = [sb("negA%d" % l, [128, 4]) for l in range(2)]
        lbb = [sb("lbb%d" % l, [128, 256]) for l in range(2)]
        omlb = [sb("omlb%d" % l, [128, 256]) for l in range(2)]
        lbf = [sb("lbf%d" % l, [128, 2]) for l in range(2)]
        omlf = [sb("omlf%d" % l, [128, 2]) for l in range(2)]
        for l in range(2):
            dma('pool', glw[l][:], w_glu[l].rearrange("(c p) n -> p c n", p=128), w=[glw[l]])
            for r in range(2):
                dma('pool', bft[l][r][:], bfull[l, r], w=[bft[l][r]])
            dma('pool', cft[l][0][:], cfull[l, 0], w=[cft[l][0]])
            dma('sp', cftf[:], cfull[l, 1], w=[cftf])
            vts(cft[l][1][:], cftf[:], -1.0, ALU.mult, [cftf], [cft[l][1]])
            for cc in range(2):
                vts(diagD[l][:, cc, :], identf, pp(l, 'ssd', cc, cc + 1), ALU.mult, [cst, ppt[l]], [diagD[l]])
            act(negA[l][:], bcv(l, 'alog'), AF.Exp, [bcs[l]], [negA[l]])
            vts(negA[l][:], negA[l][:], -1.0, ALU.mult, [negA[l]], [negA[l]])
            if l == 0:
                V(lambda e: e.memset(lbb[0][:], 0.0), [], [lbb[0]])
                V(lambda e: e.memset(lbf[0][:], 0.0), [], [lbf[0]])
            else:
                vtt(lbb[1][:], bcv(1, 'hl1'), bcv(1, 'hl0'), ALU.subtract, [bcs[1]], [lbb[1]])
                act(lbb[1][:], lbb[1][:], AF.Sigmoid, [lbb[1]], [lbb[1]])
                vtt(lbf[1][:], pp(1, 'hl1'), pp(1, 'hl0'), ALU.subtract, [ppt[1]], [lbf[1]])
                act(lbf[1][:], lbf[1][:], AF.Sigmoid, [lbf[1]], [lbf[1]])
            vts(omlb[l][:], lbb[l][:], -1.0, ALU.mult, [lbb[l]], [omlb[l]], 1.0, ALU.add)
            vts(omlf[l][:], lbf[l][:], -1.0, ALU.mult, [lbf[l]], [omlf[l]], 1.0, ALU.add)

        ZB = [es.enter_context(nc.sbuf_tensor("ZB%d" % i, [128, 2048], F32)) for i in range(4)]
        PZ = [TB(ZB[i // 4][:, (i % 4) * 512:(i % 4 + 1) * 512], "PZ%d" % i) for i in range(16)]
        tA, tB_, tC, tD, mag, bcA, bcP, cre, cim, lbr, lbi, bs_lre, bs_lim, bs_lst = PZ[0:14]
        bsvd = {'lre': bs_lre, 'lim': bs_lim, 'lst': bs_lst}
        h = sb("h", [128, 1024])
        tI = TB(h[:, 0:512].bitcast(I32), "tI", bufs=h.bs)

        def sincos(dst, ang_r, shift, rr, ww):
            vts(tC[:], ang_r, shift, ALU.add, rr, [tC])
            vcopy(tI[:], tC[:], [tC], [tI])
            vcopy(tD[:], tI[:], [tI], [tD])
            vtt(tC[:], tC[:], tD[:], ALU.subtract, [tC, tD], [tC])
            act(dst, tC[:], AF.Sin, [tC], ww, scale=6.283185)

        LpT = [[sb("LpT%d%d" % (l, r), [128, 8, 128], BF16) for r in range(2)] for l in range(2)]
        Lm = [[sb("Lm%d%d" % (l, r), [128, 1024], BF16) for r in range(2)] for l in range(2)]
        lbar = [[sb("lbar%d%d" % (l, r), [128, 8]) for r in range(2)] for l in range(2)]
        sm_a = sb("sm_a", [128, 8])
        sm_p = sb("sm_p", [128, 8])
        iota3 = C('iota').unsqueeze(1).to_broadcast([128, 4, 128])

        def t3(T):
            return T[:].rearrange("p (j t) -> p j t", j=4)

        def build_tables(l, samp):
            if not samp:
                act(sm_p[:], pp(l, 'lst'), AF.Exp, [ppt[l]], [sm_p])
                vtt(sm_a[:], pp(l, 'lre'), sm_p[:], ALU.mult, [ppt[l], sm_p], [sm_a])
                vstt(sm_p[:], pp(l, 'lim'), 1.0 / TWO_PI, sm_p[:], ALU.mult, ALU.mult, [ppt[l], sm_p], [sm_p])
            for hf in range(2):
                js = slice(4 * hf, 4 * hf + 4)
                cs = slice(512 * hf, 512 * hf + 512)

                def bsv(n):
                    return bsvd[n][:]
                for n_ in ('lre', 'lim', 'lst'):
                    o_ = BC[n_][0] + 512 * hf
                    dma('sp', bsvd[n_][:], bc_d[l, o_:o_ + 512].partition_broadcast(128), w=[bsvd[n_]])
                if not samp:
                    vtt(t3(tA), iota3, sm_a[:, js].unsqueeze(2).to_broadcast([128, 4, 128]), ALU.mult, [cst, sm_a], [tA])
                    act(mag[:], tA[:], AF.Exp, [tA], [mag])
                    vtt(t3(tB_), iota3, sm_p[:, js].unsqueeze(2).to_broadcast([128, 4, 128]), ALU.mult, [cst, sm_p], [tB_])
                    sincos(tA[:], tB_[:], 0.25, [tB_], [tA])
                    vtt(LpT[l][0][:, js, :], t3(tA), t3(mag), ALU.mult, [tA, mag], [LpT[l][0]])
                    vtt(t3(cre), t3(tA), t3(mag), ALU.mult, [tA, mag], [cre])
                    vcopy(lbar[l][0][:, js], t3(cre)[:, :, 1], [cre], [lbar[l][0]])
                    sincos(tA[:], tB_[:], 0.0, [tB_], [tA])
                    vtt(LpT[l][1][:, js, :], t3(tA), t3(mag), ALU.mult, [tA, mag], [LpT[l][1]])
                    vtt(t3(cre), t3(tA), t3(mag), ALU.mult, [tA, mag], [cre])
                    vcopy(lbar[l][1][:, js], t3(cre)[:, :, 1], [cre], [lbar[l][1]])
                act(bcP[:], bsv('lst'), AF.Exp, [bs_lst], [bcP])
                vtt(bcA[:], bsv('lre'), bcP[:], ALU.mult, [bs_lre, bcP], [bcA])
                vstt(bcP[:], bsv('lim'), 1.0 / TWO_PI, bcP[:], ALU.mult, ALU.mult, [bs_lim, bcP], [bcP])
                act(mag[:], bcA[:], AF.Exp, [bcA], [mag])
                sincos(tA[:], bcP[:], 0.25, [bcP], [tA])
                vtt(lbr[:], tA[:], mag[:], ALU.mult, [tA, mag], [lbr])
                sincos(tA[:], bcP[:], 0.0, [bcP], [tA])
                vtt(lbi[:], tA[:], mag[:], ALU.mult, [tA, mag], [lbi])
                vts(lbr[:], lbr[:], -1.0, ALU.add, [lbr], [lbr])
                vtt(tA[:], bsv('lre'), bsv('lre'), ALU.mult, [bs_lre], [tA])
                vtt(tB_[:], bsv('lim'), bsv('lim'), ALU.mult, [bs_lim], [tB_])
                vtt(tA[:], tA[:], tB_[:], ALU.add, [tA, tB_], [tA])
                V(lambda e: e.reciprocal(out=tA[:], in_=tA[:]), [tA], [tA])
                vtt(cre[:], lbr[:], bsv('lre'), ALU.mult, [lbr, bs_lre], [cre])
                vtt(tB_[:], lbi[:], bsv('lim'), ALU.mult, [lbi, bs_lim], [tB_])
                vtt(cre[:], cre[:], tB_[:], ALU.add, [cre, tB_], [cre])
                vtt(cre[:], cre[:], tA[:], ALU.mult, [cre, tA], [cre])
                vtt(cim[:], lbi[:], bsv('lre'), ALU.mult, [lbi, bs_lre], [cim])
                vtt(tB_[:], lbr[:], bsv('lim'), ALU.mult, [lbr, bs_lim], [tB_])
                vtt(cim[:], cim[:], tB_[:], ALU.subtract, [cim, tB_], [cim])
                vtt(cim[:], cim[:], tA[:], ALU.mult, [cim, tA], [cim])
                o_pc = COFF['pcol'][0] + (1 if samp else 0)
                sidx = cst[:, o_pc:o_pc + 1]
                vts(tA[:], bcA[:], sidx, ALU.mult, [bcA, cst], [tA], -1.0, ALU.mult)
                act(mag[:], tA[:], AF.Exp, [tA], [mag])
                vts(tB_[:], bcP[:], sidx, ALU.mult, [bcP, cst], [tB_], -1.0, ALU.mult)
                sincos(tA[:], tB_[:], 0.25, [tB_], [tA])
                vtt(lbr[:], tA[:], mag[:], ALU.mult, [tA, mag], [lbr])
                sincos(tA[:], tB_[:], 0.0, [tB_], [tA])
                vtt(lbi[:], tA[:], mag[:], ALU.mult, [tA, mag], [lbi])
                vtt(tA[:], lbr[:], cre[:], ALU.mult, [lbr, cre], [tA])
                vtt(tB_[:], lbi[:], cim[:], ALU.mult, [lbi, cim], [tB_])
                vtt(Lm[l][0][:, cs], tA[:], tB_[:], ALU.subtract, [tA, tB_], [Lm[l][0]])
                vtt(tA[:], lbr[:], cim[:], ALU.mult, [lbr, cim], [tA])
                vtt(tB_[:], lbi[:], cre[:], ALU.mult, [lbi, cre], [tB_])
                vtt(Lm[l][1][:, cs], tA[:], tB_[:], ALU.add, [tA, tB_], [Lm[l][1]])
        for l in range(2):
            build_tables(l, False)
        hnT = sb("hnT", [128, 8, 128], BF16)
        ocatT = sb("ocatT", [128, 8, 128], BF16)
        st = sb("st", [128, 8])
        ring = [sb("ring%d" % i, [128, 4160], BF16) for i in range(NRING)]
        ringi = [0]
        pT = ps("pT", [128, 8, 128], BF16)
        pFt = es.enter_context(nc.psum_tensor("pF", [128, 4, 128], F32))
        pFbuf = Buf("pF", excl=True)
        pF = [TB(pFt[:, i, :], "pF%d" % i, bufs=[pFbuf]) for i in range(4)]
        pM = [ps("pM%d" % i, [128, 512]) for i in range(6)]
        xpf = sb("xpf", [128, 4, 160])
        acc = sb("acc", [128, 4, 128])
        acc2 = sb("acc2", [128, 4, 128])
        oT_sb = TB(acc[0:64, :, :], "oT_sb", bufs=acc.bs)
        sa = sb("sa", [128, 2, 128])
        cfc = [sb("cfc%d" % l, [128, 11, 4, 2]) for l in range(2)]
        pf = sb("pf", [128, 256])
        pb = sb("pb", [128, 256], BF16)
        ppT = sb("ppT", [128, 2, 128], BF16)
        xpq = sb("xpq", [128, 12, 176], BF16)
        cq = [sb("cq%d" % l, [128, 12, 3]) for l in range(2)]
        ba = sb("ba", [128, 8])
        uT = sb("uT", [128, 2, 128], BF16)
        hqTs = sb("hqTs", [128, 2, 128])
        k_tok = sb("k_tok", [128, 4, 128], BF16)
        v_tok = sb("v_tok", [128, 4, 128], BF16)
        knT = sb("knT", [128, 4, 128], BF16)
        dsc = sb("dsc", [128, 64])
        gsel = sb("gsel", [128, 4, 16])
        glb = sb("glb", [128, 4, 16])
        gB = sb("gB", [128, 128])
        dtmp = sb("dtmp", [128, 128])
        dec = sb("dec", [128, 128])
        decT = sb("decT", [128, 128])
        Xf = sb("Xf", [128, 128])
        Xp = [sb("Xp%d" % i, [128, 128]) for i in range(2)]
        XpT = [sb("XpT%d" % i, [128, 128]) for i in range(2)]
        TT = sb("TT", [128, 128])
        Ru = sb("Ru", [128, 128])
        Rw = sb("Rw", [128, 128])
        u_sb = sb("u_sb", [128, 128])
        wT_b = sb("wT_b", [128, 128], BF16)
        qkTm = sb("qkTm", [128, 128], BF16)
        wq_sb = sb("wq_sb", [128, 2, 128])
        vnew = sb("vnew", [128, 128], BF16)
        t1s = sb("t1s", [128, 128])
        kdec = sb("kdec", [128, 128], BF16)
        oA = sb("oA", [128, 4, 128])
        oab = sb("oab", [128, 4, 128], BF16)
        Sd = [sb("Sd%d" % l, [128, 4, 128]) for l in range(2)]
        Sd_b = [sb("Sdb%d" % l, [128, 4, 128], BF16) for l in range(2)]
        sS_b = sb("sSb", [128, 16, 128], BF16)
        cinP = [[sb("cinP%d%d" % (l, r), [128, 8, 1]) for r in range(2)] for l in range(2)]
        cinS = [sb("cinS%d" % r, [128, 8, 16]) for r in range(2)]
        x0s = [sb("x0s%d" % r, [128, 8, 16]) for r in range(2)]
        xe = [sb("xe%d" % r, [128, 8, 16]) for r in range(2)]
        xep = [sb("xep%d" % r, [128, 8]) for r in range(2)]
        ybT = sb("ybT", [128, 2, 128])
        ybTb = sb("ybTb", [128, 2, 128], BF16)
        sgT = sb("sgT", [128, 128])
        v_h = sb("v_h", [128, 256], BF16)
        fT = sb("fT", [128, 2, 128])
        qtT = sb("qtT", [128, 2, 128], BF16)
        ktT = sb("ktT", [128, 2, 128], BF16)
        e1 = sb("e1", [128, 2, 128])
        khat = sb("khat", [128, 256], BF16)
        glh = sb("glh", [128, 2, 16])
        aTm = sb("aTm", [128, 4, 128], BF16)
        ocb = sb("ocb", [128, 256], BF16)
        Sh = [sb("Sh%d" % l, [128, 2, 64]) for l in range(2)]
        Sh_b = [sb("Shb%d" % l, [128, 2, 64], BF16) for l in range(2)]
        Shs_b = TB(sS_b[:].rearrange("p b v -> p (b v)").rearrange("p (c b v) -> p c b v", c=2, b=16), "Shsb", bufs=sS_b.bs)
        gT = sb("gT", [128, 22, 128], BF16)
        qkv_b = TB(gT[:, 0:12, :], "qkv_b", bufs=gT.bs)
        hb = TB(gT[:, 14:22, :].rearrange("p c t -> p (c t)"), "hb", bufs=gT.bs)
        sq = TB(gT[:, 12:20, :], "sq", bufs=gT.bs)
        def bufs_of(i0, i1):
            return [b for p_ in PZ[i0:i1] for b in p_.bs]
        sS = TB(ZB[1][:].rearrange("p (b v) -> p b v", b=16), "sS", bufs=bufs_of(4, 8))
        Shs = TB(ZB[1][:].rearrange("p (c b v) -> p c b v", c=2, b=16), "Shs", bufs=bufs_of(4, 8))
        scf = TB(ZB[1][:, 0:1408].rearrange("p (k q b t) -> p k q b t", k=11, q=4, b=16), "scf", bufs=bufs_of(4, 8))
        scq = TB(ZB[1][:, 1408:1984].rearrange("p (c b t) -> p c b t", c=12, b=16), "scq", bufs=bufs_of(4, 8))
        xre = TB(ZB[2][:, 0:1024].rearrange("p (j t) -> p j t", j=8), "xre", bufs=bufs_of(8, 10))
        xim = TB(ZB[2][:, 1024:2048].rearrange("p (j t) -> p j t", j=8), "xim", bufs=bufs_of(10, 12))
        yb0 = TB(ZB[3][:, 0:256], "yb0", bufs=PZ[12].bs)
        yb = TB(ZB[3][:, 256:512], "yb", bufs=PZ[12].bs)
        fto = TB(ZB[3][:, 512:768], "fto", bufs=PZ[13].bs)
        logf = TB(ZB[3][:, 768:1024], "logf", bufs=PZ[13].bs)
        omf = TB(ZB[3][:, 1024:1280], "omf", bufs=PZ[14].bs)
        b_sb = TB(ZB[3][:, 1280:1536], "b_sb", bufs=PZ[14].bs)
        oc = TB(ZB[3][:, 1536:1792], "oc", bufs=PZ[15].bs)
        hg_s = TB(ZB[3][:, 1792:2048], "hg_s", bufs=PZ[15].bs)
        gate_s = tD
        Wre = sb("Wre", [128, 1024], BF16)
        Wim = sb("Wim", [128, 1024], BF16)
        xbre = TB(Wre[:].rearrange("p (j t) -> p j t", j=8), "xbre", bufs=Wre.bs)
        xbim = TB(Wim[:].rearrange("p (j t) -> p j t", j=8), "xbim", bufs=Wim.bs)
        kdm = [sb("kdm%d" % i, [128, 128], BF16) for i in range(2)]
        khm = [sb("khm%d" % i, [128, 256], BF16) for i in range(2)]
        gsbh = [tA, tB_]
        yoh = [tC, tD]
        tmp1 = tD
        for l in range(2):
            for t_ in (cfc[l], cq[l], Sd[l], Sd_b[l], Sh[l], Sh_b[l], cinP[l][0], cinP[l][1]):
                V(lambda e, t_=t_: e.memset(t_[:], 0.0), [], [t_])

        class _RS:
            pass
        RS0 = _RS()
        RS0.dec, RS0.decT, RS0.TT, RS0.Ru, RS0.Rw, RS0.u_sb, RS0.t1s = dec, decT, TT, Ru, Rw, u_sb, t1s
        RS0.wT_b, RS0.qkTm, RS0.vnew, RS0.kdec, RS0.wq_sb, RS0.Xp, RS0.XpT = wT_b, qkTm, vnew, kdec, wq_sb, Xp, XpT
        RS0.pF, RS0.pU, RS0.pP = pF, pM[2], pM[5]
        RS1 = _RS()
        for n_ in ('dec', 'decT', 'TT', 'Ru', 'Rw', 'u_sb', 't1s'):
            setattr(RS1, n_, sb(n_ + "_1", [128, 128]))
        for n_ in ('wT_b', 'qkTm', 'vnew', 'kdec'):
            setattr(RS1, n_, sb(n_ + "_1", [128, 128], BF16))
        RS1.wq_sb = sb("wq_sb_1", [128, 2, 128])
        RS1.Xp = [sb("Xp1_%d" % i, [128, 128]) for i in range(2)]
        RS1.XpT = [sb("XpT1_%d" % i, [128, 128]) for i in range(2)]
        RS1.pF = [TB(pM[4][:, i * 128:(i + 1) * 128], "pF1_%d" % i, bufs=pM[4].bs) for i in range(4)]
        RS1.pU, RS1.pP = pM[0], pM[1]

        FB = [(xpf, acc, acc2, sa),
              (sb("xpf_b", [128, 4, 160]), sb("acc_b", [128, 4, 128]), sb("acc2_b", [128, 4, 128]), sb("sa_b", [128, 2, 128]))]

        def norm_stats():
            V(lambda e: e.memset(st[:, 0:1], 0.0), [], [st])
            act(hb[:], h[:], AF.Square, [h], [hb, st], accum=st[:, 0:1])
            vts(st[:, 1:2], st[:, 0:1], 1.0 / 1024, ALU.mult, [st], [st], EPS, ALU.add)
            act(st[:, 2:3], st[:, 1:2], AF.Sqrt, [st], [st])
            V(lambda e: e.reciprocal(out=st[:, 3:4], in_=st[:, 2:3]), [st], [st])

        def norm_T(gain_ap, gsrc):
            norm_stats()
            vts(hb[:], h[:], st[:, 3:4], ALU.mult, [h, st], [hb])
            for c in range(8):
                tr(pT[:, c, :], hb[:, c * 128:(c + 1) * 128], identb, [hb, cstb], [pT])
            vtt(hnT[:], pT[:], gain_ap.unsqueeze(2).to_broadcast([128, 8, 128]), ALU.mult, [pT, gsrc], [hnT])

        def wslot(kc, n):
            slot = ring[ringi[0] % NRING]
            ringi[0] += 1
            return slot, slot[:, 0:kc * n].rearrange("p (k n) -> p k n", k=kc)

        def wload(key):
            d3, tb, kc, n = WB[key]
            slot, wv = wslot(kc, n)
            S.dma('pool', lambda e: e.dma_start(out=wv, in_=d3), reads=[tb], writes=[slot])
            return slot, wv

        def rsqrt_small(dst, src, rr, ww, mul, add):
            vts(dst, src, mul, ALU.mult, rr, ww, add, ALU.add)
            act(dst, dst, AF.Sqrt, ww, ww)
            V(lambda e: e.reciprocal(out=dst, in_=dst), ww, ww)

        def in_proj(l, samp):
            norm_T(pp(l, 'nmix'), ppt[l])
            W = w_in[l]
            if samp:
                dma('pool', scq[:], s_cq[l], w=[scq])
                xq4 = xpq[:].rearrange("p c (b t) -> p c b t", b=16)
                vcopy(xq4[:, :, :, 0:3], scq[:], [scq], [xpq])
            else:
                vcopy(xpq[:, :, 0:3], cq[l][:], [cq[l]], [xpq])
            for g in range(3):
                slot, wv = wload(('in', l, g))
                pm = pM[g % 2]
                pm3 = pm[:].rearrange("p (q t) -> p q t", q=4)
                for q in range(4):
                    for kc in range(8):
                        mm(pm3[:, q, :], wv[:, kc, q * 128:(q + 1) * 128], hnT[:, kc, :], [slot, hnT], [pm], start=(kc == 0), stop=(kc == 7))
                if samp:
                    acopy(xq4[:, 4 * g:4 * g + 4, :, 3:11], pm[:].rearrange("p (q b t) -> p q b t", q=4, b=16), [pm], [xpq])
                else:
                    acopy(xpq[:, 4 * g:4 * g + 4, 3:131], pm3, [pm], [xpq])
            slot, wv = wload(('in', l, 3))
            for kc in range(8):
                mm(pM[2][:], hnT[:, kc, :], wv[:, kc, 0:512], [hnT, slot], [pM[2]], start=(kc == 0), stop=(kc == 7))
            for kc in range(8):
                mm(pM[3][:, 0:8], hnT[:, kc, :], wv[:, kc, 512:520], [hnT, slot], [pM[3]], start=(kc == 0), stop=(kc == 7))
            act(gate_s[:], pM[2][:], AF.Silu, [pM[2]], [gate_s])
            vcopy(ba[:], pM[3][:, 0:8], [pM[3]], [ba])
        def in_proj_rest(l, samp):
            W = w_in[l]
            slot, wv = wload(('in', l, 4))
            pm = pM[0]
            pm3 = pm[:].rearrange("p (q t) -> p q t", q=4)
            for q in range(4):
                for kc in range(8):
                    mm(pm3[:, q, :], wv[:, kc, q * 128:(q + 1) * 128], hnT[:, kc, :], [slot, hnT], [pm], start=(kc == 0), stop=(kc == 7))
            vcopy(uT[:], pm3[:, 0:2, :], [pm], [uT])
            yield
            act(hqTs[:], pm3[:, 2:4, :], AF.Silu, [pm], [hqTs])
            yield
            slot, wv = wload(('in', l, 5))
            for kc in range(8):
                mm(pM[4][:], hnT[:, kc, :], wv[:, kc, :], [hnT, slot], [pM[4]], start=(kc == 0), stop=(kc == 7))
            p53 = pM[5][:, 0:256].rearrange("p (q t) -> p q t", q=2)
            for q in range(2):
                for kc in range(8):
                    mm(p53[:, q, :], wv[:, kc, q * 128:(q + 1) * 128], hnT[:, kc, :], [slot, hnT], [pM[5]], start=(kc == 0), stop=(kc == 7))
            act(fto[:], pM[4][:, 0:256], AF.Sigmoid, [pM[4]], [fto])
            yield
            acopy(v_h[:], pM[4][:, 256:512], [pM[4]], [v_h])
            yield
            act(fT[:], p53, AF.Sigmoid, [pM[5]], [fT])
            yield
            slot, wv = wload(('in', l, 6))
            for kc in range(8):
                mm(pM[3][:, 0:256], hnT[:, kc, :], wv[:, kc, :], [hnT, slot], [pM[3]], start=(kc == 0), stop=(kc == 7))
            act(hg_s[:], pM[3][:, 0:256], AF.Silu, [pM[3]], [hg_s])
            yield


        def delta(l, samp, last):
            kind = 'S' if samp else 'P'
            nb = 16 if samp else 1
            bs = 128 // nb
            nlev = 3 if samp else 7
            cTf, cTb = C('cT' + kind), C('cT' + kind, True)
            stf = C('st' + kind)
            smf = C('sm' + kind)
            self_ = C('sel' + kind)
            dcw = ppt[l][:, PP['dcw'][0]:PP['dcw'][0] + 48].rearrange("p (c j) -> p c j", c=12)
            for g in range(3):
                if samp:
                    xq4 = xpq[:].rearrange("p c (b t) -> p c b t", b=16)
                    xs = [xq4[:, 4 * g:4 * g + 4, :, j:j + 8] for j in range(4)]
                    ws = [dcw[:, 4 * g:4 * g + 4, j:j + 1].unsqueeze(3).to_broadcast([128, 4, 16, 8]) for j in range(4)]
                    av = acc[:].rearrange("p q (b t) -> p q b t", b=16)
                    a2v = acc2[:].rearrange("p q (b t) -> p q b t", b=16)
                else:
                    xs = [xpq[:, 4 * g:4 * g + 4, j:j + 128] for j in range(4)]
                    ws = [dcw[:, 4 * g:4 * g + 4, j:j + 1].to_broadcast([128, 4, 128]) for j in range(4)]
                    av, a2v = acc[:], acc2[:]
                vtt(av, xs[0], ws[0], ALU.mult, [xpq, ppt[l]], [acc])
                yield
                for j in range(1, 4):
                    vtt(a2v, xs[j], ws[j], ALU.mult, [xpq, ppt[l]], [acc2])
                    yield
                    vtt(av, av, a2v, ALU.add, [acc, acc2], [acc])
                    yield
                act(qkv_b[:, 4 * g:4 * g + 4, :], acc[:], AF.Silu, [acc], [qkv_b])
                yield
            if samp:
                xq4 = xpq[:].rearrange("p c (b t) -> p c b t", b=16)
                vcopy(scq[:], xq4[:, :, :, 8:11], [xpq], [scq])
                yield
                dma('pool', o_scq[l], scq[:], r=[scq])
            else:
                vcopy(cq[l][:], xpq[:, :, 128:131], [xpq], [cq[l]])
                yield
                if last:
                    dma('pool', o_pcq[l], cq[l][:], r=[cq[l]])
            vtt(sq[:], qkv_b[:, 0:8, :], qkv_b[:, 0:8, :], ALU.mult, [qkv_b], [sq])
            yield
            for c in range(8):
                mm(pM[1][:, c:c + 1], sq[:, c, :], onesb[:, 0:1], [sq, cstb], [pM[1]])
            rsqrt_small(dsc[:, 0:8], pM[1][:, 0:8], [pM[1]], [dsc], 1.0, EPS)
            vts(dsc[:, 0:4], dsc[:, 0:4], 128.0 ** -0.5, ALU.mult, [dsc], [dsc])
            yield
            for hh in range(4):
                tr(pT[:, hh, :], qkv_b[:, 4 + hh, :], identb, [qkv_b, cstb], [pT])
                tr(pT[:, 4 + hh, :], qkv_b[:, 8 + hh, :], identb, [qkv_b, cstb], [pT])
            vtt(k_tok[:], pT[:, 0:4, :], dsc[:, 4:8].unsqueeze(2).to_broadcast([128, 4, 128]), ALU.mult, [pT, dsc], [k_tok])
            yield
            acopy(v_tok[:], pT[:, 4:8, :], [pT], [v_tok])
            yield
            for hh in range(4):
                tr(pT[:, hh, :], k_tok[:, hh, :], identb, [k_tok, cstb], [pT])
            acopy(knT[:], pT[:, 0:4, :], [pT], [knT])
            yield
            act(dsc[:, 8:12], ba[:, 0:4], AF.Sigmoid, [ba], [dsc])
            yield
            vts(dsc[:, 12:16], dsc[:, 8:12], -1.0, ALU.mult, [dsc], [dsc])
            yield
            vtt(dsc[:, 40:44], ba[:, 4:8], bcv(l, 'dtb'), ALU.add, [ba, bcs[l]], [dsc])
            yield
            act(dsc[:, 40:44], dsc[:, 40:44], AF.Exp, [dsc], [dsc])
            yield
            act(dsc[:, 40:44], dsc[:, 40:44], AF.Ln, [dsc], [dsc], bias=1.0)
            yield
            vtt(dsc[:, 16:20], dsc[:, 40:44], negA[l][:], ALU.mult, [dsc, negA[l]], [dsc])
            yield
            mm(pM[1][:, 8:12], cTf, dsc[:, 16:20], [cst, dsc], [pM[1]])
            mm(pM[1][:, 12:16], smf, dsc[:, 16:20], [cst, dsc], [pM[1]])
            vcopy(dsc[:, 20:24], pM[1][:, 8:12], [pM[1]], [dsc])
            yield
            vtt(dsc[:, 40:44], pM[1][:, 12:16], dsc[:, 20:24], ALU.subtract, [pM[1], dsc], [dsc])
            yield
            act(dsc[:, 24:28], dsc[:, 40:44], AF.Exp, [dsc], [dsc])
            yield
            act(dsc[:, 28:32], dsc[:, 20:24], AF.Exp, [dsc], [dsc])
            yield
            vtt(dsc[:, 32:36], dsc[:, 8:12], dsc[:, 28:32], ALU.mult, [dsc], [dsc])
            yield
            vtt(dsc[:, 36:40], dsc[:, 28:32], dsc[:, 0:4], ALU.mult, [dsc], [dsc])
            yield
            for hh in range(4):
                vts(gsel[:, hh, 0:nb], self_[:, 0:nb], dsc[:, 16 + hh:17 + hh], ALU.mult, [cst, dsc], [gsel])
                yield
            for hh in range(4):
                mm(pM[1][:, 16 + 16 * hh:16 + 16 * hh + nb], onesf, gsel[:, hh, 0:nb], [cst, gsel], [pM[1]])
            act(glb[:, :, 0:nb], pM[1][:, 16:80].rearrange("p (h b) -> p h b", h=4)[:, :, 0:nb], AF.Exp, [pM[1]], [glb])
            yield

            def head_body(hh, R):
                vts(gB[:], onesf, dsc[:, 16 + hh:17 + hh], ALU.mult, [cst, dsc], [gB])
                mm(R.pF[0][:], gB[:], cTf, [gB, cst], [R.pF[0]])
                vts(dtmp[:], R.pF[0][:], dsc[:, 20 + hh:21 + hh], ALU.subtract, [R.pF[0], dsc], [dtmp], 0.0, ALU.max)
                act(R.dec[:], dtmp[:], AF.Exp, [dtmp], [R.dec], scale=-1.0)
                yield
                vts(dtmp[:], R.pF[0][:], dsc[:, 20 + hh:21 + hh], ALU.subtract, [R.pF[0], dsc], [dtmp], 0.0, ALU.min)
                act(R.decT[:], dtmp[:], AF.Exp, [dtmp], [R.decT])
                yield
                vtt(R.decT[:], R.decT[:], cTf, ALU.mult, [R.decT, cst], [R.decT])
                mm(R.pF[1][:], knT[:, hh, :], knT[:, hh, :], [knT], [R.pF[1]])
                vstt(Xf[:], R.pF[1][:], dsc[:, 12 + hh:13 + hh], R.dec[:], ALU.mult, ALU.mult, [R.pF[1], dsc, R.dec], [Xf])
                vtt(R.Xp[0][:], Xf[:], stf, ALU.mult, [Xf, cst], [R.Xp[0]])
                yield
                tr(R.pF[3][:], R.Xp[0][:], identf, [R.Xp[0], cst], [R.pF[3]])
                acopy(R.XpT[0][:], R.pF[3][:], [R.pF[3]], [R.XpT[0]])
                yield
                vtt(R.TT[:], R.XpT[0][:], identf, ALU.add, [R.XpT[0], cst], [R.TT])
                yield
                cur = 0
                for lev in range(1, nlev):
                    nxt = 1 - cur
                    lastlev = (lev == nlev - 1)
                    mm(R.pF[2][:], R.XpT[cur][:], R.Xp[cur][:], [R.XpT[cur], R.Xp[cur]], [R.pF[2]])
                    vcopy(R.Xp[nxt][:], R.pF[2][:], [R.pF[2]], [R.Xp[nxt]])
                    yield
                    if not lastlev:
                        mm(R.pF[3][:], R.Xp[cur][:], R.XpT[cur][:], [R.XpT[cur], R.Xp[cur]], [R.pF[3]])
                        acopy(R.XpT[nxt][:], R.pF[3][:], [R.pF[3]], [R.XpT[nxt]])
                        yield
                    mm(R.pF[1][:], R.Xp[nxt][:], R.TT[:], [R.Xp[nxt], R.TT], [R.pF[1]])
                    vtt(R.TT[:], R.TT[:], R.pF[1][:], ALU.add, [R.TT, R.pF[1]], [R.TT])
                    yield
                    cur = nxt
                vts(R.Ru[:], v_tok[:, hh, :], dsc[:, 8 + hh:9 + hh], ALU.mult, [v_tok, dsc], [R.Ru])
                vts(R.Rw[:], k_tok[:, hh, :], dsc[:, 32 + hh:33 + hh], ALU.mult, [k_tok, dsc], [R.Rw])
                mm(R.pU[:, 0:128], R.TT[:], R.Ru[:], [R.TT, R.Ru], [R.pU])
                mm(R.pU[:, 128:256], R.Rw[:], R.TT[:], [R.TT, R.Rw], [R.pU])
                acopy(R.u_sb[:], R.pU[:, 0:128], [R.pU], [R.u_sb])
                yield
                acopy(R.wT_b[:], R.pU[:, 128:256], [R.pU], [R.wT_b])
                yield
                mm(R.pF[2][:], knT[:, hh, :], qkv_b[:, hh, :], [knT, qkv_b], [R.pF[2]])
                vtt(R.qkTm[:], R.pF[2][:], R.decT[:], ALU.mult, [R.pF[2], R.decT], [R.qkTm])
                yield
                if samp:
                    dma('pool', sS[:], s_dl[l, :, hh].rearrange("b k v -> k b v"), w=[sS])
                    acopy(sS_b[:], sS[:], [sS], [sS_b])
                    yield
                p33 = R.pP[:, 256:512].rearrange("p (q t) -> p q t", q=2)
                for b in range(nb):
                    Sb = sS_b[:, b, :] if samp else Sd_b[l][:, hh, :]
                    Sbt = sS_b if samp else Sd_b[l]
                    mm(p33[:, 0, b * bs:(b + 1) * bs], Sb, R.wT_b[:, b * bs:(b + 1) * bs], [Sbt, R.wT_b], [R.pP])
                    mm(p33[:, 1, b * bs:(b + 1) * bs], Sb, qkv_b[:, hh, b * bs:(b + 1) * bs], [Sbt, qkv_b], [R.pP])
                acopy(R.wq_sb[:], p33, [R.pP], [R.wq_sb])
                yield
                tr(R.pF[0][:], R.wq_sb[:, 0, :], identf, [R.wq_sb, cst], [R.pF[0]])
                tr(R.pF[1][:], R.wq_sb[:, 1, :], identf, [R.wq_sb, cst], [R.pF[1]])
                vtt(R.vnew[:], R.u_sb[:], R.pF[0][:], ALU.subtract, [R.u_sb, R.pF[0]], [R.vnew])
                yield
                act(R.t1s[:], R.pF[1][:], AF.Identity, [R.pF[1], dsc], [R.t1s], scale=dsc[:, 36 + hh:37 + hh])
                yield
                mm(R.pF[2][:], R.qkTm[:], R.vnew[:], [R.qkTm, R.vnew], [R.pF[2]])
                vstt(oA[:, hh, :], R.pF[2][:], dsc[:, hh:hh + 1], R.t1s[:], ALU.mult, ALU.add, [R.pF[2], dsc, R.t1s], [oA])
                yield
                vts(R.kdec[:], k_tok[:, hh, :], dsc[:, 24 + hh:25 + hh], ALU.mult, [k_tok, dsc], [R.kdec])
                if samp:
                    for b in range(nb):
                        pd = R.pF[b % 2]
                        km = kdm[b % 2]
                        vts(km[:], R.kdec[:], self_[:, b:b + 1], ALU.mult, [R.kdec, cst], [km])
                        mm(pd[:], km[:], R.vnew[:], [km, R.vnew], [pd])
                        vstt(sS[:, b, :], sS[:, b, :], glb[:, hh, b:b + 1], pd[:], ALU.mult, ALU.add, [sS, glb, pd], [sS])
                        yield
                    dma('pool', o_sdl[l, :, hh].rearrange("b k v -> k b v"), sS[:], r=[sS])
                else:
                    mm(R.pF[0][:], R.kdec[:], R.vnew[:], [R.kdec, R.vnew], [R.pF[0]])
                    vstt(Sd[l][:, hh, :], Sd[l][:, hh, :], glb[:, hh, 0:1], R.pF[0][:], ALU.mult, ALU.add, [Sd[l], glb, R.pF[0]], [Sd[l]])
                    yield
                    acopy(Sd_b[l][:, hh, :], Sd[l][:, hh, :], [Sd[l]], [Sd_b[l]])
                    yield
                    if last:
                        dma('pool', o_pdl[l, hh], Sd[l][:, hh, :], r=[Sd[l]])
            def run_heads(gens):
                gens = list(gens)
                while gens:
                    for g_ in list(gens):
                        try:
                            next(g_)
                        except StopIteration:
                            gens.remove(g_)
            if samp:
                for hh in range(4):
                    run_heads([head_body(hh, RS0)])
            else:
                run_heads([head_body(0, RS0), head_body(1, RS1)])
                run_heads([head_body(2, RS0), head_body(3, RS1)])
            V(lambda e: e.memset(dsc[:, 44:48], 0.0), [], [dsc])
            for hh in range(4):
                act(dtmp[:], oA[:, hh, :], AF.Square, [oA], [dtmp, dsc], accum=dsc[:, 44 + hh:45 + hh])
            rsqrt_small(dsc[:, 44:48], dsc[:, 44:48], [dsc], [dsc], 1.0 / 128, EPS)
            for hh in range(4):
                vstt(oA[:, hh, :], oA[:, hh, :], dsc[:, 44 + hh:45 + hh], bcv(l, 'dnn'), ALU.mult, ALU.mult, [oA, dsc, bcs[l]], [oA])
            vtt(oab[:], oA[:], gate_s[:].rearrange("p (h d) -> p h d", h=4), ALU.mult, [oA, gate_s], [oab])
            for hh in range(4):
                tr(pT[:, hh, :], oab[:, hh, :], identb, [oab, cstb], [pT])
            acopy(ocatT[:, 0:4, :], pT[:, 0:4, :], [pT], [ocatT])
        def s5(l, samp, last):
            kind = 'S' if samp else 'P'
            nb = 16 if samp else 1
            bs = 128 // nb
            cTb = C('cT' + kind, True)
            Lmv = Lm[l]
            if samp:
                for r in range(2):
                    dma('pool', x0s[r][:], s_ss[l, r], w=[x0s[r]])
                for (dst, a_, b_, op) in ((cinS[0], 0, 1, ALU.subtract), (cinS[1], 1, 0, ALU.add)):
                    vtt(xe[0][:], x0s[0][:], lbar[l][a_][:].unsqueeze(2).to_broadcast([128, 8, 16]), ALU.mult, [x0s[0], lbar[l][a_]], [xe[0]])
                    yield
                    vtt(xe[1][:], x0s[1][:], lbar[l][b_][:].unsqueeze(2).to_broadcast([128, 8, 16]), ALU.mult, [x0s[1], lbar[l][b_]], [xe[1]])
                    yield
                    vtt(dst[:], xe[0][:], xe[1][:], op, [xe[0], xe[1]], [dst])
                    yield
                cin = cinS
            else:
                cin = cinP[l]
            for cc in range(2):
                hs = slice(cc * 512, (cc + 1) * 512)
                mm(pM[0][:], uT[:, cc, :], bft[l][0][:, cc, :], [uT, bft[l][0]], [pM[0]])
                mm(pM[1][:], uT[:, cc, :], bft[l][1][:, cc, :], [uT, bft[l][1]], [pM[1]])
                vtt(tA[:, 0:512], pM[0][:], Lmv[0][:, hs], ALU.mult, [pM[0], Lmv[0]], [tA])
                yield
                vtt(tB_[:, 0:512], pM[1][:], Lmv[1][:, hs], ALU.mult, [pM[1], Lmv[1]], [tB_])
                yield
                vtt(Wre[:, hs], tA[:, 0:512], tB_[:, 0:512], ALU.subtract, [tA, tB_], [Wre])
                yield
                vtt(tA[:, 0:512], pM[1][:], Lmv[0][:, hs], ALU.mult, [pM[1], Lmv[0]], [tA])
                yield
                vtt(tB_[:, 0:512], pM[0][:], Lmv[1][:, hs], ALU.mult, [pM[0], Lmv[1]], [tB_])
                yield
                vtt(Wim[:, hs], tA[:, 0:512], tB_[:, 0:512], ALU.add, [tA, tB_], [Wim])
                yield
                for jj in range(4):
                    j = 4 * cc + jj
                    mm(pM[2][:, jj * 128:(jj + 1) * 128], Wre[:, j * 128:(j + 1) * 128], cTb, [Wre, cstb], [pM[2]])
                    mm(pM[4][:, jj * 128:(jj + 1) * 128], Wim[:, j * 128:(j + 1) * 128], cTb, [Wim, cstb], [pM[4]])
                js = slice(4 * cc, 4 * cc + 4)

                def v4(ap):
                    return ap.rearrange("p (j b t) -> p j b t", j=4, b=nb)

                def v4b(ap3):
                    return ap3.rearrange("p j (b t) -> p j b t", b=nb)
                ar, ai = tC, tD
                vtt(v4(ar[:, 0:512]), v4(pM[2][:]), cin[0][:, js, :].unsqueeze(3).to_broadcast([128, 4, nb, bs]), ALU.add, [pM[2], cin[0]], [ar])
                yield
                vtt(v4(ai[:, 0:512]), v4(pM[4][:]), cin[1][:, js, :].unsqueeze(3).to_broadcast([128, 4, nb, bs]), ALU.add, [pM[4], cin[1]], [ai])
                yield
                if samp:
                    Lr = LpT[l][0][:, js, 0:8].unsqueeze(2).to_broadcast([128, 4, 16, 8])
                    Li = LpT[l][1][:, js, 0:8].unsqueeze(2).to_broadcast([128, 4, 16, 8])
                else:
                    Lr = v4b(LpT[l][0][:, js, :])
                    Li = v4b(LpT[l][1][:, js, :])
                vtt(v4(tA[:, 0:512]), v4(ar[:, 0:512]), Lr, ALU.mult, [ar, LpT[l][0]], [tA])
                yield
                vtt(v4(tB_[:, 0:512]), v4(ai[:, 0:512]), Li, ALU.mult, [ai, LpT[l][1]], [tB_])
                yield
                vtt(v4b(xre[:, js, :]), v4(tA[:, 0:512]), v4(tB_[:, 0:512]), ALU.subtract, [tA, tB_], [xre])
                yield
                vtt(v4(tA[:, 0:512]), v4(ar[:, 0:512]), Li, ALU.mult, [ar, LpT[l][1]], [tA])
                yield
                vtt(v4(tB_[:, 0:512]), v4(ai[:, 0:512]), Lr, ALU.mult, [ai, LpT[l][0]], [tB_])
                yield
                vtt(v4b(xim[:, js, :]), v4(tA[:, 0:512]), v4(tB_[:, 0:512]), ALU.add, [tA, tB_], [xim])
                yield
            acopy(xbre[:], xre[:], [xre], [xbre])
            yield
            acopy(xbim[:], xim[:], [xim], [xbim])
            yield
            xr4 = xre[:].rearrange("p j (b t) -> p j b t", b=nb)
            xi4 = xim[:].rearrange("p j (b t) -> p j b t", b=nb)
            vcopy(xe[0][:, :, 0:nb], xr4[:, :, :, bs - 1], [xre], [xe[0]])
            yield
            vcopy(xe[1][:, :, 0:nb], xi4[:, :, :, bs - 1], [xim], [xe[1]])
            yield
            if samp:
                for r in range(2):
                    dma('pool', o_sss[l, r], xe[r][:], r=[xe[r]])
            else:
                for r in range(2):
                    vcopy(xep[r][:], xe[r][:, :, 0], [xe[r]], [xep[r]])
                    yield
                if last:
                    for r in range(2):
                        dma('pool', o_pss[l, r], xep[r][:], r=[xep[r]])
                vtt(tA[:, 0:8], xep[0][:], lbar[l][0][:], ALU.mult, [xep[0], lbar[l][0]], [tA])
                yield
                vtt(tB_[:, 0:8], xep[1][:], lbar[l][1][:], ALU.mult, [xep[1], lbar[l][1]], [tB_])
                yield
                vtt(cinP[l][0][:, :, 0], tA[:, 0:8], tB_[:, 0:8], ALU.subtract, [tA, tB_], [cinP[l][0]])
                yield
                vtt(tA[:, 0:8], xep[0][:], lbar[l][1][:], ALU.mult, [xep[0], lbar[l][1]], [tA])
                yield
                vtt(tB_[:, 0:8], xep[1][:], lbar[l][0][:], ALU.mult, [xep[1], lbar[l][0]], [tB_])
                yield
                vtt(cinP[l][1][:, :, 0], tA[:, 0:8], tB_[:, 0:8], ALU.add, [tA, tB_], [cinP[l][1]])
                yield
            py = pM[0]
            for cc in range(2):
                mm(py[:, cc * 128:(cc + 1) * 128], uT[:, cc, :], diagD[l][:, cc, :], [uT, diagD[l]], [py], start=True, stop=False)
                for jj in range(4):
                    j = 4 * cc + jj
                    mm(py[:, j * 32:(j + 1) * 32], xbre[:, j, :], cft[l][0][:, j, :], [xbre, cft[l][0]], [py], start=False, stop=False)
                    mm(py[:, j * 32:(j + 1) * 32], xbim[:, j, :], cft[l][1][:, j, :], [xbim, cft[l][1]], [py], start=False, stop=(jj == 3))
            acopy(yb0[:], py[:, 0:256], [py], [yb0])
            yield
            vtt(yb[:], yb0[:], yb0[:], ALU.mult, [yb0], [yb])
            yield
            vts(yb[:], yb[:], 0.044715, ALU.mult, [yb], [yb], 1.0, ALU.add)
            yield
            vtt(yb[:], yb[:], yb0[:], ALU.mult, [yb, yb0], [yb])
            yield
            act(yb[:], yb[:], AF.Tanh, [yb], [yb], scale=0.7978845608028654)
            yield
            vstt(yb[:], yb[:], 1.0, yb0[:], ALU.add, ALU.mult, [yb, yb0], [yb])
            yield
            vts(yb[:], yb[:], 0.5, ALU.mult, [yb], [yb])
            yield
            for cc in range(2):
                tr(pF[2 + cc][:], yb[:, cc * 128:(cc + 1) * 128], identf, [yb, cst], [pF[2 + cc]])
                acopy(ybT[:, cc, :], pF[2 + cc][:], [pF[2 + cc]], [ybT])
                yield
            vcopy(ybTb[:], ybT[:], [ybT], [ybTb])
            yield
            for c2 in range(2):
                for cc in range(2):
                    mm(pF[2 + c2][:], glw[l][:, cc, c2 * 128:(c2 + 1) * 128], ybTb[:, cc, :], [glw[l], ybTb], [pF[2 + c2]], start=(cc == 0), stop=(cc == 1))
                act(sgT[:], pF[2 + c2][:], AF.Sigmoid, [pF[2 + c2], ppt[l]], [sgT], bias=pp(l, 'glub', c2, c2 + 1))
                yield
                vtt(ocatT[:, 4 + c2, :], ybT[:, c2, :], sgT[:], ALU.mult, [ybT, sgT], [ocatT])
                yield

        def hgrn(l, samp, last):
            kind = 'S' if samp else 'H'
            nb = 16 if samp else 4
            bs = 128 // nb
            cTf, cTb = C('cT' + kind), C('cT' + kind, True)
            smf = C('sm' + kind)
            self_ = C('sel' + kind)
            pTf = pT[:].rearrange("p c t -> p (c t)").bitcast(F32)
            pTfb = pT
            vtt(fto[:], fto[:], omlb[l][:], ALU.mult, [fto, omlb[l]], [fto])
            yield
            vtt(fto[:], fto[:], lbb[l][:], ALU.add, [fto, lbb[l]], [fto])
            yield
            act(logf[:], fto[:], AF.Ln, [fto], [logf])
            yield
            vts(omf[:], fto[:], -1.0, ALU.mult, [fto], [omf], 1.0, ALU.add)
            yield
            for cc in range(2):
                vts(fT[:, cc, :], fT[:, cc, :], omlf[l][:, cc:cc + 1], ALU.mult, [fT, omlf[l], lbf[l]], [fT], lbf[l][:, cc:cc + 1], ALU.add)
                yield
            vts(fT[:], fT[:], -1.0, ALU.mult, [fT], [fT], 1.0, ALU.add)
            yield
            mm(pM[5][:, 0:256], cTf, logf[:], [cst, logf], [pM[5]])
            mm(pM[5][:, 256:512], smf, logf[:], [cst, logf], [pM[5]])
            pbT = pTf[:, 0:256].rearrange("p (c t) -> p c t", c=2)
            for cc in range(2):
                mm(pbT[:, cc, :], logf[:, cc * 128:(cc + 1) * 128], cTf, [logf, cst], [pTfb])
                mm(pTf[:, 256 + 16 * cc:256 + 16 * cc + nb], logf[:, cc * 128:(cc + 1) * 128], self_[:, 0:nb], [logf, cst], [pTfb])
            act(e1[:], pbT, AF.Exp, [pTfb], [e1])
            yield
            vtt(qtT[:], hqTs[:], e1[:], ALU.mult, [hqTs, e1], [qtT])
            yield
            act(e1[:], pbT, AF.Exp, [pTfb], [e1], scale=-1.0)
            yield
            vtt(ktT[:], fT[:], e1[:], ALU.mult, [fT, e1], [ktT])
            yield
            act(glh[:, :, 0:nb], pTf[:, 256:288].rearrange("p (c b) -> p c b", c=2)[:, :, 0:nb], AF.Exp, [pTfb], [glh])
            yield
            acopy(b_sb[:], pM[5][:, 0:256], [pM[5]], [b_sb])
            yield
            vtt(b_sb[:], pM[5][:, 256:512], b_sb[:], ALU.subtract, [pM[5], b_sb], [b_sb])
            yield
            act(b_sb[:], b_sb[:], AF.Exp, [b_sb], [b_sb])
            yield
            vtt(khat[:], omf[:], b_sb[:], ALU.mult, [omf, b_sb], [khat])
            yield
            pa = pTf[:, 0:512].rearrange("p (h t) -> p h t", h=4)
            for hh in range(4):
                hl, hc = hh % 2, hh // 2
                mm(pa[:, hh, :], ktT[hl * 64:(hl + 1) * 64, hc, :], qtT[hl * 64:(hl + 1) * 64, hc, :], [ktT, qtT], [pTfb], sync=True)
            vtt(aTm[:], pa, cTf.unsqueeze(1).to_broadcast([128, 4, 128]), ALU.mult, [pTfb, cst], [aTm])
            yield
            po = pM[3][0:64, :].rearrange("p (h t) -> p h t", h=4)
            for hh in range(4):
                mm(po[:, hh, :], v_h[:, hh * 64:(hh + 1) * 64], aTm[:, hh, :], [v_h, aTm], [pM[3]], start=(hh == 0), stop=False, skip=True, sync=True)
            if samp:
                for hc_ in range(2):
                    dma('pool', Shs[:, hc_], s_hg[l][:, 2 * hc_:2 * hc_ + 2].rearrange("b hl k v -> (hl k) b v"), w=[Shs])
                acopy(Shs_b[:], Shs[:], [Shs], [Shs_b])
                yield
            for j in range(nb):
                for hh in range(4):
                    hl, hc = hh % 2, hh // 2
                    if samp:
                        Sb, Sbt = Shs_b[hl * 64:(hl + 1) * 64, hc, j, :], Shs_b
                    else:
                        Sb, Sbt = Sh_b[l][hl * 64:(hl + 1) * 64, hc, :], Sh_b[l]
                    mm(po[:, hh, j * bs:(j + 1) * bs], Sb, qtT[hl * 64:(hl + 1) * 64, hc, j * bs:(j + 1) * bs], [Sbt, qtT], [pM[3]], start=False, stop=True, skip=True, sync=True)
                kh = khm[j % 2]
                vts(kh[:], khat[:], self_[:, j:j + 1], ALU.mult, [khat, cst], [kh])
                yield
                for cc in range(2):
                    pd = pF[cc]
                    mm(pd[:], kh[:, cc * 128:(cc + 1) * 128], v_h[:, cc * 128:(cc + 1) * 128], [kh, v_h], [pd], sync=(cc == 0))
                    for hl in range(2):
                        ps_ = slice(hl * 64, (hl + 1) * 64)
                        if samp:
                            vstt(Shs[ps_, cc, j, :], Shs[ps_, cc, j, :], glh[ps_, cc, j:j + 1], pd[ps_, hl * 64:(hl + 1) * 64], ALU.mult, ALU.add, [Shs, glh, pd], [Shs])
                            yield
                        else:
                            vstt(Sh[l][ps_, cc, :], Sh[l][ps_, cc, :], glh[ps_, cc, j:j + 1], pd[ps_, hl * 64:(hl + 1) * 64], ALU.mult, ALU.add, [Sh[l], glh, pd], [Sh[l]])
                            yield
                if not samp:
                    acopy(Sh_b[l][:], Sh[l][:], [Sh[l]], [Sh_b[l]])
                    yield
            if samp:
                dma('pool', o_shg[l], Shs[:], r=[Shs])
            elif last:
                dma('pool', o_phg[l], Sh[l][:], r=[Sh[l]])
            acopy(oT_sb[:], po, [pM[3]], [oT_sb])
            yield
            poc = pM[5]
            for hh in range(4):
                tr(poc[:, hh * 64:(hh + 1) * 64], oT_sb[:, hh, :], identf[0:64, 0:64], [oT_sb, cst], [poc])
            V(lambda e: e.memset(dsc[:, 48:52], 0.0), [], [dsc])
            for hh in range(4):
                act(dtmp[:, 0:64], poc[:, hh * 64:(hh + 1) * 64], AF.Square, [poc], [dtmp, dsc], accum=dsc[:, 48 + hh:49 + hh])
                yield
            rsqrt_small(dsc[:, 48:52], dsc[:, 48:52], [dsc], [dsc], 1.0 / 64, EPS)
            for hh in range(4):
                vstt(oc[:, hh * 64:(hh + 1) * 64], poc[:, hh * 64:(hh + 1) * 64], dsc[:, 48 + hh:49 + hh], bcv(l, 'hgn'), ALU.mult, ALU.mult, [poc, dsc, bcs[l]], [oc])
                yield
            vtt(ocb[:], oc[:], hg_s[:], ALU.mult, [oc, hg_s], [ocb])
            yield
            for cc in range(2):
                tr(pT[:, cc, :], ocb[:, cc * 128:(cc + 1) * 128], identb, [ocb, cstb], [pT])
            acopy(ocatT[:, 6:8, :], pT[:, 0:2, :], [pT], [ocatT])
            yield

        def out_proj(l):
            for half in range(2):
                slot, wv = wload(('out', l, half))
                pm = pM[4 + half]
                for kc in range(8):
                    mm(pm[:], ocatT[:, kc, :], wv[:, kc, :], [ocatT, slot], [pm], start=(kc == 0), stop=(kc == 7))
                hs = slice(half * 512, (half + 1) * 512)
                vtt(h[:, hs], h[:, hs], pm[:], ALU.add, [h, pm], [h])

        def ffn(l, samp, last):
            norm_T(pp(l, 'nffn'), ppt[l])
            if samp:
                dma('pool', scf[:], s_cf[l], w=[scf])
            for blk in range(11):
                xpf, acc, acc2, sa = FB[blk % 2]
                xpf3 = xpf[:, :, 0:130]
                xpf4 = xpf[:].rearrange("p q (b t) -> p q b t", b=16)
                slot, wv = wload(('up', l, blk))
                pm = pM[blk % 2]
                pm3 = pm[:].rearrange("p (q t) -> p q t", q=4)
                for q in range(4):
                    for kc in range(8):
                        mm(pm3[:, q, :], wv[:, kc, q * 128:(q + 1) * 128], hnT[:, kc, :], [slot, hnT], [pm], start=(kc == 0), stop=(kc == 7))
                o_f = PP['fcw'][0] + blk * 12
                cw = ppt[l][:, o_f:o_f + 12].rearrange("p (q j) -> p q j", q=4)
                if not samp:
                    vcopy(xpf3[:, :, 0:2], cfc[l][:, blk, :, :], [cfc[l]], [xpf])
                    acopy(xpf3[:, :, 2:130], pm3, [pm], [xpf])
                    xs = [xpf3[:, :, j:j + 128] for j in range(3)]
                    ws = [cw[:, :, j:j + 1].to_broadcast([128, 4, 128]) for j in range(3)]
                    av, a2v = acc[:], acc2[:]
                else:
                    vcopy(xpf4[:, :, :, 0:2], scf[:, blk], [scf], [xpf])
                    acopy(xpf4[:, :, :, 2:10], pm[:].rearrange("p (q b t) -> p q b t", q=4, b=16), [pm], [xpf])
                    xs = [xpf4[:, :, :, j:j + 8] for j in range(3)]
                    ws = [cw[:, :, j:j + 1].unsqueeze(3).to_broadcast([128, 4, 16, 8]) for j in range(3)]
                    av = acc[:].rearrange("p q (b t) -> p q b t", b=16)
                    a2v = acc2[:].rearrange("p q (b t) -> p q b t", b=16)
                for q in range(4):
                    act(a2v[:, q], xs[1][:, q], AF.Identity, [xpf, ppt[l]], [acc2], scale=cw[:, q, 1:2])
                vtt(av, xs[0], ws[0], ALU.mult, [xpf, ppt[l]], [acc])
                vtt(av, av, a2v, ALU.add, [acc, acc2], [acc])
                vtt(a2v, xs[2], ws[2], ALU.mult, [xpf, ppt[l]], [acc2])
                vtt(av, av, a2v, ALU.add, [acc, acc2], [acc])
                if not samp:
                    vcopy(cfc[l][:, blk, :, :], xpf3[:, :, 128:130], [xpf], [cfc[l]])
                else:
                    vcopy(scf[:, blk], xpf4[:, :, :, 8:10], [xpf], [scf])
                act(sa[:], acc[:, 0:2, :], AF.Silu, [acc], [sa])
                vtt(gT[:, 2 * blk:2 * blk + 2, :], sa[:], acc[:, 2:4, :], ALU.mult, [sa, acc], [gT])
            if samp:
                dma('pool', o_scf[l], scf[:], r=[scf])
            elif last:
                dma('pool', o_pcf[l], cfc[l][:], r=[cfc[l]])
            for half in range(2):
                pm = pM[2 + half]
                for q4 in range(4):
                    slot, wv = wload(('down', l, half, q4))
                    n_c = 6 if q4 < 3 else 4
                    c0 = q4 * 6
                    for c in range(n_c):
                        mm(pm[:], gT[:, c0 + c, :], wv[:, c, :], [gT, slot], [pm], start=(c0 + c == 0), stop=(c0 + c == 21))
                hs = slice(half * 512, (half + 1) * 512)
                vtt(h[:, hs], h[:, hs], pm[:], ALU.add, [h, pm], [h])

        def ple(l, si):
            norm_T(pp(l, 'nple'), ppt[l])
            dma('pool', pf[:], pin[l, si], w=[pf])
            vcopy(pb[:], pf[:], [pf], [pb])
            for half in range(2):
                slot, wv = wload(('pg', l, half))
                pm = pM[half]
                for kc in range(8):
                    mm(pm[:], hnT[:, kc, :], wv[:, kc, :], [hnT, slot], [pm], start=(kc == 0), stop=(kc == 7))
                act(gsbh[half][:], pm[:], AF.Sigmoid, [pm], [gsbh[half]])
            for cc in range(2):
                tr(pT[:, cc, :], pb[:, cc * 128:(cc + 1) * 128], identb, [pb, cstb], [pT])
            vcopy(ppT[:], pT[:, 0:2, :], [pT], [ppT])
            for half in range(2):
                slot, wv = wload(('pp', l, half))
                pm = pM[2 + half]
                for cc in range(2):
                    mm(pm[:], ppT[:, cc, :], wv[:, cc, :], [ppT, slot], [pm], start=(cc == 0), stop=(cc == 1))
                hs = slice(half * 512, (half + 1) * 512)
                vtt(tmp1[:], gsbh[half][:], pm[:], ALU.mult, [gsbh[half], pm], [tmp1])
                vtt(h[:, hs], h[:, hs], tmp1[:], ALU.add, [h, tmp1], [h])

        for si in range(NSUB):
            samp = (si == NSUB - 1)
            last = (si == NSUB - 2)
            if samp:
                for l in range(2):
                    build_tables(l, True)
            dma('pool', h[:], xin[si], w=[h])
            for l in range(2):
                in_proj(l, samp)
                gens_ = [in_proj_rest(l, samp), delta(l, samp, last)]
                while gens_:
                    for g_ in list(gens_):
                        try:
                            next(g_)
                        except StopIteration:
                            gens_.remove(g_)
                gens_ = [s5(l, samp, last), hgrn(l, samp, last)]
                while gens_:
                    for g_ in list(gens_):
                        try:
                            next(g_)
                        except StopIteration:
                            gens_.remove(g_)
                if KSTAGE >= 5:
                    out_proj(l)
                ffn(l, samp, last)
                ple(l, si)
            norm_stats()
            for half in range(2):
                hs = slice(half * 512, (half + 1) * 512)
                vstt(yoh[half][:], h[:, hs], st[:, 3:4], nfin[:, hs], ALU.mult, ALU.mult, [h, st, nfin], [yoh[half]])
                dma('pool', y_d[si][:, hs], yoh[half][:], r=[yoh[half]])
        S.finish()
        with nc.Block() as block:
            S.emit(nc, block)
    return nc


def _fm(v, nchunk):
    return np.ascontiguousarray(v.reshape(nchunk, 128).T)


_NC_CACHE = {}


def _prepare(inputs):
    f = {k: np.asarray(v, dtype=np.float32) for k, v in inputs.items()}
    n = 8
    pps, bcs = [], []
    for l in range(2):
        pp = np.zeros((128, NPP), np.float32)

        def put(name, arr):
            o, w = PP[name]
            pp[:, o:o + w] = arr.reshape(128, w)
        put('nmix', _fm(f['norm_mix'][l], 8))
        put('nffn', _fm(f['norm_ffn'][l], 8))
        put('nple', _fm(f['norm_ple'][l], 8))
        put('dcw', f['dn_conv_w'][l].reshape(4, 12, 128).transpose(2, 1, 0))
        fw = f['ffn_conv_w'][l].reshape(3, 2, 11, 2, 128)
        put('fcw', fw.transpose(4, 2, 1, 3, 0))
        put('lre', _fm(f['ssm_lam_re'][l].reshape(-1), 8))
        put('lim', _fm(f['ssm_lam_im'][l].reshape(-1), 8))
        put('lst', _fm(np.repeat(f['ssm_log_step'][l], 64), 8))
        put('ssd', _fm(f['ssm_d'][l], 2))
        put('glub', _fm(f['ssm_glu_b'][l], 2))
        put('hl0', _fm(f['hg_lower'][0], 2))
        put('hl1', _fm(f['hg_lower'][1], 2))
        pps.append(pp)
        bc = np.zeros((NBC,), np.float32)

        def putb(name, arr):
            o, w = BC[name]
            bc[o:o + w] = arr.reshape(w)
        putb('dnn', f['dn_norm'][l])
        putb('hgn', f['hg_norm'][l])
        putb('alog', f['dn_a_log'][l])
        putb('dtb', f['dn_dt_bias'][l])
        putb('lre', f['ssm_lam_re'][l])
        putb('lim', f['ssm_lam_im'][l])
        putb('lst', np.repeat(f['ssm_log_step'][l], 64))
        putb('hl0', f['hg_lower'][0])
        putb('hl1', f['hg_lower'][1])
        putb('nfin', f['norm_final'])
        bcs.append(bc)
    pp_all = np.stack(pps)
    bc_all = np.stack(bcs)
    bfull = np.zeros((2, 2, 128, 2, 512), np.float32)
    cfull = np.zeros((2, 2, 128, 8, 32), np.float32)
    for l in range(2):
        for r, (bn, cn) in enumerate((('ssm_b_re', 'ssm_c_re'), ('ssm_b_im', 'ssm_c_im'))):
            for g in range(16):
                cc, gg = divmod(g, 8)
                j, gl = divmod(g, 2)
                bfull[l, r, gg * 16:(gg + 1) * 16, cc, (j % 4) * 128 + gl * 64:(j % 4) * 128 + gl * 64 + 64] = f[bn][l, g].T
                cfull[l, r, gl * 64:(gl + 1) * 64, j, gl * 16:(gl + 1) * 16] = f[cn][l, g].T
    in_maps = []
    for c in range(n):
        xs = np.concatenate([f['x_prompt'][c].reshape(16, 128, 1024),
                             f['x_sample'][c * 16:(c + 1) * 16].reshape(1, 128, 1024)], axis=0)
        ps_ = np.concatenate([f['p_prompt'][:, c].reshape(2, 16, 128, 256),
                              f['p_sample'][:, c * 16:(c + 1) * 16].reshape(2, 1, 128, 256)], axis=1)
        sl = slice(c * 16, (c + 1) * 16)
        s_cq = f['state_conv_qkv'][:, sl].reshape(2, 16, 3, 12, 128).transpose(0, 4, 3, 1, 2)
        s_ss = np.stack([f['state_ssm_re'][:, sl], f['state_ssm_im'][:, sl]], axis=1).reshape(2, 2, 16, 8, 128).transpose(0, 1, 4, 3, 2)
        s_cf = f['state_conv_ffn'][:, sl].reshape(2, 16, 2, 2, 11, 2, 128).transpose(0, 6, 4, 3, 5, 1, 2).reshape(2, 128, 11, 4, 16, 2)
        in_maps.append({
            "xin": np.ascontiguousarray(xs), "pin": np.ascontiguousarray(ps_), "consts": CONSTS, "pp": pp_all, "bc": bc_all,
            "w_in": f['w_in'], "w_out": f['w_out'], "w_up": f['ffn_w_up'], "w_down": f['ffn_w_down'],
            "w_pg": f['ple_w_gate'], "w_pp": f['ple_w_proj'], "w_glu": f['ssm_glu_w'], "bfull": bfull, "cfull": cfull,
            "s_cq": np.ascontiguousarray(s_cq), "s_dl": np.ascontiguousarray(f['state_delta'][:, sl]),
            "s_ss": np.ascontiguousarray(s_ss), "s_hg": np.ascontiguousarray(f['state_hgrn'][:, sl]),
            "s_cf": np.ascontiguousarray(s_cf),
        })
    return in_maps


def kernel(**inputs):
    n = 8
    in_maps = _prepare(inputs)
    if 'nc' not in _NC_CACHE:
        _NC_CACHE['nc'] = build()
    res = run_bass_kernel_spmd(_NC_CACHE['nc'], in_maps, core_ids=list(range(n))).results
    yp = np.stack([r["y"][:16].reshape(2048, 1024) for r in res])
    ys = np.concatenate([r["y"][16].reshape(16, 8, 1024) for r in res], axis=0)

    def cat(fn, axis=1):
        return np.ascontiguousarray(np.concatenate([fn(r) for r in res], axis=axis))
    p_cq = cat(lambda r: r["o_pcq"].transpose(0, 3, 2, 1).reshape(2, 1, 3, 1536))
    p_dl = cat(lambda r: r["o_pdl"].reshape(2, 1, 4, 128, 128))
    p_sr = cat(lambda r: r["o_pss"][:, 0].transpose(0, 2, 1).reshape(2, 1, 16, 64))
    p_si = cat(lambda r: r["o_pss"][:, 1].transpose(0, 2, 1).reshape(2, 1, 16, 64))
    p_hg = cat(lambda r: r["o_phg"].reshape(2, 2, 64, 2, 64).transpose(0, 3, 1, 2, 4).reshape(2, 1, 4, 64, 64))
    p_cf = cat(lambda r: r["o_pcf"].reshape(2, 128, 11, 2, 2, 2).transpose(0, 5, 3, 2, 4, 1).reshape(2, 1, 2, 5632))
    s_cq = cat(lambda r: r["o_scq"].transpose(0, 3, 4, 2, 1).reshape(2, 16, 3, 1536))
    s_dl = cat(lambda r: r["o_sdl"])
    s_sr = cat(lambda r: r["o_sss"][:, 0].transpose(0, 3, 2, 1).reshape(2, 16, 16, 64))
    s_si = cat(lambda r: r["o_sss"][:, 1].transpose(0, 3, 2, 1).reshape(2, 16, 16, 64))
    s_hg = cat(lambda r: r["o_shg"].reshape(2, 2, 64, 2, 16, 64).transpose(0, 4, 3, 1, 2, 5).reshape(2, 16, 4, 64, 64))
    s_cf = cat(lambda r: r["o_scf"].reshape(2, 128, 11, 2, 2, 16, 2).transpose(0, 5, 6, 3, 2, 4, 1).reshape(2, 16, 2, 5632))
    return (yp, ys, p_cq, p_dl, p_sr, p_si, p_hg, p_cf, s_cq, s_dl, s_sr, s_si, s_hg, s_cf)
```

```python
import math
from contextlib import ExitStack
import numpy as np
import concourse.bass as bass
import concourse.mybir as mybir
from concourse.bass_utils import run_bass_kernel_spmd

F32 = mybir.dt.float32
BF16 = mybir.dt.bfloat16
I32 = mybir.dt.int32
ALU = mybir.AluOpType
AF = mybir.ActivationFunctionType
EPS = 1e-6
import os
NSUB = int(os.environ.get('KNSUB', '17'))
NRING = 4
SEM_EPOCH = 4000
NEPOCH = {'pe': 10, 'dve': 5, 'act': 3, 'pool': 1}
KSUB = int(os.environ.get('KSUB', '9'))
KSTAGE = int(os.environ.get('KSTAGE', '9'))
TWO_PI = 2.0 * math.pi


class Buf:
    __slots__ = ('name', 'w', 'r', 'excl')

    def __init__(self, name, excl=False):
        self.name = name
        self.w = None
        self.r = {}
        self.excl = excl


class TB:
    def __init__(self, t, name, bufs=None):
        self.t = t
        self.bs = bufs if bufs is not None else [Buf(name)]

    def __getitem__(self, k):
        return self.t[k]


class Sched:
    def __init__(self, sems):
        self.sems = sems
        self.cnt = {k: 0 for k in sems}
        self.prog = {'pe': [], 'dve': [], 'act': [], 'pool': [], 'sp': []}
        self.waited = {e: {} for e in self.prog}
        self.dma_rr = {'sp': 0, 'pool': 0, 'act': 0}
        self.epoch = {}
        self.last_pe = None
        self.dma_names = {q: sorted(n for n in sems if n.startswith('d_%s_' % q)) for q in ('sp', 'pool', 'act')}

    def _need(self, eng, s, v, force=False):
        if eng == 'pe' and s.startswith('pe_') and not force:
            return
        if self.waited[eng].get(s, 0) < v:
            self.prog[eng].append(('w', s, v))
            self.waited[eng][s] = v

    def _deps(self, eng, reads, writes):
        for b in reads:
            if b.w is not None:
                self._need(eng, *b.w)
        for b in writes:
            if b.w is not None:
                self._need(eng, *b.w)
            for s, v in b.r.items():
                self._need(eng, s, v)

    def _done(self, tok, reads, writes):
        for b in writes:
            b.w = tok
            b.r = {}
        for b in reads:
            if b.r.get(tok[0], 0) < tok[1]:
                b.r[tok[0]] = tok[1]

    def op(self, eng, fn, reads=(), writes=(), pe_sync=False):
        if pe_sync and self.last_pe is not None:
            self._need('pe', self.last_pe[0], self.last_pe[1], force=True)
        reads = [b for x in reads for b in x.bs]
        writes = [b for x in writes for b in x.bs]
        writes = writes + [b for b in reads if b.excl and b not in writes]
        reads = [b for b in reads if not b.excl]
        self._deps(eng, reads, writes)
        ep = self.epoch.setdefault(eng, 0)
        s = '%s_%d' % (eng, ep)
        if self.cnt[s] >= SEM_EPOCH:
            ep += 1
            self.epoch[eng] = ep
            s = '%s_%d' % (eng, ep)
        self.cnt[s] += 1
        tok = (s, self.cnt[s])
        self.prog[eng].append(('o', fn, s, 1))
        if eng == 'pe':
            self.last_pe = tok
        self._done(tok, reads, writes)

    def dma(self, q, fn, reads=(), writes=()):
        reads = [b for x in reads for b in x.bs]
        writes = [b for x in writes for b in x.bs]
        names = self.dma_names[q]
        s = names[self.dma_rr[q] % len(names)]
        self.dma_rr[q] += 1
        if self.cnt[s] > 0:
            self._need(q, s, self.cnt[s])
        self._deps(q, reads, writes)
        self.cnt[s] += 16
        tok = (s, self.cnt[s])
        self.prog[q].append(('o', fn, s, 16))
        self._done(tok, reads, writes)

    def finish(self):
        for q in ('sp', 'pool', 'act'):
            for s in self.dma_names[q]:
                if self.cnt[s] > 0:
                    self._need('sp', s, self.cnt[s])
        for s in self.sems:
            if not s.startswith('d_') and self.cnt[s] > 0:
                self._need('sp', s, self.cnt[s])

    def emit(self, nc, block):
        sems = self.sems

        def replay(engobj, prog):
            for it in prog:
                if it[0] == 'w':
                    engobj.wait_ge(sems[it[1]], it[2])
                else:
                    it[1](engobj).then_inc(sems[it[2]], it[3])

        @block.tensor
        def _(e):
            replay(e, self.prog['pe'])

        @block.vector
        def _(e):
            replay(e, self.prog['dve'])

        @block.scalar
        def _(e):
            replay(e, self.prog['act'])

        @block.gpsimd
        def _(e):
            replay(e, self.prog['pool'])

        @block.sync
        def _(e):
            replay(e, self.prog['sp'])


def _consts():
    p = np.arange(128)
    c = {}
    c['ident'] = np.eye(128)
    for nm, bs in (('P', 128), ('S', 8), ('H', 32)):
        same = (p[:, None] // bs) == (p[None, :] // bs)
        c['cT' + nm] = ((p[:, None] <= p[None, :]) & same)
        c['st' + nm] = ((p[None, :] < p[:, None]) & same)
        c['sm' + nm] = same
        nb = 128 // bs
        sel = np.zeros((128, 16))
        sel[p, p // bs] = 1.0
        c['sel' + nm] = sel
    c['iota'] = np.tile(np.arange(128)[None, :], (128, 1))
    c['pcol'] = np.tile(p[:, None], (1, 2))
    c['pcol'][:, 1] = p % 8
    names = ['ident', 'cTP', 'cTS', 'cTH', 'stP', 'stS', 'smP', 'smS', 'smH', 'selP', 'selS', 'selH', 'iota', 'pcol']
    offs = {}
    cols = []
    o = 0
    for n in names:
        a = c[n].astype(np.float32)
        offs[n] = (o, a.shape[1])
        o += a.shape[1]
        cols.append(a)
    return np.concatenate(cols, axis=1), offs


CONSTS, COFF = _consts()
NCONST = CONSTS.shape[1]
PP = {}
_o = 0
for _n, _w in (('nmix', 8), ('nffn', 8), ('nple', 8), ('dcw', 48), ('fcw', 132), ('lre', 8), ('lim', 8), ('lst', 8),
               ('ssd', 2), ('glub', 2), ('hl0', 2), ('hl1', 2)):
    PP[_n] = (_o, _w)
    _o += _w
NPP = _o
BC = {}
_o = 0
for _n, _w in (('dnn', 128), ('hgn', 64), ('alog', 4), ('dtb', 4), ('hl0', 256), ('hl1', 256),
               ('lre', 1024), ('lim', 1024), ('lst', 1024), ('nfin', 1024)):
    BC[_n] = (_o, _w)
    _o += _w
NBC = _o
NBCS = 712


def build():
    nc = bass.Bass("TRN2", target_bir_lowering=False)

    def din(name, shape):
        return nc.dram_tensor(name, list(shape), F32, kind="ExternalInput").ap()

    def dout(name, shape):
        return nc.dram_tensor(name, list(shape), F32, kind="ExternalOutput").ap()

    xin = din("xin", [NSUB, 128, 1024])
    pin = din("pin", [2, NSUB, 128, 256])
    consts_d = din("consts", [128, NCONST])
    pp_d = din("pp", [2, 128, NPP])
    bc_d = din("bc", [2, NBC])
    w_in = din("w_in", [2, 1024, 3336])
    w_out = din("w_out", [2, 1024, 1024])
    w_up = din("w_up", [2, 1024, 5632])
    w_down = din("w_down", [2, 2816, 1024])
    w_pg = din("w_pg", [2, 1024, 1024])
    w_pp = din("w_pp", [2, 256, 1024])
    w_glu = din("w_glu", [2, 256, 256])
    bfull = din("bfull", [2, 2, 128, 2, 512])
    cfull = din("cfull", [2, 2, 128, 8, 32])
    s_cq = din("s_cq", [2, 128, 12, 16, 3])
    s_dl = din("s_dl", [2, 16, 4, 128, 128])
    s_ss = din("s_ss", [2, 2, 128, 8, 16])
    s_hg = din("s_hg", [2, 16, 4, 64, 64])
    s_cf = din("s_cf", [2, 128, 11, 4, 16, 2])
    y_d = dout("y", [NSUB, 128, 1024])
    o_pcq = dout("o_pcq", [2, 128, 12, 3])
    o_pdl = dout("o_pdl", [2, 4, 128, 128])
    o_pss = dout("o_pss", [2, 2, 128, 8])
    o_phg = dout("o_phg", [2, 128, 2, 64])
    o_pcf = dout("o_pcf", [2, 128, 11, 4, 2])
    o_scq = dout("o_scq", [2, 128, 12, 16, 3])
    o_sdl = dout("o_sdl", [2, 16, 4, 128, 128])
    o_sss = dout("o_sss", [2, 2, 128, 8, 16])
    o_shg = dout("o_shg", [2, 128, 2, 16, 64])
    o_scf = dout("o_scf", [2, 128, 11, 4, 16, 2])

    es = ExitStack()
    with es:
        def sb(name, shape, dt=F32):
            return TB(es.enter_context(nc.sbuf_tensor(name, list(shape), dt)), name)

        def ps(name, shape, dt=F32):
            return TB(es.enter_context(nc.psum_tensor(name, list(shape), dt)), name, bufs=[Buf(name, excl=True)])

        sems = {}
        for n in ['%s_%d' % (e_, i_) for e_ in NEPOCH for i_ in range(NEPOCH[e_])] + ['d_sp_%d' % i for i in range(8)] + ['d_pool_%d' % i for i in range(6)] + ['d_act_%d' % i for i in range(2)]:
            sems[n] = es.enter_context(nc.semaphore(n))
        S = Sched(sems)

        def V(fn, r=(), w=()):
            S.op('dve', fn, r, w)

        def A(fn, r=(), w=()):
            S.op('act', fn, r, w)

        def mm(out, lhsT, rhs, r, w, start=True, stop=True, skip=False, sync=False):
            S.op('pe', lambda e: e.matmul(out, lhsT=lhsT, rhs=rhs, start=start, stop=stop, skip_group_check=skip), r, w, pe_sync=sync)

        def tr(out, in_, ident, r, w):
            S.op('pe', lambda e: e.transpose(out=out, in_=in_, identity=ident), r, w)

        def vtt(out, a, b, op, r, w):
            V(lambda e: e.tensor_tensor(out=out, in0=a, in1=b, op=op), r, w)

        def vts(out, a, s1, op0, r, w, s2=None, op1=None):
            if op1 is None:
                V(lambda e: e.tensor_scalar(out=out, in0=a, scalar1=s1, scalar2=None, op0=op0), r, w)
            else:
                V(lambda e: e.tensor_scalar(out=out, in0=a, scalar1=s1, scalar2=s2, op0=op0, op1=op1), r, w)

        def vstt(out, a, sc, b, op0, op1, r, w):
            V(lambda e: e.scalar_tensor_tensor(out=out, in0=a, scalar=sc, in1=b, op0=op0, op1=op1), r, w)

        def vcopy(out, a, r, w):
            V(lambda e: e.tensor_copy(out=out, in_=a), r, w)

        def acopy(out, a, r, w):
            A(lambda e: e.copy(out=out, in_=a), r, w)

        def act(out, a, func, r, w, bias=None, scale=None, accum=None):
            kw = {}
            if bias is not None:
                kw['bias'] = bias
            if scale is not None:
                kw['scale'] = scale
            if accum is not None:
                kw['accum_out'] = accum
            A(lambda e: e.activation(out=out, in_=a, func=func, **kw), r, w)

        def dma(q, out, in_, r=(), w=()):
            S.dma(q, lambda e: e.dma_start(out=out, in_=in_), reads=r, writes=w)

        WB = {}

        def wreg(key, kc, n, parts):
            d = nc.dram_tensor("wb_" + "_".join(str(k) for k in key), [128, kc * n], BF16, kind="Internal").ap()
            tb = TB(None, "wb" + str(key))
            d3 = d.rearrange("p (k n) -> p k n", k=kc)
            for (src, c0, ncol) in parts:
                S.dma('pool', lambda e, d3=d3, src=src, c0=c0, ncol=ncol: e.dma_start(out=d3[:, :, c0:c0 + ncol], in_=src.rearrange("(k p) n -> p k n", p=128)), writes=[tb])
            WB[key] = (d3, tb, kc, n)
        for l in range(2):
            W = w_in[l]
            for g in range(3):
                wreg(('in', l, g), 8, 512, [(W[:, g * 512:(g + 1) * 512], 0, 512)])
            wreg(('in', l, 3), 8, 520, [(W[:, 1536:2056], 0, 520)])
            wreg(('in', l, 4), 8, 512, [(W[:, 2056:2568], 0, 512)])
            wreg(('in', l, 5), 8, 512, [(W[:, 2568:3080], 0, 512)])
            wreg(('in', l, 6), 8, 256, [(W[:, 3080:3336], 0, 256)])
            for half in range(2):
                wreg(('out', l, half), 8, 512, [(w_out[l][:, half * 512:(half + 1) * 512], 0, 512)])
            for blk in range(11):
                wreg(('up', l, blk), 8, 512, [(w_up[l][:, blk * 256:(blk + 1) * 256], 0, 256),
                                              (w_up[l][:, 2816 + blk * 256:2816 + (blk + 1) * 256], 256, 256)])
            for half in range(2):
                for q4 in range(4):
                    n_c = 6 if q4 < 3 else 4
                    c0 = q4 * 6
                    wreg(('down', l, half, q4), n_c, 512, [(w_down[l][c0 * 128:(c0 + n_c) * 128, half * 512:(half + 1) * 512], 0, 512)])
                wreg(('pg', l, half), 8, 512, [(w_pg[l][:, half * 512:(half + 1) * 512], 0, 512)])
                wreg(('pp', l, half), 2, 512, [(w_pp[l][:, half * 512:(half + 1) * 512], 0, 512)])

        cst = sb("cst", [128, NCONST])
        cstb = sb("cstb", [128, NCONST], BF16)
        dma('sp', cst[:], consts_d, w=[cst])
        vcopy(cstb[:], cst[:], [cst], [cstb])

        def C(n, bf=False, cols=None):
            o, wd = COFF[n]
            if cols is not None:
                wd = cols
            return (cstb if bf else cst)[:, o:o + wd]

        identf = C('ident')
        identb = C('ident', True)
        onesb = C('smP', True)
        onesf = C('smP')
        ppt = [sb("pp%d" % l, [128, NPP]) for l in range(2)]
        bcs = [sb("bcs%d" % l, [128, NBCS]) for l in range(2)]
        nfin = sb("nfin", [128, 1024])
        dma('sp', nfin[:], bc_d[0, BC['nfin'][0]:BC['nfin'][0] + 1024].partition_broadcast(128), w=[nfin])
        for l in range(2):
            dma('sp', ppt[l][:], pp_d[l], w=[ppt[l]])
            dma('sp', bcs[l][:], bc_d[l, 0:NBCS].partition_broadcast(128), w=[bcs[l]])

        def pp(l, n, a=0, b=None):
            o, wd = PP[n]
            return ppt[l][:, o + a:o + (wd if b is None else b)]

        def bcv(l, n, a=0, b=None):
            o, wd = BC[n]
            return bcs[l][:, o + a:o + (wd if b is None else b)]

        glw = [sb("glw%d" % l, [128, 2, 256], BF16) for l in range(2)]
        bft = [[sb("bf%d%d" % (l, r), [128, 2, 512], BF16) for r in range(2)] for l in range(2)]
        cft = [[sb("cf%d%d" % (l, r), [128, 8, 32], BF16) for r in range(2)] for l in range(2)]
        cftf = sb("cftf", [128, 8, 32])
        diagD = [sb("diagD%d" % l, [128, 2, 128], BF16) for l in range(2)]
        negA = [sb("negA%d" % l, [128, 4]) for l in range(2)]
        lbb = [sb("lbb%d" % l, [128, 256]) for l in range(2)]
        omlb = [sb("omlb%d" % l, [128, 256]) for l in range(2)]
        lbf = [sb("lbf%d" % l, [128, 2]) for l in range(2)]
        omlf = [sb("omlf%d" % l, [128, 2]) for l in range(2)]
        for l in range(2):
            dma('pool', glw[l][:], w_glu[l].rearrange("(c p) n -> p c n", p=128), w=[glw[l]])
            for r in range(2):
                dma('pool', bft[l][r][:], bfull[l, r], w=[bft[l][r]])
            dma('pool', cft[l][0][:], cfull[l, 0], w=[cft[l][0]])
            dma('sp', cftf[:], cfull[l, 1], w=[cftf])
            vts(cft[l][1][:], cftf[:], -1.0, ALU.mult, [cftf], [cft[l][1]])
            for cc in range(2):
                vts(diagD[l][:, cc, :], identf, pp(l, 'ssd', cc, cc + 1), ALU.mult, [cst, ppt[l]], [diagD[l]])
            act(negA[l][:], bcv(l, 'alog'), AF.Exp, [bcs[l]], [negA[l]])
            vts(negA[l][:], negA[l][:], -1.0, ALU.mult, [negA[l]], [negA[l]])
            if l == 0:
                V(lambda e: e.memset(lbb[0][:], 0.0), [], [lbb[0]])
                V(lambda e: e.memset(lbf[0][:], 0.0), [], [lbf[0]])
            else:
                vtt(lbb[1][:], bcv(1, 'hl1'), bcv(1, 'hl0'), ALU.subtract, [bcs[1]], [lbb[1]])
                act(lbb[1][:], lbb[1][:], AF.Sigmoid, [lbb[1]], [lbb[1]])
                vtt(lbf[1][:], pp(1, 'hl1'), pp(1, 'hl0'), ALU.subtract, [ppt[1]], [lbf[1]])
                act(lbf[1][:], lbf[1][:], AF.Sigmoid, [lbf[1]], [lbf[1]])
            vts(omlb[l][:], lbb[l][:], -1.0, ALU.mult, [lbb[l]], [omlb[l]], 1.0, ALU.add)
            vts(omlf[l][:], lbf[l][:], -1.0, ALU.mult, [lbf[l]], [omlf[l]], 1.0, ALU.add)

        ZB = [es.enter_context(nc.sbuf_tensor("ZB%d" % i, [128, 2048], F32)) for i in range(4)]
        PZ = [TB(ZB[i // 4][:, (i % 4) * 512:(i % 4 + 1) * 512], "PZ%d" % i) for i in range(16)]
        tA, tB_, tC, tD, mag, bcA, bcP, cre, cim, lbr, lbi, bs_lre, bs_lim, bs_lst = PZ[0:14]
        bsvd = {'lre': bs_lre, 'lim': bs_lim, 'lst': bs_lst}
        h = sb("h", [128, 1024])
        tI = TB(h[:, 0:512].bitcast(I32), "tI", bufs=h.bs)

        def sincos(dst, ang_r, shift, rr, ww):
            vts(tC[:], ang_r, shift, ALU.add, rr, [tC])
            vcopy(tI[:], tC[:], [tC], [tI])
            vcopy(tD[:], tI[:], [tI], [tD])
            vtt(tC[:], tC[:], tD[:], ALU.subtract, [tC, tD], [tC])
            act(dst, tC[:], AF.Sin, [tC], ww, scale=6.283185)

        LpT = [[sb("LpT%d%d" % (l, r), [128, 8, 128], BF16) for r in range(2)] for l in range(2)]
        Lm = [[sb("Lm%d%d" % (l, r), [128, 1024], BF16) for r in range(2)] for l in range(2)]
        lbar = [[sb("lbar%d%d" % (l, r), [128, 8]) for r in range(2)] for l in range(2)]
        sm_a = sb("sm_a", [128, 8])
        sm_p = sb("sm_p", [128, 8])
        iota3 = C('iota').unsqueeze(1).to_broadcast([128, 4, 128])

        def t3(T):
            return T[:].rearrange("p (j t) -> p j t", j=4)

        def build_tables(l, samp):
            if not samp:
                act(sm_p[:], pp(l, 'lst'), AF.Exp, [ppt[l]], [sm_p])
                vtt(sm_a[:], pp(l, 'lre'), sm_p[:], ALU.mult, [ppt[l], sm_p], [sm_a])
                vstt(sm_p[:], pp(l, 'lim'), 1.0 / TWO_PI, sm_p[:], ALU.mult, ALU.mult, [ppt[l], sm_p], [sm_p])
            for hf in range(2):
                js = slice(4 * hf, 4 * hf + 4)
                cs = slice(512 * hf, 512 * hf + 512)

                def bsv(n):
                    return bsvd[n][:]
                for n_ in ('lre', 'lim', 'lst'):
                    o_ = BC[n_][0] + 512 * hf
                    dma('sp', bsvd[n_][:], bc_d[l, o_:o_ + 512].partition_broadcast(128), w=[bsvd[n_]])
                if not samp:
                    vtt(t3(tA), iota3, sm_a[:, js].unsqueeze(2).to_broadcast([128, 4, 128]), ALU.mult, [cst, sm_a], [tA])
                    act(mag[:], tA[:], AF.Exp, [tA], [mag])
                    vtt(t3(tB_), iota3, sm_p[:, js].unsqueeze(2).to_broadcast([128, 4, 128]), ALU.mult, [cst, sm_p], [tB_])
                    sincos(tA[:], tB_[:], 0.25, [tB_], [tA])
                    vtt(LpT[l][0][:, js, :], t3(tA), t3(mag), ALU.mult, [tA, mag], [LpT[l][0]])
                    vtt(t3(cre), t3(tA), t3(mag), ALU.mult, [tA, mag], [cre])
                    vcopy(lbar[l][0][:, js], t3(cre)[:, :, 1], [cre], [lbar[l][0]])
                    sincos(tA[:], tB_[:], 0.0, [tB_], [tA])
                    vtt(LpT[l][1][:, js, :], t3(tA), t3(mag), ALU.mult, [tA, mag], [LpT[l][1]])
                    vtt(t3(cre), t3(tA), t3(mag), ALU.mult, [tA, mag], [cre])
                    vcopy(lbar[l][1][:, js], t3(cre)[:, :, 1], [cre], [lbar[l][1]])
                act(bcP[:], bsv('lst'), AF.Exp, [bs_lst], [bcP])
                vtt(bcA[:], bsv('lre'), bcP[:], ALU.mult, [bs_lre, bcP], [bcA])
                vstt(bcP[:], bsv('lim'), 1.0 / TWO_PI, bcP[:], ALU.mult, ALU.mult, [bs_lim, bcP], [bcP])
                act(mag[:], bcA[:], AF.Exp, [bcA], [mag])
                sincos(tA[:], bcP[:], 0.25, [bcP], [tA])
                vtt(lbr[:], tA[:], mag[:], ALU.mult, [tA, mag], [lbr])
                sincos(tA[:], bcP[:], 0.0, [bcP], [tA])
                vtt(lbi[:], tA[:], mag[:], ALU.mult, [tA, mag], [lbi])
                vts(lbr[:], lbr[:], -1.0, ALU.add, [lbr], [lbr])
                vtt(tA[:], bsv('lre'), bsv('lre'), ALU.mult, [bs_lre], [tA])
                vtt(tB_[:], bsv('lim'), bsv('lim'), ALU.mult, [bs_lim], [tB_])
                vtt(tA[:], tA[:], tB_[:], ALU.add, [tA, tB_], [tA])
                V(lambda e: e.reciprocal(out=tA[:], in_=tA[:]), [tA], [tA])
                vtt(cre[:], lbr[:], bsv('lre'), ALU.mult, [lbr, bs_lre], [cre])
                vtt(tB_[:], lbi[:], bsv('lim'), ALU.mult, [lbi, bs_lim], [tB_])
                vtt(cre[:], cre[:], tB_[:], ALU.add, [cre, tB_], [cre])
                vtt(cre[:], cre[:], tA[:], ALU.mult, [cre, tA], [cre])
                vtt(cim[:], lbi[:], bsv('lre'), ALU.mult, [lbi, bs_lre], [cim])
                vtt(tB_[:], lbr[:], bsv('lim'), ALU.mult, [lbr, bs_lim], [tB_])
                vtt(cim[:], cim[:], tB_[:], ALU.subtract, [cim, tB_], [cim])
                vtt(cim[:], cim[:], tA[:], ALU.mult, [cim, tA], [cim])
                o_pc = COFF['pcol'][0] + (1 if samp else 0)
                sidx = cst[:, o_pc:o_pc + 1]
                vts(tA[:], bcA[:], sidx, ALU.mult, [bcA, cst], [tA], -1.0, ALU.mult)
                act(mag[:], tA[:], AF.Exp, [tA], [mag])
                vts(tB_[:], bcP[:], sidx, ALU.mult, [bcP, cst], [tB_], -1.0, ALU.mult)
                sincos(tA[:], tB_[:], 0.25, [tB_], [tA])
                vtt(lbr[:], tA[:], mag[:], ALU.mult, [tA, mag], [lbr])
                sincos(tA[:], tB_[:], 0.0, [tB_], [tA])
                vtt(lbi[:], tA[:], mag[:], ALU.mult, [tA, mag], [lbi])
                vtt(tA[:], lbr[:], cre[:], ALU.mult, [lbr, cre], [tA])
                vtt(tB_[:], lbi[:], cim[:], ALU.mult, [lbi, cim], [tB_])
                vtt(Lm[l][0][:, cs], tA[:], tB_[:], ALU.subtract, [tA, tB_], [Lm[l][0]])
                vtt(tA[:], lbr[:], cim[:], ALU.mult, [lbr, cim], [tA])
                vtt(tB_[:], lbi[:], cre[:], ALU.mult, [lbi, cre], [tB_])
                vtt(Lm[l][1][:, cs], tA[:], tB_[:], ALU.add, [tA, tB_], [Lm[l][1]])
        for l in range(2):
            build_tables(l, False)
        hnT = sb("hnT", [128, 8, 128], BF16)
        ocatT = sb("ocatT", [128, 8, 128], BF16)
        st = sb("st", [128, 8])
        ring = [sb("ring%d" % i, [128, 4160], BF16) for i in range(NRING)]
        ringi = [0]
        pT = ps("pT", [128, 8, 128], BF16)
        pFt = es.enter_context(nc.psum_tensor("pF", [128, 4, 128], F32))
        pFbuf = Buf("pF", excl=True)
        pF = [TB(pFt[:, i, :], "pF%d" % i, bufs=[pFbuf]) for i in range(4)]
        pM = [ps("pM%d" % i, [128, 512]) for i in range(6)]
        xpf = sb("xpf", [128, 4, 160])
        acc = sb("acc", [128, 4, 128])
        acc2 = sb("acc2", [128, 4, 128])
        oT_sb = TB(acc[0:64, :, :], "oT_sb", bufs=acc.bs)
        sa = sb("sa", [128, 2, 128])
        cfc = [sb("cfc%d" % l, [128, 11, 4, 2]) for l in range(2)]
        pf = sb("pf", [128, 256])
        pb = sb("pb", [128, 256], BF16)
        ppT = sb("ppT", [128, 2, 128], BF16)
        xpq = sb("xpq", [128, 12, 176], BF16)
        cq = [sb("cq%d" % l, [128, 12, 3]) for l in range(2)]
        ba = sb("ba", [128, 8])
        uT = sb("uT", [128, 2, 128], BF16)
        hqTs = sb("hqTs", [128, 2, 128])
        k_tok = sb("k_tok", [128, 4, 128], BF16)
        v_tok = sb("v_tok", [128, 4, 128], BF16)
        knT = sb("knT", [128, 4, 128], BF16)
        dsc = sb("dsc", [128, 64])
        gsel = sb("gsel", [128, 4, 16])
        glb = sb("glb", [128, 4, 16])
        gB = sb("gB", [128, 128])
        dtmp = sb("dtmp", [128, 128])
        dec = sb("dec", [128, 128])
        decT = sb("decT", [128, 128])
        Xf = sb("Xf", [128, 128])
        Xp = [sb("Xp%d" % i, [128, 128]) for i in range(2)]
        XpT = [sb("XpT%d" % i, [128, 128]) for i in range(2)]
        TT = sb("TT", [128, 128])
        Ru = sb("Ru", [128, 128])
        Rw = sb("Rw", [128, 128])
        u_sb = sb("u_sb", [128, 128])
        wT_b = sb("wT_b", [128, 128], BF16)
        qkTm = sb("qkTm", [128, 128], BF16)
        wq_sb = sb("wq_sb", [128, 2, 128])
        vnew = sb("vnew", [128, 128], BF16)
        t1s = sb("t1s", [128, 128])
        kdec = sb("kdec", [128, 128], BF16)
        oA = sb("oA", [128, 4, 128])
        oab = sb("oab", [128, 4, 128], BF16)
        Sd = [sb("Sd%d" % l, [128, 4, 128]) for l in range(2)]
        Sd_b = [sb("Sdb%d" % l, [128, 4, 128], BF16) for l in range(2)]
        sS_b = sb("sSb", [128, 16, 128], BF16)
        cinP = [[sb("cinP%d%d" % (l, r), [128, 8, 1]) for r in range(2)] for l in range(2)]
        cinS = [sb("cinS%d" % r, [128, 8, 16]) for r in range(2)]
        x0s = [sb("x0s%d" % r, [128, 8, 16]) for r in range(2)]
        xe = [sb("xe%d" % r, [128, 8, 16]) for r in range(2)]
        xep = [sb("xep%d" % r, [128, 8]) for r in range(2)]
        ybT = sb("ybT", [128, 2, 128])
        ybTb = sb("ybTb", [128, 2, 128], BF16)
        sgT = sb("sgT", [128, 128])
        v_h = sb("v_h", [128, 256], BF16)
        fT = sb("fT", [128, 2, 128])
        qtT = sb("qtT", [128, 2, 128], BF16)
        ktT = sb("ktT", [128, 2, 128], BF16)
        e1 = sb("e1", [128, 2, 128])
        khat = sb("khat", [128, 256], BF16)
        glh = sb("glh", [128, 2, 16])
        aTm = sb("aTm", [128, 4, 128], BF16)
        ocb = sb("ocb", [128, 256], BF16)
        Sh = [sb("Sh%d" % l, [128, 2, 64]) for l in range(2)]
        Sh_b = [sb("Shb%d" % l, [128, 2, 64], BF16) for l in range(2)]
        Shs_b = TB(sS_b[:].rearrange("p b v -> p (b v)").rearrange("p (c b v) -> p c b v", c=2, b=16), "Shsb", bufs=sS_b.bs)
        gT = sb("gT", [128, 22, 128], BF16)
        qkv_b = TB(gT[:, 0:12, :], "qkv_b", bufs=gT.bs)
        hb = TB(gT[:, 14:22, :].rearrange("p c t -> p (c t)"), "hb", bufs=gT.bs)
        sq = TB(gT[:, 12:20, :], "sq", bufs=gT.bs)
        def bufs_of(i0, i1):
            return [b for p_ in PZ[i0:i1] for b in p_.bs]
        sS = TB(ZB[1][:].rearrange("p (b v) -> p b v", b=16), "sS", bufs=bufs_of(4, 8))
        Shs = TB(ZB[1][:].rearrange("p (c b v) -> p c b v", c=2, b=16), "Shs", bufs=bufs_of(4, 8))
        scf = TB(ZB[1][:, 0:1408].rearrange("p (k q b t) -> p k q b t", k=11, q=4, b=16), "scf", bufs=bufs_of(4, 8))
        scq = TB(ZB[1][:, 1408:1984].rearrange("p (c b t) -> p c b t", c=12, b=16), "scq", bufs=bufs_of(4, 8))
        xre = TB(ZB[2][:, 0:1024].rearrange("p (j t) -> p j t", j=8), "xre", bufs=bufs_of(8, 10))
        xim = TB(ZB[2][:, 1024:2048].rearrange("p (j t) -> p j t", j=8), "xim", bufs=bufs_of(10, 12))
        yb0 = TB(ZB[3][:, 0:256], "yb0", bufs=PZ[12].bs)
        yb = TB(ZB[3][:, 256:512], "yb", bufs=PZ[12].bs)
        fto = TB(ZB[3][:, 512:768], "fto", bufs=PZ[13].bs)
        logf = TB(ZB[3][:, 768:1024], "logf", bufs=PZ[13].bs)
        omf = TB(ZB[3][:, 1024:1280], "omf", bufs=PZ[14].bs)
        b_sb = TB(ZB[3][:, 1280:1536], "b_sb", bufs=PZ[14].bs)
        oc = TB(ZB[3][:, 1536:1792], "oc", bufs=PZ[15].bs)
        hg_s = TB(ZB[3][:, 1792:2048], "hg_s", bufs=PZ[15].bs)
        gate_s = tD
        Wre = sb("Wre", [128, 1024], BF16)
        Wim = sb("Wim", [128, 1024], BF16)
        xbre = TB(Wre[:].rearrange("p (j t) -> p j t", j=8), "xbre", bufs=Wre.bs)
        xbim = TB(Wim[:].rearrange("p (j t) -> p j t", j=8), "xbim", bufs=Wim.bs)
        kdm = [sb("kdm%d" % i, [128, 128], BF16) for i in range(2)]
        khm = [sb("khm%d" % i, [128, 256], BF16) for i in range(2)]
        gsbh = [tA, tB_]
        yoh = [tC, tD]
        tmp1 = tD
        for l in range(2):
            for t_ in (cfc[l], cq[l], Sd[l], Sd_b[l], Sh[l], Sh_b[l], cinP[l][0], cinP[l][1]):
                V(lambda e, t_=t_: e.memset(t_[:], 0.0), [], [t_])

        class _RS:
            pass
        RS0 = _RS()
        RS0.dec, RS0.decT, RS0.TT, RS0.Ru, RS0.Rw, RS0.u_sb, RS0.t1s = dec, decT, TT, Ru, Rw, u_sb, t1s
        RS0.wT_b, RS0.qkTm, RS0.vnew, RS0.kdec, RS0.wq_sb, RS0.Xp, RS0.XpT = wT_b, qkTm, vnew, kdec, wq_sb, Xp, XpT
        RS0.pF, RS0.pU, RS0.pP = pF, pM[2], pM[5]
        RS1 = _RS()
        for n_ in ('dec', 'decT', 'TT', 'Ru', 'Rw', 'u_sb', 't1s'):
            setattr(RS1, n_, sb(n_ + "_1", [128, 128]))
        for n_ in ('wT_b', 'qkTm', 'vnew', 'kdec'):
            setattr(RS1, n_, sb(n_ + "_1", [128, 128], BF16))
        RS1.wq_sb = sb("wq_sb_1", [128, 2, 128])
        RS1.Xp = [sb("Xp1_%d" % i, [128, 128]) for i in range(2)]
        RS1.XpT = [sb("XpT1_%d" % i, [128, 128]) for i in range(2)]
        RS1.pF = [TB(pM[4][:, i * 128:(i + 1) * 128], "pF1_%d" % i, bufs=pM[4].bs) for i in range(4)]
        RS1.pU, RS1.pP = pM[0], pM[1]

        FB = [(xpf, acc, acc2, sa),
              (sb("xpf_b", [128, 4, 160]), sb("acc_b", [128, 4, 128]), sb("acc2_b", [128, 4, 128]), sb("sa_b", [128, 2, 128]))]

        def norm_stats():
            V(lambda e: e.memset(st[:, 0:1], 0.0), [], [st])
            act(hb[:], h[:], AF.Square, [h], [hb, st], accum=st[:, 0:1])
            vts(st[:, 1:2], st[:, 0:1], 1.0 / 1024, ALU.mult, [st], [st], EPS, ALU.add)
            act(st[:, 2:3], st[:, 1:2], AF.Sqrt, [st], [st])
            V(lambda e: e.reciprocal(out=st[:, 3:4], in_=st[:, 2:3]), [st], [st])

        def norm_T(gain_ap, gsrc):
            norm_stats()
            vts(hb[:], h[:], st[:, 3:4], ALU.mult, [h, st], [hb])
            for c in range(8):
                tr(pT[:, c, :], hb[:, c * 128:(c + 1) * 128], identb, [hb, cstb], [pT])
            vtt(hnT[:], pT[:], gain_ap.unsqueeze(2).to_broadcast([128, 8, 128]), ALU.mult, [pT, gsrc], [hnT])

        def wslot(kc, n):
            slot = ring[ringi[0] % NRING]
            ringi[0] += 1
            return slot, slot[:, 0:kc * n].rearrange("p (k n) -> p k n", k=kc)

        def wload(key):
            d3, tb, kc, n = WB[key]
            slot, wv = wslot(kc, n)
            S.dma('pool', lambda e: e.dma_start(out=wv, in_=d3), reads=[tb], writes=[slot])
            return slot, wv

        def rsqrt_small(dst, src, rr, ww, mul, add):
            vts(dst, src, mul, ALU.mult, rr, ww, add, ALU.add)
            act(dst, dst, AF.Sqrt, ww, ww)
            V(lambda e: e.reciprocal(out=dst, in_=dst), ww, ww)

        def in_proj(l, samp):
            norm_T(pp(l, 'nmix'), ppt[l])
            W = w_in[l]
            if samp:
                dma('pool', scq[:], s_cq[l], w=[scq])
                xq4 = xpq[:].rearrange("p c (b t) -> p c b t", b=16)
                vcopy(xq4[:, :, :, 0:3], scq[:], [scq], [xpq])
            else:
                vcopy(xpq[:, :, 0:3], cq[l][:], [cq[l]], [xpq])
            for g in range(3):
                slot, wv = wload(('in', l, g))
                pm = pM[g % 2]
                pm3 = pm[:].rearrange("p (q t) -> p q t", q=4)
                for q in range(4):
                    for kc in range(8):
                        mm(pm3[:, q, :], wv[:, kc, q * 128:(q + 1) * 128], hnT[:, kc, :], [slot, hnT], [pm], start=(kc == 0), stop=(kc == 7))
                if samp:
                    acopy(xq4[:, 4 * g:4 * g + 4, :, 3:11], pm[:].rearrange("p (q b t) -> p q b t", q=4, b=16), [pm], [xpq])
                else:
                    acopy(xpq[:, 4 * g:4 * g + 4, 3:131], pm3, [pm], [xpq])
            slot, wv = wload(('in', l, 3))
            for kc in range(8):
                mm(pM[2][:], hnT[:, kc, :], wv[:, kc, 0:512], [hnT, slot], [pM[2]], start=(kc == 0), stop=(kc == 7))
            for kc in range(8):
                mm(pM[3][:, 0:8], hnT[:, kc, :], wv[:, kc, 512:520], [hnT, slot], [pM[3]], start=(kc == 0), stop=(kc == 7))
            act(gate_s[:], pM[2][:], AF.Silu, [pM[2]], [gate_s])
            vcopy(ba[:], pM[3][:, 0:8], [pM[3]], [ba])
        def in_proj_rest(l, samp):
            W = w_in[l]
            slot, wv = wload(('in', l, 4))
            pm = pM[0]
            pm3 = pm[:].rearrange("p (q t) -> p q t", q=4)
            for q in range(4):
                for kc in range(8):
                    mm(pm3[:, q, :], wv[:, kc, q * 128:(q + 1) * 128], hnT[:, kc, :], [slot, hnT], [pm], start=(kc == 0), stop=(kc == 7))
            vcopy(uT[:], pm3[:, 0:2, :], [pm], [uT])
            yield
            act(hqTs[:], pm3[:, 2:4, :], AF.Silu, [pm], [hqTs])
            yield
            slot, wv = wload(('in', l, 5))
            for kc in range(8):
                mm(pM[4][:], hnT[:, kc, :], wv[:, kc, :], [hnT, slot], [pM[4]], start=(kc == 0), stop=(kc == 7))
            p53 = pM[5][:, 0:256].rearrange("p (q t) -> p q t", q=2)
            for q in range(2):
                for kc in range(8):
                    mm(p53[:, q, :], wv[:, kc, q * 128:(q + 1) * 128], hnT[:, kc, :], [slot, hnT], [pM[5]], start=(kc == 0), stop=(kc == 7))
            act(fto[:], pM[4][:, 0:256], AF.Sigmoid, [pM[4]], [fto])
            yield
            acopy(v_h[:], pM[4][:, 256:512], [pM[4]], [v_h])
            yield
            act(fT[:], p53, AF.Sigmoid, [pM[5]], [fT])
            yield
            slot, wv = wload(('in', l, 6))
            for kc in range(8):
                mm(pM[3][:, 0:256], hnT[:, kc, :], wv[:, kc, :], [hnT, slot], [pM[3]], start=(kc == 0), stop=(kc == 7))
            act(hg_s[:], pM[3][:, 0:256], AF.Silu, [pM[3]], [hg_s])
            yield


        def delta(l, samp, last):
            kind = 'S' if samp else 'P'
            nb = 16 if samp else 1
            bs = 128 // nb
            nlev = 3 if samp else 7
            cTf, cTb = C('cT' + kind), C('cT' + kind, True)
            stf = C('st' + kind)
            smf = C('sm' + kind)
            self_ = C('sel' + kind)
            dcw = ppt[l][:, PP['dcw'][0]:PP['dcw'][0] + 48].rearrange("p (c j) -> p c j", c=12)
            for g in range(3):
                if samp:
                    xq4 = xpq[:].rearrange("p c (b t) -> p c b t", b=16)
                    xs = [xq4[:, 4 * g:4 * g + 4, :, j:j + 8] for j in range(4)]
                    ws = [dcw[:, 4 * g:4 * g + 4, j:j + 1].unsqueeze(3).to_broadcast([128, 4, 16, 8]) for j in range(4)]
                    av = acc[:].rearrange("p q (b t) -> p q b t", b=16)
                    a2v = acc2[:].rearrange("p q (b t) -> p q b t", b=16)
                else:
                    xs = [xpq[:, 4 * g:4 * g + 4, j:j + 128] for j in range(4)]
                    ws = [dcw[:, 4 * g:4 * g + 4, j:j + 1].to_broadcast([128, 4, 128]) for j in range(4)]
                    av, a2v = acc[:], acc2[:]
                vtt(av, xs[0], ws[0], ALU.mult, [xpq, ppt[l]], [acc])
                yield
                for j in range(1, 4):
                    vtt(a2v, xs[j], ws[j], ALU.mult, [xpq, ppt[l]], [acc2])
                    yield
                    vtt(av, av, a2v, ALU.add, [acc, acc2], [acc])
                    yield
                act(qkv_b[:, 4 * g:4 * g + 4, :], acc[:], AF.Silu, [acc], [qkv_b])
                yield
            if samp:
                xq4 = xpq[:].rearrange("p c (b t) -> p c b t", b=16)
                vcopy(scq[:], xq4[:, :, :, 8:11], [xpq], [scq])
                yield
                dma('pool', o_scq[l], scq[:], r=[scq])
            else:
                vcopy(cq[l][:], xpq[:, :, 128:131], [xpq], [cq[l]])
                yield
                if last:
                    dma('pool', o_pcq[l], cq[l][:], r=[cq[l]])
            vtt(sq[:], qkv_b[:, 0:8, :], qkv_b[:, 0:8, :], ALU.mult, [qkv_b], [sq])
            yield
            for c in range(8):
                mm(pM[1][:, c:c + 1], sq[:, c, :], onesb[:, 0:1], [sq, cstb], [pM[1]])
            rsqrt_small(dsc[:, 0:8], pM[1][:, 0:8], [pM[1]], [dsc], 1.0, EPS)
            vts(dsc[:, 0:4], dsc[:, 0:4], 128.0 ** -0.5, ALU.mult, [dsc], [dsc])
            yield
            for hh in range(4):
                tr(pT[:, hh, :], qkv_b[:, 4 + hh, :], identb, [qkv_b, cstb], [pT])
                tr(pT[:, 4 + hh, :], qkv_b[:, 8 + hh, :], identb, [qkv_b, cstb], [pT])
            vtt(k_tok[:], pT[:, 0:4, :], dsc[:, 4:8].unsqueeze(2).to_broadcast([128, 4, 128]), ALU.mult, [pT, dsc], [k_tok])
            yield
            acopy(v_tok[:], pT[:, 4:8, :], [pT], [v_tok])
            yield
            for hh in range(4):
                tr(pT[:, hh, :], k_tok[:, hh, :], identb, [k_tok, cstb], [pT])
            acopy(knT[:], pT[:, 0:4, :], [pT], [knT])
            yield
            act(dsc[:, 8:12], ba[:, 0:4], AF.Sigmoid, [ba], [dsc])
            yield
            vts(dsc[:, 12:16], dsc[:, 8:12], -1.0, ALU.mult, [dsc], [dsc])
            yield
            vtt(dsc[:, 40:44], ba[:, 4:8], bcv(l, 'dtb'), ALU.add, [ba, bcs[l]], [dsc])
            yield
            act(dsc[:, 40:44], dsc[:, 40:44], AF.Exp, [dsc], [dsc])
            yield
            act(dsc[:, 40:44], dsc[:, 40:44], AF.Ln, [dsc], [dsc], bias=1.0)
            yield
            vtt(dsc[:, 16:20], dsc[:, 40:44], negA[l][:], ALU.mult, [dsc, negA[l]], [dsc])
            yield
            mm(pM[1][:, 8:12], cTf, dsc[:, 16:20], [cst, dsc], [pM[1]])
            mm(pM[1][:, 12:16], smf, dsc[:, 16:20], [cst, dsc], [pM[1]])
            vcopy(dsc[:, 20:24], pM[1][:, 8:12], [pM[1]], [dsc])
            yield
            vtt(dsc[:, 40:44], pM[1][:, 12:16], dsc[:, 20:24], ALU.subtract, [pM[1], dsc], [dsc])
            yield
            act(dsc[:, 24:28], dsc[:, 40:44], AF.Exp, [dsc], [dsc])
            yield
            act(dsc[:, 28:32], dsc[:, 20:24], AF.Exp, [dsc], [dsc])
            yield
            vtt(dsc[:, 32:36], dsc[:, 8:12], dsc[:, 28:32], ALU.mult, [dsc], [dsc])
            yield
            vtt(dsc[:, 36:40], dsc[:, 28:32], dsc[:, 0:4], ALU.mult, [dsc], [dsc])
            yield
            for hh in range(4):
                vts(gsel[:, hh, 0:nb], self_[:, 0:nb], dsc[:, 16 + hh:17 + hh], ALU.mult, [cst, dsc], [gsel])
                yield
            for hh in range(4):
                mm(pM[1][:, 16 + 16 * hh:16 + 16 * hh + nb], onesf, gsel[:, hh, 0:nb], [cst, gsel], [pM[1]])
            act(glb[:, :, 0:nb], pM[1][:, 16:80].rearrange("p (h b) -> p h b", h=4)[:, :, 0:nb], AF.Exp, [pM[1]], [glb])
            yield

            def head_body(hh, R):
                vts(gB[:], onesf, dsc[:, 16 + hh:17 + hh], ALU.mult, [cst, dsc], [gB])
                mm(R.pF[0][:], gB[:], cTf, [gB, cst], [R.pF[0]])
                vts(dtmp[:], R.pF[0][:], dsc[:, 20 + hh:21 + hh], ALU.subtract, [R.pF[0], dsc], [dtmp], 0.0, ALU.max)
                act(R.dec[:], dtmp[:], AF.Exp, [dtmp], [R.dec], scale=-1.0)
                yield
                vts(dtmp[:], R.pF[0][:], dsc[:, 20 + hh:21 + hh], ALU.subtract, [R.pF[0], dsc], [dtmp], 0.0, ALU.min)
                act(R.decT[:], dtmp[:], AF.Exp, [dtmp], [R.decT])
                yield
                vtt(R.decT[:], R.decT[:], cTf, ALU.mult, [R.decT, cst], [R.decT])
                mm(R.pF[1][:], knT[:, hh, :], knT[:, hh, :], [knT], [R.pF[1]])
                vstt(Xf[:], R.pF[1][:], dsc[:, 12 + hh:13 + hh], R.dec[:], ALU.mult, ALU.mult, [R.pF[1], dsc, R.dec], [Xf])
                vtt(R.Xp[0][:], Xf[:], stf, ALU.mult, [Xf, cst], [R.Xp[0]])
                yield
                tr(R.pF[3][:], R.Xp[0][:], identf, [R.Xp[0], cst], [R.pF[3]])
                acopy(R.XpT[0][:], R.pF[3][:], [R.pF[3]], [R.XpT[0]])
                yield
                vtt(R.TT[:], R.XpT[0][:], identf, ALU.add, [R.XpT[0], cst], [R.TT])
                yield
                cur = 0
                for lev in range(1, nlev):
                    nxt = 1 - cur
                    lastlev = (lev == nlev - 1)
                    mm(R.pF[2][:], R.XpT[cur][:], R.Xp[cur][:], [R.XpT[cur], R.Xp[cur]], [R.pF[2]])
                    vcopy(R.Xp[nxt][:], R.pF[2][:], [R.pF[2]], [R.Xp[nxt]])
                    yield
                    if not lastlev:
                        mm(R.pF[3][:], R.Xp[cur][:], R.XpT[cur][:], [R.XpT[cur], R.Xp[cur]], [R.pF[3]])
                        acopy(R.XpT[nxt][:], R.pF[3][:], [R.pF[3]], [R.XpT[nxt]])
                        yield
                    mm(R.pF[1][:], R.Xp[nxt][:], R.TT[:], [R.Xp[nxt], R.TT], [R.pF[1]])
                    vtt(R.TT[:], R.TT[:], R.pF[1][:], ALU.add, [R.TT, R.pF[1]], [R.TT])
                    yield
                    cur = nxt
                vts(R.Ru[:], v_tok[:, hh, :], dsc[:, 8 + hh:9 + hh], ALU.mult, [v_tok, dsc], [R.Ru])
                vts(R.Rw[:], k_tok[:, hh, :], dsc[:, 32 + hh:33 + hh], ALU.mult, [k_tok, dsc], [R.Rw])
                mm(R.pU[:, 0:128], R.TT[:], R.Ru[:], [R.TT, R.Ru], [R.pU])
                mm(R.pU[:, 128:256], R.Rw[:], R.TT[:], [R.TT, R.Rw], [R.pU])
                acopy(R.u_sb[:], R.pU[:, 0:128], [R.pU], [R.u_sb])
                yield
                acopy(R.wT_b[:], R.pU[:, 128:256], [R.pU], [R.wT_b])
                yield
                mm(R.pF[2][:], knT[:, hh, :], qkv_b[:, hh, :], [knT, qkv_b], [R.pF[2]])
                vtt(R.qkTm[:], R.pF[2][:], R.decT[:], ALU.mult, [R.pF[2], R.decT], [R.qkTm])
                yield
                if samp:
                    dma('pool', sS[:], s_dl[l, :, hh].rearrange("b k v -> k b v"), w=[sS])
                    acopy(sS_b[:], sS[:], [sS], [sS_b])
                    yield
                p33 = R.pP[:, 256:512].rearrange("p (q t) -> p q t", q=2)
                for b in range(nb):
                    Sb = sS_b[:, b, :] if samp else Sd_b[l][:, hh, :]
                    Sbt = sS_b if samp else Sd_b[l]
                    mm(p33[:, 0, b * bs:(b + 1) * bs], Sb, R.wT_b[:, b * bs:(b + 1) * bs], [Sbt, R.wT_b], [R.pP])
                    mm(p33[:, 1, b * bs:(b + 1) * bs], Sb, qkv_b[:, hh, b * bs:(b + 1) * bs], [Sbt, qkv_b], [R.pP])
                acopy(R.wq_sb[:], p33, [R.pP], [R.wq_sb])
                yield
                tr(R.pF[0][:], R.wq_sb[:, 0, :], identf, [R.wq_sb, cst], [R.pF[0]])
                tr(R.pF[1][:], R.wq_sb[:, 1, :], identf, [R.wq_sb, cst], [R.pF[1]])
                vtt(R.vnew[:], R.u_sb[:], R.pF[0][:], ALU.subtract, [R.u_sb, R.pF[0]], [R.vnew])
                yield
                act(R.t1s[:], R.pF[1][:], AF.Identity, [R.pF[1], dsc], [R.t1s], scale=dsc[:, 36 + hh:37 + hh])
                yield
                mm(R.pF[2][:], R.qkTm[:], R.vnew[:], [R.qkTm, R.vnew], [R.pF[2]])
                vstt(oA[:, hh, :], R.pF[2][:], dsc[:, hh:hh + 1], R.t1s[:], ALU.mult, ALU.add, [R.pF[2], dsc, R.t1s], [oA])
                yield
                vts(R.kdec[:], k_tok[:, hh, :], dsc[:, 24 + hh:25 + hh], ALU.mult, [k_tok, dsc], [R.kdec])
                if samp:
                    for b in range(nb):
                        pd = R.pF[b % 2]
                        km = kdm[b % 2]
                        vts(km[:], R.kdec[:], self_[:, b:b + 1], ALU.mult, [R.kdec, cst], [km])
                        mm(pd[:], km[:], R.vnew[:], [km, R.vnew], [pd])
                        vstt(sS[:, b, :], sS[:, b, :], glb[:, hh, b:b + 1], pd[:], ALU.mult, ALU.add, [sS, glb, pd], [sS])
                        yield
                    dma('pool', o_sdl[l, :, hh].rearrange("b k v -> k b v"), sS[:], r=[sS])
                else:
                    mm(R.pF[0][:], R.kdec[:], R.vnew[:], [R.kdec, R.vnew], [R.pF[0]])
                    vstt(Sd[l][:, hh, :], Sd[l][:, hh, :], glb[:, hh, 0:1], R.pF[0][:], ALU.mult, ALU.add, [Sd[l], glb, R.pF[0]], [Sd[l]])
                    yield
                    acopy(Sd_b[l][:, hh, :], Sd[l][:, hh, :], [Sd[l]], [Sd_b[l]])
                    yield
                    if last:
                        dma('pool', o_pdl[l, hh], Sd[l][:, hh, :], r=[Sd[l]])
            def run_heads(gens):
                gens = list(gens)
                while gens:
                    for g_ in list(gens):
                        try:
                            next(g_)
                        except StopIteration:
                            gens.remove(g_)
            if samp:
                for hh in range(4):
                    run_heads([head_body(hh, RS0)])
            else:
                run_heads([head_body(0, RS0), head_body(1, RS1)])
                run_heads([head_body(2, RS0), head_body(3, RS1)])
            V(lambda e: e.memset(dsc[:, 44:48], 0.0), [], [dsc])
            for hh in range(4):
                act(dtmp[:], oA[:, hh, :], AF.Square, [oA], [dtmp, dsc], accum=dsc[:, 44 + hh:45 + hh])
            rsqrt_small(dsc[:, 44:48], dsc[:, 44:48], [dsc], [dsc], 1.0 / 128, EPS)
            for hh in range(4):
                vstt(oA[:, hh, :], oA[:, hh, :], dsc[:, 44 + hh:45 + hh], bcv(l, 'dnn'), ALU.mult, ALU.mult, [oA, dsc, bcs[l]], [oA])
            vtt(oab[:], oA[:], gate_s[:].rearrange("p (h d) -> p h d", h=4), ALU.mult, [oA, gate_s], [oab])
            for hh in range(4):
                tr(pT[:, hh, :], oab[:, hh, :], identb, [oab, cstb], [pT])
            acopy(ocatT[:, 0:4, :], pT[:, 0:4, :], [pT], [ocatT])
        def s5(l, samp, last):
            kind = 'S' if samp else 'P'
            nb = 16 if samp else 1
            bs = 128 // nb
            cTb = C('cT' + kind, True)
            Lmv = Lm[l]
            if samp:
                for r in range(2):
                    dma('pool', x0s[r][:], s_ss[l, r], w=[x0s[r]])
                for (dst, a_, b_, op) in ((cinS[0], 0, 1, ALU.subtract), (cinS[1], 1, 0, ALU.add)):
                    vtt(xe[0][:], x0s[0][:], lbar[l][a_][:].unsqueeze(2).to_broadcast([128, 8, 16]), ALU.mult, [x0s[0], lbar[l][a_]], [xe[0]])
                    yield
                    vtt(xe[1][:], x0s[1][:], lbar[l][b_][:].unsqueeze(2).to_broadcast([128, 8, 16]), ALU.mult, [x0s[1], lbar[l][b_]], [xe[1]])
                    yield
                    vtt(dst[:], xe[0][:], xe[1][:], op, [xe[0], xe[1]], [dst])
                    yield
                cin = cinS
            else:
                cin = cinP[l]
            for cc in range(2):
                hs = slice(cc * 512, (cc + 1) * 512)
                mm(pM[0][:], uT[:, cc, :], bft[l][0][:, cc, :], [uT, bft[l][0]], [pM[0]])
                mm(pM[1][:], uT[:, cc, :], bft[l][1][:, cc, :], [uT, bft[l][1]], [pM[1]])
                vtt(tA[:, 0:512], pM[0][:], Lmv[0][:, hs], ALU.mult, [pM[0], Lmv[0]], [tA])
                yield
                vtt(tB_[:, 0:512], pM[1][:], Lmv[1][:, hs], ALU.mult, [pM[1], Lmv[1]], [tB_])
                yield
                vtt(Wre[:, hs], tA[:, 0:512], tB_[:, 0:512], ALU.subtract, [tA, tB_], [Wre])
                yield
                vtt(tA[:, 0:512], pM[1][:], Lmv[0][:, hs], ALU.mult, [pM[1], Lmv[0]], [tA])
                yield
                vtt(tB_[:, 0:512], pM[0][:], Lmv[1][:, hs], ALU.mult, [pM[0], Lmv[1]], [tB_])
                yield
                vtt(Wim[:, hs], tA[:, 0:512], tB_[:, 0:512], ALU.add, [tA, tB_], [Wim])
                yield
                for jj in range(4):
                    j = 4 * cc + jj
                    mm(pM[2][:, jj * 128:(jj + 1) * 128], Wre[:, j * 128:(j + 1) * 128], cTb, [Wre, cstb], [pM[2]])
                    mm(pM[4][:, jj * 128:(jj + 1) * 128], Wim[:, j * 128:(j + 1) * 128], cTb, [Wim, cstb], [pM[4]])
                js = slice(4 * cc, 4 * cc + 4)

                def v4(ap):
                    return ap.rearrange("p (j b t) -> p j b t", j=4, b=nb)

                def v4b(ap3):
                    return ap3.rearrange("p j (b t) -> p j b t", b=nb)
                ar, ai = tC, tD
                vtt(v4(ar[:, 0:512]), v4(pM[2][:]), cin[0][:, js, :].unsqueeze(3).to_broadcast([128, 4, nb, bs]), ALU.add, [pM[2], cin[0]], [ar])
                yield
                vtt(v4(ai[:, 0:512]), v4(pM[4][:]), cin[1][:, js, :].unsqueeze(3).to_broadcast([128, 4, nb, bs]), ALU.add, [pM[4], cin[1]], [ai])
                yield
                if samp:
                    Lr = LpT[l][0][:, js, 0:8].unsqueeze(2).to_broadcast([128, 4, 16, 8])
                    Li = LpT[l][1][:, js, 0:8].unsqueeze(2).to_broadcast([128, 4, 16, 8])
                else:
                    Lr = v4b(LpT[l][0][:, js, :])
                    Li = v4b(LpT[l][1][:, js, :])
                vtt(v4(tA[:, 0:512]), v4(ar[:, 0:512]), Lr, ALU.mult, [ar, LpT[l][0]], [tA])
                yield
                vtt(v4(tB_[:, 0:512]), v4(ai[:, 0:512]), Li, ALU.mult, [ai, LpT[l][1]], [tB_])
                yield
                vtt(v4b(xre[:, js, :]), v4(tA[:, 0:512]), v4(tB_[:, 0:512]), ALU.subtract, [tA, tB_], [xre])
                yield
                vtt(v4(tA[:, 0:512]), v4(ar[:, 0:512]), Li, ALU.mult, [ar, LpT[l][1]], [tA])
                yield
                vtt(v4(tB_[:, 0:512]), v4(ai[:, 0:512]), Lr, ALU.mult, [ai, LpT[l][0]], [tB_])
                yield
                vtt(v4b(xim[:, js, :]), v4(tA[:, 0:512]), v4(tB_[:, 0:512]), ALU.add, [tA, tB_], [xim])
                yield
            acopy(xbre[:], xre[:], [xre], [xbre])
            yield
            acopy(xbim[:], xim[:], [xim], [xbim])
            yield
            xr4 = xre[:].rearrange("p j (b t) -> p j b t", b=nb)
            xi4 = xim[:].rearrange("p j (b t) -> p j b t", b=nb)
            vcopy(xe[0][:, :, 0:nb], xr4[:, :, :, bs - 1], [xre], [xe[0]])
            yield
            vcopy(xe[1][:, :, 0:nb], xi4[:, :, :, bs - 1], [xim], [xe[1]])
            yield
            if samp:
                for r in range(2):
                    dma('pool', o_sss[l, r], xe[r][:], r=[xe[r]])
            else:
                for r in range(2):
                    vcopy(xep[r][:], xe[r][:, :, 0], [xe[r]], [xep[r]])
                    yield
                if last:
                    for r in range(2):
                        dma('pool', o_pss[l, r], xep[r][:], r=[xep[r]])
                vtt(tA[:, 0:8], xep[0][:], lbar[l][0][:], ALU.mult, [xep[0], lbar[l][0]], [tA])
                yield
                vtt(tB_[:, 0:8], xep[1][:], lbar[l][1][:], ALU.mult, [xep[1], lbar[l][1]], [tB_])
                yield
                vtt(cinP[l][0][:, :, 0], tA[:, 0:8], tB_[:, 0:8], ALU.subtract, [tA, tB_], [cinP[l][0]])
                yield
                vtt(tA[:, 0:8], xep[0][:], lbar[l][1][:], ALU.mult, [xep[0], lbar[l][1]], [tA])
                yield
                vtt(tB_[:, 0:8], xep[1][:], lbar[l][0][:], ALU.mult, [xep[1], lbar[l][0]], [tB_])
                yield
                vtt(cinP[l][1][:, :, 0], tA[:, 0:8], tB_[:, 0:8], ALU.add, [tA, tB_], [cinP[l][1]])
                yield
            py = pM[0]
            for cc in range(2):
                mm(py[:, cc * 128:(cc + 1) * 128], uT[:, cc, :], diagD[l][:, cc, :], [uT, diagD[l]], [py], start=True, stop=False)
                for jj in range(4):
                    j = 4 * cc + jj
                    mm(py[:, j * 32:(j + 1) * 32], xbre[:, j, :], cft[l][0][:, j, :], [xbre, cft[l][0]], [py], start=False, stop=False)
                    mm(py[:, j * 32:(j + 1) * 32], xbim[:, j, :], cft[l][1][:, j, :], [xbim, cft[l][1]], [py], start=False, stop=(jj == 3))
            acopy(yb0[:], py[:, 0:256], [py], [yb0])
            yield
            vtt(yb[:], yb0[:], yb0[:], ALU.mult, [yb0], [yb])
            yield
            vts(yb[:], yb[:], 0.044715, ALU.mult, [yb], [yb], 1.0, ALU.add)
            yield
            vtt(yb[:], yb[:], yb0[:], ALU.mult, [yb, yb0], [yb])
            yield
            act(yb[:], yb[:], AF.Tanh, [yb], [yb], scale=0.7978845608028654)
            yield
            vstt(yb[:], yb[:], 1.0, yb0[:], ALU.add, ALU.mult, [yb, yb0], [yb])
            yield
            vts(yb[:], yb[:], 0.5, ALU.mult, [yb], [yb])
            yield
            for cc in range(2):
                tr(pF[2 + cc][:], yb[:, cc * 128:(cc + 1) * 128], identf, [yb, cst], [pF[2 + cc]])
                acopy(ybT[:, cc, :], pF[2 + cc][:], [pF[2 + cc]], [ybT])
                yield
            vcopy(ybTb[:], ybT[:], [ybT], [ybTb])
            yield
            for c2 in range(2):
                for cc in range(2):
                    mm(pF[2 + c2][:], glw[l][:, cc, c2 * 128:(c2 + 1) * 128], ybTb[:, cc, :], [glw[l], ybTb], [pF[2 + c2]], start=(cc == 0), stop=(cc == 1))
                act(sgT[:], pF[2 + c2][:], AF.Sigmoid, [pF[2 + c2], ppt[l]], [sgT], bias=pp(l, 'glub', c2, c2 + 1))
                yield
                vtt(ocatT[:, 4 + c2, :], ybT[:, c2, :], sgT[:], ALU.mult, [ybT, sgT], [ocatT])
                yield

        def hgrn(l, samp, last):
            kind = 'S' if samp else 'H'
            nb = 16 if samp else 4
            bs = 128 // nb
            cTf, cTb = C('cT' + kind), C('cT' + kind, True)
            smf = C('sm' + kind)
            self_ = C('sel' + kind)
            pTf = pT[:].rearrange("p c t -> p (c t)").bitcast(F32)
            pTfb = pT
            vtt(fto[:], fto[:], omlb[l][:], ALU.mult, [fto, omlb[l]], [fto])
            yield
            vtt(fto[:], fto[:], lbb[l][:], ALU.add, [fto, lbb[l]], [fto])
            yield
            act(logf[:], fto[:], AF.Ln, [fto], [logf])
            yield
            vts(omf[:], fto[:], -1.0, ALU.mult, [fto], [omf], 1.0, ALU.add)
            yield
            for cc in range(2):
                vts(fT[:, cc, :], fT[:, cc, :], omlf[l][:, cc:cc + 1], ALU.mult, [fT, omlf[l], lbf[l]], [fT], lbf[l][:, cc:cc + 1], ALU.add)
                yield
            vts(fT[:], fT[:], -1.0, ALU.mult, [fT], [fT], 1.0, ALU.add)
            yield
            mm(pM[5][:, 0:256], cTf, logf[:], [cst, logf], [pM[5]])
            mm(pM[5][:, 256:512], smf, logf[:], [cst, logf], [pM[5]])
            pbT = pTf[:, 0:256].rearrange("p (c t) -> p c t", c=2)
            for cc in range(2):
                mm(pbT[:, cc, :], logf[:, cc * 128:(cc + 1) * 128], cTf, [logf, cst], [pTfb])
                mm(pTf[:, 256 + 16 * cc:256 + 16 * cc + nb], logf[:, cc * 128:(cc + 1) * 128], self_[:, 0:nb], [logf, cst], [pTfb])
            act(e1[:], pbT, AF.Exp, [pTfb], [e1])
            yield
            vtt(qtT[:], hqTs[:], e1[:], ALU.mult, [hqTs, e1], [qtT])
            yield
            act(e1[:], pbT, AF.Exp, [pTfb], [e1], scale=-1.0)
            yield
            vtt(ktT[:], fT[:], e1[:], ALU.mult, [fT, e1], [ktT])
            yield
            act(glh[:, :, 0:nb], pTf[:, 256:288].rearrange("p (c b) -> p c b", c=2)[:, :, 0:nb], AF.Exp, [pTfb], [glh])
            yield
            acopy(b_sb[:], pM[5][:, 0:256], [pM[5]], [b_sb])
            yield
            vtt(b_sb[:], pM[5][:, 256:512], b_sb[:], ALU.subtract, [pM[5], b_sb], [b_sb])
            yield
            act(b_sb[:], b_sb[:], AF.Exp, [b_sb], [b_sb])
            yield
            vtt(khat[:], omf[:], b_sb[:], ALU.mult, [omf, b_sb], [khat])
            yield
            pa = pTf[:, 0:512].rearrange("p (h t) -> p h t", h=4)
            for hh in range(4):
                hl, hc = hh % 2, hh // 2
                mm(pa[:, hh, :], ktT[hl * 64:(hl + 1) * 64, hc, :], qtT[hl * 64:(hl + 1) * 64, hc, :], [ktT, qtT], [pTfb], sync=True)
            vtt(aTm[:], pa, cTf.unsqueeze(1).to_broadcast([128, 4, 128]), ALU.mult, [pTfb, cst], [aTm])
            yield
            po = pM[3][0:64, :].rearrange("p (h t) -> p h t", h=4)
            for hh in range(4):
                mm(po[:, hh, :], v_h[:, hh * 64:(hh + 1) * 64], aTm[:, hh, :], [v_h, aTm], [pM[3]], start=(hh == 0), stop=False, skip=True, sync=True)
            if samp:
                for hc_ in range(2):
                    dma('pool', Shs[:, hc_], s_hg[l][:, 2 * hc_:2 * hc_ + 2].rearrange("b hl k v -> (hl k) b v"), w=[Shs])
                acopy(Shs_b[:], Shs[:], [Shs], [Shs_b])
                yield
            for j in range(nb):
                for hh in range(4):
                    hl, hc = hh % 2, hh // 2
                    if samp:
                        Sb, Sbt = Shs_b[hl * 64:(hl + 1) * 64, hc, j, :], Shs_b
                    else:
                        Sb, Sbt = Sh_b[l][hl * 64:(hl + 1) * 64, hc, :], Sh_b[l]
                    mm(po[:, hh, j * bs:(j + 1) * bs], Sb, qtT[hl * 64:(hl + 1) * 64, hc, j * bs:(j + 1) * bs], [Sbt, qtT], [pM[3]], start=False, stop=True, skip=True, sync=True)
                kh = khm[j % 2]
                vts(kh[:], khat[:], self_[:, j:j + 1], ALU.mult, [khat, cst], [kh])
                yield
                for cc in range(2):
                    pd = pF[cc]
                    mm(pd[:], kh[:, cc * 128:(cc + 1) * 128], v_h[:, cc * 128:(cc + 1) * 128], [kh, v_h], [pd], sync=(cc == 0))
                    for hl in range(2):
                        ps_ = slice(hl * 64, (hl + 1) * 64)
                        if samp:
                            vstt(Shs[ps_, cc, j, :], Shs[ps_, cc, j, :], glh[ps_, cc, j:j + 1], pd[ps_, hl * 64:(hl + 1) * 64], ALU.mult, ALU.add, [Shs, glh, pd], [Shs])
                            yield
                        else:
                            vstt(Sh[l][ps_, cc, :], Sh[l][ps_, cc, :], glh[ps_, cc, j:j + 1], pd[ps_, hl * 64:(hl + 1) * 64], ALU.mult, ALU.add, [Sh[l], glh, pd], [Sh[l]])
                            yield
                if not samp:
                    acopy(Sh_b[l][:], Sh[l][:], [Sh[l]], [Sh_b[l]])
                    yield
            if samp:
                dma('pool', o_shg[l], Shs[:], r=[Shs])
            elif last:
                dma('pool', o_phg[l], Sh[l][:], r=[Sh[l]])
            acopy(oT_sb[:], po, [pM[3]], [oT_sb])
            yield
            poc = pM[5]
            for hh in range(4):
                tr(poc[:, hh * 64:(hh + 1) * 64], oT_sb[:, hh, :], identf[0:64, 0:64], [oT_sb, cst], [poc])
            V(lambda e: e.memset(dsc[:, 48:52], 0.0), [], [dsc])
            for hh in range(4):
                act(dtmp[:, 0:64], poc[:, hh * 64:(hh + 1) * 64], AF.Square, [poc], [dtmp, dsc], accum=dsc[:, 48 + hh:49 + hh])
                yield
            rsqrt_small(dsc[:, 48:52], dsc[:, 48:52], [dsc], [dsc], 1.0 / 64, EPS)
            for hh in range(4):
                vstt(oc[:, hh * 64:(hh + 1) * 64], poc[:, hh * 64:(hh + 1) * 64], dsc[:, 48 + hh:49 + hh], bcv(l, 'hgn'), ALU.mult, ALU.mult, [poc, dsc, bcs[l]], [oc])
                yield
            vtt(ocb[:], oc[:], hg_s[:], ALU.mult, [oc, hg_s], [ocb])
            yield
            for cc in range(2):
                tr(pT[:, cc, :], ocb[:, cc * 128:(cc + 1) * 128], identb, [ocb, cstb], [pT])
            acopy(ocatT[:, 6:8, :], pT[:, 0:2, :], [pT], [ocatT])
            yield

        def out_proj(l):
            for half in range(2):
                slot, wv = wload(('out', l, half))
                pm = pM[4 + half]
                for kc in range(8):
                    mm(pm[:], ocatT[:, kc, :], wv[:, kc, :], [ocatT, slot], [pm], start=(kc == 0), stop=(kc == 7))
                hs = slice(half * 512, (half + 1) * 512)
                vtt(h[:, hs], h[:, hs], pm[:], ALU.add, [h, pm], [h])

        def ffn(l, samp, last):
            norm_T(pp(l, 'nffn'), ppt[l])
            if samp:
                dma('pool', scf[:], s_cf[l], w=[scf])
            def views(blk):
                xpf, acc, acc2, sa = FB[blk % 2]
                o_f = PP['fcw'][0] + blk * 12
                cw = ppt[l][:, o_f:o_f + 12].rearrange("p (q j) -> p q j", q=4)
                if not samp:
                    xpf3 = xpf[:, :, 0:130]
                    xs = [xpf3[:, :, j:j + 128] for j in range(3)]
                    ws = [cw[:, :, j:j + 1].to_broadcast([128, 4, 128]) for j in range(3)]
                    av, a2v = acc[:], acc2[:]
                    cin_v, new_v, cout_v = xpf3[:, :, 0:2], xpf3[:, :, 2:130], xpf3[:, :, 128:130]
                else:
                    xpf4 = xpf[:].rearrange("p q (b t) -> p q b t", b=16)
                    xs = [xpf4[:, :, :, j:j + 8] for j in range(3)]
                    ws = [cw[:, :, j:j + 1].unsqueeze(3).to_broadcast([128, 4, 16, 8]) for j in range(3)]
                    av = acc[:].rearrange("p q (b t) -> p q b t", b=16)
                    a2v = acc2[:].rearrange("p q (b t) -> p q b t", b=16)
                    cin_v, new_v, cout_v = xpf4[:, :, :, 0:2], xpf4[:, :, :, 2:10], xpf4[:, :, :, 8:10]
                return xpf, acc, acc2, sa, cw, xs, ws, av, a2v, cin_v, new_v, cout_v

            def front(blk):
                xpf, acc, acc2, sa, cw, xs, ws, av, a2v, cin_v, new_v, cout_v = views(blk)
                slot, wv = wload(('up', l, blk))
                pm = pM[blk % 2]
                pm3 = pm[:].rearrange("p (q t) -> p q t", q=4)
                for q in range(4):
                    for kc in range(8):
                        mm(pm3[:, q, :], wv[:, kc, q * 128:(q + 1) * 128], hnT[:, kc, :], [slot, hnT], [pm], start=(kc == 0), stop=(kc == 7))
                if not samp:
                    vcopy(cin_v, cfc[l][:, blk, :, :], [cfc[l]], [xpf])
                    acopy(new_v, pm3, [pm], [xpf])
                else:
                    vcopy(cin_v, scf[:, blk], [scf], [xpf])
                    acopy(new_v, pm[:].rearrange("p (q b t) -> p q b t", q=4, b=16), [pm], [xpf])
                for q in range(4):
                    act(a2v[:, q], xs[1][:, q], AF.Identity, [xpf, ppt[l]], [acc2], scale=cw[:, q, 1:2])

            def conv(blk):
                xpf, acc, acc2, sa, cw, xs, ws, av, a2v, cin_v, new_v, cout_v = views(blk)
                vtt(av, xs[0], ws[0], ALU.mult, [xpf, ppt[l]], [acc])
                vtt(av, av, a2v, ALU.add, [acc, acc2], [acc])
                vtt(a2v, xs[2], ws[2], ALU.mult, [xpf, ppt[l]], [acc2])
                vtt(av, av, a2v, ALU.add, [acc, acc2], [acc])
                if not samp:
                    vcopy(cfc[l][:, blk, :, :], cout_v, [xpf], [cfc[l]])
                else:
                    vcopy(scf[:, blk], cout_v, [xpf], [scf])

            def back_silu(blk):
                xpf, acc, acc2, sa = FB[blk % 2]
                act(sa[:], acc[:, 0:2, :], AF.Silu, [acc], [sa])

            def back_mult(blk):
                xpf, acc, acc2, sa = FB[blk % 2]
                vtt(gT[:, 2 * blk:2 * blk + 2, :], sa[:], acc[:, 2:4, :], ALU.mult, [sa, acc], [gT])
            front(0)
            conv(0)
            for blk in range(11):
                if blk + 1 < 11:
                    front(blk + 1)
                back_silu(blk)
                if blk + 1 < 11:
                    conv(blk + 1)
                back_mult(blk)
            if samp:
                dma('pool', o_scf[l], scf[:], r=[scf])
            elif last:
                dma('pool', o_pcf[l], cfc[l][:], r=[cfc[l]])
            for half in range(2):
                pm = pM[2 + half]
                for q4 in range(4):
                    slot, wv = wload(('down', l, half, q4))
                    n_c = 6 if q4 < 3 else 4
                    c0 = q4 * 6
                    for c in range(n_c):
                        mm(pm[:], gT[:, c0 + c, :], wv[:, c, :], [gT, slot], [pm], start=(c0 + c == 0), stop=(c0 + c == 21))
                hs = slice(half * 512, (half + 1) * 512)
                vtt(h[:, hs], h[:, hs], pm[:], ALU.add, [h, pm], [h])

        def ple(l, si):
            norm_T(pp(l, 'nple'), ppt[l])
            dma('pool', pf[:], pin[l, si], w=[pf])
            vcopy(pb[:], pf[:], [pf], [pb])
            for half in range(2):
                slot, wv = wload(('pg', l, half))
                pm = pM[half]
                for kc in range(8):
                    mm(pm[:], hnT[:, kc, :], wv[:, kc, :], [hnT, slot], [pm], start=(kc == 0), stop=(kc == 7))
                act(gsbh[half][:], pm[:], AF.Sigmoid, [pm], [gsbh[half]])
            for cc in range(2):
                tr(pT[:, cc, :], pb[:, cc * 128:(cc + 1) * 128], identb, [pb, cstb], [pT])
            vcopy(ppT[:], pT[:, 0:2, :], [pT], [ppT])
            for half in range(2):
                slot, wv = wload(('pp', l, half))
                pm = pM[2 + half]
                for cc in range(2):
                    mm(pm[:], ppT[:, cc, :], wv[:, cc, :], [ppT, slot], [pm], start=(cc == 0), stop=(cc == 1))
                hs = slice(half * 512, (half + 1) * 512)
                vtt(tmp1[:], gsbh[half][:], pm[:], ALU.mult, [gsbh[half], pm], [tmp1])
                vtt(h[:, hs], h[:, hs], tmp1[:], ALU.add, [h, tmp1], [h])

        for si in range(NSUB):
            samp = (si == NSUB - 1)
            last = (si == NSUB - 2)
            if samp:
                for l in range(2):
                    build_tables(l, True)
            dma('pool', h[:], xin[si], w=[h])
            for l in range(2):
                in_proj(l, samp)
                gens_ = [in_proj_rest(l, samp), delta(l, samp, last)]
                while gens_:
                    for g_ in list(gens_):
                        try:
                            next(g_)
                        except StopIteration:
                            gens_.remove(g_)
                gens_ = [s5(l, samp, last), hgrn(l, samp, last)]
                while gens_:
                    for g_ in list(gens_):
                        try:
                            next(g_)
                        except StopIteration:
                            gens_.remove(g_)
                if KSTAGE >= 5:
                    out_proj(l)
                ffn(l, samp, last)
                ple(l, si)
            norm_stats()
            for half in range(2):
                hs = slice(half * 512, (half + 1) * 512)
                vstt(yoh[half][:], h[:, hs], st[:, 3:4], nfin[:, hs], ALU.mult, ALU.mult, [h, st, nfin], [yoh[half]])
                dma('pool', y_d[si][:, hs], yoh[half][:], r=[yoh[half]])
        S.finish()
        with nc.Block() as block:
            S.emit(nc, block)
    return nc


def _fm(v, nchunk):
    return np.ascontiguousarray(v.reshape(nchunk, 128).T)


_NC_CACHE = {}


def _prepare(inputs):
    f = {k: np.asarray(v, dtype=np.float32) for k, v in inputs.items()}
    n = 8
    pps, bcs = [], []
    for l in range(2):
        pp = np.zeros((128, NPP), np.float32)

        def put(name, arr):
            o, w = PP[name]
            pp[:, o:o + w] = arr.reshape(128, w)
        put('nmix', _fm(f['norm_mix'][l], 8))
        put('nffn', _fm(f['norm_ffn'][l], 8))
        put('nple', _fm(f['norm_ple'][l], 8))
        put('dcw', f['dn_conv_w'][l].reshape(4, 12, 128).transpose(2, 1, 0))
        fw = f['ffn_conv_w'][l].reshape(3, 2, 11, 2, 128)
        put('fcw', fw.transpose(4, 2, 1, 3, 0))
        put('lre', _fm(f['ssm_lam_re'][l].reshape(-1), 8))
        put('lim', _fm(f['ssm_lam_im'][l].reshape(-1), 8))
        put('lst', _fm(np.repeat(f['ssm_log_step'][l], 64), 8))
        put('ssd', _fm(f['ssm_d'][l], 2))
        put('glub', _fm(f['ssm_glu_b'][l], 2))
        put('hl0', _fm(f['hg_lower'][0], 2))
        put('hl1', _fm(f['hg_lower'][1], 2))
        pps.append(pp)
        bc = np.zeros((NBC,), np.float32)

        def putb(name, arr):
            o, w = BC[name]
            bc[o:o + w] = arr.reshape(w)
        putb('dnn', f['dn_norm'][l])
        putb('hgn', f['hg_norm'][l])
        putb('alog', f['dn_a_log'][l])
        putb('dtb', f['dn_dt_bias'][l])
        putb('lre', f['ssm_lam_re'][l])
        putb('lim', f['ssm_lam_im'][l])
        putb('lst', np.repeat(f['ssm_log_step'][l], 64))
        putb('hl0', f['hg_lower'][0])
        putb('hl1', f['hg_lower'][1])
        putb('nfin', f['norm_final'])
        bcs.append(bc)
    pp_all = np.stack(pps)
    bc_all = np.stack(bcs)
    bfull = np.zeros((2, 2, 128, 2, 512), np.float32)
    cfull = np.zeros((2, 2, 128, 8, 32), np.float32)
    for l in range(2):
        for r, (bn, cn) in enumerate((('ssm_b_re', 'ssm_c_re'), ('ssm_b_im', 'ssm_c_im'))):
            for g in range(16):
                cc, gg = divmod(g, 8)
                j, gl = divmod(g, 2)
                bfull[l, r, gg * 16:(gg + 1) * 16, cc, (j % 4) * 128 + gl * 64:(j % 4) * 128 + gl * 64 + 64] = f[bn][l, g].T
                cfull[l, r, gl * 64:(gl + 1) * 64, j, gl * 16:(gl + 1) * 16] = f[cn][l, g].T
    in_maps = []
    for c in range(n):
        xs = np.concatenate([f['x_prompt'][c].reshape(16, 128, 1024),
                             f['x_sample'][c * 16:(c + 1) * 16].reshape(1, 128, 1024)], axis=0)
        ps_ = np.concatenate([f['p_prompt'][:, c].reshape(2, 16, 128, 256),
                              f['p_sample'][:, c * 16:(c + 1) * 16].reshape(2, 1, 128, 256)], axis=1)
        sl = slice(c * 16, (c + 1) * 16)
        s_cq = f['state_conv_qkv'][:, sl].reshape(2, 16, 3, 12, 128).transpose(0, 4, 3, 1, 2)
        s_ss = np.stack([f['state_ssm_re'][:, sl], f['state_ssm_im'][:, sl]], axis=1).reshape(2, 2, 16, 8, 128).transpose(0, 1, 4, 3, 2)
        s_cf = f['state_conv_ffn'][:, sl].reshape(2, 16, 2, 2, 11, 2, 128).transpose(0, 6, 4, 3, 5, 1, 2).reshape(2, 128, 11, 4, 16, 2)
        in_maps.append({
            "xin": np.ascontiguousarray(xs), "pin": np.ascontiguousarray(ps_), "consts": CONSTS, "pp": pp_all, "bc": bc_all,
            "w_in": f['w_in'], "w_out": f['w_out'], "w_up": f['ffn_w_up'], "w_down": f['ffn_w_down'],
            "w_pg": f['ple_w_gate'], "w_pp": f['ple_w_proj'], "w_glu": f['ssm_glu_w'], "bfull": bfull, "cfull": cfull,
            "s_cq": np.ascontiguousarray(s_cq), "s_dl": np.ascontiguousarray(f['state_delta'][:, sl]),
            "s_ss": np.ascontiguousarray(s_ss), "s_hg": np.ascontiguousarray(f['state_hgrn'][:, sl]),
            "s_cf": np.ascontiguousarray(s_cf),
        })
    return in_maps


def kernel(**inputs):
    n = 8
    in_maps = _prepare(inputs)
    if 'nc' not in _NC_CACHE:
        _NC_CACHE['nc'] = build()
    res = run_bass_kernel_spmd(_NC_CACHE['nc'], in_maps, core_ids=list(range(n))).results
    yp = np.stack([r["y"][:16].reshape(2048, 1024) for r in res])
    ys = np.concatenate([r["y"][16].reshape(16, 8, 1024) for r in res], axis=0)

    def cat(fn, axis=1):
        return np.ascontiguousarray(np.concatenate([fn(r) for r in res], axis=axis))
    p_cq = cat(lambda r: r["o_pcq"].transpose(0, 3, 2, 1).reshape(2, 1, 3, 1536))
    p_dl = cat(lambda r: r["o_pdl"].reshape(2, 1, 4, 128, 128))
    p_sr = cat(lambda r: r["o_pss"][:, 0].transpose(0, 2, 1).reshape(2, 1, 16, 64))
    p_si = cat(lambda r: r["o_pss"][:, 1].transpose(0, 2, 1).reshape(2, 1, 16, 64))
    p_hg = cat(lambda r: r["o_phg"].reshape(2, 2, 64, 2, 64).transpose(0, 3, 1, 2, 4).reshape(2, 1, 4, 64, 64))
    p_cf = cat(lambda r: r["o_pcf"].reshape(2, 128, 11, 2, 2, 2).transpose(0, 5, 3, 2, 4, 1).reshape(2, 1, 2, 5632))
    s_cq = cat(lambda r: r["o_scq"].transpose(0, 3, 4, 2, 1).reshape(2, 16, 3, 1536))
    s_dl = cat(lambda r: r["o_sdl"])
    s_sr = cat(lambda r: r["o_sss"][:, 0].transpose(0, 3, 2, 1).reshape(2, 16, 16, 64))
    s_si = cat(lambda r: r["o_sss"][:, 1].transpose(0, 3, 2, 1).reshape(2, 16, 16, 64))
    s_hg = cat(lambda r: r["o_shg"].reshape(2, 2, 64, 2, 16, 64).transpose(0, 4, 3, 1, 2, 5).reshape(2, 16, 4, 64, 64))
    s_cf = cat(lambda r: r["o_scf"].reshape(2, 128, 11, 2, 2, 16, 2).transpose(0, 5, 6, 3, 2, 4, 1).reshape(2, 16, 2, 5632))
    return (yp, ys, p_cq, p_dl, p_sr, p_si, p_hg, p_cf, s_cq, s_dl, s_sr, s_si, s_hg, s_cf)
```

```python
import math
from contextlib import ExitStack
import numpy as np
import concourse.bass as bass
import concourse.mybir as mybir
from concourse.bass_utils import run_bass_kernel_spmd

F32 = mybir.dt.float32
BF16 = mybir.dt.bfloat16
I32 = mybir.dt.int32
ALU = mybir.AluOpType
AF = mybir.ActivationFunctionType
EPS = 1e-6
import os
NSUB = int(os.environ.get('KNSUB', '17'))
NRING = 4
SEM_EPOCH = 4000
NEPOCH = {'pe': 10, 'dve': 5, 'act': 3, 'pool': 1}
KSUB = int(os.environ.get('KSUB', '9'))
KSTAGE = int(os.environ.get('KSTAGE', '9'))
TWO_PI = 2.0 * math.pi


class Buf:
    __slots__ = ('name', 'w', 'r', 'excl')

    def __init__(self, name, excl=False):
        self.name = name
        self.w = None
        self.r = {}
        self.excl = excl


class TB:
    def __init__(self, t, name, bufs=None):
        self.t = t
        self.bs = bufs if bufs is not None else [Buf(name)]

    def __getitem__(self, k):
        return self.t[k]


class Sched:
    def __init__(self, sems):
        self.sems = sems
        self.cnt = {k: 0 for k in sems}
        self.prog = {'pe': [], 'dve': [], 'act': [], 'pool': [], 'sp': []}
        self.waited = {e: {} for e in self.prog}
        self.dma_rr = {'sp': 0, 'pool': 0, 'act': 0}
        self.epoch = {}
        self.last_pe = None
        self.dma_names = {q: sorted(n for n in sems if n.startswith('d_%s_' % q)) for q in ('sp', 'pool', 'act')}

    def _need(self, eng, s, v, force=False):
        if eng == 'pe' and s.startswith('pe_') and not force:
            return
        if self.waited[eng].get(s, 0) < v:
            self.prog[eng].append(('w', s, v))
            self.waited[eng][s] = v

    def _deps(self, eng, reads, writes):
        for b in reads:
            if b.w is not None:
                self._need(eng, *b.w)
        for b in writes:
            if b.w is not None:
                self._need(eng, *b.w)
            for s, v in b.r.items():
                self._need(eng, s, v)

    def _done(self, tok, reads, writes):
        for b in writes:
            b.w = tok
            b.r = {}
        for b in reads:
            if b.r.get(tok[0], 0) < tok[1]:
                b.r[tok[0]] = tok[1]

    def op(self, eng, fn, reads=(), writes=(), pe_sync=False):
        if pe_sync and self.last_pe is not None:
            self._need('pe', self.last_pe[0], self.last_pe[1], force=True)
        reads = [b for x in reads for b in x.bs]
        writes = [b for x in writes for b in x.bs]
        writes = writes + [b for b in reads if b.excl and b not in writes]
        reads = [b for b in reads if not b.excl]
        self._deps(eng, reads, writes)
        ep = self.epoch.setdefault(eng, 0)
        s = '%s_%d' % (eng, ep)
        if self.cnt[s] >= SEM_EPOCH:
            ep += 1
            self.epoch[eng] = ep
            s = '%s_%d' % (eng, ep)
        self.cnt[s] += 1
        tok = (s, self.cnt[s])
        self.prog[eng].append(('o', fn, s, 1))
        if eng == 'pe':
            self.last_pe = tok
        self._done(tok, reads, writes)

    def dma(self, q, fn, reads=(), writes=()):
        reads = [b for x in reads for b in x.bs]
        writes = [b for x in writes for b in x.bs]
        names = self.dma_names[q]
        s = names[self.dma_rr[q] % len(names)]
        self.dma_rr[q] += 1
        if self.cnt[s] > 0:
            self._need(q, s, self.cnt[s])
        self._deps(q, reads, writes)
        self.cnt[s] += 16
        tok = (s, self.cnt[s])
        self.prog[q].append(('o', fn, s, 16))
        self._done(tok, reads, writes)

    def finish(self):
        for q in ('sp', 'pool', 'act'):
            for s in self.dma_names[q]:
                if self.cnt[s] > 0:
                    self._need('sp', s, self.cnt[s])
        for s in self.sems:
            if not s.startswith('d_') and self.cnt[s] > 0:
                self._need('sp', s, self.cnt[s])

    def emit(self, nc, block):
        sems = self.sems

        def replay(engobj, prog):
            for it in prog:
                if it[0] == 'w':
                    engobj.wait_ge(sems[it[1]], it[2])
                else:
                    it[1](engobj).then_inc(sems[it[2]], it[3])

        @block.tensor
        def _(e):
            replay(e, self.prog['pe'])

        @block.vector
        def _(e):
            replay(e, self.prog['dve'])

        @block.scalar
        def _(e):
            replay(e, self.prog['act'])

        @block.gpsimd
        def _(e):
            replay(e, self.prog['pool'])

        @block.sync
        def _(e):
            replay(e, self.prog['sp'])


def _consts():
    p = np.arange(128)
    c = {}
    c['ident'] = np.eye(128)
    for nm, bs in (('P', 128), ('S', 8), ('H', 32)):
        same = (p[:, None] // bs) == (p[None, :] // bs)
        c['cT' + nm] = ((p[:, None] <= p[None, :]) & same)
        c['st' + nm] = ((p[None, :] < p[:, None]) & same)
        c['sm' + nm] = same
        nb = 128 // bs
        sel = np.zeros((128, 16))
        sel[p, p // bs] = 1.0
        c['sel' + nm] = sel
    c['iota'] = np.tile(np.arange(128)[None, :], (128, 1))
    c['pcol'] = np.tile(p[:, None], (1, 2))
    c['pcol'][:, 1] = p % 8
    names = ['ident', 'cTP', 'cTS', 'cTH', 'stP', 'stS', 'smP', 'smS', 'smH', 'selP', 'selS', 'selH', 'iota', 'pcol']
    offs = {}
    cols = []
    o = 0
    for n in names:
        a = c[n].astype(np.float32)
        offs[n] = (o, a.shape[1])
        o += a.shape[1]
        cols.append(a)
    return np.concatenate(cols, axis=1), offs


CONSTS, COFF = _consts()
NCONST = CONSTS.shape[1]
PP = {}
_o = 0
for _n, _w in (('nmix', 8), ('nffn', 8), ('nple', 8), ('dcw', 48), ('fcw', 132), ('lre', 8), ('lim', 8), ('lst', 8),
               ('ssd', 2), ('glub', 2), ('hl0', 2), ('hl1', 2)):
    PP[_n] = (_o, _w)
    _o += _w
NPP = _o
BC = {}
_o = 0
for _n, _w in (('dnn', 128), ('hgn', 64), ('alog', 4), ('dtb', 4), ('hl0', 256), ('hl1', 256),
               ('lre', 1024), ('lim', 1024), ('lst', 1024), ('nfin', 1024)):
    BC[_n] = (_o, _w)
    _o += _w
NBC = _o
NBCS = 712


def build():
    nc = bass.Bass("TRN2", target_bir_lowering=False)

    def din(name, shape):
        return nc.dram_tensor(name, list(shape), F32, kind="ExternalInput").ap()

    def dout(name, shape):
        return nc.dram_tensor(name, list(shape), F32, kind="ExternalOutput").ap()

    xin = din("xin", [NSUB, 128, 1024])
    pin = din("pin", [2, NSUB, 128, 256])
    consts_d = din("consts", [128, NCONST])
    pp_d = din("pp", [2, 128, NPP])
    bc_d = din("bc", [2, NBC])
    w_in = din("w_in", [2, 1024, 3336])
    w_out = din("w_out", [2, 1024, 1024])
    w_up = din("w_up", [2, 1024, 5632])
    w_down = din("w_down", [2, 2816, 1024])
    w_pg = din("w_pg", [2, 1024, 1024])
    w_pp = din("w_pp", [2, 256, 1024])
    w_glu = din("w_glu", [2, 256, 256])
    bfull = din("bfull", [2, 2, 128, 2, 512])
    cfull = din("cfull", [2, 2, 128, 8, 32])
    s_cq = din("s_cq", [2, 128, 12, 16, 3])
    s_dl = din("s_dl", [2, 16, 4, 128, 128])
    s_ss = din("s_ss", [2, 2, 128, 8, 16])
    s_hg = din("s_hg", [2, 16, 4, 64, 64])
    s_cf = din("s_cf", [2, 128, 11, 4, 16, 2])
    y_d = dout("y", [NSUB, 128, 1024])
    o_pcq = dout("o_pcq", [2, 128, 12, 3])
    o_pdl = dout("o_pdl", [2, 4, 128, 128])
    o_pss = dout("o_pss", [2, 2, 128, 8])
    o_phg = dout("o_phg", [2, 128, 2, 64])
    o_pcf = dout("o_pcf", [2, 128, 11, 4, 2])
    o_scq = dout("o_scq", [2, 128, 12, 16, 3])
    o_sdl = dout("o_sdl", [2, 16, 4, 128, 128])
    o_sss = dout("o_sss", [2, 2, 128, 8, 16])
    o_shg = dout("o_shg", [2, 128, 2, 16, 64])
    o_scf = dout("o_scf", [2, 128, 11, 4, 16, 2])

    es = ExitStack()
    with es:
        def sb(name, shape, dt=F32):
            return TB(es.enter_context(nc.sbuf_tensor(name, list(shape), dt)), name)

        def ps(name, shape, dt=F32):
            return TB(es.enter_context(nc.psum_tensor(name, list(shape), dt)), name, bufs=[Buf(name, excl=True)])

        sems = {}
        for n in ['%s_%d' % (e_, i_) for e_ in NEPOCH for i_ in range(NEPOCH[e_])] + ['d_sp_%d' % i for i in range(8)] + ['d_pool_%d' % i for i in range(6)] + ['d_act_%d' % i for i in range(2)]:
            sems[n] = es.enter_context(nc.semaphore(n))
        S = Sched(sems)

        def V(fn, r=(), w=()):
            S.op('dve', fn, r, w)

        def A(fn, r=(), w=()):
            S.op('act', fn, r, w)

        def mm(out, lhsT, rhs, r, w, start=True, stop=True, skip=False, sync=False):
            S.op('pe', lambda e: e.matmul(out, lhsT=lhsT, rhs=rhs, start=start, stop=stop, skip_group_check=skip), r, w, pe_sync=sync)

        def tr(out, in_, ident, r, w):
            S.op('pe', lambda e: e.transpose(out=out, in_=in_, identity=ident), r, w)

        def vtt(out, a, b, op, r, w):
            V(lambda e: e.tensor_tensor(out=out, in0=a, in1=b, op=op), r, w)

        def vts(out, a, s1, op0, r, w, s2=None, op1=None):
            if op1 is None:
                V(lambda e: e.tensor_scalar(out=out, in0=a, scalar1=s1, scalar2=None, op0=op0), r, w)
            else:
                V(lambda e: e.tensor_scalar(out=out, in0=a, scalar1=s1, scalar2=s2, op0=op0, op1=op1), r, w)

        def vstt(out, a, sc, b, op0, op1, r, w):
            V(lambda e: e.scalar_tensor_tensor(out=out, in0=a, scalar=sc, in1=b, op0=op0, op1=op1), r, w)

        def vcopy(out, a, r, w):
            V(lambda e: e.tensor_copy(out=out, in_=a), r, w)

        def acopy(out, a, r, w):
            A(lambda e: e.copy(out=out, in_=a), r, w)

        def act(out, a, func, r, w, bias=None, scale=None, accum=None):
            kw = {}
            if bias is not None:
                kw['bias'] = bias
            if scale is not None:
                kw['scale'] = scale
            if accum is not None:
                kw['accum_out'] = accum
            A(lambda e: e.activation(out=out, in_=a, func=func, **kw), r, w)

        def dma(q, out, in_, r=(), w=()):
            S.dma(q, lambda e: e.dma_start(out=out, in_=in_), reads=r, writes=w)

        WB = {}

        def wreg(key, kc, n, parts):
            d = nc.dram_tensor("wb_" + "_".join(str(k) for k in key), [128, kc * n], BF16, kind="Internal").ap()
            tb = TB(None, "wb" + str(key))
            d3 = d.rearrange("p (k n) -> p k n", k=kc)
            for (src, c0, ncol) in parts:
                S.dma('pool', lambda e, d3=d3, src=src, c0=c0, ncol=ncol: e.dma_start(out=d3[:, :, c0:c0 + ncol], in_=src.rearrange("(k p) n -> p k n", p=128)), writes=[tb])
            WB[key] = (d3, tb, kc, n)
        for l in range(2):
            W = w_in[l]
            for g in range(3):
                wreg(('in', l, g), 8, 512, [(W[:, g * 512:(g + 1) * 512], 0, 512)])
            wreg(('in', l, 3), 8, 520, [(W[:, 1536:2056], 0, 520)])
            wreg(('in', l, 4), 8, 512, [(W[:, 2056:2568], 0, 512)])
            wreg(('in', l, 5), 8, 512, [(W[:, 2568:3080], 0, 512)])
            wreg(('in', l, 6), 8, 256, [(W[:, 3080:3336], 0, 256)])
            for half in range(2):
                wreg(('out', l, half), 8, 512, [(w_out[l][:, half * 512:(half + 1) * 512], 0, 512)])
            for blk in range(11):
                wreg(('up', l, blk), 8, 512, [(w_up[l][:, blk * 256:(blk + 1) * 256], 0, 256),
                                              (w_up[l][:, 2816 + blk * 256:2816 + (blk + 1) * 256], 256, 256)])
            for half in range(2):
                for q4 in range(4):
                    n_c = 6 if q4 < 3 else 4
                    c0 = q4 * 6
                    wreg(('down', l, half, q4), n_c, 512, [(w_down[l][c0 * 128:(c0 + n_c) * 128, half * 512:(half + 1) * 512], 0, 512)])
                wreg(('pg', l, half), 8, 512, [(w_pg[l][:, half * 512:(half + 1) * 512], 0, 512)])
                wreg(('pp', l, half), 2, 512, [(w_pp[l][:, half * 512:(half + 1) * 512], 0, 512)])

        cst = sb("cst", [128, NCONST])
        cstb = sb("cstb", [128, NCONST], BF16)
        dma('sp', cst[:], consts_d, w=[cst])
        vcopy(cstb[:], cst[:], [cst], [cstb])

        def C(n, bf=False, cols=None):
            o, wd = COFF[n]
            if cols is not None:
                wd = cols
            return (cstb if bf else cst)[:, o:o + wd]

        identf = C('ident')
        identb = C('ident', True)
        onesb = C('smP', True)
        onesf = C('smP')
        ppt = [sb("pp%d" % l, [128, NPP]) for l in range(2)]
        bcs = [sb("bcs%d" % l, [128, NBCS]) for l in range(2)]
        nfin = sb("nfin", [128, 1024])
        dma('sp', nfin[:], bc_d[0, BC['nfin'][0]:BC['nfin'][0] + 1024].partition_broadcast(128), w=[nfin])
        for l in range(2):
            dma('sp', ppt[l][:], pp_d[l], w=[ppt[l]])
            dma('sp', bcs[l][:], bc_d[l, 0:NBCS].partition_broadcast(128), w=[bcs[l]])

        def pp(l, n, a=0, b=None):
            o, wd = PP[n]
            return ppt[l][:, o + a:o + (wd if b is None else b)]

        def bcv(l, n, a=0, b=None):
            o, wd = BC[n]
            return bcs[l][:, o + a:o + (wd if b is None else b)]

        glw = [sb("glw%d" % l, [128, 2, 256], BF16) for l in range(2)]
        bft = [[sb("bf%d%d" % (l, r), [128, 2, 512], BF16) for r in range(2)] for l in range(2)]
        cft = [[sb("cf%d%d" % (l, r), [128, 8, 32], BF16) for r in range(2)] for l in range(2)]
        cftf = sb("cftf", [128, 8, 32])
        diagD = [sb("diagD%d" % l, [128, 2, 128], BF16) for l in range(2)]
        negA = [sb("negA%d" % l, [128, 4]) for l in range(2)]
        lbb = [sb("lbb%d" % l, [128, 256]) for l in range(2)]
        omlb = [sb("omlb%d" % l, [128, 256]) for l in range(2)]
        lbf = [sb("lbf%d" % l, [128, 2]) for l in range(2)]
        omlf = [sb("omlf%d" % l, [128, 2]) for l in range(2)]
        for l in range(2):
            dma('pool', glw[l][:], w_glu[l].rearrange("(c p) n -> p c n", p=128), w=[glw[l]])
            for r in range(2):
                dma('pool', bft[l][r][:], bfull[l, r], w=[bft[l][r]])
            dma('pool', cft[l][0][:], cfull[l, 0], w=[cft[l][0]])
            dma('sp', cftf[:], cfull[l, 1], w=[cftf])
            vts(cft[l][1][:], cftf[:], -1.0, ALU.mult, [cftf], [cft[l][1]])
            for cc in range(2):
                vts(diagD[l][:, cc, :], identf, pp(l, 'ssd', cc, cc + 1), ALU.mult, [cst, ppt[l]], [diagD[l]])
            act(negA[l][:], bcv(l, 'alog'), AF.Exp, [bcs[l]], [negA[l]])
            vts(negA[l][:], negA[l][:], -1.0, ALU.mult, [negA[l]], [negA[l]])
            if l == 0:
                V(lambda e: e.memset(lbb[0][:], 0.0), [], [lbb[0]])
                V(lambda e: e.memset(lbf[0][:], 0.0), [], [lbf[0]])
            else:
                vtt(lbb[1][:], bcv(1, 'hl1'), bcv(1, 'hl0'), ALU.subtract, [bcs[1]], [lbb[1]])
                act(lbb[1][:], lbb[1][:], AF.Sigmoid, [lbb[1]], [lbb[1]])
                vtt(lbf[1][:], pp(1, 'hl1'), pp(1, 'hl0'), ALU.subtract, [ppt[1]], [lbf[1]])
                act(lbf[1][:], lbf[1][:], AF.Sigmoid, [lbf[1]], [lbf[1]])
            vts(omlb[l][:], lbb[l][:], -1.0, ALU.mult, [lbb[l]], [omlb[l]], 1.0, ALU.add)
            vts(omlf[l][:], lbf[l][:], -1.0, ALU.mult, [lbf[l]], [omlf[l]], 1.0, ALU.add)

        ZB = [es.enter_context(nc.sbuf_tensor("ZB%d" % i, [128, 2048], F32)) for i in range(4)]
        PZ = [TB(ZB[i // 4][:, (i % 4) * 512:(i % 4 + 1) * 512], "PZ%d" % i) for i in range(16)]
        tA, tB_, tC, tD, mag, bcA, bcP, cre, cim, lbr, lbi, bs_lre, bs_lim, bs_lst = PZ[0:14]
        bsvd = {'lre': bs_lre, 'lim': bs_lim, 'lst': bs_lst}
        h = sb("h", [128, 1024])
        tI = TB(h[:, 0:512].bitcast(I32), "tI", bufs=h.bs)

        def sincos(dst, ang_r, shift, rr, ww):
            vts(tC[:], ang_r, shift, ALU.add, rr, [tC])
            vcopy(tI[:], tC[:], [tC], [tI])
            vcopy(tD[:], tI[:], [tI], [tD])
            vtt(tC[:], tC[:], tD[:], ALU.subtract, [tC, tD], [tC])
            act(dst, tC[:], AF.Sin, [tC], ww, scale=6.283185)

        LpT = [[sb("LpT%d%d" % (l, r), [128, 8, 128], BF16) for r in range(2)] for l in range(2)]
        Lm = [[sb("Lm%d%d" % (l, r), [128, 1024], BF16) for r in range(2)] for l in range(2)]
        lbar = [[sb("lbar%d%d" % (l, r), [128, 8]) for r in range(2)] for l in range(2)]
        sm_a = sb("sm_a", [128, 8])
        sm_p = sb("sm_p", [128, 8])
        iota3 = C('iota').unsqueeze(1).to_broadcast([128, 4, 128])

        def t3(T):
            return T[:].rearrange("p (j t) -> p j t", j=4)

        def build_tables(l, samp):
            if not samp:
                act(sm_p[:], pp(l, 'lst'), AF.Exp, [ppt[l]], [sm_p])
                vtt(sm_a[:], pp(l, 'lre'), sm_p[:], ALU.mult, [ppt[l], sm_p], [sm_a])
                vstt(sm_p[:], pp(l, 'lim'), 1.0 / TWO_PI, sm_p[:], ALU.mult, ALU.mult, [ppt[l], sm_p], [sm_p])
            for hf in range(2):
                js = slice(4 * hf, 4 * hf + 4)
                cs = slice(512 * hf, 512 * hf + 512)

                def bsv(n):
                    return bsvd[n][:]
                for n_ in ('lre', 'lim', 'lst'):
                    o_ = BC[n_][0] + 512 * hf
                    dma('sp', bsvd[n_][:], bc_d[l, o_:o_ + 512].partition_broadcast(128), w=[bsvd[n_]])
                if not samp:
                    vtt(t3(tA), iota3, sm_a[:, js].unsqueeze(2).to_broadcast([128, 4, 128]), ALU.mult, [cst, sm_a], [tA])
                    act(mag[:], tA[:], AF.Exp, [tA], [mag])
                    vtt(t3(tB_), iota3, sm_p[:, js].unsqueeze(2).to_broadcast([128, 4, 128]), ALU.mult, [cst, sm_p], [tB_])
                    sincos(tA[:], tB_[:], 0.25, [tB_], [tA])
                    vtt(LpT[l][0][:, js, :], t3(tA), t3(mag), ALU.mult, [tA, mag], [LpT[l][0]])
                    vtt(t3(cre), t3(tA), t3(mag), ALU.mult, [tA, mag], [cre])
                    vcopy(lbar[l][0][:, js], t3(cre)[:, :, 1], [cre], [lbar[l][0]])
                    sincos(tA[:], tB_[:], 0.0, [tB_], [tA])
                    vtt(LpT[l][1][:, js, :], t3(tA), t3(mag), ALU.mult, [tA, mag], [LpT[l][1]])
                    vtt(t3(cre), t3(tA), t3(mag), ALU.mult, [tA, mag], [cre])
                    vcopy(lbar[l][1][:, js], t3(cre)[:, :, 1], [cre], [lbar[l][1]])
                act(bcP[:], bsv('lst'), AF.Exp, [bs_lst], [bcP])
                vtt(bcA[:], bsv('lre'), bcP[:], ALU.mult, [bs_lre, bcP], [bcA])
                vstt(bcP[:], bsv('lim'), 1.0 / TWO_PI, bcP[:], ALU.mult, ALU.mult, [bs_lim, bcP], [bcP])
                act(mag[:], bcA[:], AF.Exp, [bcA], [mag])
                sincos(tA[:], bcP[:], 0.25, [bcP], [tA])
                vtt(lbr[:], tA[:], mag[:], ALU.mult, [tA, mag], [lbr])
                sincos(tA[:], bcP[:], 0.0, [bcP], [tA])
                vtt(lbi[:], tA[:], mag[:], ALU.mult, [tA, mag], [lbi])
                vts(lbr[:], lbr[:], -1.0, ALU.add, [lbr], [lbr])
                vtt(tA[:], bsv('lre'), bsv('lre'), ALU.mult, [bs_lre], [tA])
                vtt(tB_[:], bsv('lim'), bsv('lim'), ALU.mult, [bs_lim], [tB_])
                vtt(tA[:], tA[:], tB_[:], ALU.add, [tA, tB_], [tA])
                V(lambda e: e.reciprocal(out=tA[:], in_=tA[:]), [tA], [tA])
                vtt(cre[:], lbr[:], bsv('lre'), ALU.mult, [lbr, bs_lre], [cre])
                vtt(tB_[:], lbi[:], bsv('lim'), ALU.mult, [lbi, bs_lim], [tB_])
                vtt(cre[:], cre[:], tB_[:], ALU.add, [cre, tB_], [cre])
                vtt(cre[:], cre[:], tA[:], ALU.mult, [cre, tA], [cre])
                vtt(cim[:], lbi[:], bsv('lre'), ALU.mult, [lbi, bs_lre], [cim])
                vtt(tB_[:], lbr[:], bsv('lim'), ALU.mult, [lbr, bs_lim], [tB_])
                vtt(cim[:], cim[:], tB_[:], ALU.subtract, [cim, tB_], [cim])
                vtt(cim[:], cim[:], tA[:], ALU.mult, [cim, tA], [cim])
                o_pc = COFF['pcol'][0] + (1 if samp else 0)
                sidx = cst[:, o_pc:o_pc + 1]
                vts(tA[:], bcA[:], sidx, ALU.mult, [bcA, cst], [tA], -1.0, ALU.mult)
                act(mag[:], tA[:], AF.Exp, [tA], [mag])
                vts(tB_[:], bcP[:], sidx, ALU.mult, [bcP, cst], [tB_], -1.0, ALU.mult)
                sincos(tA[:], tB_[:], 0.25, [tB_], [tA])
                vtt(lbr[:], tA[:], mag[:], ALU.mult, [tA, mag], [lbr])
                sincos(tA[:], tB_[:], 0.0, [tB_], [tA])
                vtt(lbi[:], tA[:], mag[:], ALU.mult, [tA, mag], [lbi])
                vtt(tA[:], lbr[:], cre[:], ALU.mult, [lbr, cre], [tA])
                vtt(tB_[:], lbi[:], cim[:], ALU.mult, [lbi, cim], [tB_])
                vtt(Lm[l][0][:, cs], tA[:], tB_[:], ALU.subtract, [tA, tB_], [Lm[l][0]])
                vtt(tA[:], lbr[:], cim[:], ALU.mult, [lbr, cim], [tA])
                vtt(tB_[:], lbi[:], cre[:], ALU.mult, [lbi, cre], [tB_])
                vtt(Lm[l][1][:, cs], tA[:], tB_[:], ALU.add, [tA, tB_], [Lm[l][1]])
        for l in range(2):
            build_tables(l, False)
        hnT = sb("hnT", [128, 8, 128], BF16)
        ocatT = sb("ocatT", [128, 8, 128], BF16)
        st = sb("st", [128, 8])
        ring = [sb("ring%d" % i, [128, 4160], BF16) for i in range(NRING)]
        ringi = [0]
        pT = ps("pT", [128, 8, 128], BF16)
        pFt = es.enter_context(nc.psum_tensor("pF", [128, 4, 128], F32))
        pFbuf = Buf("pF", excl=True)
        pF = [TB(pFt[:, i, :], "pF%d" % i, bufs=[pFbuf]) for i in range(4)]
        pM = [ps("pM%d" % i, [128, 512]) for i in range(6)]
        xpf = sb("xpf", [128, 4, 160])
        acc = sb("acc", [128, 4, 128])
        acc2 = sb("acc2", [128, 4, 128])
        oT_sb = TB(acc[0:64, :, :], "oT_sb", bufs=acc.bs)
        sa = sb("sa", [128, 2, 128])
        cfc = [sb("cfc%d" % l, [128, 11, 4, 2]) for l in range(2)]
        pf = sb("pf", [128, 256])
        pb = sb("pb", [128, 256], BF16)
        ppT = sb("ppT", [128, 2, 128], BF16)
        xpq = sb("xpq", [128, 12, 176], BF16)
        cq = [sb("cq%d" % l, [128, 12, 3]) for l in range(2)]
        ba = sb("ba", [128, 8])
        uT = sb("uT", [128, 2, 128], BF16)
        hqTs = sb("hqTs", [128, 2, 128])
        k_tok = sb("k_tok", [128, 4, 128], BF16)
        v_tok = sb("v_tok", [128, 4, 128], BF16)
        knT = sb("knT", [128, 4, 128], BF16)
        dsc = sb("dsc", [128, 64])
        gsel = sb("gsel", [128, 4, 16])
        glb = sb("glb", [128, 4, 16])
        gB = sb("gB", [128, 128])
        dtmp = sb("dtmp", [128, 128])
        dec = sb("dec", [128, 128])
        decT = sb("decT", [128, 128])
        Xf = sb("Xf", [128, 128])
        Xp = [sb("Xp%d" % i, [128, 128]) for i in range(2)]
        XpT = [sb("XpT%d" % i, [128, 128]) for i in range(2)]
        TT = sb("TT", [128, 128])
        Ru = sb("Ru", [128, 128])
        Rw = sb("Rw", [128, 128])
        u_sb = sb("u_sb", [128, 128])
        wT_b = sb("wT_b", [128, 128], BF16)
        qkTm = sb("qkTm", [128, 128], BF16)
        wq_sb = sb("wq_sb", [128, 2, 128])
        vnew = sb("vnew", [128, 128], BF16)
        t1s = sb("t1s", [128, 128])
        kdec = sb("kdec", [128, 128], BF16)
        oA = sb("oA", [128, 4, 128])
        oab = sb("oab", [128, 4, 128], BF16)
        Sd = [sb("Sd%d" % l, [128, 4, 128]) for l in range(2)]
        Sd_b = [sb("Sdb%d" % l, [128, 4, 128], BF16) for l in range(2)]
        sS_b = sb("sSb", [128, 16, 128], BF16)
        cinP = [[sb("cinP%d%d" % (l, r), [128, 8, 1]) for r in range(2)] for l in range(2)]
        cinS = [sb("cinS%d" % r, [128, 8, 16]) for r in range(2)]
        x0s = [sb("x0s%d" % r, [128, 8, 16]) for r in range(2)]
        xe = [sb("xe%d" % r, [128, 8, 16]) for r in range(2)]
        xep = [sb("xep%d" % r, [128, 8]) for r in range(2)]
        ybT = sb("ybT", [128, 2, 128])
        ybTb = sb("ybTb", [128, 2, 128], BF16)
        sgT = sb("sgT", [128, 128])
        v_h = sb("v_h", [128, 256], BF16)
        fT = sb("fT", [128, 2, 128])
        qtT = sb("qtT", [128, 2, 128], BF16)
        ktT = sb("ktT", [128, 2, 128], BF16)
        e1 = sb("e1", [128, 2, 128])
        khat = sb("khat", [128, 256], BF16)
        glh = sb("glh", [128, 2, 16])
        aTm = sb("aTm", [128, 4, 128], BF16)
        ocb = sb("ocb", [128, 256], BF16)
        Sh = [sb("Sh%d" % l, [128, 2, 64]) for l in range(2)]
        Sh_b = [sb("Shb%d" % l, [128, 2, 64], BF16) for l in range(2)]
        Shs_b = TB(sS_b[:].rearrange("p b v -> p (b v)").rearrange("p (c b v) -> p c b v", c=2, b=16), "Shsb", bufs=sS_b.bs)
        gT = sb("gT", [128, 22, 128], BF16)
        qkv_b = TB(gT[:, 0:12, :], "qkv_b", bufs=gT.bs)
        hb = TB(gT[:, 14:22, :].rearrange("p c t -> p (c t)"), "hb", bufs=gT.bs)
        sq = TB(gT[:, 12:20, :], "sq", bufs=gT.bs)
        def bufs_of(i0, i1):
            return [b for p_ in PZ[i0:i1] for b in p_.bs]
        sS = TB(ZB[1][:].rearrange("p (b v) -> p b v", b=16), "sS", bufs=bufs_of(4, 8))
        Shs = TB(ZB[1][:].rearrange("p (c b v) -> p c b v", c=2, b=16), "Shs", bufs=bufs_of(4, 8))
        scf = TB(ZB[1][:, 0:1408].rearrange("p (k q b t) -> p k q b t", k=11, q=4, b=16), "scf", bufs=bufs_of(4, 8))
        scq = TB(ZB[1][:, 1408:1984].rearrange("p (c b t) -> p c b t", c=12, b=16), "scq", bufs=bufs_of(4, 8))
        xre = TB(ZB[2][:, 0:1024].rearrange("p (j t) -> p j t", j=8), "xre", bufs=bufs_of(8, 10))
        xim = TB(ZB[2][:, 1024:2048].rearrange("p (j t) -> p j t", j=8), "xim", bufs=bufs_of(10, 12))
        yb0 = TB(ZB[3][:, 0:256], "yb0", bufs=PZ[12].bs)
        yb = TB(ZB[3][:, 256:512], "yb", bufs=PZ[12].bs)
        fto = TB(ZB[3][:, 512:768], "fto", bufs=PZ[13].bs)
        logf = TB(ZB[3][:, 768:1024], "logf", bufs=PZ[13].bs)
        omf = TB(ZB[3][:, 1024:1280], "omf", bufs=PZ[14].bs)
        b_sb = TB(ZB[3][:, 1280:1536], "b_sb", bufs=PZ[14].bs)
        oc = TB(ZB[3][:, 1536:1792], "oc", bufs=PZ[15].bs)
        hg_s = TB(ZB[3][:, 1792:2048], "hg_s", bufs=PZ[15].bs)
        gate_s = tD
        Wre = sb("Wre", [128, 1024], BF16)
        Wim = sb("Wim", [128, 1024], BF16)
        xbre = TB(Wre[:].rearrange("p (j t) -> p j t", j=8), "xbre", bufs=Wre.bs)
        xbim = TB(Wim[:].rearrange("p (j t) -> p j t", j=8), "xbim", bufs=Wim.bs)
        kdm = [sb("kdm%d" % i, [128, 128], BF16) for i in range(2)]
        khm = [sb("khm%d" % i, [128, 256], BF16) for i in range(2)]
        gsbh = [tA, tB_]
        yoh = [tC, tD]
        tmp1 = tD
        for l in range(2):
            for t_ in (cfc[l], cq[l], Sd[l], Sd_b[l], Sh[l], Sh_b[l], cinP[l][0], cinP[l][1]):
                V(lambda e, t_=t_: e.memset(t_[:], 0.0), [], [t_])

        class _RS:
            pass
        RS0 = _RS()
        RS0.dec, RS0.decT, RS0.TT, RS0.Ru, RS0.Rw, RS0.u_sb, RS0.t1s = dec, decT, TT, Ru, Rw, u_sb, t1s
        RS0.wT_b, RS0.qkTm, RS0.vnew, RS0.kdec, RS0.wq_sb, RS0.Xp, RS0.XpT = wT_b, qkTm, vnew, kdec, wq_sb, Xp, XpT
        RS0.pF, RS0.pU, RS0.pP = pF, pM[2], pM[5]
        RS1 = _RS()
        for n_ in ('dec', 'decT', 'TT', 'Ru', 'Rw', 'u_sb', 't1s'):
            setattr(RS1, n_, sb(n_ + "_1", [128, 128]))
        for n_ in ('wT_b', 'qkTm', 'vnew', 'kdec'):
            setattr(RS1, n_, sb(n_ + "_1", [128, 128], BF16))
        RS1.wq_sb = sb("wq_sb_1", [128, 2, 128])
        RS1.Xp = [sb("Xp1_%d" % i, [128, 128]) for i in range(2)]
        RS1.XpT = [sb("XpT1_%d" % i, [128, 128]) for i in range(2)]
        RS1.pF = [TB(pM[4][:, i * 128:(i + 1) * 128], "pF1_%d" % i, bufs=pM[4].bs) for i in range(4)]
        RS1.pU, RS1.pP = pM[0], pM[1]

        FB = [(xpf, acc, acc2, sa),
              (sb("xpf_b", [128, 4, 160]), sb("acc_b", [128, 4, 128]), sb("acc2_b", [128, 4, 128]), sb("sa_b", [128, 2, 128]))]

        def norm_stats():
            V(lambda e: e.memset(st[:, 0:1], 0.0), [], [st])
            act(hb[:], h[:], AF.Square, [h], [hb, st], accum=st[:, 0:1])
            vts(st[:, 1:2], st[:, 0:1], 1.0 / 1024, ALU.mult, [st], [st], EPS, ALU.add)
            act(st[:, 2:3], st[:, 1:2], AF.Sqrt, [st], [st])
            V(lambda e: e.reciprocal(out=st[:, 3:4], in_=st[:, 2:3]), [st], [st])

        def norm_T(gain_ap, gsrc):
            norm_stats()
            vts(hb[:], h[:], st[:, 3:4], ALU.mult, [h, st], [hb])
            for c in range(8):
                tr(pT[:, c, :], hb[:, c * 128:(c + 1) * 128], identb, [hb, cstb], [pT])
            vtt(hnT[:], pT[:], gain_ap.unsqueeze(2).to_broadcast([128, 8, 128]), ALU.mult, [pT, gsrc], [hnT])

        def wslot(kc, n):
            slot = ring[ringi[0] % NRING]
            ringi[0] += 1
            return slot, slot[:, 0:kc * n].rearrange("p (k n) -> p k n", k=kc)

        def wload(key):
            d3, tb, kc, n = WB[key]
            slot, wv = wslot(kc, n)
            S.dma('pool', lambda e: e.dma_start(out=wv, in_=d3), reads=[tb], writes=[slot])
            return slot, wv

        def rsqrt_small(dst, src, rr, ww, mul, add):
            vts(dst, src, mul, ALU.mult, rr, ww, add, ALU.add)
            act(dst, dst, AF.Sqrt, ww, ww)
            V(lambda e: e.reciprocal(out=dst, in_=dst), ww, ww)

        def in_proj(l, samp):
            norm_T(pp(l, 'nmix'), ppt[l])
            W = w_in[l]
            if samp:
                dma('pool', scq[:], s_cq[l], w=[scq])
                xq4 = xpq[:].rearrange("p c (b t) -> p c b t", b=16)
                vcopy(xq4[:, :, :, 0:3], scq[:], [scq], [xpq])
            else:
                vcopy(xpq[:, :, 0:3], cq[l][:], [cq[l]], [xpq])
            for g in range(3):
                slot, wv = wload(('in', l, g))
                pm = pM[g % 2]
                pm3 = pm[:].rearrange("p (q t) -> p q t", q=4)
                for q in range(4):
                    for kc in range(8):
                        mm(pm3[:, q, :], wv[:, kc, q * 128:(q + 1) * 128], hnT[:, kc, :], [slot, hnT], [pm], start=(kc == 0), stop=(kc == 7))
                if samp:
                    acopy(xq4[:, 4 * g:4 * g + 4, :, 3:11], pm[:].rearrange("p (q b t) -> p q b t", q=4, b=16), [pm], [xpq])
                else:
                    acopy(xpq[:, 4 * g:4 * g + 4, 3:131], pm3, [pm], [xpq])
            slot, wv = wload(('in', l, 3))
            for kc in range(8):
                mm(pM[2][:], hnT[:, kc, :], wv[:, kc, 0:512], [hnT, slot], [pM[2]], start=(kc == 0), stop=(kc == 7))
            for kc in range(8):
                mm(pM[3][:, 0:8], hnT[:, kc, :], wv[:, kc, 512:520], [hnT, slot], [pM[3]], start=(kc == 0), stop=(kc == 7))
            act(gate_s[:], pM[2][:], AF.Silu, [pM[2]], [gate_s])
            vcopy(ba[:], pM[3][:, 0:8], [pM[3]], [ba])
        def in_proj_rest(l, samp):
            W = w_in[l]
            slot, wv = wload(('in', l, 4))
            pm = pM[0]
            pm3 = pm[:].rearrange("p (q t) -> p q t", q=4)
            for q in range(4):
                for kc in range(8):
                    mm(pm3[:, q, :], wv[:, kc, q * 128:(q + 1) * 128], hnT[:, kc, :], [slot, hnT], [pm], start=(kc == 0), stop=(kc == 7))
            vcopy(uT[:], pm3[:, 0:2, :], [pm], [uT])
            yield
            act(hqTs[:], pm3[:, 2:4, :], AF.Silu, [pm], [hqTs])
            yield
            slot, wv = wload(('in', l, 5))
            for kc in range(8):
                mm(pM[4][:], hnT[:, kc, :], wv[:, kc, :], [hnT, slot], [pM[4]], start=(kc == 0), stop=(kc == 7))
            p53 = pM[5][:, 0:256].rearrange("p (q t) -> p q t", q=2)
            for q in range(2):
                for kc in range(8):
                    mm(p53[:, q, :], wv[:, kc, q * 128:(q + 1) * 128], hnT[:, kc, :], [slot, hnT], [pM[5]], start=(kc == 0), stop=(kc == 7))
            act(fto[:], pM[4][:, 0:256], AF.Sigmoid, [pM[4]], [fto])
            yield
            acopy(v_h[:], pM[4][:, 256:512], [pM[4]], [v_h])
            yield
            act(fT[:], p53, AF.Sigmoid, [pM[5]], [fT])
            yield
            slot, wv = wload(('in', l, 6))
            for kc in range(8):
                mm(pM[3][:, 0:256], hnT[:, kc, :], wv[:, kc, :], [hnT, slot], [pM[3]], start=(kc == 0), stop=(kc == 7))
            act(hg_s[:], pM[3][:, 0:256], AF.Silu, [pM[3]], [hg_s])
            yield


        def delta(l, samp, last):
            kind = 'S' if samp else 'P'
            nb = 16 if samp else 1
            bs = 128 // nb
            nlev = 3 if samp else 7
            cTf, cTb = C('cT' + kind), C('cT' + kind, True)
            stf = C('st' + kind)
            smf = C('sm' + kind)
            self_ = C('sel' + kind)
            dcw = ppt[l][:, PP['dcw'][0]:PP['dcw'][0] + 48].rearrange("p (c j) -> p c j", c=12)
            for g in range(3):
                if samp:
                    xq4 = xpq[:].rearrange("p c (b t) -> p c b t", b=16)
                    xs = [xq4[:, 4 * g:4 * g + 4, :, j:j + 8] for j in range(4)]
                    ws = [dcw[:, 4 * g:4 * g + 4, j:j + 1].unsqueeze(3).to_broadcast([128, 4, 16, 8]) for j in range(4)]
                    av = acc[:].rearrange("p q (b t) -> p q b t", b=16)
                    a2v = acc2[:].rearrange("p q (b t) -> p q b t", b=16)
                else:
                    xs = [xpq[:, 4 * g:4 * g + 4, j:j + 128] for j in range(4)]
                    ws = [dcw[:, 4 * g:4 * g + 4, j:j + 1].to_broadcast([128, 4, 128]) for j in range(4)]
                    av, a2v = acc[:], acc2[:]
                vtt(av, xs[0], ws[0], ALU.mult, [xpq, ppt[l]], [acc])
                yield
                for j in range(1, 4):
                    vtt(a2v, xs[j], ws[j], ALU.mult, [xpq, ppt[l]], [acc2])
                    yield
                    vtt(av, av, a2v, ALU.add, [acc, acc2], [acc])
                    yield
                act(qkv_b[:, 4 * g:4 * g + 4, :], acc[:], AF.Silu, [acc], [qkv_b])
                yield
            if samp:
                xq4 = xpq[:].rearrange("p c (b t) -> p c b t", b=16)
                vcopy(scq[:], xq4[:, :, :, 8:11], [xpq], [scq])
                yield
                dma('pool', o_scq[l], scq[:], r=[scq])
            else:
                vcopy(cq[l][:], xpq[:, :, 128:131], [xpq], [cq[l]])
                yield
                if last:
                    dma('pool', o_pcq[l], cq[l][:], r=[cq[l]])
            vtt(sq[:], qkv_b[:, 0:8, :], qkv_b[:, 0:8, :], ALU.mult, [qkv_b], [sq])
            yield
            for c in range(8):
                mm(pM[1][:, c:c + 1], sq[:, c, :], onesb[:, 0:1], [sq, cstb], [pM[1]])
            rsqrt_small(dsc[:, 0:8], pM[1][:, 0:8], [pM[1]], [dsc], 1.0, EPS)
            vts(dsc[:, 0:4], dsc[:, 0:4], 128.0 ** -0.5, ALU.mult, [dsc], [dsc])
            yield
            for hh in range(4):
                tr(pT[:, hh, :], qkv_b[:, 4 + hh, :], identb, [qkv_b, cstb], [pT])
                tr(pT[:, 4 + hh, :], qkv_b[:, 8 + hh, :], identb, [qkv_b, cstb], [pT])
            vtt(k_tok[:], pT[:, 0:4, :], dsc[:, 4:8].unsqueeze(2).to_broadcast([128, 4, 128]), ALU.mult, [pT, dsc], [k_tok])
            yield
            acopy(v_tok[:], pT[:, 4:8, :], [pT], [v_tok])
            yield
            for hh in range(4):
                tr(pT[:, hh, :], k_tok[:, hh, :], identb, [k_tok, cstb], [pT])
            acopy(knT[:], pT[:, 0:4, :], [pT], [knT])
            yield
            act(dsc[:, 8:12], ba[:, 0:4], AF.Sigmoid, [ba], [dsc])
            yield
            vts(dsc[:, 12:16], dsc[:, 8:12], -1.0, ALU.mult, [dsc], [dsc])
            yield
            vtt(dsc[:, 40:44], ba[:, 4:8], bcv(l, 'dtb'), ALU.add, [ba, bcs[l]], [dsc])
            yield
            act(dsc[:, 40:44], dsc[:, 40:44], AF.Exp, [dsc], [dsc])
            yield
            act(dsc[:, 40:44], dsc[:, 40:44], AF.Ln, [dsc], [dsc], bias=1.0)
            yield
            vtt(dsc[:, 16:20], dsc[:, 40:44], negA[l][:], ALU.mult, [dsc, negA[l]], [dsc])
            yield
            mm(pM[1][:, 8:12], cTf, dsc[:, 16:20], [cst, dsc], [pM[1]])
            mm(pM[1][:, 12:16], smf, dsc[:, 16:20], [cst, dsc], [pM[1]])
            vcopy(dsc[:, 20:24], pM[1][:, 8:12], [pM[1]], [dsc])
            yield
            vtt(dsc[:, 40:44], pM[1][:, 12:16], dsc[:, 20:24], ALU.subtract, [pM[1], dsc], [dsc])
            yield
            act(dsc[:, 24:28], dsc[:, 40:44], AF.Exp, [dsc], [dsc])
            yield
            act(dsc[:, 28:32], dsc[:, 20:24], AF.Exp, [dsc], [dsc])
            yield
            vtt(dsc[:, 32:36], dsc[:, 8:12], dsc[:, 28:32], ALU.mult, [dsc], [dsc])
            yield
            vtt(dsc[:, 36:40], dsc[:, 28:32], dsc[:, 0:4], ALU.mult, [dsc], [dsc])
            yield
            for hh in range(4):
                vts(gsel[:, hh, 0:nb], self_[:, 0:nb], dsc[:, 16 + hh:17 + hh], ALU.mult, [cst, dsc], [gsel])
                yield
            for hh in range(4):
                mm(pM[1][:, 16 + 16 * hh:16 + 16 * hh + nb], onesf, gsel[:, hh, 0:nb], [cst, gsel], [pM[1]])
            act(glb[:, :, 0:nb], pM[1][:, 16:80].rearrange("p (h b) -> p h b", h=4)[:, :, 0:nb], AF.Exp, [pM[1]], [glb])
            yield

            def head_body(hh, R):
                vts(gB[:], onesf, dsc[:, 16 + hh:17 + hh], ALU.mult, [cst, dsc], [gB])
                mm(R.pF[0][:], gB[:], cTf, [gB, cst], [R.pF[0]])
                vts(dtmp[:], R.pF[0][:], dsc[:, 20 + hh:21 + hh], ALU.subtract, [R.pF[0], dsc], [dtmp], 0.0, ALU.max)
                act(R.dec[:], dtmp[:], AF.Exp, [dtmp], [R.dec], scale=-1.0)
                yield
                vts(dtmp[:], R.pF[0][:], dsc[:, 20 + hh:21 + hh], ALU.subtract, [R.pF[0], dsc], [dtmp], 0.0, ALU.min)
                act(R.decT[:], dtmp[:], AF.Exp, [dtmp], [R.decT])
                yield
                vtt(R.decT[:], R.decT[:], cTf, ALU.mult, [R.decT, cst], [R.decT])
                mm(R.pF[1][:], knT[:, hh, :], knT[:, hh, :], [knT], [R.pF[1]])
                vstt(Xf[:], R.pF[1][:], dsc[:, 12 + hh:13 + hh], R.dec[:], ALU.mult, ALU.mult, [R.pF[1], dsc, R.dec], [Xf])
                vtt(R.Xp[0][:], Xf[:], stf, ALU.mult, [Xf, cst], [R.Xp[0]])
                yield
                tr(R.pF[3][:], R.Xp[0][:], identf, [R.Xp[0], cst], [R.pF[3]])
                acopy(R.XpT[0][:], R.pF[3][:], [R.pF[3]], [R.XpT[0]])
                yield
                vtt(R.TT[:], R.XpT[0][:], identf, ALU.add, [R.XpT[0], cst], [R.TT])
                yield
                cur = 0
                for lev in range(1, nlev):
                    nxt = 1 - cur
                    lastlev = (lev == nlev - 1)
                    mm(R.pF[2][:], R.XpT[cur][:], R.Xp[cur][:], [R.XpT[cur], R.Xp[cur]], [R.pF[2]])
                    vcopy(R.Xp[nxt][:], R.pF[2][:], [R.pF[2]], [R.Xp[nxt]])
                    yield
                    if not lastlev:
                        mm(R.pF[3][:], R.Xp[cur][:], R.XpT[cur][:], [R.XpT[cur], R.Xp[cur]], [R.pF[3]])
                        acopy(R.XpT[nxt][:], R.pF[3][:], [R.pF[3]], [R.XpT[nxt]])
                        yield
                    mm(R.pF[1][:], R.Xp[nxt][:], R.TT[:], [R.Xp[nxt], R.TT], [R.pF[1]])
                    vtt(R.TT[:], R.TT[:], R.pF[1][:], ALU.add, [R.TT, R.pF[1]], [R.TT])
                    yield
                    cur = nxt
                vts(R.Ru[:], v_tok[:, hh, :], dsc[:, 8 + hh:9 + hh], ALU.mult, [v_tok, dsc], [R.Ru])
                vts(R.Rw[:], k_tok[:, hh, :], dsc[:, 32 + hh:33 + hh], ALU.mult, [k_tok, dsc], [R.Rw])
                mm(R.pU[:, 0:128], R.TT[:], R.Ru[:], [R.TT, R.Ru], [R.pU])
                mm(R.pU[:, 128:256], R.Rw[:], R.TT[:], [R.TT, R.Rw], [R.pU])
                acopy(R.u_sb[:], R.pU[:, 0:128], [R.pU], [R.u_sb])
                yield
                acopy(R.wT_b[:], R.pU[:, 128:256], [R.pU], [R.wT_b])
                yield
                mm(R.pF[2][:], knT[:, hh, :], qkv_b[:, hh, :], [knT, qkv_b], [R.pF[2]])
                vtt(R.qkTm[:], R.pF[2][:], R.decT[:], ALU.mult, [R.pF[2], R.decT], [R.qkTm])
                yield
                if samp:
                    dma('pool', sS[:], s_dl[l, :, hh].rearrange("b k v -> k b v"), w=[sS])
                    acopy(sS_b[:], sS[:], [sS], [sS_b])
                    yield
                p33 = R.pP[:, 256:512].rearrange("p (q t) -> p q t", q=2)
                for b in range(nb):
                    Sb = sS_b[:, b, :] if samp else Sd_b[l][:, hh, :]
                    Sbt = sS_b if samp else Sd_b[l]
                    mm(p33[:, 0, b * bs:(b + 1) * bs], Sb, R.wT_b[:, b * bs:(b + 1) * bs], [Sbt, R.wT_b], [R.pP])
                    mm(p33[:, 1, b * bs:(b + 1) * bs], Sb, qkv_b[:, hh, b * bs:(b + 1) * bs], [Sbt, qkv_b], [R.pP])
                acopy(R.wq_sb[:], p33, [R.pP], [R.wq_sb])
                yield
                tr(R.pF[0][:], R.wq_sb[:, 0, :], identf, [R.wq_sb, cst], [R.pF[0]])
                tr(R.pF[1][:], R.wq_sb[:, 1, :], identf, [R.wq_sb, cst], [R.pF[1]])
                vtt(R.vnew[:], R.u_sb[:], R.pF[0][:], ALU.subtract, [R.u_sb, R.pF[0]], [R.vnew])
                yield
                act(R.t1s[:], R.pF[1][:], AF.Identity, [R.pF[1], dsc], [R.t1s], scale=dsc[:, 36 + hh:37 + hh])
                yield
                mm(R.pF[2][:], R.qkTm[:], R.vnew[:], [R.qkTm, R.vnew], [R.pF[2]])
                vstt(oA[:, hh, :], R.pF[2][:], dsc[:, hh:hh + 1], R.t1s[:], ALU.mult, ALU.add, [R.pF[2], dsc, R.t1s], [oA])
                yield
                vts(R.kdec[:], k_tok[:, hh, :], dsc[:, 24 + hh:25 + hh], ALU.mult, [k_tok, dsc], [R.kdec])
                if samp:
                    for b in range(nb):
                        pd = R.pF[b % 2]
                        km = kdm[b % 2]
                        vts(km[:], R.kdec[:], self_[:, b:b + 1], ALU.mult, [R.kdec, cst], [km])
                        mm(pd[:], km[:], R.vnew[:], [km, R.vnew], [pd])
                        vstt(sS[:, b, :], sS[:, b, :], glb[:, hh, b:b + 1], pd[:], ALU.mult, ALU.add, [sS, glb, pd], [sS])
                        yield
                    dma('pool', o_sdl[l, :, hh].rearrange("b k v -> k b v"), sS[:], r=[sS])
                else:
                    mm(R.pF[0][:], R.kdec[:], R.vnew[:], [R.kdec, R.vnew], [R.pF[0]])
                    vstt(Sd[l][:, hh, :], Sd[l][:, hh, :], glb[:, hh, 0:1], R.pF[0][:], ALU.mult, ALU.add, [Sd[l], glb, R.pF[0]], [Sd[l]])
                    yield
                    acopy(Sd_b[l][:, hh, :], Sd[l][:, hh, :], [Sd[l]], [Sd_b[l]])
                    yield
                    if last:
                        dma('pool', o_pdl[l, hh], Sd[l][:, hh, :], r=[Sd[l]])
            def run_heads(gens):
                gens = list(gens)
                while gens:
                    for g_ in list(gens):
                        try:
                            next(g_)
                        except StopIteration:
                            gens.remove(g_)
            if samp:
                for hh in range(4):
                    run_heads([head_body(hh, RS0)])
            else:
                run_heads([head_body(0, RS0), head_body(1, RS1)])
                run_heads([head_body(2, RS0), head_body(3, RS1)])
            V(lambda e: e.memset(dsc[:, 44:48], 0.0), [], [dsc])
            for hh in range(4):
                act(dtmp[:], oA[:, hh, :], AF.Square, [oA], [dtmp, dsc], accum=dsc[:, 44 + hh:45 + hh])
            rsqrt_small(dsc[:, 44:48], dsc[:, 44:48], [dsc], [dsc], 1.0 / 128, EPS)
            for hh in range(4):
                vstt(oA[:, hh, :], oA[:, hh, :], dsc[:, 44 + hh:45 + hh], bcv(l, 'dnn'), ALU.mult, ALU.mult, [oA, dsc, bcs[l]], [oA])
            vtt(oab[:], oA[:], gate_s[:].rearrange("p (h d) -> p h d", h=4), ALU.mult, [oA, gate_s], [oab])
            for hh in range(4):
                tr(pT[:, hh, :], oab[:, hh, :], identb, [oab, cstb], [pT])
            acopy(ocatT[:, 0:4, :], pT[:, 0:4, :], [pT], [ocatT])
        def s5(l, samp, last):
            kind = 'S' if samp else 'P'
            nb = 16 if samp else 1
            bs = 128 // nb
            cTb = C('cT' + kind, True)
            Lmv = Lm[l]
            if samp:
                for r in range(2):
                    dma('pool', x0s[r][:], s_ss[l, r], w=[x0s[r]])
                for (dst, a_, b_, op) in ((cinS[0], 0, 1, ALU.subtract), (cinS[1], 1, 0, ALU.add)):
                    vtt(xe[0][:], x0s[0][:], lbar[l][a_][:].unsqueeze(2).to_broadcast([128, 8, 16]), ALU.mult, [x0s[0], lbar[l][a_]], [xe[0]])
                    yield
                    vtt(xe[1][:], x0s[1][:], lbar[l][b_][:].unsqueeze(2).to_broadcast([128, 8, 16]), ALU.mult, [x0s[1], lbar[l][b_]], [xe[1]])
                    yield
                    vtt(dst[:], xe[0][:], xe[1][:], op, [xe[0], xe[1]], [dst])
                    yield
                cin = cinS
            else:
                cin = cinP[l]
            for cc in range(2):
                hs = slice(cc * 512, (cc + 1) * 512)
                mm(pM[0][:], uT[:, cc, :], bft[l][0][:, cc, :], [uT, bft[l][0]], [pM[0]])
                mm(pM[1][:], uT[:, cc, :], bft[l][1][:, cc, :], [uT, bft[l][1]], [pM[1]])
                vtt(tA[:, 0:512], pM[0][:], Lmv[0][:, hs], ALU.mult, [pM[0], Lmv[0]], [tA])
                yield
                vtt(tB_[:, 0:512], pM[1][:], Lmv[1][:, hs], ALU.mult, [pM[1], Lmv[1]], [tB_])
                yield
                vtt(Wre[:, hs], tA[:, 0:512], tB_[:, 0:512], ALU.subtract, [tA, tB_], [Wre])
                yield
                vtt(tA[:, 0:512], pM[1][:], Lmv[0][:, hs], ALU.mult, [pM[1], Lmv[0]], [tA])
                yield
                vtt(tB_[:, 0:512], pM[0][:], Lmv[1][:, hs], ALU.mult, [pM[0], Lmv[1]], [tB_])
                yield
                vtt(Wim[:, hs], tA[:, 0:512], tB_[:, 0:512], ALU.add, [tA, tB_], [Wim])
                yield
                for jj in range(4):
                    j = 4 * cc + jj
                    mm(pM[2][:, jj * 128:(jj + 1) * 128], Wre[:, j * 128:(j + 1) * 128], cTb, [Wre, cstb], [pM[2]])
                    mm(pM[4][:, jj * 128:(jj + 1) * 128], Wim[:, j * 128:(j + 1) * 128], cTb, [Wim, cstb], [pM[4]])
                js = slice(4 * cc, 4 * cc + 4)

                def v4(ap):
                    return ap.rearrange("p (j b t) -> p j b t", j=4, b=nb)

                def v4b(ap3):
                    return ap3.rearrange("p j (b t) -> p j b t", b=nb)
                ar, ai = tC, tD
                vtt(v4(ar[:, 0:512]), v4(pM[2][:]), cin[0][:, js, :].unsqueeze(3).to_broadcast([128, 4, nb, bs]), ALU.add, [pM[2], cin[0]], [ar])
                yield
                vtt(v4(ai[:, 0:512]), v4(pM[4][:]), cin[1][:, js, :].unsqueeze(3).to_broadcast([128, 4, nb, bs]), ALU.add, [pM[4], cin[1]], [ai])
                yield
                if samp:
                    Lr = LpT[l][0][:, js, 0:8].unsqueeze(2).to_broadcast([128, 4, 16, 8])
                    Li = LpT[l][1][:, js, 0:8].unsqueeze(2).to_broadcast([128, 4, 16, 8])
                else:
                    Lr = v4b(LpT[l][0][:, js, :])
                    Li = v4b(LpT[l][1][:, js, :])
                vtt(v4(tA[:, 0:512]), v4(ar[:, 0:512]), Lr, ALU.mult, [ar, LpT[l][0]], [tA])
                yield
                vtt(v4(tB_[:, 0:512]), v4(ai[:, 0:512]), Li, ALU.mult, [ai, LpT[l][1]], [tB_])
                yield
                vtt(v4b(xre[:, js, :]), v4(tA[:, 0:512]), v4(tB_[:, 0:512]), ALU.subtract, [tA, tB_], [xre])
                yield
                vtt(v4(tA[:, 0:512]), v4(ar[:, 0:512]), Li, ALU.mult, [ar, LpT[l][1]], [tA])
                yield
                vtt(v4(tB_[:, 0:512]), v4(ai[:, 0:512]), Lr, ALU.mult, [ai, LpT[l][0]], [tB_])
                yield
                vtt(v4b(xim[:, js, :]), v4(tA[:, 0:512]), v4(tB_[:, 0:512]), ALU.add, [tA, tB_], [xim])
                yield
            acopy(xbre[:], xre[:], [xre], [xbre])
            yield
            acopy(xbim[:], xim[:], [xim], [xbim])
            yield
            xr4 = xre[:].rearrange("p j (b t) -> p j b t", b=nb)
            xi4 = xim[:].rearrange("p j (b t) -> p j b t", b=nb)
            vcopy(xe[0][:, :, 0:nb], xr4[:, :, :, bs - 1], [xre], [xe[0]])
            yield
            vcopy(xe[1][:, :, 0:nb], xi4[:, :, :, bs - 1], [xim], [xe[1]])
            yield
            if samp:
                for r in range(2):
                    dma('pool', o_sss[l, r], xe[r][:], r=[xe[r]])
            else:
                for r in range(2):
                    vcopy(xep[r][:], xe[r][:, :, 0], [xe[r]], [xep[r]])
                    yield
                if last:
                    for r in range(2):
                        dma('pool', o_pss[l, r], xep[r][:], r=[xep[r]])
                vtt(tA[:, 0:8], xep[0][:], lbar[l][0][:], ALU.mult, [xep[0], lbar[l][0]], [tA])
                yield
                vtt(tB_[:, 0:8], xep[1][:], lbar[l][1][:], ALU.mult, [xep[1], lbar[l][1]], [tB_])
                yield
                vtt(cinP[l][0][:, :, 0], tA[:, 0:8], tB_[:, 0:8], ALU.subtract, [tA, tB_], [cinP[l][0]])
                yield
                vtt(tA[:, 0:8], xep[0][:], lbar[l][1][:], ALU.mult, [xep[0], lbar[l][1]], [tA])
                yield
                vtt(tB_[:, 0:8], xep[1][:], lbar[l][0][:], ALU.mult, [xep[1], lbar[l][0]], [tB_])
                yield
                vtt(cinP[l][1][:, :, 0], tA[:, 0:8], tB_[:, 0:8], ALU.add, [tA, tB_], [cinP[l][1]])
                yield
            py = pM[0]
            for cc in range(2):
                mm(py[:, cc * 128:(cc + 1) * 128], uT[:, cc, :], diagD[l][:, cc, :], [uT, diagD[l]], [py], start=True, stop=False)
                for jj in range(4):
                    j = 4 * cc + jj
                    mm(py[:, j * 32:(j + 1) * 32], xbre[:, j, :], cft[l][0][:, j, :], [xbre, cft[l][0]], [py], start=False, stop=False)
                    mm(py[:, j * 32:(j + 1) * 32], xbim[:, j, :], cft[l][1][:, j, :], [xbim, cft[l][1]], [py], start=False, stop=(jj == 3))
            acopy(yb0[:], py[:, 0:256], [py], [yb0])
            yield
            vtt(yb[:], yb0[:], yb0[:], ALU.mult, [yb0], [yb])
            yield
            vts(yb[:], yb[:], 0.044715, ALU.mult, [yb], [yb], 1.0, ALU.add)
            yield
            vtt(yb[:], yb[:], yb0[:], ALU.mult, [yb, yb0], [yb])
            yield
            act(yb[:], yb[:], AF.Tanh, [yb], [yb], scale=0.7978845608028654)
            yield
            vstt(yb[:], yb[:], 1.0, yb0[:], ALU.add, ALU.mult, [yb, yb0], [yb])
            yield
            vts(yb[:], yb[:], 0.5, ALU.mult, [yb], [yb])
            yield
            for cc in range(2):
                tr(pF[2 + cc][:], yb[:, cc * 128:(cc + 1) * 128], identf, [yb, cst], [pF[2 + cc]])
                acopy(ybT[:, cc, :], pF[2 + cc][:], [pF[2 + cc]], [ybT])
                yield
            vcopy(ybTb[:], ybT[:], [ybT], [ybTb])
            yield
            for c2 in range(2):
                for cc in range(2):
                    mm(pF[2 + c2][:], glw[l][:, cc, c2 * 128:(c2 + 1) * 128], ybTb[:, cc, :], [glw[l], ybTb], [pF[2 + c2]], start=(cc == 0), stop=(cc == 1))
                act(sgT[:], pF[2 + c2][:], AF.Sigmoid, [pF[2 + c2], ppt[l]], [sgT], bias=pp(l, 'glub', c2, c2 + 1))
                yield
                vtt(ocatT[:, 4 + c2, :], ybT[:, c2, :], sgT[:], ALU.mult, [ybT, sgT], [ocatT])
                yield

        def hgrn(l, samp, last):
            kind = 'S' if samp else 'H'
            nb = 16 if samp else 4
            bs = 128 // nb
            cTf, cTb = C('cT' + kind), C('cT' + kind, True)
            smf = C('sm' + kind)
            self_ = C('sel' + kind)
            pTf = pT[:].rearrange("p c t -> p (c t)").bitcast(F32)
            pTfb = pT
            vtt(fto[:], fto[:], omlb[l][:], ALU.mult, [fto, omlb[l]], [fto])
            yield
            vtt(fto[:], fto[:], lbb[l][:], ALU.add, [fto, lbb[l]], [fto])
            yield
            act(logf[:], fto[:], AF.Ln, [fto], [logf])
            yield
            vts(omf[:], fto[:], -1.0, ALU.mult, [fto], [omf], 1.0, ALU.add)
            yield
            for cc in range(2):
                vts(fT[:, cc, :], fT[:, cc, :], omlf[l][:, cc:cc + 1], ALU.mult, [fT, omlf[l], lbf[l]], [fT], lbf[l][:, cc:cc + 1], ALU.add)
                yield
            vts(fT[:], fT[:], -1.0, ALU.mult, [fT], [fT], 1.0, ALU.add)
            yield
            mm(pM[5][:, 0:256], cTf, logf[:], [cst, logf], [pM[5]])
            mm(pM[5][:, 256:512], smf, logf[:], [cst, logf], [pM[5]])
            pbT = pTf[:, 0:256].rearrange("p (c t) -> p c t", c=2)
            for cc in range(2):
                mm(pbT[:, cc, :], logf[:, cc * 128:(cc + 1) * 128], cTf, [logf, cst], [pTfb])
                mm(pTf[:, 256 + 16 * cc:256 + 16 * cc + nb], logf[:, cc * 128:(cc + 1) * 128], self_[:, 0:nb], [logf, cst], [pTfb])
            act(e1[:], pbT, AF.Exp, [pTfb], [e1])
            yield
            vtt(qtT[:], hqTs[:], e1[:], ALU.mult, [hqTs, e1], [qtT])
            yield
            act(e1[:], pbT, AF.Exp, [pTfb], [e1], scale=-1.0)
            yield
            vtt(ktT[:], fT[:], e1[:], ALU.mult, [fT, e1], [ktT])
            yield
            act(glh[:, :, 0:nb], pTf[:, 256:288].rearrange("p (c b) -> p c b", c=2)[:, :, 0:nb], AF.Exp, [pTfb], [glh])
            yield
            acopy(b_sb[:], pM[5][:, 0:256], [pM[5]], [b_sb])
            yield
            vtt(b_sb[:], pM[5][:, 256:512], b_sb[:], ALU.subtract, [pM[5], b_sb], [b_sb])
            yield
            act(b_sb[:], b_sb[:], AF.Exp, [b_sb], [b_sb])
            yield
            vtt(khat[:], omf[:], b_sb[:], ALU.mult, [omf, b_sb], [khat])
            yield
            pa = pTf[:, 0:512].rearrange("p (h t) -> p h t", h=4)
            for hh in range(4):
                hl, hc = hh % 2, hh // 2
                mm(pa[:, hh, :], ktT[hl * 64:(hl + 1) * 64, hc, :], qtT[hl * 64:(hl + 1) * 64, hc, :], [ktT, qtT], [pTfb], sync=True)
            vtt(aTm[:], pa, cTf.unsqueeze(1).to_broadcast([128, 4, 128]), ALU.mult, [pTfb, cst], [aTm])
            yield
            po = pM[3][0:64, :].rearrange("p (h t) -> p h t", h=4)
            for hh in range(4):
                mm(po[:, hh, :], v_h[:, hh * 64:(hh + 1) * 64], aTm[:, hh, :], [v_h, aTm], [pM[3]], start=(hh == 0), stop=False, skip=True, sync=True)
            if samp:
                for hc_ in range(2):
                    dma('pool', Shs[:, hc_], s_hg[l][:, 2 * hc_:2 * hc_ + 2].rearrange("b hl k v -> (hl k) b v"), w=[Shs])
                acopy(Shs_b[:], Shs[:], [Shs], [Shs_b])
                yield
            for j in range(nb):
                for hh in range(4):
                    hl, hc = hh % 2, hh // 2
                    if samp:
                        Sb, Sbt = Shs_b[hl * 64:(hl + 1) * 64, hc, j, :], Shs_b
                    else:
                        Sb, Sbt = Sh_b[l][hl * 64:(hl + 1) * 64, hc, :], Sh_b[l]
                    mm(po[:, hh, j * bs:(j + 1) * bs], Sb, qtT[hl * 64:(hl + 1) * 64, hc, j * bs:(j + 1) * bs], [Sbt, qtT], [pM[3]], start=False, stop=True, skip=True, sync=True)
                kh = khm[j % 2]
                vts(kh[:], khat[:], self_[:, j:j + 1], ALU.mult, [khat, cst], [kh])
                yield
                for cc in range(2):
                    pd = pF[cc]
                    mm(pd[:], kh[:, cc * 128:(cc + 1) * 128], v_h[:, cc * 128:(cc + 1) * 128], [kh, v_h], [pd], sync=(cc == 0))
                    for hl in range(2):
                        ps_ = slice(hl * 64, (hl + 1) * 64)
                        if samp:
                            vstt(Shs[ps_, cc, j, :], Shs[ps_, cc, j, :], glh[ps_, cc, j:j + 1], pd[ps_, hl * 64:(hl + 1) * 64], ALU.mult, ALU.add, [Shs, glh, pd], [Shs])
                            yield
                        else:
                            vstt(Sh[l][ps_, cc, :], Sh[l][ps_, cc, :], glh[ps_, cc, j:j + 1], pd[ps_, hl * 64:(hl + 1) * 64], ALU.mult, ALU.add, [Sh[l], glh, pd], [Sh[l]])
                            yield
                if not samp:
                    acopy(Sh_b[l][:], Sh[l][:], [Sh[l]], [Sh_b[l]])
                    yield
            if samp:
                dma('pool', o_shg[l], Shs[:], r=[Shs])
            elif last:
                dma('pool', o_phg[l], Sh[l][:], r=[Sh[l]])
            acopy(oT_sb[:], po, [pM[3]], [oT_sb])
            yield
            poc = pM[5]
            for hh in range(4):
                tr(poc[:, hh * 64:(hh + 1) * 64], oT_sb[:, hh, :], identf[0:64, 0:64], [oT_sb, cst], [poc])
            V(lambda e: e.memset(dsc[:, 48:52], 0.0), [], [dsc])
            for hh in range(4):
                act(dtmp[:, 0:64], poc[:, hh * 64:(hh + 1) * 64], AF.Square, [poc], [dtmp, dsc], accum=dsc[:, 48 + hh:49 + hh])
                yield
            rsqrt_small(dsc[:, 48:52], dsc[:, 48:52], [dsc], [dsc], 1.0 / 64, EPS)
            for hh in range(4):
                vstt(oc[:, hh * 64:(hh + 1) * 64], poc[:, hh * 64:(hh + 1) * 64], dsc[:, 48 + hh:49 + hh], bcv(l, 'hgn'), ALU.mult, ALU.mult, [poc, dsc, bcs[l]], [oc])
                yield
            vtt(ocb[:], oc[:], hg_s[:], ALU.mult, [oc, hg_s], [ocb])
            yield
            for cc in range(2):
                tr(pT[:, cc, :], ocb[:, cc * 128:(cc + 1) * 128], identb, [ocb, cstb], [pT])
            acopy(ocatT[:, 6:8, :], pT[:, 0:2, :], [pT], [ocatT])
            yield

        def out_proj(l):
            for half in range(2):
                slot, wv = wload(('out', l, half))
                pm = pM[4 + half]
                for kc in range(8):
                    mm(pm[:], ocatT[:, kc, :], wv[:, kc, :], [ocatT, slot], [pm], start=(kc == 0), stop=(kc == 7))
                hs = slice(half * 512, (half + 1) * 512)
                vtt(h[:, hs], h[:, hs], pm[:], ALU.add, [h, pm], [h])

        def ffn(l, samp, last):
            norm_T(pp(l, 'nffn'), ppt[l])
            if samp:
                dma('pool', scf[:], s_cf[l], w=[scf])
            def views(blk):
                xpf, acc, acc2, sa = FB[blk % 2]
                o_f = PP['fcw'][0] + blk * 12
                cw = ppt[l][:, o_f:o_f + 12].rearrange("p (q j) -> p q j", q=4)
                if not samp:
                    xpf3 = xpf[:, :, 0:130]
                    xs = [xpf3[:, :, j:j + 128] for j in range(3)]
                    ws = [cw[:, :, j:j + 1].to_broadcast([128, 4, 128]) for j in range(3)]
                    av, a2v = acc[:], acc2[:]
                    cin_v, new_v, cout_v = xpf3[:, :, 0:2], xpf3[:, :, 2:130], xpf3[:, :, 128:130]
                else:
                    xpf4 = xpf[:].rearrange("p q (b t) -> p q b t", b=16)
                    xs = [xpf4[:, :, :, j:j + 8] for j in range(3)]
                    ws = [cw[:, :, j:j + 1].unsqueeze(3).to_broadcast([128, 4, 16, 8]) for j in range(3)]
                    av = acc[:].rearrange("p q (b t) -> p q b t", b=16)
                    a2v = acc2[:].rearrange("p q (b t) -> p q b t", b=16)
                    cin_v, new_v, cout_v = xpf4[:, :, :, 0:2], xpf4[:, :, :, 2:10], xpf4[:, :, :, 8:10]
                return xpf, acc, acc2, sa, cw, xs, ws, av, a2v, cin_v, new_v, cout_v

            def front(blk):
                xpf, acc, acc2, sa, cw, xs, ws, av, a2v, cin_v, new_v, cout_v = views(blk)
                slot, wv = wload(('up', l, blk))
                pm = pM[blk % 2]
                pm3 = pm[:].rearrange("p (q t) -> p q t", q=4)
                for q in range(4):
                    for kc in range(8):
                        mm(pm3[:, q, :], wv[:, kc, q * 128:(q + 1) * 128], hnT[:, kc, :], [slot, hnT], [pm], start=(kc == 0), stop=(kc == 7))
                if not samp:
                    acopy(cin_v, cfc[l][:, blk, :, :], [cfc[l]], [xpf])
                    acopy(new_v, pm3, [pm], [xpf])
                else:
                    acopy(cin_v, scf[:, blk], [scf], [xpf])
                    acopy(new_v, pm[:].rearrange("p (q b t) -> p q b t", q=4, b=16), [pm], [xpf])
                for q in range(4):
                    act(a2v[:, q], xs[1][:, q], AF.Identity, [xpf, ppt[l]], [acc2], scale=cw[:, q, 1:2])

            def conv(blk):
                xpf, acc, acc2, sa, cw, xs, ws, av, a2v, cin_v, new_v, cout_v = views(blk)
                vtt(av, xs[0], ws[0], ALU.mult, [xpf, ppt[l]], [acc])
                vtt(av, av, a2v, ALU.add, [acc, acc2], [acc])
                vtt(a2v, xs[2], ws[2], ALU.mult, [xpf, ppt[l]], [acc2])
                vtt(av, av, a2v, ALU.add, [acc, acc2], [acc])
                if not samp:
                    acopy(cfc[l][:, blk, :, :], cout_v, [xpf], [cfc[l]])
                else:
                    acopy(scf[:, blk], cout_v, [xpf], [scf])

            def back_silu(blk):
                xpf, acc, acc2, sa = FB[blk % 2]
                act(sa[:], acc[:, 0:2, :], AF.Silu, [acc], [sa])

            def back_mult(blk):
                xpf, acc, acc2, sa = FB[blk % 2]
                vtt(gT[:, 2 * blk:2 * blk + 2, :], sa[:], acc[:, 2:4, :], ALU.mult, [sa, acc], [gT])
            front(0)
            conv(0)
            for blk in range(11):
                if blk + 1 < 11:
                    front(blk + 1)
                back_silu(blk)
                if blk + 1 < 11:
                    conv(blk + 1)
                back_mult(blk)
            if samp:
                dma('pool', o_scf[l], scf[:], r=[scf])
            elif last:
                dma('pool', o_pcf[l], cfc[l][:], r=[cfc[l]])
            for half in range(2):
                pm = pM[2 + half]
                for q4 in range(4):
                    slot, wv = wload(('down', l, half, q4))
                    n_c = 6 if q4 < 3 else 4
                    c0 = q4 * 6
                    for c in range(n_c):
                        mm(pm[:], gT[:, c0 + c, :], wv[:, c, :], [gT, slot], [pm], start=(c0 + c == 0), stop=(c0 + c == 21))
                hs = slice(half * 512, (half + 1) * 512)
                vtt(h[:, hs], h[:, hs], pm[:], ALU.add, [h, pm], [h])

        def ple(l, si):
            norm_T(pp(l, 'nple'), ppt[l])
            dma('pool', pf[:], pin[l, si], w=[pf])
            vcopy(pb[:], pf[:], [pf], [pb])
            for half in range(2):
                slot, wv = wload(('pg', l, half))
                pm = pM[half]
                for kc in range(8):
                    mm(pm[:], hnT[:, kc, :], wv[:, kc, :], [hnT, slot], [pm], start=(kc == 0), stop=(kc == 7))
                act(gsbh[half][:], pm[:], AF.Sigmoid, [pm], [gsbh[half]])
            for cc in range(2):
                tr(pT[:, cc, :], pb[:, cc * 128:(cc + 1) * 128], identb, [pb, cstb], [pT])
            vcopy(ppT[:], pT[:, 0:2, :], [pT], [ppT])
            for half in range(2):
                slot, wv = wload(('pp', l, half))
                pm = pM[2 + half]
                for cc in range(2):
                    mm(pm[:], ppT[:, cc, :], wv[:, cc, :], [ppT, slot], [pm], start=(cc == 0), stop=(cc == 1))
                hs = slice(half * 512, (half + 1) * 512)
                vtt(tmp1[:], gsbh[half][:], pm[:], ALU.mult, [gsbh[half], pm], [tmp1])
                vtt(h[:, hs], h[:, hs], tmp1[:], ALU.add, [h, tmp1], [h])

        for si in range(NSUB):
            samp = (si == NSUB - 1)
            last = (si == NSUB - 2)
            if samp:
                for l in range(2):
                    build_tables(l, True)
            dma('pool', h[:], xin[si], w=[h])
            for l in range(2):
                in_proj(l, samp)
                gens_ = [in_proj_rest(l, samp), delta(l, samp, last)]
                while gens_:
                    for g_ in list(gens_):
                        try:
                            next(g_)
                        except StopIteration:
                            gens_.remove(g_)
                gens_ = [s5(l, samp, last), hgrn(l, samp, last)]
                while gens_:
                    for g_ in list(gens_):
                        try:
                            next(g_)
                        except StopIteration:
                            gens_.remove(g_)
                if KSTAGE >= 5:
                    out_proj(l)
                ffn(l, samp, last)
                ple(l, si)
            norm_stats()
            for half in range(2):
                hs = slice(half * 512, (half + 1) * 512)
                vstt(yoh[half][:], h[:, hs], st[:, 3:4], nfin[:, hs], ALU.mult, ALU.mult, [h, st, nfin], [yoh[half]])
                dma('pool', y_d[si][:, hs], yoh[half][:], r=[yoh[half]])
        S.finish()
        with nc.Block() as block:
            S.emit(nc, block)
    return nc


def _fm(v, nchunk):
    return np.ascontiguousarray(v.reshape(nchunk, 128).T)


_NC_CACHE = {}


def _prepare(inputs):
    f = {k: np.asarray(v, dtype=np.float32) for k, v in inputs.items()}
    n = 8
    pps, bcs = [], []
    for l in range(2):
        pp = np.zeros((128, NPP), np.float32)

        def put(name, arr):
            o, w = PP[name]
            pp[:, o:o + w] = arr.reshape(128, w)
        put('nmix', _fm(f['norm_mix'][l], 8))
        put('nffn', _fm(f['norm_ffn'][l], 8))
        put('nple', _fm(f['norm_ple'][l], 8))
        put('dcw', f['dn_conv_w'][l].reshape(4, 12, 128).transpose(2, 1, 0))
        fw = f['ffn_conv_w'][l].reshape(3, 2, 11, 2, 128)
        put('fcw', fw.transpose(4, 2, 1, 3, 0))
        put('lre', _fm(f['ssm_lam_re'][l].reshape(-1), 8))
        put('lim', _fm(f['ssm_lam_im'][l].reshape(-1), 8))
        put('lst', _fm(np.repeat(f['ssm_log_step'][l], 64), 8))
        put('ssd', _fm(f['ssm_d'][l], 2))
        put('glub', _fm(f['ssm_glu_b'][l], 2))
        put('hl0', _fm(f['hg_lower'][0], 2))
        put('hl1', _fm(f['hg_lower'][1], 2))
        pps.append(pp)
        bc = np.zeros((NBC,), np.float32)

        def putb(name, arr):
            o, w = BC[name]
            bc[o:o + w] = arr.reshape(w)
        putb('dnn', f['dn_norm'][l])
        putb('hgn', f['hg_norm'][l])
        putb('alog', f['dn_a_log'][l])
        putb('dtb', f['dn_dt_bias'][l])
        putb('lre', f['ssm_lam_re'][l])
        putb('lim', f['ssm_lam_im'][l])
        putb('lst', np.repeat(f['ssm_log_step'][l], 64))
        putb('hl0', f['hg_lower'][0])
        putb('hl1', f['hg_lower'][1])
        putb('nfin', f['norm_final'])
        bcs.append(bc)
    pp_all = np.stack(pps)
    bc_all = np.stack(bcs)
    bfull = np.zeros((2, 2, 128, 2, 512), np.float32)
    cfull = np.zeros((2, 2, 128, 8, 32), np.float32)
    for l in range(2):
        for r, (bn, cn) in enumerate((('ssm_b_re', 'ssm_c_re'), ('ssm_b_im', 'ssm_c_im'))):
            for g in range(16):
                cc, gg = divmod(g, 8)
                j, gl = divmod(g, 2)
                bfull[l, r, gg * 16:(gg + 1) * 16, cc, (j % 4) * 128 + gl * 64:(j % 4) * 128 + gl * 64 + 64] = f[bn][l, g].T
                cfull[l, r, gl * 64:(gl + 1) * 64, j, gl * 16:(gl + 1) * 16] = f[cn][l, g].T
    in_maps = []
    for c in range(n):
        xs = np.concatenate([f['x_prompt'][c].reshape(16, 128, 1024),
                             f['x_sample'][c * 16:(c + 1) * 16].reshape(1, 128, 1024)], axis=0)
        ps_ = np.concatenate([f['p_prompt'][:, c].reshape(2, 16, 128, 256),
                              f['p_sample'][:, c * 16:(c + 1) * 16].reshape(2, 1, 128, 256)], axis=1)
        sl = slice(c * 16, (c + 1) * 16)
        s_cq = f['state_conv_qkv'][:, sl].reshape(2, 16, 3, 12, 128).transpose(0, 4, 3, 1, 2)
        s_ss = np.stack([f['state_ssm_re'][:, sl], f['state_ssm_im'][:, sl]], axis=1).reshape(2, 2, 16, 8, 128).transpose(0, 1, 4, 3, 2)
        s_cf = f['state_conv_ffn'][:, sl].reshape(2, 16, 2, 2, 11, 2, 128).transpose(0, 6, 4, 3, 5, 1, 2).reshape(2, 128, 11, 4, 16, 2)
        in_maps.append({
            "xin": np.ascontiguousarray(xs), "pin": np.ascontiguousarray(ps_), "consts": CONSTS, "pp": pp_all, "bc": bc_all,
            "w_in": f['w_in'], "w_out": f['w_out'], "w_up": f['ffn_w_up'], "w_down": f['ffn_w_down'],
            "w_pg": f['ple_w_gate'], "w_pp": f['ple_w_proj'], "w_glu": f['ssm_glu_w'], "bfull": bfull, "cfull": cfull,
            "s_cq": np.ascontiguousarray(s_cq), "s_dl": np.ascontiguousarray(f['state_delta'][:, sl]),
            "s_ss": np.ascontiguousarray(s_ss), "s_hg": np.ascontiguousarray(f['state_hgrn'][:, sl]),
            "s_cf": np.ascontiguousarray(s_cf),
        })
    return in_maps


def kernel(**inputs):
    n = 8
    in_maps = _prepare(inputs)
    if 'nc' not in _NC_CACHE:
        _NC_CACHE['nc'] = build()
    res = run_bass_kernel_spmd(_NC_CACHE['nc'], in_maps, core_ids=list(range(n))).results
    yp = np.stack([r["y"][:16].reshape(2048, 1024) for r in res])
    ys = np.concatenate([r["y"][16].reshape(16, 8, 1024) for r in res], axis=0)

    def cat(fn, axis=1):
        return np.ascontiguousarray(np.concatenate([fn(r) for r in res], axis=axis))
    p_cq = cat(lambda r: r["o_pcq"].transpose(0, 3, 2, 1).reshape(2, 1, 3, 1536))
    p_dl = cat(lambda r: r["o_pdl"].reshape(2, 1, 4, 128, 128))
    p_sr = cat(lambda r: r["o_pss"][:, 0].transpose(0, 2, 1).reshape(2, 1, 16, 64))
    p_si = cat(lambda r: r["o_pss"][:, 1].transpose(0, 2, 1).reshape(2, 1, 16, 64))
    p_hg = cat(lambda r: r["o_phg"].reshape(2, 2, 64, 2, 64).transpose(0, 3, 1, 2, 4).reshape(2, 1, 4, 64, 64))
    p_cf = cat(lambda r: r["o_pcf"].reshape(2, 128, 11, 2, 2, 2).transpose(0, 5, 3, 2, 4, 1).reshape(2, 1, 2, 5632))
    s_cq = cat(lambda r: r["o_scq"].transpose(0, 3, 4, 2, 1).reshape(2, 16, 3, 1536))
    s_dl = cat(lambda r: r["o_sdl"])
    s_sr = cat(lambda r: r["o_sss"][:, 0].transpose(0, 3, 2, 1).reshape(2, 16, 16, 64))
    s_si = cat(lambda r: r["o_sss"][:, 1].transpose(0, 3, 2, 1).reshape(2, 16, 16, 64))
    s_hg = cat(lambda r: r["o_shg"].reshape(2, 2, 64, 2, 16, 64).transpose(0, 4, 3, 1, 2, 5).reshape(2, 16, 4, 64, 64))
    s_cf = cat(lambda r: r["o_scf"].reshape(2, 128, 11, 2, 2, 16, 2).transpose(0, 5, 6, 3, 2, 4, 1).reshape(2, 16, 2, 5632))
    return (yp, ys, p_cq, p_dl, p_sr, p_si, p_hg, p_cf, s_cq, s_dl, s_sr, s_si, s_hg, s_cf)
```

```python
import math
from contextlib import ExitStack
import numpy as np
import concourse.bass as bass
import concourse.mybir as mybir
from concourse.bass_utils import run_bass_kernel_spmd

F32 = mybir.dt.float32
BF16 = mybir.dt.bfloat16
I32 = mybir.dt.int32
ALU = mybir.AluOpType
AF = mybir.ActivationFunctionType
EPS = 1e-6
import os
NSUB = int(os.environ.get('KNSUB', '17'))
NRING = 4
SEM_EPOCH = 4000
NEPOCH = {'pe': 10, 'dve': 5, 'act': 3, 'pool': 1}
KSUB = int(os.environ.get('KSUB', '9'))
KSTAGE = int(os.environ.get('KSTAGE', '9'))
TWO_PI = 2.0 * math.pi


class Buf:
    __slots__ = ('name', 'w', 'r', 'excl')

    def __init__(self, name, excl=False):
        self.name = name
        self.w = None
        self.r = {}
        self.excl = excl


class TB:
    def __init__(self, t, name, bufs=None):
        self.t = t
        self.bs = bufs if bufs is not None else [Buf(name)]

    def __getitem__(self, k):
        return self.t[k]


class Sched:
    def __init__(self, sems):
        self.sems = sems
        self.cnt = {k: 0 for k in sems}
        self.prog = {'pe': [], 'dve': [], 'act': [], 'pool': [], 'sp': []}
        self.waited = {e: {} for e in self.prog}
        self.dma_rr = {'sp': 0, 'pool': 0, 'act': 0}
        self.epoch = {}
        self.last_pe = None
        self.dma_names = {q: sorted(n for n in sems if n.startswith('d_%s_' % q)) for q in ('sp', 'pool', 'act')}

    def _need(self, eng, s, v, force=False):
        if eng == 'pe' and s.startswith('pe_') and not force:
            return
        if self.waited[eng].get(s, 0) < v:
            self.prog[eng].append(('w', s, v))
            self.waited[eng][s] = v

    def _deps(self, eng, reads, writes):
        for b in reads:
            if b.w is not None:
                self._need(eng, *b.w)
        for b in writes:
            if b.w is not None:
                self._need(eng, *b.w)
            for s, v in b.r.items():
                self._need(eng, s, v)

    def _done(self, tok, reads, writes):
        for b in writes:
            b.w = tok
            b.r = {}
        for b in reads:
            if b.r.get(tok[0], 0) < tok[1]:
                b.r[tok[0]] = tok[1]

    def op(self, eng, fn, reads=(), writes=(), pe_sync=False):
        if pe_sync and self.last_pe is not None:
            self._need('pe', self.last_pe[0], self.last_pe[1], force=True)
        reads = [b for x in reads for b in x.bs]
        writes = [b for x in writes for b in x.bs]
        writes = writes + [b for b in reads if b.excl and b not in writes]
        reads = [b for b in reads if not b.excl]
        self._deps(eng, reads, writes)
        ep = self.epoch.setdefault(eng, 0)
        s = '%s_%d' % (eng, ep)
        if self.cnt[s] >= SEM_EPOCH:
            ep += 1
            self.epoch[eng] = ep
            s = '%s_%d' % (eng, ep)
        self.cnt[s] += 1
        tok = (s, self.cnt[s])
        self.prog[eng].append(('o', fn, s, 1))
        if eng == 'pe':
            self.last_pe = tok
        self._done(tok, reads, writes)

    def dma(self, q, fn, reads=(), writes=()):
        reads = [b for x in reads for b in x.bs]
        writes = [b for x in writes for b in x.bs]
        names = self.dma_names[q]
        s = names[self.dma_rr[q] % len(names)]
        self.dma_rr[q] += 1
        if self.cnt[s] > 0:
            self._need(q, s, self.cnt[s])
        self._deps(q, reads, writes)
        self.cnt[s] += 16
        tok = (s, self.cnt[s])
        self.prog[q].append(('o', fn, s, 16))
        self._done(tok, reads, writes)

    def finish(self):
        for q in ('sp', 'pool', 'act'):
            for s in self.dma_names[q]:
                if self.cnt[s] > 0:
                    self._need('sp', s, self.cnt[s])
        for s in self.sems:
            if not s.startswith('d_') and self.cnt[s] > 0:
                self._need('sp', s, self.cnt[s])

    def emit(self, nc, block):
        sems = self.sems

        def replay(engobj, prog):
            for it in prog:
                if it[0] == 'w':
                    engobj.wait_ge(sems[it[1]], it[2])
                else:
                    it[1](engobj).then_inc(sems[it[2]], it[3])

        @block.tensor
        def _(e):
            replay(e, self.prog['pe'])

        @block.vector
        def _(e):
            replay(e, self.prog['dve'])

        @block.scalar
        def _(e):
            replay(e, self.prog['act'])

        @block.gpsimd
        def _(e):
            replay(e, self.prog['pool'])

        @block.sync
        def _(e):
            replay(e, self.prog['sp'])


def _consts():
    p = np.arange(128)
    c = {}
    c['ident'] = np.eye(128)
    for nm, bs in (('P', 128), ('S', 8), ('H', 32)):
        same = (p[:, None] // bs) == (p[None, :] // bs)
        c['cT' + nm] = ((p[:, None] <= p[None, :]) & same)
        c['st' + nm] = ((p[None, :] < p[:, None]) & same)
        c['sm' + nm] = same
        nb = 128 // bs
        sel = np.zeros((128, 16))
        sel[p, p // bs] = 1.0
        c['sel' + nm] = sel
    c['iota'] = np.tile(np.arange(128)[None, :], (128, 1))
    c['pcol'] = np.tile(p[:, None], (1, 2))
    c['pcol'][:, 1] = p % 8
    names = ['ident', 'cTP', 'cTS', 'cTH', 'stP', 'stS', 'smP', 'smS', 'smH', 'selP', 'selS', 'selH', 'iota', 'pcol']
    offs = {}
    cols = []
    o = 0
    for n in names:
        a = c[n].astype(np.float32)
        offs[n] = (o, a.shape[1])
        o += a.shape[1]
        cols.append(a)
    return np.concatenate(cols, axis=1), offs


CONSTS, COFF = _consts()
NCONST = CONSTS.shape[1]
PP = {}
_o = 0
for _n, _w in (('nmix', 8), ('nffn', 8), ('nple', 8), ('dcw', 48), ('fcw', 132), ('lre', 8), ('lim', 8), ('lst', 8),
               ('ssd', 2), ('glub', 2), ('hl0', 2), ('hl1', 2)):
    PP[_n] = (_o, _w)
    _o += _w
NPP = _o
BC = {}
_o = 0
for _n, _w in (('dnn', 128), ('hgn', 64), ('alog', 4), ('dtb', 4), ('hl0', 256), ('hl1', 256),
               ('lre', 1024), ('lim', 1024), ('lst', 1024), ('nfin', 1024)):
    BC[_n] = (_o, _w)
    _o += _w
NBC = _o
NBCS = 712


def build():
    nc = bass.Bass("TRN2", target_bir_lowering=False)

    def din(name, shape):
        return nc.dram_tensor(name, list(shape), F32, kind="ExternalInput").ap()

    def dout(name, shape):
        return nc.dram_tensor(name, list(shape), F32, kind="ExternalOutput").ap()

    xin = din("xin", [NSUB, 128, 1024])
    pin = din("pin", [2, NSUB, 128, 256])
    consts_d = din("consts", [128, NCONST])
    pp_d = din("pp", [2, 128, NPP])
    bc_d = din("bc", [2, NBC])
    w_in = din("w_in", [2, 1024, 3336])
    w_out = din("w_out", [2, 1024, 1024])
    w_up = din("w_up", [2, 1024, 5632])
    w_down = din("w_down", [2, 2816, 1024])
    w_pg = din("w_pg", [2, 1024, 1024])
    w_pp = din("w_pp", [2, 256, 1024])
    w_glu = din("w_glu", [2, 256, 256])
    bfull = din("bfull", [2, 2, 128, 2, 512])
    cfull = din("cfull", [2, 2, 128, 8, 32])
    s_cq = din("s_cq", [2, 128, 12, 16, 3])
    s_dl = din("s_dl", [2, 16, 4, 128, 128])
    s_ss = din("s_ss", [2, 2, 128, 8, 16])
    s_hg = din("s_hg", [2, 16, 4, 64, 64])
    s_cf = din("s_cf", [2, 128, 11, 4, 16, 2])
    y_d = dout("y", [NSUB, 128, 1024])
    o_pcq = dout("o_pcq", [2, 128, 12, 3])
    o_pdl = dout("o_pdl", [2, 4, 128, 128])
    o_pss = dout("o_pss", [2, 2, 128, 8])
    o_phg = dout("o_phg", [2, 128, 2, 64])
    o_pcf = dout("o_pcf", [2, 128, 11, 4, 2])
    o_scq = dout("o_scq", [2, 128, 12, 16, 3])
    o_sdl = dout("o_sdl", [2, 16, 4, 128, 128])
    o_sss = dout("o_sss", [2, 2, 128, 8, 16])
    o_shg = dout("o_shg", [2, 128, 2, 16, 64])
    o_scf = dout("o_scf", [2, 128, 11, 4, 16, 2])

    es = ExitStack()
    with es:
        def sb(name, shape, dt=F32):
            return TB(es.enter_context(nc.sbuf_tensor(name, list(shape), dt)), name)

        def ps(name, shape, dt=F32):
            return TB(es.enter_context(nc.psum_tensor(name, list(shape), dt)), name, bufs=[Buf(name, excl=True)])

        sems = {}
        for n in ['%s_%d' % (e_, i_) for e_ in NEPOCH for i_ in range(NEPOCH[e_])] + ['d_sp_%d' % i for i in range(8)] + ['d_pool_%d' % i for i in range(6)] + ['d_act_%d' % i for i in range(2)]:
            sems[n] = es.enter_context(nc.semaphore(n))
        S = Sched(sems)

        def V(fn, r=(), w=()):
            S.op('dve', fn, r, w)

        def A(fn, r=(), w=()):
            S.op('act', fn, r, w)

        def mm(out, lhsT, rhs, r, w, start=True, stop=True, skip=False, sync=False):
            S.op('pe', lambda e: e.matmul(out, lhsT=lhsT, rhs=rhs, start=start, stop=stop, skip_group_check=skip), r, w, pe_sync=sync)

        def tr(out, in_, ident, r, w):
            S.op('pe', lambda e: e.transpose(out=out, in_=in_, identity=ident), r, w)

        def vtt(out, a, b, op, r, w):
            V(lambda e: e.tensor_tensor(out=out, in0=a, in1=b, op=op), r, w)

        def vts(out, a, s1, op0, r, w, s2=None, op1=None):
            if op1 is None:
                V(lambda e: e.tensor_scalar(out=out, in0=a, scalar1=s1, scalar2=None, op0=op0), r, w)
            else:
                V(lambda e: e.tensor_scalar(out=out, in0=a, scalar1=s1, scalar2=s2, op0=op0, op1=op1), r, w)

        def vstt(out, a, sc, b, op0, op1, r, w):
            V(lambda e: e.scalar_tensor_tensor(out=out, in0=a, scalar=sc, in1=b, op0=op0, op1=op1), r, w)

        def vcopy(out, a, r, w):
            V(lambda e: e.tensor_copy(out=out, in_=a), r, w)

        def acopy(out, a, r, w):
            A(lambda e: e.copy(out=out, in_=a), r, w)

        def act(out, a, func, r, w, bias=None, scale=None, accum=None):
            kw = {}
            if bias is not None:
                kw['bias'] = bias
            if scale is not None:
                kw['scale'] = scale
            if accum is not None:
                kw['accum_out'] = accum
            A(lambda e: e.activation(out=out, in_=a, func=func, **kw), r, w)

        def dma(q, out, in_, r=(), w=()):
            S.dma(q, lambda e: e.dma_start(out=out, in_=in_), reads=r, writes=w)

        WB = {}

        def wreg(key, kc, n, parts):
            d = nc.dram_tensor("wb_" + "_".join(str(k) for k in key), [128, kc * n], BF16, kind="Internal").ap()
            tb = TB(None, "wb" + str(key))
            d3 = d.rearrange("p (k n) -> p k n", k=kc)
            for (src, c0, ncol) in parts:
                S.dma('pool', lambda e, d3=d3, src=src, c0=c0, ncol=ncol: e.dma_start(out=d3[:, :, c0:c0 + ncol], in_=src.rearrange("(k p) n -> p k n", p=128)), writes=[tb])
            WB[key] = (d3, tb, kc, n)
        for l in range(2):
            W = w_in[l]
            for g in range(3):
                wreg(('in', l, g), 8, 512, [(W[:, g * 512:(g + 1) * 512], 0, 512)])
            wreg(('in', l, 3), 8, 520, [(W[:, 1536:2056], 0, 520)])
            wreg(('in', l, 4), 8, 512, [(W[:, 2056:2568], 0, 512)])
            wreg(('in', l, 5), 8, 512, [(W[:, 2568:3080], 0, 512)])
            wreg(('in', l, 6), 8, 256, [(W[:, 3080:3336], 0, 256)])
            for half in range(2):
                wreg(('out', l, half), 8, 512, [(w_out[l][:, half * 512:(half + 1) * 512], 0, 512)])
            for blk in range(11):
                wreg(('up', l, blk), 8, 512, [(w_up[l][:, blk * 256:(blk + 1) * 256], 0, 256),
                                              (w_up[l][:, 2816 + blk * 256:2816 + (blk + 1) * 256], 256, 256)])
            for half in range(2):
                for q4 in range(4):
                    n_c = 6 if q4 < 3 else 4
                    c0 = q4 * 6
                    wreg(('down', l, half, q4), n_c, 512, [(w_down[l][c0 * 128:(c0 + n_c) * 128, half * 512:(half + 1) * 512], 0, 512)])
                wreg(('pg', l, half), 8, 512, [(w_pg[l][:, half * 512:(half + 1) * 512], 0, 512)])
                wreg(('pp', l, half), 2, 512, [(w_pp[l][:, half * 512:(half + 1) * 512], 0, 512)])

        cst = sb("cst", [128, NCONST])
        cstb = sb("cstb", [128, NCONST], BF16)
        dma('sp', cst[:], consts_d, w=[cst])
        vcopy(cstb[:], cst[:], [cst], [cstb])

        def C(n, bf=False, cols=None):
            o, wd = COFF[n]
            if cols is not None:
                wd = cols
            return (cstb if bf else cst)[:, o:o + wd]

        identf = C('ident')
        identb = C('ident', True)
        onesb = C('smP', True)
        onesf = C('smP')
        ppt = [sb("pp%d" % l, [128, NPP]) for l in range(2)]
        bcs = [sb("bcs%d" % l, [128, NBCS]) for l in range(2)]
        nfin = sb("nfin", [128, 1024])
        dma('sp', nfin[:], bc_d[0, BC['nfin'][0]:BC['nfin'][0] + 1024].partition_broadcast(128), w=[nfin])
        for l in range(2):
            dma('sp', ppt[l][:], pp_d[l], w=[ppt[l]])
            dma('sp', bcs[l][:], bc_d[l, 0:NBCS].partition_broadcast(128), w=[bcs[l]])

        def pp(l, n, a=0, b=None):
            o, wd = PP[n]
            return ppt[l][:, o + a:o + (wd if b is None else b)]

        def bcv(l, n, a=0, b=None):
            o, wd = BC[n]
            return bcs[l][:, o + a:o + (wd if b is None else b)]

        glw = [sb("glw%d" % l, [128, 2, 256], BF16) for l in range(2)]
        bft = [[sb("bf%d%d" % (l, r), [128, 2, 512], BF16) for r in range(2)] for l in range(2)]
        cft = [[sb("cf%d%d" % (l, r), [128, 8, 32], BF16) for r in range(2)] for l in range(2)]
        cftf = sb("cftf", [128, 8, 32])
        diagD = [sb("diagD%d" % l, [128, 2, 128], BF16) for l in range(2)]
        negA = [sb("negA%d" % l, [128, 4]) for l in range(2)]
        lbb = [sb("lbb%d" % l, [128, 256]) for l in range(2)]
        omlb = [sb("omlb%d" % l, [128, 256]) for l in range(2)]
        lbf = [sb("lbf%d" % l, [128, 2]) for l in range(2)]
        omlf = [sb("omlf%d" % l, [128, 2]) for l in range(2)]
        for l in range(2):
            dma('pool', glw[l][:], w_glu[l].rearrange("(c p) n -> p c n", p=128), w=[glw[l]])
            for r in range(2):
                dma('pool', bft[l][r][:], bfull[l, r], w=[bft[l][r]])
            dma('pool', cft[l][0][:], cfull[l, 0], w=[cft[l][0]])
            dma('sp', cftf[:], cfull[l, 1], w=[cftf])
            vts(cft[l][1][:], cftf[:], -1.0, ALU.mult, [cftf], [cft[l][1]])
            for cc in range(2):
                vts(diagD[l][:, cc, :], identf, pp(l, 'ssd', cc, cc + 1), ALU.mult, [cst, ppt[l]], [diagD[l]])
            act(negA[l][:], bcv(l, 'alog'), AF.Exp, [bcs[l]], [negA[l]])
            vts(negA[l][:], negA[l][:], -1.0, ALU.mult, [negA[l]], [negA[l]])
            if l == 0:
                V(lambda e: e.memset(lbb[0][:], 0.0), [], [lbb[0]])
                V(lambda e: e.memset(lbf[0][:], 0.0), [], [lbf[0]])
            else:
                vtt(lbb[1][:], bcv(1, 'hl1'), bcv(1, 'hl0'), ALU.subtract, [bcs[1]], [lbb[1]])
                act(lbb[1][:], lbb[1][:], AF.Sigmoid, [lbb[1]], [lbb[1]])
                vtt(lbf[1][:], pp(1, 'hl1'), pp(1, 'hl0'), ALU.subtract, [ppt[1]], [lbf[1]])
                act(lbf[1][:], lbf[1][:], AF.Sigmoid, [lbf[1]], [lbf[1]])
            vts(omlb[l][:], lbb[l][:], -1.0, ALU.mult, [lbb[l]], [omlb[l]], 1.0, ALU.add)
            vts(omlf[l][:], lbf[l][:], -1.0, ALU.mult, [lbf[l]], [omlf[l]], 1.0, ALU.add)

        ZB = [es.enter_context(nc.sbuf_tensor("ZB%d" % i, [128, 2048], F32)) for i in range(4)]
        PZ = [TB(ZB[i // 4][:, (i % 4) * 512:(i % 4 + 1) * 512], "PZ%d" % i) for i in range(16)]
        tA, tB_, tC, tD, mag, bcA, bcP, cre, cim, lbr, lbi, bs_lre, bs_lim, bs_lst = PZ[0:14]
        bsvd = {'lre': bs_lre, 'lim': bs_lim, 'lst': bs_lst}
        h = sb("h", [128, 1024])
        tI = TB(h[:, 0:512].bitcast(I32), "tI", bufs=h.bs)

        def sincos(dst, ang_r, shift, rr, ww):
            vts(tC[:], ang_r, shift, ALU.add, rr, [tC])
            vcopy(tI[:], tC[:], [tC], [tI])
            vcopy(tD[:], tI[:], [tI], [tD])
            vtt(tC[:], tC[:], tD[:], ALU.subtract, [tC, tD], [tC])
            act(dst, tC[:], AF.Sin, [tC], ww, scale=6.283185)

        LpT = [[sb("LpT%d%d" % (l, r), [128, 8, 128], BF16) for r in range(2)] for l in range(2)]
        Lm = [[sb("Lm%d%d" % (l, r), [128, 1024], BF16) for r in range(2)] for l in range(2)]
        lbar = [[sb("lbar%d%d" % (l, r), [128, 8]) for r in range(2)] for l in range(2)]
        sm_a = sb("sm_a", [128, 8])
        sm_p = sb("sm_p", [128, 8])
        iota3 = C('iota').unsqueeze(1).to_broadcast([128, 4, 128])

        def t3(T):
            return T[:].rearrange("p (j t) -> p j t", j=4)

        def build_tables(l, samp):
            if not samp:
                act(sm_p[:], pp(l, 'lst'), AF.Exp, [ppt[l]], [sm_p])
                vtt(sm_a[:], pp(l, 'lre'), sm_p[:], ALU.mult, [ppt[l], sm_p], [sm_a])
                vstt(sm_p[:], pp(l, 'lim'), 1.0 / TWO_PI, sm_p[:], ALU.mult, ALU.mult, [ppt[l], sm_p], [sm_p])
            for hf in range(2):
                js = slice(4 * hf, 4 * hf + 4)
                cs = slice(512 * hf, 512 * hf + 512)

                def bsv(n):
                    return bsvd[n][:]
                for n_ in ('lre', 'lim', 'lst'):
                    o_ = BC[n_][0] + 512 * hf
                    dma('sp', bsvd[n_][:], bc_d[l, o_:o_ + 512].partition_broadcast(128), w=[bsvd[n_]])
                if not samp:
                    vtt(t3(tA), iota3, sm_a[:, js].unsqueeze(2).to_broadcast([128, 4, 128]), ALU.mult, [cst, sm_a], [tA])
                    act(mag[:], tA[:], AF.Exp, [tA], [mag])
                    vtt(t3(tB_), iota3, sm_p[:, js].unsqueeze(2).to_broadcast([128, 4, 128]), ALU.mult, [cst, sm_p], [tB_])
                    sincos(tA[:], tB_[:], 0.25, [tB_], [tA])
                    vtt(LpT[l][0][:, js, :], t3(tA), t3(mag), ALU.mult, [tA, mag], [LpT[l][0]])
                    vtt(t3(cre), t3(tA), t3(mag), ALU.mult, [tA, mag], [cre])
                    vcopy(lbar[l][0][:, js], t3(cre)[:, :, 1], [cre], [lbar[l][0]])
                    sincos(tA[:], tB_[:], 0.0, [tB_], [tA])
                    vtt(LpT[l][1][:, js, :], t3(tA), t3(mag), ALU.mult, [tA, mag], [LpT[l][1]])
                    vtt(t3(cre), t3(tA), t3(mag), ALU.mult, [tA, mag], [cre])
                    vcopy(lbar[l][1][:, js], t3(cre)[:, :, 1], [cre], [lbar[l][1]])
                act(bcP[:], bsv('lst'), AF.Exp, [bs_lst], [bcP])
                vtt(bcA[:], bsv('lre'), bcP[:], ALU.mult, [bs_lre, bcP], [bcA])
                vstt(bcP[:], bsv('lim'), 1.0 / TWO_PI, bcP[:], ALU.mult, ALU.mult, [bs_lim, bcP], [bcP])
                act(mag[:], bcA[:], AF.Exp, [bcA], [mag])
                sincos(tA[:], bcP[:], 0.25, [bcP], [tA])
                vtt(lbr[:], tA[:], mag[:], ALU.mult, [tA, mag], [lbr])
                sincos(tA[:], bcP[:], 0.0, [bcP], [tA])
                vtt(lbi[:], tA[:], mag[:], ALU.mult, [tA, mag], [lbi])
                vts(lbr[:], lbr[:], -1.0, ALU.add, [lbr], [lbr])
                vtt(tA[:], bsv('lre'), bsv('lre'), ALU.mult, [bs_lre], [tA])
                vtt(tB_[:], bsv('lim'), bsv('lim'), ALU.mult, [bs_lim], [tB_])
                vtt(tA[:], tA[:], tB_[:], ALU.add, [tA, tB_], [tA])
                V(lambda e: e.reciprocal(out=tA[:], in_=tA[:]), [tA], [tA])
                vtt(cre[:], lbr[:], bsv('lre'), ALU.mult, [lbr, bs_lre], [cre])
                vtt(tB_[:], lbi[:], bsv('lim'), ALU.mult, [lbi, bs_lim], [tB_])
                vtt(cre[:], cre[:], tB_[:], ALU.add, [cre, tB_], [cre])
                vtt(cre[:], cre[:], tA[:], ALU.mult, [cre, tA], [cre])
                vtt(cim[:], lbi[:], bsv('lre'), ALU.mult, [lbi, bs_lre], [cim])
                vtt(tB_[:], lbr[:], bsv('lim'), ALU.mult, [lbr, bs_lim], [tB_])
                vtt(cim[:], cim[:], tB_[:], ALU.subtract, [cim, tB_], [cim])
                vtt(cim[:], cim[:], tA[:], ALU.mult, [cim, tA], [cim])
                o_pc = COFF['pcol'][0] + (1 if samp else 0)
                sidx = cst[:, o_pc:o_pc + 1]
                vts(tA[:], bcA[:], sidx, ALU.mult, [bcA, cst], [tA], -1.0, ALU.mult)
                act(mag[:], tA[:], AF.Exp, [tA], [mag])
                vts(tB_[:], bcP[:], sidx, ALU.mult, [bcP, cst], [tB_], -1.0, ALU.mult)
                sincos(tA[:], tB_[:], 0.25, [tB_], [tA])
                vtt(lbr[:], tA[:], mag[:], ALU.mult, [tA, mag], [lbr])
                sincos(tA[:], tB_[:], 0.0, [tB_], [tA])
                vtt(lbi[:], tA[:], mag[:], ALU.mult, [tA, mag], [lbi])
                vtt(tA[:], lbr[:], cre[:], ALU.mult, [lbr, cre], [tA])
                vtt(tB_[:], lbi[:], cim[:], ALU.mult, [lbi, cim], [tB_])
                vtt(Lm[l][0][:, cs], tA[:], tB_[:], ALU.subtract, [tA, tB_], [Lm[l][0]])
                vtt(tA[:], lbr[:], cim[:], ALU.mult, [lbr, cim], [tA])
                vtt(tB_[:], lbi[:], cre[:], ALU.mult, [lbi, cre], [tB_])
                vtt(Lm[l][1][:, cs], tA[:], tB_[:], ALU.add, [tA, tB_], [Lm[l][1]])
        for l in range(2):
            build_tables(l, False)
        hnT = sb("hnT", [128, 8, 128], BF16)
        ocatT = sb("ocatT", [128, 8, 128], BF16)
        st = sb("st", [128, 8])
        ring = [sb("ring%d" % i, [128, 4160], BF16) for i in range(NRING)]
        ringi = [0]
        pT = ps("pT", [128, 8, 128], BF16)
        pFt = es.enter_context(nc.psum_tensor("pF", [128, 4, 128], F32))
        pFbuf = Buf("pF", excl=True)
        pF = [TB(pFt[:, i, :], "pF%d" % i, bufs=[pFbuf]) for i in range(4)]
        pM = [ps("pM%d" % i, [128, 512]) for i in range(6)]
        acc = sb("acc", [128, 4, 128])
        acc2 = sb("acc2", [128, 4, 128])
        oT_sb = TB(acc[0:64, :, :], "oT_sb", bufs=acc.bs)
        cfc = [sb("cfc%d" % l, [128, 11, 4, 2]) for l in range(2)]
        pf = sb("pf", [128, 256])
        pb = sb("pb", [128, 256], BF16)
        ppT = sb("ppT", [128, 2, 128], BF16)
        xpq = sb("xpq", [128, 12, 176], BF16)
        cq = [sb("cq%d" % l, [128, 12, 3]) for l in range(2)]
        ba = sb("ba", [128, 8])
        uT = sb("uT", [128, 2, 128], BF16)
        hqTs = sb("hqTs", [128, 2, 128])
        k_tok = sb("k_tok", [128, 4, 128], BF16)
        v_tok = sb("v_tok", [128, 4, 128], BF16)
        knT = sb("knT", [128, 4, 128], BF16)
        dsc = sb("dsc", [128, 64])
        gsel = sb("gsel", [128, 4, 16])
        glb = sb("glb", [128, 4, 16])
        gB = sb("gB", [128, 128])
        dtmp = sb("dtmp", [128, 128])
        dec = sb("dec", [128, 128])
        decT = sb("decT", [128, 128])
        Xf = sb("Xf", [128, 128])
        Xp = [sb("Xp%d" % i, [128, 128]) for i in range(2)]
        XpT = [sb("XpT%d" % i, [128, 128]) for i in range(2)]
        TT = sb("TT", [128, 128])
        Ru = sb("Ru", [128, 128])
        Rw = sb("Rw", [128, 128])
        u_sb = sb("u_sb", [128, 128])
        wT_b = sb("wT_b", [128, 128], BF16)
        qkTm = sb("qkTm", [128, 128], BF16)
        wq_sb = sb("wq_sb", [128, 2, 128])
        vnew = sb("vnew", [128, 128], BF16)
        t1s = sb("t1s", [128, 128])
        kdec = sb("kdec", [128, 128], BF16)
        oA = sb("oA", [128, 4, 128])
        oab = sb("oab", [128, 4, 128], BF16)
        Sd = [sb("Sd%d" % l, [128, 4, 128]) for l in range(2)]
        Sd_b = [sb("Sdb%d" % l, [128, 4, 128], BF16) for l in range(2)]
        sS_b = sb("sSb", [128, 16, 128], BF16)
        cinP = [[sb("cinP%d%d" % (l, r), [128, 8, 1]) for r in range(2)] for l in range(2)]
        cinS = [sb("cinS%d" % r, [128, 8, 16]) for r in range(2)]
        x0s = [sb("x0s%d" % r, [128, 8, 16]) for r in range(2)]
        xe = [sb("xe%d" % r, [128, 8, 16]) for r in range(2)]
        xep = [sb("xep%d" % r, [128, 8]) for r in range(2)]
        ybT = sb("ybT", [128, 2, 128])
        ybTb = sb("ybTb", [128, 2, 128], BF16)
        sgT = sb("sgT", [128, 128])
        v_h = sb("v_h", [128, 256], BF16)
        fT = sb("fT", [128, 2, 128])
        qtT = sb("qtT", [128, 2, 128], BF16)
        ktT = sb("ktT", [128, 2, 128], BF16)
        e1 = sb("e1", [128, 2, 128])
        khat = sb("khat", [128, 256], BF16)
        glh = sb("glh", [128, 2, 16])
        aTm = sb("aTm", [128, 4, 128], BF16)
        ocb = sb("ocb", [128, 256], BF16)
        Sh = [sb("Sh%d" % l, [128, 2, 64]) for l in range(2)]
        Sh_b = [sb("Shb%d" % l, [128, 2, 64], BF16) for l in range(2)]
        Shs_b = TB(sS_b[:].rearrange("p b v -> p (b v)").rearrange("p (c b v) -> p c b v", c=2, b=16), "Shsb", bufs=sS_b.bs)
        gT = sb("gT", [128, 22, 128], BF16)
        qkv_b = TB(gT[:, 0:12, :], "qkv_b", bufs=gT.bs)
        hb = TB(gT[:, 14:22, :].rearrange("p c t -> p (c t)"), "hb", bufs=gT.bs)
        sq = TB(gT[:, 12:20, :], "sq", bufs=gT.bs)
        def bufs_of(i0, i1):
            return [b for p_ in PZ[i0:i1] for b in p_.bs]
        sS = TB(ZB[1][:].rearrange("p (b v) -> p b v", b=16), "sS", bufs=bufs_of(4, 8))
        Shs = TB(ZB[1][:].rearrange("p (c b v) -> p c b v", c=2, b=16), "Shs", bufs=bufs_of(4, 8))
        scf = TB(ZB[1][:, 0:1408].rearrange("p (k q b t) -> p k q b t", k=11, q=4, b=16), "scf", bufs=bufs_of(4, 8))
        scq = TB(ZB[1][:, 1408:1984].rearrange("p (c b t) -> p c b t", c=12, b=16), "scq", bufs=bufs_of(4, 8))
        xre = TB(ZB[2][:, 0:1024].rearrange("p (j t) -> p j t", j=8), "xre", bufs=bufs_of(8, 10))
        xim = TB(ZB[2][:, 1024:2048].rearrange("p (j t) -> p j t", j=8), "xim", bufs=bufs_of(10, 12))
        yb0 = TB(ZB[3][:, 0:256], "yb0", bufs=PZ[12].bs)
        yb = TB(ZB[3][:, 256:512], "yb", bufs=PZ[12].bs)
        fto = TB(ZB[3][:, 512:768], "fto", bufs=PZ[13].bs)
        logf = TB(ZB[3][:, 768:1024], "logf", bufs=PZ[13].bs)
        omf = TB(ZB[3][:, 1024:1280], "omf", bufs=PZ[14].bs)
        b_sb = TB(ZB[3][:, 1280:1536], "b_sb", bufs=PZ[14].bs)
        oc = TB(ZB[3][:, 1536:1792], "oc", bufs=PZ[15].bs)
        hg_s = TB(ZB[3][:, 1792:2048], "hg_s", bufs=PZ[15].bs)
        gate_s = tD
        Wre = sb("Wre", [128, 1024], BF16)
        Wim = sb("Wim", [128, 1024], BF16)
        xbre = TB(Wre[:].rearrange("p (j t) -> p j t", j=8), "xbre", bufs=Wre.bs)
        xbim = TB(Wim[:].rearrange("p (j t) -> p j t", j=8), "xbim", bufs=Wim.bs)
        kdm = [sb("kdm%d" % i, [128, 128], BF16) for i in range(2)]
        khm = [sb("khm%d" % i, [128, 256], BF16) for i in range(2)]
        gsbh = [tA, tB_]
        yoh = [tC, tD]
        tmp1 = tD
        for l in range(2):
            for t_ in (cfc[l], cq[l], Sd[l], Sd_b[l], Sh[l], Sh_b[l], cinP[l][0], cinP[l][1]):
                V(lambda e, t_=t_: e.memset(t_[:], 0.0), [], [t_])

        class _RS:
            pass
        RS0 = _RS()
        RS0.dec, RS0.decT, RS0.TT, RS0.Ru, RS0.Rw, RS0.u_sb, RS0.t1s = dec, decT, TT, Ru, Rw, u_sb, t1s
        RS0.wT_b, RS0.qkTm, RS0.vnew, RS0.kdec, RS0.wq_sb, RS0.Xp, RS0.XpT = wT_b, qkTm, vnew, kdec, wq_sb, Xp, XpT
        RS0.pF, RS0.pU, RS0.pP = pF, pM[2], pM[5]
        RS1 = _RS()
        for n_ in ('dec', 'decT', 'TT', 'Ru', 'Rw', 'u_sb', 't1s'):
            setattr(RS1, n_, sb(n_ + "_1", [128, 128]))
        for n_ in ('wT_b', 'qkTm', 'vnew', 'kdec'):
            setattr(RS1, n_, sb(n_ + "_1", [128, 128], BF16))
        RS1.wq_sb = sb("wq_sb_1", [128, 2, 128])
        RS1.Xp = [sb("Xp1_%d" % i, [128, 128]) for i in range(2)]
        RS1.XpT = [sb("XpT1_%d" % i, [128, 128]) for i in range(2)]
        RS1.pF = [TB(pM[4][:, i * 128:(i + 1) * 128], "pF1_%d" % i, bufs=pM[4].bs) for i in range(4)]
        RS1.pU, RS1.pP = pM[0], pM[1]

        FB = [(sb("xpf_%d" % i, [128, 4, 160], BF16), sb("accf_%d" % i, [128, 4, 128], BF16),
               sb("acc2f_%d" % i, [128, 4, 128], BF16), sb("saf_%d" % i, [128, 2, 128], BF16)) for i in range(2)]
        cwb = [sb("cwb%d" % l, [128, 132], BF16) for l in range(2)]
        for l in range(2):
            vcopy(cwb[l][:], ppt[l][:, PP['fcw'][0]:PP['fcw'][0] + 132], [ppt[l]], [cwb[l]])

        def norm_stats():
            V(lambda e: e.memset(st[:, 0:1], 0.0), [], [st])
            act(hb[:], h[:], AF.Square, [h], [hb, st], accum=st[:, 0:1])
            vts(st[:, 1:2], st[:, 0:1], 1.0 / 1024, ALU.mult, [st], [st], EPS, ALU.add)
            act(st[:, 2:3], st[:, 1:2], AF.Sqrt, [st], [st])
            V(lambda e: e.reciprocal(out=st[:, 3:4], in_=st[:, 2:3]), [st], [st])

        def norm_T(gain_ap, gsrc):
            norm_stats()
            vts(hb[:], h[:], st[:, 3:4], ALU.mult, [h, st], [hb])
            for c in range(8):
                tr(pT[:, c, :], hb[:, c * 128:(c + 1) * 128], identb, [hb, cstb], [pT])
            vtt(hnT[:], pT[:], gain_ap.unsqueeze(2).to_broadcast([128, 8, 128]), ALU.mult, [pT, gsrc], [hnT])

        def wslot(kc, n):
            slot = ring[ringi[0] % NRING]
            ringi[0] += 1
            return slot, slot[:, 0:kc * n].rearrange("p (k n) -> p k n", k=kc)

        def wload(key):
            d3, tb, kc, n = WB[key]
            slot, wv = wslot(kc, n)
            S.dma('pool', lambda e: e.dma_start(out=wv, in_=d3), reads=[tb], writes=[slot])
            return slot, wv

        def rsqrt_small(dst, src, rr, ww, mul, add):
            vts(dst, src, mul, ALU.mult, rr, ww, add, ALU.add)
            act(dst, dst, AF.Sqrt, ww, ww)
            V(lambda e: e.reciprocal(out=dst, in_=dst), ww, ww)

        def in_proj(l, samp):
            norm_T(pp(l, 'nmix'), ppt[l])
            W = w_in[l]
            if samp:
                dma('pool', scq[:], s_cq[l], w=[scq])
                xq4 = xpq[:].rearrange("p c (b t) -> p c b t", b=16)
                vcopy(xq4[:, :, :, 0:3], scq[:], [scq], [xpq])
            else:
                vcopy(xpq[:, :, 0:3], cq[l][:], [cq[l]], [xpq])
            for g in range(3):
                slot, wv = wload(('in', l, g))
                pm = pM[g % 2]
                pm3 = pm[:].rearrange("p (q t) -> p q t", q=4)
                for q in range(4):
                    for kc in range(8):
                        mm(pm3[:, q, :], wv[:, kc, q * 128:(q + 1) * 128], hnT[:, kc, :], [slot, hnT], [pm], start=(kc == 0), stop=(kc == 7))
                if samp:
                    acopy(xq4[:, 4 * g:4 * g + 4, :, 3:11], pm[:].rearrange("p (q b t) -> p q b t", q=4, b=16), [pm], [xpq])
                else:
                    acopy(xpq[:, 4 * g:4 * g + 4, 3:131], pm3, [pm], [xpq])
            slot, wv = wload(('in', l, 3))
            for kc in range(8):
                mm(pM[2][:], hnT[:, kc, :], wv[:, kc, 0:512], [hnT, slot], [pM[2]], start=(kc == 0), stop=(kc == 7))
            for kc in range(8):
                mm(pM[3][:, 0:8], hnT[:, kc, :], wv[:, kc, 512:520], [hnT, slot], [pM[3]], start=(kc == 0), stop=(kc == 7))
            act(gate_s[:], pM[2][:], AF.Silu, [pM[2]], [gate_s])
            vcopy(ba[:], pM[3][:, 0:8], [pM[3]], [ba])
        def in_proj_rest(l, samp):
            W = w_in[l]
            slot, wv = wload(('in', l, 4))
            pm = pM[0]
            pm3 = pm[:].rearrange("p (q t) -> p q t", q=4)
            for q in range(4):
                for kc in range(8):
                    mm(pm3[:, q, :], wv[:, kc, q * 128:(q + 1) * 128], hnT[:, kc, :], [slot, hnT], [pm], start=(kc == 0), stop=(kc == 7))
            vcopy(uT[:], pm3[:, 0:2, :], [pm], [uT])
            yield
            act(hqTs[:], pm3[:, 2:4, :], AF.Silu, [pm], [hqTs])
            yield
            slot, wv = wload(('in', l, 5))
            for kc in range(8):
                mm(pM[4][:], hnT[:, kc, :], wv[:, kc, :], [hnT, slot], [pM[4]], start=(kc == 0), stop=(kc == 7))
            p53 = pM[5][:, 0:256].rearrange("p (q t) -> p q t", q=2)
            for q in range(2):
                for kc in range(8):
                    mm(p53[:, q, :], wv[:, kc, q * 128:(q + 1) * 128], hnT[:, kc, :], [slot, hnT], [pM[5]], start=(kc == 0), stop=(kc == 7))
            act(fto[:], pM[4][:, 0:256], AF.Sigmoid, [pM[4]], [fto])
            yield
            acopy(v_h[:], pM[4][:, 256:512], [pM[4]], [v_h])
            yield
            act(fT[:], p53, AF.Sigmoid, [pM[5]], [fT])
            yield
            slot, wv = wload(('in', l, 6))
            for kc in range(8):
                mm(pM[3][:, 0:256], hnT[:, kc, :], wv[:, kc, :], [hnT, slot], [pM[3]], start=(kc == 0), stop=(kc == 7))
            act(hg_s[:], pM[3][:, 0:256], AF.Silu, [pM[3]], [hg_s])
            yield


        def delta(l, samp, last):
            kind = 'S' if samp else 'P'
            nb = 16 if samp else 1
            bs = 128 // nb
            nlev = 3 if samp else 7
            cTf, cTb = C('cT' + kind), C('cT' + kind, True)
            stf = C('st' + kind)
            smf = C('sm' + kind)
            self_ = C('sel' + kind)
            dcw = ppt[l][:, PP['dcw'][0]:PP['dcw'][0] + 48].rearrange("p (c j) -> p c j", c=12)
            for g in range(3):
                if samp:
                    xq4 = xpq[:].rearrange("p c (b t) -> p c b t", b=16)
                    xs = [xq4[:, 4 * g:4 * g + 4, :, j:j + 8] for j in range(4)]
                    ws = [dcw[:, 4 * g:4 * g + 4, j:j + 1].unsqueeze(3).to_broadcast([128, 4, 16, 8]) for j in range(4)]
                    av = acc[:].rearrange("p q (b t) -> p q b t", b=16)
                    a2v = acc2[:].rearrange("p q (b t) -> p q b t", b=16)
                else:
                    xs = [xpq[:, 4 * g:4 * g + 4, j:j + 128] for j in range(4)]
                    ws = [dcw[:, 4 * g:4 * g + 4, j:j + 1].to_broadcast([128, 4, 128]) for j in range(4)]
                    av, a2v = acc[:], acc2[:]
                vtt(av, xs[0], ws[0], ALU.mult, [xpq, ppt[l]], [acc])
                yield
                for j in range(1, 4):
                    vtt(a2v, xs[j], ws[j], ALU.mult, [xpq, ppt[l]], [acc2])
                    yield
                    vtt(av, av, a2v, ALU.add, [acc, acc2], [acc])
                    yield
                act(qkv_b[:, 4 * g:4 * g + 4, :], acc[:], AF.Silu, [acc], [qkv_b])
                yield
            if samp:
                xq4 = xpq[:].rearrange("p c (b t) -> p c b t", b=16)
                vcopy(scq[:], xq4[:, :, :, 8:11], [xpq], [scq])
                yield
                dma('pool', o_scq[l], scq[:], r=[scq])
            else:
                vcopy(cq[l][:], xpq[:, :, 128:131], [xpq], [cq[l]])
                yield
                if last:
                    dma('pool', o_pcq[l], cq[l][:], r=[cq[l]])
            vtt(sq[:], qkv_b[:, 0:8, :], qkv_b[:, 0:8, :], ALU.mult, [qkv_b], [sq])
            yield
            for c in range(8):
                mm(pM[1][:, c:c + 1], sq[:, c, :], onesb[:, 0:1], [sq, cstb], [pM[1]])
            rsqrt_small(dsc[:, 0:8], pM[1][:, 0:8], [pM[1]], [dsc], 1.0, EPS)
            vts(dsc[:, 0:4], dsc[:, 0:4], 128.0 ** -0.5, ALU.mult, [dsc], [dsc])
            yield
            for hh in range(4):
                tr(pT[:, hh, :], qkv_b[:, 4 + hh, :], identb, [qkv_b, cstb], [pT])
                tr(pT[:, 4 + hh, :], qkv_b[:, 8 + hh, :], identb, [qkv_b, cstb], [pT])
            vtt(k_tok[:], pT[:, 0:4, :], dsc[:, 4:8].unsqueeze(2).to_broadcast([128, 4, 128]), ALU.mult, [pT, dsc], [k_tok])
            yield
            acopy(v_tok[:], pT[:, 4:8, :], [pT], [v_tok])
            yield
            for hh in range(4):
                tr(pT[:, hh, :], k_tok[:, hh, :], identb, [k_tok, cstb], [pT])
            acopy(knT[:], pT[:, 0:4, :], [pT], [knT])
            yield
            act(dsc[:, 8:12], ba[:, 0:4], AF.Sigmoid, [ba], [dsc])
            yield
            vts(dsc[:, 12:16], dsc[:, 8:12], -1.0, ALU.mult, [dsc], [dsc])
            yield
            vtt(dsc[:, 40:44], ba[:, 4:8], bcv(l, 'dtb'), ALU.add, [ba, bcs[l]], [dsc])
            yield
            act(dsc[:, 40:44], dsc[:, 40:44], AF.Exp, [dsc], [dsc])
            yield
            act(dsc[:, 40:44], dsc[:, 40:44], AF.Ln, [dsc], [dsc], bias=1.0)
            yield
            vtt(dsc[:, 16:20], dsc[:, 40:44], negA[l][:], ALU.mult, [dsc, negA[l]], [dsc])
            yield
            mm(pM[1][:, 8:12], cTf, dsc[:, 16:20], [cst, dsc], [pM[1]])
            mm(pM[1][:, 12:16], smf, dsc[:, 16:20], [cst, dsc], [pM[1]])
            vcopy(dsc[:, 20:24], pM[1][:, 8:12], [pM[1]], [dsc])
            yield
            vtt(dsc[:, 40:44], pM[1][:, 12:16], dsc[:, 20:24], ALU.subtract, [pM[1], dsc], [dsc])
            yield
            act(dsc[:, 24:28], dsc[:, 40:44], AF.Exp, [dsc], [dsc])
            yield
            act(dsc[:, 28:32], dsc[:, 20:24], AF.Exp, [dsc], [dsc])
            yield
            vtt(dsc[:, 32:36], dsc[:, 8:12], dsc[:, 28:32], ALU.mult, [dsc], [dsc])
            yield
            vtt(dsc[:, 36:40], dsc[:, 28:32], dsc[:, 0:4], ALU.mult, [dsc], [dsc])
            yield
            for hh in range(4):
                vts(gsel[:, hh, 0:nb], self_[:, 0:nb], dsc[:, 16 + hh:17 + hh], ALU.mult, [cst, dsc], [gsel])
                yield
            for hh in range(4):
                mm(pM[1][:, 16 + 16 * hh:16 + 16 * hh + nb], onesf, gsel[:, hh, 0:nb], [cst, gsel], [pM[1]])
            act(glb[:, :, 0:nb], pM[1][:, 16:80].rearrange("p (h b) -> p h b", h=4)[:, :, 0:nb], AF.Exp, [pM[1]], [glb])
            yield

            def head_body(hh, R):
                vts(gB[:], onesf, dsc[:, 16 + hh:17 + hh], ALU.mult, [cst, dsc], [gB])
                mm(R.pF[0][:], gB[:], cTf, [gB, cst], [R.pF[0]])
                vts(dtmp[:], R.pF[0][:], dsc[:, 20 + hh:21 + hh], ALU.subtract, [R.pF[0], dsc], [dtmp], 0.0, ALU.max)
                act(R.dec[:], dtmp[:], AF.Exp, [dtmp], [R.dec], scale=-1.0)
                yield
                vts(dtmp[:], R.pF[0][:], dsc[:, 20 + hh:21 + hh], ALU.subtract, [R.pF[0], dsc], [dtmp], 0.0, ALU.min)
                act(R.decT[:], dtmp[:], AF.Exp, [dtmp], [R.decT])
                yield
                vtt(R.decT[:], R.decT[:], cTf, ALU.mult, [R.decT, cst], [R.decT])
                mm(R.pF[1][:], knT[:, hh, :], knT[:, hh, :], [knT], [R.pF[1]])
                vstt(Xf[:], R.pF[1][:], dsc[:, 12 + hh:13 + hh], R.dec[:], ALU.mult, ALU.mult, [R.pF[1], dsc, R.dec], [Xf])
                vtt(R.Xp[0][:], Xf[:], stf, ALU.mult, [Xf, cst], [R.Xp[0]])
                yield
                tr(R.pF[3][:], R.Xp[0][:], identf, [R.Xp[0], cst], [R.pF[3]])
                acopy(R.XpT[0][:], R.pF[3][:], [R.pF[3]], [R.XpT[0]])
                yield
                vtt(R.TT[:], R.XpT[0][:], identf, ALU.add, [R.XpT[0], cst], [R.TT])
                yield
                cur = 0
                for lev in range(1, nlev):
                    nxt = 1 - cur
                    lastlev = (lev == nlev - 1)
                    mm(R.pF[2][:], R.XpT[cur][:], R.Xp[cur][:], [R.XpT[cur], R.Xp[cur]], [R.pF[2]])
                    vcopy(R.Xp[nxt][:], R.pF[2][:], [R.pF[2]], [R.Xp[nxt]])
                    yield
                    if not lastlev:
                        mm(R.pF[3][:], R.Xp[cur][:], R.XpT[cur][:], [R.XpT[cur], R.Xp[cur]], [R.pF[3]])
                        acopy(R.XpT[nxt][:], R.pF[3][:], [R.pF[3]], [R.XpT[nxt]])
                        yield
                    mm(R.pF[1][:], R.Xp[nxt][:], R.TT[:], [R.Xp[nxt], R.TT], [R.pF[1]])
                    vtt(R.TT[:], R.TT[:], R.pF[1][:], ALU.add, [R.TT, R.pF[1]], [R.TT])
                    yield
                    cur = nxt
                vts(R.Ru[:], v_tok[:, hh, :], dsc[:, 8 + hh:9 + hh], ALU.mult, [v_tok, dsc], [R.Ru])
                vts(R.Rw[:], k_tok[:, hh, :], dsc[:, 32 + hh:33 + hh], ALU.mult, [k_tok, dsc], [R.Rw])
                mm(R.pU[:, 0:128], R.TT[:], R.Ru[:], [R.TT, R.Ru], [R.pU])
                mm(R.pU[:, 128:256], R.Rw[:], R.TT[:], [R.TT, R.Rw], [R.pU])
                acopy(R.u_sb[:], R.pU[:, 0:128], [R.pU], [R.u_sb])
                yield
                acopy(R.wT_b[:], R.pU[:, 128:256], [R.pU], [R.wT_b])
                yield
                mm(R.pF[2][:], knT[:, hh, :], qkv_b[:, hh, :], [knT, qkv_b], [R.pF[2]])
                vtt(R.qkTm[:], R.pF[2][:], R.decT[:], ALU.mult, [R.pF[2], R.decT], [R.qkTm])
                yield
                if samp:
                    dma('pool', sS[:], s_dl[l, :, hh].rearrange("b k v -> k b v"), w=[sS])
                    acopy(sS_b[:], sS[:], [sS], [sS_b])
                    yield
                p33 = R.pP[:, 256:512].rearrange("p (q t) -> p q t", q=2)
                for b in range(nb):
                    Sb = sS_b[:, b, :] if samp else Sd_b[l][:, hh, :]
                    Sbt = sS_b if samp else Sd_b[l]
                    mm(p33[:, 0, b * bs:(b + 1) * bs], Sb, R.wT_b[:, b * bs:(b + 1) * bs], [Sbt, R.wT_b], [R.pP])
                    mm(p33[:, 1, b * bs:(b + 1) * bs], Sb, qkv_b[:, hh, b * bs:(b + 1) * bs], [Sbt, qkv_b], [R.pP])
                acopy(R.wq_sb[:], p33, [R.pP], [R.wq_sb])
                yield
                tr(R.pF[0][:], R.wq_sb[:, 0, :], identf, [R.wq_sb, cst], [R.pF[0]])
                tr(R.pF[1][:], R.wq_sb[:, 1, :], identf, [R.wq_sb, cst], [R.pF[1]])
                vtt(R.vnew[:], R.u_sb[:], R.pF[0][:], ALU.subtract, [R.u_sb, R.pF[0]], [R.vnew])
                yield
                act(R.t1s[:], R.pF[1][:], AF.Identity, [R.pF[1], dsc], [R.t1s], scale=dsc[:, 36 + hh:37 + hh])
                yield
                mm(R.pF[2][:], R.qkTm[:], R.vnew[:], [R.qkTm, R.vnew], [R.pF[2]])
                vstt(oA[:, hh, :], R.pF[2][:], dsc[:, hh:hh + 1], R.t1s[:], ALU.mult, ALU.add, [R.pF[2], dsc, R.t1s], [oA])
                yield
                vts(R.kdec[:], k_tok[:, hh, :], dsc[:, 24 + hh:25 + hh], ALU.mult, [k_tok, dsc], [R.kdec])
                if samp:
                    for b in range(nb):
                        pd = R.pF[b % 2]
                        km = kdm[b % 2]
                        vts(km[:], R.kdec[:], self_[:, b:b + 1], ALU.mult, [R.kdec, cst], [km])
                        mm(pd[:], km[:], R.vnew[:], [km, R.vnew], [pd])
                        vstt(sS[:, b, :], sS[:, b, :], glb[:, hh, b:b + 1], pd[:], ALU.mult, ALU.add, [sS, glb, pd], [sS])
                        yield
                    dma('pool', o_sdl[l, :, hh].rearrange("b k v -> k b v"), sS[:], r=[sS])
                else:
                    mm(R.pF[0][:], R.kdec[:], R.vnew[:], [R.kdec, R.vnew], [R.pF[0]])
                    vstt(Sd[l][:, hh, :], Sd[l][:, hh, :], glb[:, hh, 0:1], R.pF[0][:], ALU.mult, ALU.add, [Sd[l], glb, R.pF[0]], [Sd[l]])
                    yield
                    acopy(Sd_b[l][:, hh, :], Sd[l][:, hh, :], [Sd[l]], [Sd_b[l]])
                    yield
                    if last:
                        dma('pool', o_pdl[l, hh], Sd[l][:, hh, :], r=[Sd[l]])
            def run_heads(gens):
                gens = list(gens)
                while gens:
                    for g_ in list(gens):
                        try:
                            next(g_)
                        except StopIteration:
                            gens.remove(g_)
            if samp:
                for hh in range(4):
                    run_heads([head_body(hh, RS0)])
            else:
                run_heads([head_body(0, RS0), head_body(1, RS1)])
                run_heads([head_body(2, RS0), head_body(3, RS1)])
            V(lambda e: e.memset(dsc[:, 44:48], 0.0), [], [dsc])
            for hh in range(4):
                act(dtmp[:], oA[:, hh, :], AF.Square, [oA], [dtmp, dsc], accum=dsc[:, 44 + hh:45 + hh])
            rsqrt_small(dsc[:, 44:48], dsc[:, 44:48], [dsc], [dsc], 1.0 / 128, EPS)
            for hh in range(4):
                vstt(oA[:, hh, :], oA[:, hh, :], dsc[:, 44 + hh:45 + hh], bcv(l, 'dnn'), ALU.mult, ALU.mult, [oA, dsc, bcs[l]], [oA])
            vtt(oab[:], oA[:], gate_s[:].rearrange("p (h d) -> p h d", h=4), ALU.mult, [oA, gate_s], [oab])
            for hh in range(4):
                tr(pT[:, hh, :], oab[:, hh, :], identb, [oab, cstb], [pT])
            acopy(ocatT[:, 0:4, :], pT[:, 0:4, :], [pT], [ocatT])
        def s5(l, samp, last):
            kind = 'S' if samp else 'P'
            nb = 16 if samp else 1
            bs = 128 // nb
            cTb = C('cT' + kind, True)
            Lmv = Lm[l]
            if samp:
                for r in range(2):
                    dma('pool', x0s[r][:], s_ss[l, r], w=[x0s[r]])
                for (dst, a_, b_, op) in ((cinS[0], 0, 1, ALU.subtract), (cinS[1], 1, 0, ALU.add)):
                    vtt(xe[0][:], x0s[0][:], lbar[l][a_][:].unsqueeze(2).to_broadcast([128, 8, 16]), ALU.mult, [x0s[0], lbar[l][a_]], [xe[0]])
                    yield
                    vtt(xe[1][:], x0s[1][:], lbar[l][b_][:].unsqueeze(2).to_broadcast([128, 8, 16]), ALU.mult, [x0s[1], lbar[l][b_]], [xe[1]])
                    yield
                    vtt(dst[:], xe[0][:], xe[1][:], op, [xe[0], xe[1]], [dst])
                    yield
                cin = cinS
            else:
                cin = cinP[l]
            for cc in range(2):
                hs = slice(cc * 512, (cc + 1) * 512)
                mm(pM[0][:], uT[:, cc, :], bft[l][0][:, cc, :], [uT, bft[l][0]], [pM[0]])
                mm(pM[1][:], uT[:, cc, :], bft[l][1][:, cc, :], [uT, bft[l][1]], [pM[1]])
                vtt(tA[:, 0:512], pM[0][:], Lmv[0][:, hs], ALU.mult, [pM[0], Lmv[0]], [tA])
                yield
                vtt(tB_[:, 0:512], pM[1][:], Lmv[1][:, hs], ALU.mult, [pM[1], Lmv[1]], [tB_])
                yield
                vtt(Wre[:, hs], tA[:, 0:512], tB_[:, 0:512], ALU.subtract, [tA, tB_], [Wre])
                yield
                vtt(tA[:, 0:512], pM[1][:], Lmv[0][:, hs], ALU.mult, [pM[1], Lmv[0]], [tA])
                yield
                vtt(tB_[:, 0:512], pM[0][:], Lmv[1][:, hs], ALU.mult, [pM[0], Lmv[1]], [tB_])
                yield
                vtt(Wim[:, hs], tA[:, 0:512], tB_[:, 0:512], ALU.add, [tA, tB_], [Wim])
                yield
                for jj in range(4):
                    j = 4 * cc + jj
                    mm(pM[2][:, jj * 128:(jj + 1) * 128], Wre[:, j * 128:(j + 1) * 128], cTb, [Wre, cstb], [pM[2]])
                    mm(pM[4][:, jj * 128:(jj + 1) * 128], Wim[:, j * 128:(j + 1) * 128], cTb, [Wim, cstb], [pM[4]])
                js = slice(4 * cc, 4 * cc + 4)

                def v4(ap):
                    return ap.rearrange("p (j b t) -> p j b t", j=4, b=nb)

                def v4b(ap3):
                    return ap3.rearrange("p j (b t) -> p j b t", b=nb)
                ar, ai = tC, tD
                vtt(v4(ar[:, 0:512]), v4(pM[2][:]), cin[0][:, js, :].unsqueeze(3).to_broadcast([128, 4, nb, bs]), ALU.add, [pM[2], cin[0]], [ar])
                yield
                vtt(v4(ai[:, 0:512]), v4(pM[4][:]), cin[1][:, js, :].unsqueeze(3).to_broadcast([128, 4, nb, bs]), ALU.add, [pM[4], cin[1]], [ai])
                yield
                if samp:
                    Lr = LpT[l][0][:, js, 0:8].unsqueeze(2).to_broadcast([128, 4, 16, 8])
                    Li = LpT[l][1][:, js, 0:8].unsqueeze(2).to_broadcast([128, 4, 16, 8])
                else:
                    Lr = v4b(LpT[l][0][:, js, :])
                    Li = v4b(LpT[l][1][:, js, :])
                vtt(v4(tA[:, 0:512]), v4(ar[:, 0:512]), Lr, ALU.mult, [ar, LpT[l][0]], [tA])
                yield
                vtt(v4(tB_[:, 0:512]), v4(ai[:, 0:512]), Li, ALU.mult, [ai, LpT[l][1]], [tB_])
                yield
                vtt(v4b(xre[:, js, :]), v4(tA[:, 0:512]), v4(tB_[:, 0:512]), ALU.subtract, [tA, tB_], [xre])
                yield
                vtt(v4(tA[:, 0:512]), v4(ar[:, 0:512]), Li, ALU.mult, [ar, LpT[l][1]], [tA])
                yield
                vtt(v4(tB_[:, 0:512]), v4(ai[:, 0:512]), Lr, ALU.mult, [ai, LpT[l][0]], [tB_])
                yield
                vtt(v4b(xim[:, js, :]), v4(tA[:, 0:512]), v4(tB_[:, 0:512]), ALU.add, [tA, tB_], [xim])
                yield
            acopy(xbre[:], xre[:], [xre], [xbre])
            yield
            acopy(xbim[:], xim[:], [xim], [xbim])
            yield
            xr4 = xre[:].rearrange("p j (b t) -> p j b t", b=nb)
            xi4 = xim[:].rearrange("p j (b t) -> p j b t", b=nb)
            vcopy(xe[0][:, :, 0:nb], xr4[:, :, :, bs - 1], [xre], [xe[0]])
            yield
            vcopy(xe[1][:, :, 0:nb], xi4[:, :, :, bs - 1], [xim], [xe[1]])
            yield
            if samp:
                for r in range(2):
                    dma('pool', o_sss[l, r], xe[r][:], r=[xe[r]])
            else:
                for r in range(2):
                    vcopy(xep[r][:], xe[r][:, :, 0], [xe[r]], [xep[r]])
                    yield
                if last:
                    for r in range(2):
                        dma('pool', o_pss[l, r], xep[r][:], r=[xep[r]])
                vtt(tA[:, 0:8], xep[0][:], lbar[l][0][:], ALU.mult, [xep[0], lbar[l][0]], [tA])
                yield
                vtt(tB_[:, 0:8], xep[1][:], lbar[l][1][:], ALU.mult, [xep[1], lbar[l][1]], [tB_])
                yield
                vtt(cinP[l][0][:, :, 0], tA[:, 0:8], tB_[:, 0:8], ALU.subtract, [tA, tB_], [cinP[l][0]])
                yield
                vtt(tA[:, 0:8], xep[0][:], lbar[l][1][:], ALU.mult, [xep[0], lbar[l][1]], [tA])
                yield
                vtt(tB_[:, 0:8], xep[1][:], lbar[l][0][:], ALU.mult, [xep[1], lbar[l][0]], [tB_])
                yield
                vtt(cinP[l][1][:, :, 0], tA[:, 0:8], tB_[:, 0:8], ALU.add, [tA, tB_], [cinP[l][1]])
                yield
            py = pM[0]
            for cc in range(2):
                mm(py[:, cc * 128:(cc + 1) * 128], uT[:, cc, :], diagD[l][:, cc, :], [uT, diagD[l]], [py], start=True, stop=False)
                for jj in range(4):
                    j = 4 * cc + jj
                    mm(py[:, j * 32:(j + 1) * 32], xbre[:, j, :], cft[l][0][:, j, :], [xbre, cft[l][0]], [py], start=False, stop=False)
                    mm(py[:, j * 32:(j + 1) * 32], xbim[:, j, :], cft[l][1][:, j, :], [xbim, cft[l][1]], [py], start=False, stop=(jj == 3))
            acopy(yb0[:], py[:, 0:256], [py], [yb0])
            yield
            vtt(yb[:], yb0[:], yb0[:], ALU.mult, [yb0], [yb])
            yield
            vts(yb[:], yb[:], 0.044715, ALU.mult, [yb], [yb], 1.0, ALU.add)
            yield
            vtt(yb[:], yb[:], yb0[:], ALU.mult, [yb, yb0], [yb])
            yield
            act(yb[:], yb[:], AF.Tanh, [yb], [yb], scale=0.7978845608028654)
            yield
            vstt(yb[:], yb[:], 1.0, yb0[:], ALU.add, ALU.mult, [yb, yb0], [yb])
            yield
            vts(yb[:], yb[:], 0.5, ALU.mult, [yb], [yb])
            yield
            for cc in range(2):
                tr(pF[2 + cc][:], yb[:, cc * 128:(cc + 1) * 128], identf, [yb, cst], [pF[2 + cc]])
                acopy(ybT[:, cc, :], pF[2 + cc][:], [pF[2 + cc]], [ybT])
                yield
            vcopy(ybTb[:], ybT[:], [ybT], [ybTb])
            yield
            for c2 in range(2):
                for cc in range(2):
                    mm(pF[2 + c2][:], glw[l][:, cc, c2 * 128:(c2 + 1) * 128], ybTb[:, cc, :], [glw[l], ybTb], [pF[2 + c2]], start=(cc == 0), stop=(cc == 1))
                act(sgT[:], pF[2 + c2][:], AF.Sigmoid, [pF[2 + c2], ppt[l]], [sgT], bias=pp(l, 'glub', c2, c2 + 1))
                yield
                vtt(ocatT[:, 4 + c2, :], ybT[:, c2, :], sgT[:], ALU.mult, [ybT, sgT], [ocatT])
                yield

        def hgrn(l, samp, last):
            kind = 'S' if samp else 'H'
            nb = 16 if samp else 4
            bs = 128 // nb
            cTf, cTb = C('cT' + kind), C('cT' + kind, True)
            smf = C('sm' + kind)
            self_ = C('sel' + kind)
            pTf = pT[:].rearrange("p c t -> p (c t)").bitcast(F32)
            pTfb = pT
            vtt(fto[:], fto[:], omlb[l][:], ALU.mult, [fto, omlb[l]], [fto])
            yield
            vtt(fto[:], fto[:], lbb[l][:], ALU.add, [fto, lbb[l]], [fto])
            yield
            act(logf[:], fto[:], AF.Ln, [fto], [logf])
            yield
            vts(omf[:], fto[:], -1.0, ALU.mult, [fto], [omf], 1.0, ALU.add)
            yield
            for cc in range(2):
                vts(fT[:, cc, :], fT[:, cc, :], omlf[l][:, cc:cc + 1], ALU.mult, [fT, omlf[l], lbf[l]], [fT], lbf[l][:, cc:cc + 1], ALU.add)
                yield
            vts(fT[:], fT[:], -1.0, ALU.mult, [fT], [fT], 1.0, ALU.add)
            yield
            mm(pM[5][:, 0:256], cTf, logf[:], [cst, logf], [pM[5]])
            mm(pM[5][:, 256:512], smf, logf[:], [cst, logf], [pM[5]])
            pbT = pTf[:, 0:256].rearrange("p (c t) -> p c t", c=2)
            for cc in range(2):
                mm(pbT[:, cc, :], logf[:, cc * 128:(cc + 1) * 128], cTf, [logf, cst], [pTfb])
                mm(pTf[:, 256 + 16 * cc:256 + 16 * cc + nb], logf[:, cc * 128:(cc + 1) * 128], self_[:, 0:nb], [logf, cst], [pTfb])
            act(e1[:], pbT, AF.Exp, [pTfb], [e1])
            yield
            vtt(qtT[:], hqTs[:], e1[:], ALU.mult, [hqTs, e1], [qtT])
            yield
            act(e1[:], pbT, AF.Exp, [pTfb], [e1], scale=-1.0)
            yield
            vtt(ktT[:], fT[:], e1[:], ALU.mult, [fT, e1], [ktT])
            yield
            act(glh[:, :, 0:nb], pTf[:, 256:288].rearrange("p (c b) -> p c b", c=2)[:, :, 0:nb], AF.Exp, [pTfb], [glh])
            yield
            acopy(b_sb[:], pM[5][:, 0:256], [pM[5]], [b_sb])
            yield
            vtt(b_sb[:], pM[5][:, 256:512], b_sb[:], ALU.subtract, [pM[5], b_sb], [b_sb])
            yield
            act(b_sb[:], b_sb[:], AF.Exp, [b_sb], [b_sb])
            yield
            vtt(khat[:], omf[:], b_sb[:], ALU.mult, [omf, b_sb], [khat])
            yield
            pa = pTf[:, 0:512].rearrange("p (h t) -> p h t", h=4)
            for hh in range(4):
                hl, hc = hh % 2, hh // 2
                mm(pa[:, hh, :], ktT[hl * 64:(hl + 1) * 64, hc, :], qtT[hl * 64:(hl + 1) * 64, hc, :], [ktT, qtT], [pTfb], sync=True)
            vtt(aTm[:], pa, cTf.unsqueeze(1).to_broadcast([128, 4, 128]), ALU.mult, [pTfb, cst], [aTm])
            yield
            po = pM[3][0:64, :].rearrange("p (h t) -> p h t", h=4)
            for hh in range(4):
                mm(po[:, hh, :], v_h[:, hh * 64:(hh + 1) * 64], aTm[:, hh, :], [v_h, aTm], [pM[3]], start=(hh == 0), stop=False, skip=True, sync=True)
            if samp:
                for hc_ in range(2):
                    dma('pool', Shs[:, hc_], s_hg[l][:, 2 * hc_:2 * hc_ + 2].rearrange("b hl k v -> (hl k) b v"), w=[Shs])
                acopy(Shs_b[:], Shs[:], [Shs], [Shs_b])
                yield
            for j in range(nb):
                for hh in range(4):
                    hl, hc = hh % 2, hh // 2
                    if samp:
                        Sb, Sbt = Shs_b[hl * 64:(hl + 1) * 64, hc, j, :], Shs_b
                    else:
                        Sb, Sbt = Sh_b[l][hl * 64:(hl + 1) * 64, hc, :], Sh_b[l]
                    mm(po[:, hh, j * bs:(j + 1) * bs], Sb, qtT[hl * 64:(hl + 1) * 64, hc, j * bs:(j + 1) * bs], [Sbt, qtT], [pM[3]], start=False, stop=True, skip=True, sync=True)
                kh = khm[j % 2]
                vts(kh[:], khat[:], self_[:, j:j + 1], ALU.mult, [khat, cst], [kh])
                yield
                for cc in range(2):
                    pd = pF[cc]
                    mm(pd[:], kh[:, cc * 128:(cc + 1) * 128], v_h[:, cc * 128:(cc + 1) * 128], [kh, v_h], [pd], sync=(cc == 0))
                    for hl in range(2):
                        ps_ = slice(hl * 64, (hl + 1) * 64)
                        if samp:
                            vstt(Shs[ps_, cc, j, :], Shs[ps_, cc, j, :], glh[ps_, cc, j:j + 1], pd[ps_, hl * 64:(hl + 1) * 64], ALU.mult, ALU.add, [Shs, glh, pd], [Shs])
                            yield
                        else:
                            vstt(Sh[l][ps_, cc, :], Sh[l][ps_, cc, :], glh[ps_, cc, j:j + 1], pd[ps_, hl * 64:(hl + 1) * 64], ALU.mult, ALU.add, [Sh[l], glh, pd], [Sh[l]])
                            yield
                if not samp:
                    acopy(Sh_b[l][:], Sh[l][:], [Sh[l]], [Sh_b[l]])
                    yield
            if samp:
                dma('pool', o_shg[l], Shs[:], r=[Shs])
            elif last:
                dma('pool', o_phg[l], Sh[l][:], r=[Sh[l]])
            acopy(oT_sb[:], po, [pM[3]], [oT_sb])
            yield
            poc = pM[5]
            for hh in range(4):
                tr(poc[:, hh * 64:(hh + 1) * 64], oT_sb[:, hh, :], identf[0:64, 0:64], [oT_sb, cst], [poc])
            V(lambda e: e.memset(dsc[:, 48:52], 0.0), [], [dsc])
            for hh in range(4):
                act(dtmp[:, 0:64], poc[:, hh * 64:(hh + 1) * 64], AF.Square, [poc], [dtmp, dsc], accum=dsc[:, 48 + hh:49 + hh])
                yield
            rsqrt_small(dsc[:, 48:52], dsc[:, 48:52], [dsc], [dsc], 1.0 / 64, EPS)
            for hh in range(4):
                vstt(oc[:, hh * 64:(hh + 1) * 64], poc[:, hh * 64:(hh + 1) * 64], dsc[:, 48 + hh:49 + hh], bcv(l, 'hgn'), ALU.mult, ALU.mult, [poc, dsc, bcs[l]], [oc])
                yield
            vtt(ocb[:], oc[:], hg_s[:], ALU.mult, [oc, hg_s], [ocb])
            yield
            for cc in range(2):
                tr(pT[:, cc, :], ocb[:, cc * 128:(cc + 1) * 128], identb, [ocb, cstb], [pT])
            acopy(ocatT[:, 6:8, :], pT[:, 0:2, :], [pT], [ocatT])
            yield

        def out_proj(l):
            for half in range(2):
                slot, wv = wload(('out', l, half))
                pm = pM[4 + half]
                for kc in range(8):
                    mm(pm[:], ocatT[:, kc, :], wv[:, kc, :], [ocatT, slot], [pm], start=(kc == 0), stop=(kc == 7))
                hs = slice(half * 512, (half + 1) * 512)
                vtt(h[:, hs], h[:, hs], pm[:], ALU.add, [h, pm], [h])

        def ffn(l, samp, last):
            norm_T(pp(l, 'nffn'), ppt[l])
            if samp:
                dma('pool', scf[:], s_cf[l], w=[scf])
            def views(blk):
                xpf, acc, acc2, sa = FB[blk % 2]
                o_f = PP['fcw'][0] + blk * 12
                cw = ppt[l][:, o_f:o_f + 12].rearrange("p (q j) -> p q j", q=4)
                cwh = cwb[l][:, blk * 12:blk * 12 + 12].rearrange("p (q j) -> p q j", q=4)
                if not samp:
                    xpf3 = xpf[:, :, 0:130]
                    xs = [xpf3[:, :, j:j + 128] for j in range(3)]
                    ws = [cwh[:, :, j:j + 1].to_broadcast([128, 4, 128]) for j in range(3)]
                    av, a2v = acc[:], acc2[:]
                    cin_v, new_v, cout_v = xpf3[:, :, 0:2], xpf3[:, :, 2:130], xpf3[:, :, 128:130]
                else:
                    xpf4 = xpf[:].rearrange("p q (b t) -> p q b t", b=16)
                    xs = [xpf4[:, :, :, j:j + 8] for j in range(3)]
                    ws = [cwh[:, :, j:j + 1].unsqueeze(3).to_broadcast([128, 4, 16, 8]) for j in range(3)]
                    av = acc[:].rearrange("p q (b t) -> p q b t", b=16)
                    a2v = acc2[:].rearrange("p q (b t) -> p q b t", b=16)
                    cin_v, new_v, cout_v = xpf4[:, :, :, 0:2], xpf4[:, :, :, 2:10], xpf4[:, :, :, 8:10]
                return xpf, acc, acc2, sa, cw, xs, ws, av, a2v, cin_v, new_v, cout_v

            def front(blk):
                xpf, acc, acc2, sa, cw, xs, ws, av, a2v, cin_v, new_v, cout_v = views(blk)
                slot, wv = wload(('up', l, blk))
                pm = pM[blk % 2]
                pm3 = pm[:].rearrange("p (q t) -> p q t", q=4)
                for q in range(4):
                    for kc in range(8):
                        mm(pm3[:, q, :], wv[:, kc, q * 128:(q + 1) * 128], hnT[:, kc, :], [slot, hnT], [pm], start=(kc == 0), stop=(kc == 7))
                if not samp:
                    acopy(cin_v, cfc[l][:, blk, :, :], [cfc[l]], [xpf])
                    acopy(new_v, pm3, [pm], [xpf])
                else:
                    acopy(cin_v, scf[:, blk], [scf], [xpf])
                    acopy(new_v, pm[:].rearrange("p (q b t) -> p q b t", q=4, b=16), [pm], [xpf])
                for q in range(4):
                    act(a2v[:, q], xs[1][:, q], AF.Identity, [xpf, ppt[l]], [acc2], scale=cw[:, q, 1:2])

            def conv(blk):
                xpf, acc, acc2, sa, cw, xs, ws, av, a2v, cin_v, new_v, cout_v = views(blk)
                vtt(av, xs[0], ws[0], ALU.mult, [xpf, cwb[l]], [acc])
                vtt(av, av, a2v, ALU.add, [acc, acc2], [acc])
                vtt(a2v, xs[2], ws[2], ALU.mult, [xpf, cwb[l]], [acc2])
                vtt(av, av, a2v, ALU.add, [acc, acc2], [acc])
                if not samp:
                    acopy(cfc[l][:, blk, :, :], cout_v, [xpf], [cfc[l]])
                else:
                    acopy(scf[:, blk], cout_v, [xpf], [scf])

            def back_silu(blk):
                xpf, acc, acc2, sa = FB[blk % 2]
                act(sa[:], acc[:, 0:2, :], AF.Silu, [acc], [sa])

            def back_mult(blk):
                xpf, acc, acc2, sa = FB[blk % 2]
                vtt(gT[:, 2 * blk:2 * blk + 2, :], sa[:], acc[:, 2:4, :], ALU.mult, [sa, acc], [gT])
            front(0)
            conv(0)
            for blk in range(11):
                if blk + 1 < 11:
                    front(blk + 1)
                back_silu(blk)
                if blk + 1 < 11:
                    conv(blk + 1)
                back_mult(blk)
            if samp:
                dma('pool', o_scf[l], scf[:], r=[scf])
            elif last:
                dma('pool', o_pcf[l], cfc[l][:], r=[cfc[l]])
            for half in range(2):
                pm = pM[2 + half]
                for q4 in range(4):
                    slot, wv = wload(('down', l, half, q4))
                    n_c = 6 if q4 < 3 else 4
                    c0 = q4 * 6
                    for c in range(n_c):
                        mm(pm[:], gT[:, c0 + c, :], wv[:, c, :], [gT, slot], [pm], start=(c0 + c == 0), stop=(c0 + c == 21))
                hs = slice(half * 512, (half + 1) * 512)
                vtt(h[:, hs], h[:, hs], pm[:], ALU.add, [h, pm], [h])

        def ple(l, si):
            norm_T(pp(l, 'nple'), ppt[l])
            dma('pool', pf[:], pin[l, si], w=[pf])
            vcopy(pb[:], pf[:], [pf], [pb])
            for half in range(2):
                slot, wv = wload(('pg', l, half))
                pm = pM[half]
                for kc in range(8):
                    mm(pm[:], hnT[:, kc, :], wv[:, kc, :], [hnT, slot], [pm], start=(kc == 0), stop=(kc == 7))
                act(gsbh[half][:], pm[:], AF.Sigmoid, [pm], [gsbh[half]])
            for cc in range(2):
                tr(pT[:, cc, :], pb[:, cc * 128:(cc + 1) * 128], identb, [pb, cstb], [pT])
            vcopy(ppT[:], pT[:, 0:2, :], [pT], [ppT])
            for half in range(2):
                slot, wv = wload(('pp', l, half))
                pm = pM[2 + half]
                for cc in range(2):
                    mm(pm[:], ppT[:, cc, :], wv[:, cc, :], [ppT, slot], [pm], start=(cc == 0), stop=(cc == 1))
                hs = slice(half * 512, (half + 1) * 512)
                vtt(tmp1[:], gsbh[half][:], pm[:], ALU.mult, [gsbh[half], pm], [tmp1])
                vtt(h[:, hs], h[:, hs], tmp1[:], ALU.add, [h, tmp1], [h])

        for si in range(NSUB):
            samp = (si == NSUB - 1)
            last = (si == NSUB - 2)
            if samp:
                for l in range(2):
                    build_tables(l, True)
            dma('pool', h[:], xin[si], w=[h])
            for l in range(2):
                in_proj(l, samp)
                gens_ = [in_proj_rest(l, samp), delta(l, samp, last)]
                while gens_:
                    for g_ in list(gens_):
                        try:
                            next(g_)
                        except StopIteration:
                            gens_.remove(g_)
                gens_ = [s5(l, samp, last), hgrn(l, samp, last)]
                while gens_:
                    for g_ in list(gens_):
                        try:
                            next(g_)
                        except StopIteration:
                            gens_.remove(g_)
                if KSTAGE >= 5:
                    out_proj(l)
                ffn(l, samp, last)
                ple(l, si)
            norm_stats()
            for half in range(2):
                hs = slice(half * 512, (half + 1) * 512)
                vstt(yoh[half][:], h[:, hs], st[:, 3:4], nfin[:, hs], ALU.mult, ALU.mult, [h, st, nfin], [yoh[half]])
                dma('pool', y_d[si][:, hs], yoh[half][:], r=[yoh[half]])
        S.finish()
        with nc.Block() as block:
            S.emit(nc, block)
    return nc


def _fm(v, nchunk):
    return np.ascontiguousarray(v.reshape(nchunk, 128).T)


_NC_CACHE = {}


def _prepare(inputs):
    f = {k: np.asarray(v, dtype=np.float32) for k, v in inputs.items()}
    n = 8
    pps, bcs = [], []
    for l in range(2):
        pp = np.zeros((128, NPP), np.float32)

        def put(name, arr):
            o, w = PP[name]
            pp[:, o:o + w] = arr.reshape(128, w)
        put('nmix', _fm(f['norm_mix'][l], 8))
        put('nffn', _fm(f['norm_ffn'][l], 8))
        put('nple', _fm(f['norm_ple'][l], 8))
        put('dcw', f['dn_conv_w'][l].reshape(4, 12, 128).transpose(2, 1, 0))
        fw = f['ffn_conv_w'][l].reshape(3, 2, 11, 2, 128)
        put('fcw', fw.transpose(4, 2, 1, 3, 0))
        put('lre', _fm(f['ssm_lam_re'][l].reshape(-1), 8))
        put('lim', _fm(f['ssm_lam_im'][l].reshape(-1), 8))
        put('lst', _fm(np.repeat(f['ssm_log_step'][l], 64), 8))
        put('ssd', _fm(f['ssm_d'][l], 2))
        put('glub', _fm(f['ssm_glu_b'][l], 2))
        put('hl0', _fm(f['hg_lower'][0], 2))
        put('hl1', _fm(f['hg_lower'][1], 2))
        pps.append(pp)
        bc = np.zeros((NBC,), np.float32)

        def putb(name, arr):
            o, w = BC[name]
            bc[o:o + w] = arr.reshape(w)
        putb('dnn', f['dn_norm'][l])
        putb('hgn', f['hg_norm'][l])
        putb('alog', f['dn_a_log'][l])
        putb('dtb', f['dn_dt_bias'][l])
        putb('lre', f['ssm_lam_re'][l])
        putb('lim', f['ssm_lam_im'][l])
        putb('lst', np.repeat(f['ssm_log_step'][l], 64))
        putb('hl0', f['hg_lower'][0])
        putb('hl1', f['hg_lower'][1])
        putb('nfin', f['norm_final'])
        bcs.append(bc)
    pp_all = np.stack(pps)
    bc_all = np.stack(bcs)
    bfull = np.zeros((2, 2, 128, 2, 512), np.float32)
    cfull = np.zeros((2, 2, 128, 8, 32), np.float32)
    for l in range(2):
        for r, (bn, cn) in enumerate((('ssm_b_re', 'ssm_c_re'), ('ssm_b_im', 'ssm_c_im'))):
            for g in range(16):
                cc, gg = divmod(g, 8)
                j, gl = divmod(g, 2)
                bfull[l, r, gg * 16:(gg + 1) * 16, cc, (j % 4) * 128 + gl * 64:(j % 4) * 128 + gl * 64 + 64] = f[bn][l, g].T
                cfull[l, r, gl * 64:(gl + 1) * 64, j, gl * 16:(gl + 1) * 16] = f[cn][l, g].T
    in_maps = []
    for c in range(n):
        xs = np.concatenate([f['x_prompt'][c].reshape(16, 128, 1024),
                             f['x_sample'][c * 16:(c + 1) * 16].reshape(1, 128, 1024)], axis=0)
        ps_ = np.concatenate([f['p_prompt'][:, c].reshape(2, 16, 128, 256),
                              f['p_sample'][:, c * 16:(c + 1) * 16].reshape(2, 1, 128, 256)], axis=1)
        sl = slice(c * 16, (c + 1) * 16)
        s_cq = f['state_conv_qkv'][:, sl].reshape(2, 16, 3, 12, 128).transpose(0, 4, 3, 1, 2)
        s_ss = np.stack([f['state_ssm_re'][:, sl], f['state_ssm_im'][:, sl]], axis=1).reshape(2, 2, 16, 8, 128).transpose(0, 1, 4, 3, 2)
        s_cf = f['state_conv_ffn'][:, sl].reshape(2, 16, 2, 2, 11, 2, 128).transpose(0, 6, 4, 3, 5, 1, 2).reshape(2, 128, 11, 4, 16, 2)
        in_maps.append({
            "xin": np.ascontiguousarray(xs), "pin": np.ascontiguousarray(ps_), "consts": CONSTS, "pp": pp_all, "bc": bc_all,
            "w_in": f['w_in'], "w_out": f['w_out'], "w_up": f['ffn_w_up'], "w_down": f['ffn_w_down'],
            "w_pg": f['ple_w_gate'], "w_pp": f['ple_w_proj'], "w_glu": f['ssm_glu_w'], "bfull": bfull, "cfull": cfull,
            "s_cq": np.ascontiguousarray(s_cq), "s_dl": np.ascontiguousarray(f['state_delta'][:, sl]),
            "s_ss": np.ascontiguousarray(s_ss), "s_hg": np.ascontiguousarray(f['state_hgrn'][:, sl]),
            "s_cf": np.ascontiguousarray(s_cf),
        })
    return in_maps


def kernel(**inputs):
    n = 8
    in_maps = _prepare(inputs)
    if 'nc' not in _NC_CACHE:
        _NC_CACHE['nc'] = build()
    res = run_bass_kernel_spmd(_NC_CACHE['nc'], in_maps, core_ids=list(range(n))).results
    yp = np.stack([r["y"][:16].reshape(2048, 1024) for r in res])
    ys = np.concatenate([r["y"][16].reshape(16, 8, 1024) for r in res], axis=0)

    def cat(fn, axis=1):
        return np.ascontiguousarray(np.concatenate([fn(r) for r in res], axis=axis))
    p_cq = cat(lambda r: r["o_pcq"].transpose(0, 3, 2, 1).reshape(2, 1, 3, 1536))
    p_dl = cat(lambda r: r["o_pdl"].reshape(2, 1, 4, 128, 128))
    p_sr = cat(lambda r: r["o_pss"][:, 0].transpose(0, 2, 1).reshape(2, 1, 16, 64))
    p_si = cat(lambda r: r["o_pss"][:, 1].transpose(0, 2, 1).reshape(2, 1, 16, 64))
    p_hg = cat(lambda r: r["o_phg"].reshape(2, 2, 64, 2, 64).transpose(0, 3, 1, 2, 4).reshape(2, 1, 4, 64, 64))
    p_cf = cat(lambda r: r["o_pcf"].reshape(2, 128, 11, 2, 2, 2).transpose(0, 5, 3, 2, 4, 1).reshape(2, 1, 2, 5632))
    s_cq = cat(lambda r: r["o_scq"].transpose(0, 3, 4, 2, 1).reshape(2, 16, 3, 1536))
    s_dl = cat(lambda r: r["o_sdl"])
    s_sr = cat(lambda r: r["o_sss"][:, 0].transpose(0, 3, 2, 1).reshape(2, 16, 16, 64))
    s_si = cat(lambda r: r["o_sss"][:, 1].transpose(0, 3, 2, 1).reshape(2, 16, 16, 64))
    s_hg = cat(lambda r: r["o_shg"].reshape(2, 2, 64, 2, 16, 64).transpose(0, 4, 3, 1, 2, 5).reshape(2, 16, 4, 64, 64))
    s_cf = cat(lambda r: r["o_scf"].reshape(2, 128, 11, 2, 2, 16, 2).transpose(0, 5, 6, 3, 2, 4, 1).reshape(2, 16, 2, 5632))
    return (yp, ys, p_cq, p_dl, p_sr, p_si, p_hg, p_cf, s_cq, s_dl, s_sr, s_si, s_hg, s_cf)
```

```python
import math
from contextlib import ExitStack
import numpy as np
import concourse.bass as bass
import concourse.mybir as mybir
from concourse.bass_utils import run_bass_kernel_spmd

F32 = mybir.dt.float32
BF16 = mybir.dt.bfloat16
I32 = mybir.dt.int32
ALU = mybir.AluOpType
AF = mybir.ActivationFunctionType
EPS = 1e-6
import os
NSUB = int(os.environ.get('KNSUB', '17'))
NRING = 4
SEM_EPOCH = 4000
NEPOCH = {'pe': 10, 'dve': 5, 'act': 3, 'pool': 1}
KSUB = int(os.environ.get('KSUB', '9'))
KSTAGE = int(os.environ.get('KSTAGE', '9'))
TWO_PI = 2.0 * math.pi


class Buf:
    __slots__ = ('name', 'w', 'r', 'excl')

    def __init__(self, name, excl=False):
        self.name = name
        self.w = None
        self.r = {}
        self.excl = excl


class TB:
    def __init__(self, t, name, bufs=None):
        self.t = t
        self.bs = bufs if bufs is not None else [Buf(name)]

    def __getitem__(self, k):
        return self.t[k]


class Sched:
    def __init__(self, sems):
        self.sems = sems
        self.cnt = {k: 0 for k in sems}
        self.prog = {'pe': [], 'dve': [], 'act': [], 'pool': [], 'sp': []}
        self.waited = {e: {} for e in self.prog}
        self.dma_rr = {'sp': 0, 'pool': 0, 'act': 0}
        self.epoch = {}
        self.last_pe = None
        self.dma_names = {q: sorted(n for n in sems if n.startswith('d_%s_' % q)) for q in ('sp', 'pool', 'act')}

    def _need(self, eng, s, v, force=False):
        if eng == 'pe' and s.startswith('pe_') and not force:
            return
        if self.waited[eng].get(s, 0) < v:
            self.prog[eng].append(('w', s, v))
            self.waited[eng][s] = v

    def _deps(self, eng, reads, writes):
        for b in reads:
            if b.w is not None:
                self._need(eng, *b.w)
        for b in writes:
            if b.w is not None:
                self._need(eng, *b.w)
            for s, v in b.r.items():
                self._need(eng, s, v)

    def _done(self, tok, reads, writes):
        for b in writes:
            b.w = tok
            b.r = {}
        for b in reads:
            if b.r.get(tok[0], 0) < tok[1]:
                b.r[tok[0]] = tok[1]

    def op(self, eng, fn, reads=(), writes=(), pe_sync=False):
        if pe_sync and self.last_pe is not None:
            self._need('pe', self.last_pe[0], self.last_pe[1], force=True)
        reads = [b for x in reads for b in x.bs]
        writes = [b for x in writes for b in x.bs]
        writes = writes + [b for b in reads if b.excl and b not in writes]
        reads = [b for b in reads if not b.excl]
        self._deps(eng, reads, writes)
        ep = self.epoch.setdefault(eng, 0)
        s = '%s_%d' % (eng, ep)
        if self.cnt[s] >= SEM_EPOCH:
            ep += 1
            self.epoch[eng] = ep
            s = '%s_%d' % (eng, ep)
        self.cnt[s] += 1
        tok = (s, self.cnt[s])
        self.prog[eng].append(('o', fn, s, 1))
        if eng == 'pe':
            self.last_pe = tok
        self._done(tok, reads, writes)

    def dma(self, q, fn, reads=(), writes=()):
        reads = [b for x in reads for b in x.bs]
        writes = [b for x in writes for b in x.bs]
        names = self.dma_names[q]
        s = names[self.dma_rr[q] % len(names)]
        self.dma_rr[q] += 1
        if self.cnt[s] > 0:
            self._need(q, s, self.cnt[s])
        self._deps(q, reads, writes)
        self.cnt[s] += 16
        tok = (s, self.cnt[s])
        self.prog[q].append(('o', fn, s, 16))
        self._done(tok, reads, writes)

    def finish(self):
        for q in ('sp', 'pool', 'act'):
            for s in self.dma_names[q]:
                if self.cnt[s] > 0:
                    self._need('sp', s, self.cnt[s])
        for s in self.sems:
            if not s.startswith('d_') and self.cnt[s] > 0:
                self._need('sp', s, self.cnt[s])

    def emit(self, nc, block):
        sems = self.sems

        def replay(engobj, prog):
            for it in prog:
                if it[0] == 'w':
                    engobj.wait_ge(sems[it[1]], it[2])
                else:
                    it[1](engobj).then_inc(sems[it[2]], it[3])

        @block.tensor
        def _(e):
            replay(e, self.prog['pe'])

        @block.vector
        def _(e):
            replay(e, self.prog['dve'])

        @block.scalar
        def _(e):
            replay(e, self.prog['act'])

        @block.gpsimd
        def _(e):
            replay(e, self.prog['pool'])

        @block.sync
        def _(e):
            replay(e, self.prog['sp'])


def _consts():
    p = np.arange(128)
    c = {}
    c['ident'] = np.eye(128)
    for nm, bs in (('P', 128), ('S', 8), ('H', 32)):
        same = (p[:, None] // bs) == (p[None, :] // bs)
        c['cT' + nm] = ((p[:, None] <= p[None, :]) & same)
        c['st' + nm] = ((p[None, :] < p[:, None]) & same)
        c['sm' + nm] = same
        nb = 128 // bs
        sel = np.zeros((128, 16))
        sel[p, p // bs] = 1.0
        c['sel' + nm] = sel
    c['iota'] = np.tile(np.arange(128)[None, :], (128, 1))
    c['pcol'] = np.tile(p[:, None], (1, 2))
    c['pcol'][:, 1] = p % 8
    names = ['ident', 'cTP', 'cTS', 'cTH', 'stP', 'stS', 'smP', 'smS', 'smH', 'selP', 'selS', 'selH', 'iota', 'pcol']
    offs = {}
    cols = []
    o = 0
    for n in names:
        a = c[n].astype(np.float32)
        offs[n] = (o, a.shape[1])
        o += a.shape[1]
        cols.append(a)
    return np.concatenate(cols, axis=1), offs


CONSTS, COFF = _consts()
NCONST = CONSTS.shape[1]
PP = {}
_o = 0
for _n, _w in (('nmix', 8), ('nffn', 8), ('nple', 8), ('dcw', 48), ('fcw', 132), ('lre', 8), ('lim', 8), ('lst', 8),
               ('ssd', 2), ('glub', 2), ('hl0', 2), ('hl1', 2)):
    PP[_n] = (_o, _w)
    _o += _w
NPP = _o
BC = {}
_o = 0
for _n, _w in (('dnn', 128), ('hgn', 64), ('alog', 4), ('dtb', 4), ('hl0', 256), ('hl1', 256),
               ('lre', 1024), ('lim', 1024), ('lst', 1024), ('nfin', 1024)):
    BC[_n] = (_o, _w)
    _o += _w
NBC = _o
NBCS = 712


def build():
    nc = bass.Bass("TRN2", target_bir_lowering=False)

    def din(name, shape):
        return nc.dram_tensor(name, list(shape), F32, kind="ExternalInput").ap()

    def dout(name, shape):
        return nc.dram_tensor(name, list(shape), F32, kind="ExternalOutput").ap()

    xin = din("xin", [NSUB, 128, 1024])
    pin = din("pin", [2, NSUB, 128, 256])
    consts_d = din("consts", [128, NCONST])
    pp_d = din("pp", [2, 128, NPP])
    bc_d = din("bc", [2, NBC])
    w_in = din("w_in", [2, 1024, 3336])
    w_out = din("w_out", [2, 1024, 1024])
    w_up = din("w_up", [2, 1024, 5632])
    w_down = din("w_down", [2, 2816, 1024])
    w_pg = din("w_pg", [2, 1024, 1024])
    w_pp = din("w_pp", [2, 256, 1024])
    w_glu = din("w_glu", [2, 256, 256])
    bfull = din("bfull", [2, 2, 128, 2, 512])
    cfull = din("cfull", [2, 2, 128, 8, 32])
    s_cq = din("s_cq", [2, 128, 12, 16, 3])
    s_dl = din("s_dl", [2, 16, 4, 128, 128])
    s_ss = din("s_ss", [2, 2, 128, 8, 16])
    s_hg = din("s_hg", [2, 16, 4, 64, 64])
    s_cf = din("s_cf", [2, 128, 11, 4, 16, 2])
    y_d = dout("y", [NSUB, 128, 1024])
    o_pcq = dout("o_pcq", [2, 128, 12, 3])
    o_pdl = dout("o_pdl", [2, 4, 128, 128])
    o_pss = dout("o_pss", [2, 2, 128, 8])
    o_phg = dout("o_phg", [2, 128, 2, 64])
    o_pcf = dout("o_pcf", [2, 128, 11, 4, 2])
    o_scq = dout("o_scq", [2, 128, 12, 16, 3])
    o_sdl = dout("o_sdl", [2, 16, 4, 128, 128])
    o_sss = dout("o_sss", [2, 2, 128, 8, 16])
    o_shg = dout("o_shg", [2, 128, 2, 16, 64])
    o_scf = dout("o_scf", [2, 128, 11, 4, 16, 2])

    es = ExitStack()
    with es:
        def sb(name, shape, dt=F32):
            return TB(es.enter_context(nc.sbuf_tensor(name, list(shape), dt)), name)

        def ps(name, shape, dt=F32):
            return TB(es.enter_context(nc.psum_tensor(name, list(shape), dt)), name, bufs=[Buf(name, excl=True)])

        sems = {}
        for n in ['%s_%d' % (e_, i_) for e_ in NEPOCH for i_ in range(NEPOCH[e_])] + ['d_sp_%d' % i for i in range(8)] + ['d_pool_%d' % i for i in range(6)] + ['d_act_%d' % i for i in range(2)]:
            sems[n] = es.enter_context(nc.semaphore(n))
        S = Sched(sems)

        def V(fn, r=(), w=()):
            S.op('dve', fn, r, w)

        def A(fn, r=(), w=()):
            S.op('act', fn, r, w)

        def mm(out, lhsT, rhs, r, w, start=True, stop=True, skip=False, sync=False):
            S.op('pe', lambda e: e.matmul(out, lhsT=lhsT, rhs=rhs, start=start, stop=stop, skip_group_check=skip), r, w, pe_sync=sync)

        def tr(out, in_, ident, r, w):
            S.op('pe', lambda e: e.transpose(out=out, in_=in_, identity=ident), r, w)

        def vtt(out, a, b, op, r, w):
            V(lambda e: e.tensor_tensor(out=out, in0=a, in1=b, op=op), r, w)

        def vts(out, a, s1, op0, r, w, s2=None, op1=None):
            if op1 is None:
                V(lambda e: e.tensor_scalar(out=out, in0=a, scalar1=s1, scalar2=None, op0=op0), r, w)
            else:
                V(lambda e: e.tensor_scalar(out=out, in0=a, scalar1=s1, scalar2=s2, op0=op0, op1=op1), r, w)

        def vstt(out, a, sc, b, op0, op1, r, w):
            V(lambda e: e.scalar_tensor_tensor(out=out, in0=a, scalar=sc, in1=b, op0=op0, op1=op1), r, w)

        def vcopy(out, a, r, w):
            V(lambda e: e.tensor_copy(out=out, in_=a), r, w)

        def acopy(out, a, r, w):
            A(lambda e: e.copy(out=out, in_=a), r, w)

        def act(out, a, func, r, w, bias=None, scale=None, accum=None):
            kw = {}
            if bias is not None:
                kw['bias'] = bias
            if scale is not None:
                kw['scale'] = scale
            if accum is not None:
                kw['accum_out'] = accum
            A(lambda e: e.activation(out=out, in_=a, func=func, **kw), r, w)

        def dma(q, out, in_, r=(), w=()):
            S.dma(q, lambda e: e.dma_start(out=out, in_=in_), reads=r, writes=w)

        WB = {}

        def wreg(key, kc, n, parts):
            d = nc.dram_tensor("wb_" + "_".join(str(k) for k in key), [128, kc * n], BF16, kind="Internal").ap()
            tb = TB(None, "wb" + str(key))
            d3 = d.rearrange("p (k n) -> p k n", k=kc)
            for (src, c0, ncol) in parts:
                S.dma('pool', lambda e, d3=d3, src=src, c0=c0, ncol=ncol: e.dma_start(out=d3[:, :, c0:c0 + ncol], in_=src.rearrange("(k p) n -> p k n", p=128)), writes=[tb])
            WB[key] = (d3, tb, kc, n)
        for l in range(2):
            W = w_in[l]
            for g in range(3):
                wreg(('in', l, g), 8, 512, [(W[:, g * 512:(g + 1) * 512], 0, 512)])
            wreg(('in', l, 3), 8, 520, [(W[:, 1536:2056], 0, 520)])
            wreg(('in', l, 4), 8, 512, [(W[:, 2056:2568], 0, 512)])
            wreg(('in', l, 5), 8, 512, [(W[:, 2568:3080], 0, 512)])
            wreg(('in', l, 6), 8, 256, [(W[:, 3080:3336], 0, 256)])
            for half in range(2):
                wreg(('out', l, half), 8, 512, [(w_out[l][:, half * 512:(half + 1) * 512], 0, 512)])
            for blk in range(11):
                wreg(('up', l, blk), 8, 512, [(w_up[l][:, blk * 256:(blk + 1) * 256], 0, 256),
                                              (w_up[l][:, 2816 + blk * 256:2816 + (blk + 1) * 256], 256, 256)])
            for half in range(2):
                for q4 in range(4):
                    n_c = 6 if q4 < 3 else 4
                    c0 = q4 * 6
                    wreg(('down', l, half, q4), n_c, 512, [(w_down[l][c0 * 128:(c0 + n_c) * 128, half * 512:(half + 1) * 512], 0, 512)])
                wreg(('pg', l, half), 8, 512, [(w_pg[l][:, half * 512:(half + 1) * 512], 0, 512)])
                wreg(('pp', l, half), 2, 512, [(w_pp[l][:, half * 512:(half + 1) * 512], 0, 512)])

        cst = sb("cst", [128, NCONST])
        cstb = sb("cstb", [128, NCONST], BF16)
        dma('sp', cst[:], consts_d, w=[cst])
        vcopy(cstb[:], cst[:], [cst], [cstb])

        def C(n, bf=False, cols=None):
            o, wd = COFF[n]
            if cols is not None:
                wd = cols
            return (cstb if bf else cst)[:, o:o + wd]

        identf = C('ident')
        identb = C('ident', True)
        onesb = C('smP', True)
        onesf = C('smP')
        ppt = [sb("pp%d" % l, [128, NPP]) for l in range(2)]
        bcs = [sb("bcs%d" % l, [128, NBCS]) for l in range(2)]
        nfin = sb("nfin", [128, 1024])
        dma('sp', nfin[:], bc_d[0, BC['nfin'][0]:BC['nfin'][0] + 1024].partition_broadcast(128), w=[nfin])
        for l in range(2):
            dma('sp', ppt[l][:], pp_d[l], w=[ppt[l]])
            dma('sp', bcs[l][:], bc_d[l, 0:NBCS].partition_broadcast(128), w=[bcs[l]])

        def pp(l, n, a=0, b=None):
            o, wd = PP[n]
            return ppt[l][:, o + a:o + (wd if b is None else b)]

        def bcv(l, n, a=0, b=None):
            o, wd = BC[n]
            return bcs[l][:, o + a:o + (wd if b is None else b)]

        glw = [sb("glw%d" % l, [128, 2, 256], BF16) for l in range(2)]
        bft = [[sb("bf%d%d" % (l, r), [128, 2, 512], BF16) for r in range(2)] for l in range(2)]
        cft = [[sb("cf%d%d" % (l, r), [128, 8, 32], BF16) for r in range(2)] for l in range(2)]
        cftf = sb("cftf", [128, 8, 32])
        diagD = [sb("diagD%d" % l, [128, 2, 128], BF16) for l in range(2)]
        negA = [sb("negA%d" % l, [128, 4]) for l in range(2)]
        lbb = [sb("lbb%d" % l, [128, 256]) for l in range(2)]
        omlb = [sb("omlb%d" % l, [128, 256]) for l in range(2)]
        lbf = [sb("lbf%d" % l, [128, 2]) for l in range(2)]
        omlf = [sb("omlf%d" % l, [128, 2]) for l in range(2)]
        for l in range(2):
            dma('pool', glw[l][:], w_glu[l].rearrange("(c p) n -> p c n", p=128), w=[glw[l]])
            for r in range(2):
                dma('pool', bft[l][r][:], bfull[l, r], w=[bft[l][r]])
            dma('pool', cft[l][0][:], cfull[l, 0], w=[cft[l][0]])
            dma('sp', cftf[:], cfull[l, 1], w=[cftf])
            vts(cft[l][1][:], cftf[:], -1.0, ALU.mult, [cftf], [cft[l][1]])
            for cc in range(2):
                vts(diagD[l][:, cc, :], identf, pp(l, 'ssd', cc, cc + 1), ALU.mult, [cst, ppt[l]], [diagD[l]])
            act(negA[l][:], bcv(l, 'alog'), AF.Exp, [bcs[l]], [negA[l]])
            vts(negA[l][:], negA[l][:], -1.0, ALU.mult, [negA[l]], [negA[l]])
            if l == 0:
                V(lambda e: e.memset(lbb[0][:], 0.0), [], [lbb[0]])
                V(lambda e: e.memset(lbf[0][:], 0.0), [], [lbf[0]])
            else:
                vtt(lbb[1][:], bcv(1, 'hl1'), bcv(1, 'hl0'), ALU.subtract, [bcs[1]], [lbb[1]])
                act(lbb[1][:], lbb[1][:], AF.Sigmoid, [lbb[1]], [lbb[1]])
                vtt(lbf[1][:], pp(1, 'hl1'), pp(1, 'hl0'), ALU.subtract, [ppt[1]], [lbf[1]])
                act(lbf[1][:], lbf[1][:], AF.Sigmoid, [lbf[1]], [lbf[1]])
            vts(omlb[l][:], lbb[l][:], -1.0, ALU.mult, [lbb[l]], [omlb[l]], 1.0, ALU.add)
            vts(omlf[l][:], lbf[l][:], -1.0, ALU.mult, [lbf[l]], [omlf[l]], 1.0, ALU.add)

        ZB = [es.enter_context(nc.sbuf_tensor("ZB%d" % i, [128, 2048], F32)) for i in range(4)]
        PZ = [TB(ZB[i // 4][:, (i % 4) * 512:(i % 4 + 1) * 512], "PZ%d" % i) for i in range(16)]
        tA, tB_, tC, tD, mag, bcA, bcP, cre, cim, lbr, lbi, bs_lre, bs_lim, bs_lst = PZ[0:14]
        bsvd = {'lre': bs_lre, 'lim': bs_lim, 'lst': bs_lst}
        h = sb("h", [128, 1024])
        tI = TB(h[:, 0:512].bitcast(I32), "tI", bufs=h.bs)

        def sincos(dst, ang_r, shift, rr, ww):
            vts(tC[:], ang_r, shift, ALU.add, rr, [tC])
            vcopy(tI[:], tC[:], [tC], [tI])
            vcopy(tD[:], tI[:], [tI], [tD])
            vtt(tC[:], tC[:], tD[:], ALU.subtract, [tC, tD], [tC])
            act(dst, tC[:], AF.Sin, [tC], ww, scale=6.283185)

        LpT = [[sb("LpT%d%d" % (l, r), [128, 8, 128], BF16) for r in range(2)] for l in range(2)]
        Lm = [[sb("Lm%d%d" % (l, r), [128, 1024], BF16) for r in range(2)] for l in range(2)]
        lbar = [[sb("lbar%d%d" % (l, r), [128, 8]) for r in range(2)] for l in range(2)]
        sm_a = sb("sm_a", [128, 8])
        sm_p = sb("sm_p", [128, 8])
        iota3 = C('iota').unsqueeze(1).to_broadcast([128, 4, 128])

        def t3(T):
            return T[:].rearrange("p (j t) -> p j t", j=4)

        def build_tables(l, samp):
            if not samp:
                act(sm_p[:], pp(l, 'lst'), AF.Exp, [ppt[l]], [sm_p])
                vtt(sm_a[:], pp(l, 'lre'), sm_p[:], ALU.mult, [ppt[l], sm_p], [sm_a])
                vstt(sm_p[:], pp(l, 'lim'), 1.0 / TWO_PI, sm_p[:], ALU.mult, ALU.mult, [ppt[l], sm_p], [sm_p])
            for hf in range(2):
                js = slice(4 * hf, 4 * hf + 4)
                cs = slice(512 * hf, 512 * hf + 512)

                def bsv(n):
                    return bsvd[n][:]
                for n_ in ('lre', 'lim', 'lst'):
                    o_ = BC[n_][0] + 512 * hf
                    dma('sp', bsvd[n_][:], bc_d[l, o_:o_ + 512].partition_broadcast(128), w=[bsvd[n_]])
                if not samp:
                    vtt(t3(tA), iota3, sm_a[:, js].unsqueeze(2).to_broadcast([128, 4, 128]), ALU.mult, [cst, sm_a], [tA])
                    act(mag[:], tA[:], AF.Exp, [tA], [mag])
                    vtt(t3(tB_), iota3, sm_p[:, js].unsqueeze(2).to_broadcast([128, 4, 128]), ALU.mult, [cst, sm_p], [tB_])
                    sincos(tA[:], tB_[:], 0.25, [tB_], [tA])
                    vtt(LpT[l][0][:, js, :], t3(tA), t3(mag), ALU.mult, [tA, mag], [LpT[l][0]])
                    vtt(t3(cre), t3(tA), t3(mag), ALU.mult, [tA, mag], [cre])
                    vcopy(lbar[l][0][:, js], t3(cre)[:, :, 1], [cre], [lbar[l][0]])
                    sincos(tA[:], tB_[:], 0.0, [tB_], [tA])
                    vtt(LpT[l][1][:, js, :], t3(tA), t3(mag), ALU.mult, [tA, mag], [LpT[l][1]])
                    vtt(t3(cre), t3(tA), t3(mag), ALU.mult, [tA, mag], [cre])
                    vcopy(lbar[l][1][:, js], t3(cre)[:, :, 1], [cre], [lbar[l][1]])
                act(bcP[:], bsv('lst'), AF.Exp, [bs_lst], [bcP])
                vtt(bcA[:], bsv('lre'), bcP[:], ALU.mult, [bs_lre, bcP], [bcA])
                vstt(bcP[:], bsv('lim'), 1.0 / TWO_PI, bcP[:], ALU.mult, ALU.mult, [bs_lim, bcP], [bcP])
                act(mag[:], bcA[:], AF.Exp, [bcA], [mag])
                sincos(tA[:], bcP[:], 0.25, [bcP], [tA])
                vtt(lbr[:], tA[:], mag[:], ALU.mult, [tA, mag], [lbr])
                sincos(tA[:], bcP[:], 0.0, [bcP], [tA])
                vtt(lbi[:], tA[:], mag[:], ALU.mult, [tA, mag], [lbi])
                vts(lbr[:], lbr[:], -1.0, ALU.add, [lbr], [lbr])
                vtt(tA[:], bsv('lre'), bsv('lre'), ALU.mult, [bs_lre], [tA])
                vtt(tB_[:], bsv('lim'), bsv('lim'), ALU.mult, [bs_lim], [tB_])
                vtt(tA[:], tA[:], tB_[:], ALU.add, [tA, tB_], [tA])
                V(lambda e: e.reciprocal(out=tA[:], in_=tA[:]), [tA], [tA])
                vtt(cre[:], lbr[:], bsv('lre'), ALU.mult, [lbr, bs_lre], [cre])
                vtt(tB_[:], lbi[:], bsv('lim'), ALU.mult, [lbi, bs_lim], [tB_])
                vtt(cre[:], cre[:], tB_[:], ALU.add, [cre, tB_], [cre])
                vtt(cre[:], cre[:], tA[:], ALU.mult, [cre, tA], [cre])
                vtt(cim[:], lbi[:], bsv('lre'), ALU.mult, [lbi, bs_lre], [cim])
                vtt(tB_[:], lbr[:], bsv('lim'), ALU.mult, [lbr, bs_lim], [tB_])
                vtt(cim[:], cim[:], tB_[:], ALU.subtract, [cim, tB_], [cim])
                vtt(cim[:], cim[:], tA[:], ALU.mult, [cim, tA], [cim])
                o_pc = COFF['pcol'][0] + (1 if samp else 0)
                sidx = cst[:, o_pc:o_pc + 1]
                vts(tA[:], bcA[:], sidx, ALU.mult, [bcA, cst], [tA], -1.0, ALU.mult)
                act(mag[:], tA[:], AF.Exp, [tA], [mag])
                vts(tB_[:], bcP[:], sidx, ALU.mult, [bcP, cst], [tB_], -1.0, ALU.mult)
                sincos(tA[:], tB_[:], 0.25, [tB_], [tA])
                vtt(lbr[:], tA[:], mag[:], ALU.mult, [tA, mag], [lbr])
                sincos(tA[:], tB_[:], 0.0, [tB_], [tA])
                vtt(lbi[:], tA[:], mag[:], ALU.mult, [tA, mag], [lbi])
                vtt(tA[:], lbr[:], cre[:], ALU.mult, [lbr, cre], [tA])
                vtt(tB_[:], lbi[:], cim[:], ALU.mult, [lbi, cim], [tB_])
                vtt(Lm[l][0][:, cs], tA[:], tB_[:], ALU.subtract, [tA, tB_], [Lm[l][0]])
                vtt(tA[:], lbr[:], cim[:], ALU.mult, [lbr, cim], [tA])
                vtt(tB_[:], lbi[:], cre[:], ALU.mult, [lbi, cre], [tB_])
                vtt(Lm[l][1][:, cs], tA[:], tB_[:], ALU.add, [tA, tB_], [Lm[l][1]])
        for l in range(2):
            build_tables(l, False)
        hnT = sb("hnT", [128, 8, 128], BF16)
        ocatT = sb("ocatT", [128, 8, 128], BF16)
        st = sb("st", [128, 8])
        ring = [sb("ring%d" % i, [128, 4160], BF16) for i in range(NRING)]
        ringi = [0]
        pT = ps("pT", [128, 8, 128], BF16)
        pFt = es.enter_context(nc.psum_tensor("pF", [128, 4, 128], F32))
        pFbuf = Buf("pF", excl=True)
        pF = [TB(pFt[:, i, :], "pF%d" % i, bufs=[pFbuf]) for i in range(4)]
        pM = [ps("pM%d" % i, [128, 512]) for i in range(6)]
        acc = sb("acc", [128, 4, 128])
        acc2 = sb("acc2", [128, 4, 128])
        oT_sb = TB(acc[0:64, :, :], "oT_sb", bufs=acc.bs)
        cfc = [sb("cfc%d" % l, [128, 11, 4, 2]) for l in range(2)]
        pf = sb("pf", [128, 256])
        pb = sb("pb", [128, 256], BF16)
        ppT = sb("ppT", [128, 2, 128], BF16)
        xpq = sb("xpq", [128, 12, 176], BF16)
        cq = [sb("cq%d" % l, [128, 12, 3]) for l in range(2)]
        ba = sb("ba", [128, 8])
        uT = sb("uT", [128, 2, 128], BF16)
        hqTs = sb("hqTs", [128, 2, 128])
        k_tok = sb("k_tok", [128, 4, 128], BF16)
        v_tok = sb("v_tok", [128, 4, 128], BF16)
        knT = sb("knT", [128, 4, 128], BF16)
        dsc = sb("dsc", [128, 64])
        gsel = sb("gsel", [128, 4, 16])
        glb = sb("glb", [128, 4, 16])
        gB = sb("gB", [128, 128])
        dtmp = sb("dtmp", [128, 128])
        dec = sb("dec", [128, 128])
        decT = sb("decT", [128, 128])
        Xf = sb("Xf", [128, 128])
        Xp = [sb("Xp%d" % i, [128, 128]) for i in range(2)]
        XpT = [sb("XpT%d" % i, [128, 128]) for i in range(2)]
        TT = sb("TT", [128, 128])
        Ru = sb("Ru", [128, 128])
        Rw = sb("Rw", [128, 128])
        u_sb = sb("u_sb", [128, 128])
        wT_b = sb("wT_b", [128, 128], BF16)
        qkTm = sb("qkTm", [128, 128], BF16)
        wq_sb = sb("wq_sb", [128, 2, 128])
        vnew = sb("vnew", [128, 128], BF16)
        t1s = sb("t1s", [128, 128])
        kdec = sb("kdec", [128, 128], BF16)
        oA = sb("oA", [128, 4, 128])
        oab = sb("oab", [128, 4, 128], BF16)
        Sd = [sb("Sd%d" % l, [128, 4, 128]) for l in range(2)]
        Sd_b = [sb("Sdb%d" % l, [128, 4, 128], BF16) for l in range(2)]
        sS_b = sb("sSb", [128, 16, 128], BF16)
        cinP = [[sb("cinP%d%d" % (l, r), [128, 8, 1]) for r in range(2)] for l in range(2)]
        cinS = [sb("cinS%d" % r, [128, 8, 16]) for r in range(2)]
        x0s = [sb("x0s%d" % r, [128, 8, 16]) for r in range(2)]
        xe = [sb("xe%d" % r, [128, 8, 16]) for r in range(2)]
        xep = [sb("xep%d" % r, [128, 8]) for r in range(2)]
        ybT = sb("ybT", [128, 2, 128])
        ybTb = sb("ybTb", [128, 2, 128], BF16)
        sgT = sb("sgT", [128, 128])
        v_h = sb("v_h", [128, 256], BF16)
        fT = sb("fT", [128, 2, 128])
        qtT = sb("qtT", [128, 2, 128], BF16)
        ktT = sb("ktT", [128, 2, 128], BF16)
        e1 = sb("e1", [128, 2, 128])
        khat = sb("khat", [128, 256], BF16)
        glh = sb("glh", [128, 2, 16])
        aTm = sb("aTm", [128, 4, 128], BF16)
        ocb = sb("ocb", [128, 256], BF16)
        Sh = [sb("Sh%d" % l, [128, 2, 64]) for l in range(2)]
        Sh_b = [sb("Shb%d" % l, [128, 2, 64], BF16) for l in range(2)]
        Shs_b = TB(sS_b[:].rearrange("p b v -> p (b v)").rearrange("p (c b v) -> p c b v", c=2, b=16), "Shsb", bufs=sS_b.bs)
        gT = sb("gT", [128, 22, 128], BF16)
        qkv_b = TB(gT[:, 0:12, :], "qkv_b", bufs=gT.bs)
        hb = TB(gT[:, 14:22, :].rearrange("p c t -> p (c t)"), "hb", bufs=gT.bs)
        sq = TB(gT[:, 12:20, :], "sq", bufs=gT.bs)
        def bufs_of(i0, i1):
            return [b for p_ in PZ[i0:i1] for b in p_.bs]
        sS = TB(ZB[1][:].rearrange("p (b v) -> p b v", b=16), "sS", bufs=bufs_of(4, 8))
        Shs = TB(ZB[1][:].rearrange("p (c b v) -> p c b v", c=2, b=16), "Shs", bufs=bufs_of(4, 8))
        scf = TB(ZB[1][:, 0:1408].rearrange("p (k q b t) -> p k q b t", k=11, q=4, b=16), "scf", bufs=bufs_of(4, 8))
        scq = TB(ZB[1][:, 1408:1984].rearrange("p (c b t) -> p c b t", c=12, b=16), "scq", bufs=bufs_of(4, 8))
        xre = TB(ZB[2][:, 0:1024].rearrange("p (j t) -> p j t", j=8), "xre", bufs=bufs_of(8, 10))
        xim = TB(ZB[2][:, 1024:2048].rearrange("p (j t) -> p j t", j=8), "xim", bufs=bufs_of(10, 12))
        yb0 = TB(ZB[3][:, 0:256], "yb0", bufs=PZ[12].bs)
        yb = TB(ZB[3][:, 256:512], "yb", bufs=PZ[12].bs)
        fto = TB(ZB[3][:, 512:768], "fto", bufs=PZ[13].bs)
        logf = TB(ZB[3][:, 768:1024], "logf", bufs=PZ[13].bs)
        omf = TB(ZB[3][:, 1024:1280], "omf", bufs=PZ[14].bs)
        b_sb = TB(ZB[3][:, 1280:1536], "b_sb", bufs=PZ[14].bs)
        oc = TB(ZB[3][:, 1536:1792], "oc", bufs=PZ[15].bs)
        hg_s = TB(ZB[3][:, 1792:2048], "hg_s", bufs=PZ[15].bs)
        gate_s = tD
        Wre = sb("Wre", [128, 1024], BF16)
        Wim = sb("Wim", [128, 1024], BF16)
        xbre = TB(Wre[:].rearrange("p (j t) -> p j t", j=8), "xbre", bufs=Wre.bs)
        xbim = TB(Wim[:].rearrange("p (j t) -> p j t", j=8), "xbim", bufs=Wim.bs)
        kdm = [sb("kdm%d" % i, [128, 128], BF16) for i in range(2)]
        khm = [sb("khm%d" % i, [128, 256], BF16) for i in range(2)]
        gsbh = [tA, tB_]
        yoh = [tC, tD]
        tmp1 = tD
        for l in range(2):
            for t_ in (cfc[l], cq[l], Sd[l], Sd_b[l], Sh[l], Sh_b[l], cinP[l][0], cinP[l][1]):
                V(lambda e, t_=t_: e.memset(t_[:], 0.0), [], [t_])

        class _RS:
            pass
        RS0 = _RS()
        RS0.dec, RS0.decT, RS0.TT, RS0.Ru, RS0.Rw, RS0.u_sb, RS0.t1s = dec, decT, TT, Ru, Rw, u_sb, t1s
        RS0.wT_b, RS0.qkTm, RS0.vnew, RS0.kdec, RS0.wq_sb, RS0.Xp, RS0.XpT = wT_b, qkTm, vnew, kdec, wq_sb, Xp, XpT
        RS0.pF, RS0.pU, RS0.pP = pF, pM[2], pM[5]
        RS1 = _RS()
        for n_ in ('dec', 'decT', 'TT', 'Ru', 'Rw', 'u_sb', 't1s'):
            setattr(RS1, n_, sb(n_ + "_1", [128, 128]))
        for n_ in ('wT_b', 'qkTm', 'vnew', 'kdec'):
            setattr(RS1, n_, sb(n_ + "_1", [128, 128], BF16))
        RS1.wq_sb = sb("wq_sb_1", [128, 2, 128])
        RS1.Xp = [sb("Xp1_%d" % i, [128, 128]) for i in range(2)]
        RS1.XpT = [sb("XpT1_%d" % i, [128, 128]) for i in range(2)]
        RS1.pF = [TB(pM[4][:, i * 128:(i + 1) * 128], "pF1_%d" % i, bufs=pM[4].bs) for i in range(4)]
        RS1.pU, RS1.pP = pM[0], pM[1]

        FB = [(sb("xpf_%d" % i, [128, 4, 160], BF16), sb("accf_%d" % i, [128, 4, 128], BF16),
               sb("acc2f_%d" % i, [128, 4, 128], BF16), sb("saf_%d" % i, [128, 2, 128], BF16)) for i in range(2)]
        cwb = [sb("cwb%d" % l, [128, 132], BF16) for l in range(2)]
        dcwb = [sb("dcwb%d" % l, [128, 48], BF16) for l in range(2)]
        dacc = sb("dacc", [128, 4, 128], BF16)
        dacc2 = sb("dacc2", [128, 4, 128], BF16)
        for l in range(2):
            vcopy(cwb[l][:], ppt[l][:, PP['fcw'][0]:PP['fcw'][0] + 132], [ppt[l]], [cwb[l]])
            vcopy(dcwb[l][:], ppt[l][:, PP['dcw'][0]:PP['dcw'][0] + 48], [ppt[l]], [dcwb[l]])

        def norm_stats():
            V(lambda e: e.memset(st[:, 0:1], 0.0), [], [st])
            act(hb[:], h[:], AF.Square, [h], [hb, st], accum=st[:, 0:1])
            vts(st[:, 1:2], st[:, 0:1], 1.0 / 1024, ALU.mult, [st], [st], EPS, ALU.add)
            act(st[:, 2:3], st[:, 1:2], AF.Sqrt, [st], [st])
            V(lambda e: e.reciprocal(out=st[:, 3:4], in_=st[:, 2:3]), [st], [st])

        def norm_T(gain_ap, gsrc):
            norm_stats()
            vts(hb[:], h[:], st[:, 3:4], ALU.mult, [h, st], [hb])
            for c in range(8):
                tr(pT[:, c, :], hb[:, c * 128:(c + 1) * 128], identb, [hb, cstb], [pT])
            vtt(hnT[:], pT[:], gain_ap.unsqueeze(2).to_broadcast([128, 8, 128]), ALU.mult, [pT, gsrc], [hnT])

        def wslot(kc, n):
            slot = ring[ringi[0] % NRING]
            ringi[0] += 1
            return slot, slot[:, 0:kc * n].rearrange("p (k n) -> p k n", k=kc)

        def wload(key):
            d3, tb, kc, n = WB[key]
            slot, wv = wslot(kc, n)
            S.dma('pool', lambda e: e.dma_start(out=wv, in_=d3), reads=[tb], writes=[slot])
            return slot, wv

        def rsqrt_small(dst, src, rr, ww, mul, add):
            vts(dst, src, mul, ALU.mult, rr, ww, add, ALU.add)
            act(dst, dst, AF.Sqrt, ww, ww)
            V(lambda e: e.reciprocal(out=dst, in_=dst), ww, ww)

        def in_proj(l, samp):
            norm_T(pp(l, 'nmix'), ppt[l])
            W = w_in[l]
            if samp:
                dma('pool', scq[:], s_cq[l], w=[scq])
                xq4 = xpq[:].rearrange("p c (b t) -> p c b t", b=16)
                vcopy(xq4[:, :, :, 0:3], scq[:], [scq], [xpq])
            else:
                acopy(xpq[:, :, 0:3], cq[l][:], [cq[l]], [xpq])
            for g in range(3):
                slot, wv = wload(('in', l, g))
                pm = pM[g % 2]
                pm3 = pm[:].rearrange("p (q t) -> p q t", q=4)
                for q in range(4):
                    for kc in range(8):
                        mm(pm3[:, q, :], wv[:, kc, q * 128:(q + 1) * 128], hnT[:, kc, :], [slot, hnT], [pm], start=(kc == 0), stop=(kc == 7))
                if samp:
                    acopy(xq4[:, 4 * g:4 * g + 4, :, 3:11], pm[:].rearrange("p (q b t) -> p q b t", q=4, b=16), [pm], [xpq])
                else:
                    acopy(xpq[:, 4 * g:4 * g + 4, 3:131], pm3, [pm], [xpq])
            slot, wv = wload(('in', l, 3))
            for kc in range(8):
                mm(pM[2][:], hnT[:, kc, :], wv[:, kc, 0:512], [hnT, slot], [pM[2]], start=(kc == 0), stop=(kc == 7))
            for kc in range(8):
                mm(pM[3][:, 0:8], hnT[:, kc, :], wv[:, kc, 512:520], [hnT, slot], [pM[3]], start=(kc == 0), stop=(kc == 7))
            act(gate_s[:], pM[2][:], AF.Silu, [pM[2]], [gate_s])
            vcopy(ba[:], pM[3][:, 0:8], [pM[3]], [ba])
        def in_proj_rest(l, samp):
            W = w_in[l]
            slot, wv = wload(('in', l, 4))
            pm = pM[0]
            pm3 = pm[:].rearrange("p (q t) -> p q t", q=4)
            for q in range(4):
                for kc in range(8):
                    mm(pm3[:, q, :], wv[:, kc, q * 128:(q + 1) * 128], hnT[:, kc, :], [slot, hnT], [pm], start=(kc == 0), stop=(kc == 7))
            vcopy(uT[:], pm3[:, 0:2, :], [pm], [uT])
            yield
            act(hqTs[:], pm3[:, 2:4, :], AF.Silu, [pm], [hqTs])
            yield
            slot, wv = wload(('in', l, 5))
            for kc in range(8):
                mm(pM[4][:], hnT[:, kc, :], wv[:, kc, :], [hnT, slot], [pM[4]], start=(kc == 0), stop=(kc == 7))
            p53 = pM[5][:, 0:256].rearrange("p (q t) -> p q t", q=2)
            for q in range(2):
                for kc in range(8):
                    mm(p53[:, q, :], wv[:, kc, q * 128:(q + 1) * 128], hnT[:, kc, :], [slot, hnT], [pM[5]], start=(kc == 0), stop=(kc == 7))
            act(fto[:], pM[4][:, 0:256], AF.Sigmoid, [pM[4]], [fto])
            yield
            acopy(v_h[:], pM[4][:, 256:512], [pM[4]], [v_h])
            yield
            act(fT[:], p53, AF.Sigmoid, [pM[5]], [fT])
            yield
            slot, wv = wload(('in', l, 6))
            for kc in range(8):
                mm(pM[3][:, 0:256], hnT[:, kc, :], wv[:, kc, :], [hnT, slot], [pM[3]], start=(kc == 0), stop=(kc == 7))
            act(hg_s[:], pM[3][:, 0:256], AF.Silu, [pM[3]], [hg_s])
            yield


        def delta(l, samp, last):
            kind = 'S' if samp else 'P'
            nb = 16 if samp else 1
            bs = 128 // nb
            nlev = 3 if samp else 7
            cTf, cTb = C('cT' + kind), C('cT' + kind, True)
            stf = C('st' + kind)
            smf = C('sm' + kind)
            self_ = C('sel' + kind)
            dcw = dcwb[l][:].rearrange("p (c j) -> p c j", c=12)
            for g in range(3):
                if samp:
                    xq4 = xpq[:].rearrange("p c (b t) -> p c b t", b=16)
                    xs = [xq4[:, 4 * g:4 * g + 4, :, j:j + 8] for j in range(4)]
                    ws = [dcw[:, 4 * g:4 * g + 4, j:j + 1].unsqueeze(3).to_broadcast([128, 4, 16, 8]) for j in range(4)]
                    av = dacc[:].rearrange("p q (b t) -> p q b t", b=16)
                    a2v = dacc2[:].rearrange("p q (b t) -> p q b t", b=16)
                else:
                    xs = [xpq[:, 4 * g:4 * g + 4, j:j + 128] for j in range(4)]
                    ws = [dcw[:, 4 * g:4 * g + 4, j:j + 1].to_broadcast([128, 4, 128]) for j in range(4)]
                    av, a2v = dacc[:], dacc2[:]
                vtt(av, xs[0], ws[0], ALU.mult, [xpq, dcwb[l]], [dacc])
                yield
                for j in range(1, 4):
                    vtt(a2v, xs[j], ws[j], ALU.mult, [xpq, dcwb[l]], [dacc2])
                    yield
                    vtt(av, av, a2v, ALU.add, [dacc, dacc2], [dacc])
                    yield
                act(qkv_b[:, 4 * g:4 * g + 4, :], dacc[:], AF.Silu, [dacc], [qkv_b])
                yield
            if samp:
                xq4 = xpq[:].rearrange("p c (b t) -> p c b t", b=16)
                vcopy(scq[:], xq4[:, :, :, 8:11], [xpq], [scq])
                yield
                dma('pool', o_scq[l], scq[:], r=[scq])
            else:
                acopy(cq[l][:], xpq[:, :, 128:131], [xpq], [cq[l]])
                yield
                if last:
                    dma('pool', o_pcq[l], cq[l][:], r=[cq[l]])
            vtt(sq[:], qkv_b[:, 0:8, :], qkv_b[:, 0:8, :], ALU.mult, [qkv_b], [sq])
            yield
            for c in range(8):
                mm(pM[1][:, c:c + 1], sq[:, c, :], onesb[:, 0:1], [sq, cstb], [pM[1]])
            rsqrt_small(dsc[:, 0:8], pM[1][:, 0:8], [pM[1]], [dsc], 1.0, EPS)
            vts(dsc[:, 0:4], dsc[:, 0:4], 128.0 ** -0.5, ALU.mult, [dsc], [dsc])
            yield
            for hh in range(4):
                tr(pT[:, hh, :], qkv_b[:, 4 + hh, :], identb, [qkv_b, cstb], [pT])
                tr(pT[:, 4 + hh, :], qkv_b[:, 8 + hh, :], identb, [qkv_b, cstb], [pT])
            vtt(k_tok[:], pT[:, 0:4, :], dsc[:, 4:8].unsqueeze(2).to_broadcast([128, 4, 128]), ALU.mult, [pT, dsc], [k_tok])
            yield
            acopy(v_tok[:], pT[:, 4:8, :], [pT], [v_tok])
            yield
            for hh in range(4):
                tr(pT[:, hh, :], k_tok[:, hh, :], identb, [k_tok, cstb], [pT])
            acopy(knT[:], pT[:, 0:4, :], [pT], [knT])
            yield
            act(dsc[:, 8:12], ba[:, 0:4], AF.Sigmoid, [ba], [dsc])
            yield
            vts(dsc[:, 12:16], dsc[:, 8:12], -1.0, ALU.mult, [dsc], [dsc])
            yield
            vtt(dsc[:, 40:44], ba[:, 4:8], bcv(l, 'dtb'), ALU.add, [ba, bcs[l]], [dsc])
            yield
            act(dsc[:, 40:44], dsc[:, 40:44], AF.Exp, [dsc], [dsc])
            yield
            act(dsc[:, 40:44], dsc[:, 40:44], AF.Ln, [dsc], [dsc], bias=1.0)
            yield
            vtt(dsc[:, 16:20], dsc[:, 40:44], negA[l][:], ALU.mult, [dsc, negA[l]], [dsc])
            yield
            mm(pM[1][:, 8:12], cTf, dsc[:, 16:20], [cst, dsc], [pM[1]])
            mm(pM[1][:, 12:16], smf, dsc[:, 16:20], [cst, dsc], [pM[1]])
            vcopy(dsc[:, 20:24], pM[1][:, 8:12], [pM[1]], [dsc])
            yield
            vtt(dsc[:, 40:44], pM[1][:, 12:16], dsc[:, 20:24], ALU.subtract, [pM[1], dsc], [dsc])
            yield
            act(dsc[:, 24:28], dsc[:, 40:44], AF.Exp, [dsc], [dsc])
            yield
            act(dsc[:, 28:32], dsc[:, 20:24], AF.Exp, [dsc], [dsc])
            yield
            vtt(dsc[:, 32:36], dsc[:, 8:12], dsc[:, 28:32], ALU.mult, [dsc], [dsc])
            yield
            vtt(dsc[:, 36:40], dsc[:, 28:32], dsc[:, 0:4], ALU.mult, [dsc], [dsc])
            yield
            for hh in range(4):
                vts(gsel[:, hh, 0:nb], self_[:, 0:nb], dsc[:, 16 + hh:17 + hh], ALU.mult, [cst, dsc], [gsel])
                yield
            for hh in range(4):
                mm(pM[1][:, 16 + 16 * hh:16 + 16 * hh + nb], onesf, gsel[:, hh, 0:nb], [cst, gsel], [pM[1]])
            act(glb[:, :, 0:nb], pM[1][:, 16:80].rearrange("p (h b) -> p h b", h=4)[:, :, 0:nb], AF.Exp, [pM[1]], [glb])
            yield

            def head_body(hh, R):
                vts(gB[:], onesf, dsc[:, 16 + hh:17 + hh], ALU.mult, [cst, dsc], [gB])
                mm(R.pF[0][:], gB[:], cTf, [gB, cst], [R.pF[0]])
                vts(dtmp[:], R.pF[0][:], dsc[:, 20 + hh:21 + hh], ALU.subtract, [R.pF[0], dsc], [dtmp], 0.0, ALU.max)
                act(R.dec[:], dtmp[:], AF.Exp, [dtmp], [R.dec], scale=-1.0)
                yield
                vts(dtmp[:], R.pF[0][:], dsc[:, 20 + hh:21 + hh], ALU.subtract, [R.pF[0], dsc], [dtmp], 0.0, ALU.min)
                act(R.decT[:], dtmp[:], AF.Exp, [dtmp], [R.decT])
                yield
                vtt(R.decT[:], R.decT[:], cTf, ALU.mult, [R.decT, cst], [R.decT])
                mm(R.pF[1][:], knT[:, hh, :], knT[:, hh, :], [knT], [R.pF[1]])
                vstt(Xf[:], R.pF[1][:], dsc[:, 12 + hh:13 + hh], R.dec[:], ALU.mult, ALU.mult, [R.pF[1], dsc, R.dec], [Xf])
                vtt(R.Xp[0][:], Xf[:], stf, ALU.mult, [Xf, cst], [R.Xp[0]])
                yield
                tr(R.pF[3][:], R.Xp[0][:], identf, [R.Xp[0], cst], [R.pF[3]])
                acopy(R.XpT[0][:], R.pF[3][:], [R.pF[3]], [R.XpT[0]])
                yield
                vtt(R.TT[:], R.XpT[0][:], identf, ALU.add, [R.XpT[0], cst], [R.TT])
                yield
                cur = 0
                for lev in range(1, nlev):
                    nxt = 1 - cur
                    lastlev = (lev == nlev - 1)
                    mm(R.pF[2][:], R.XpT[cur][:], R.Xp[cur][:], [R.XpT[cur], R.Xp[cur]], [R.pF[2]])
                    vcopy(R.Xp[nxt][:], R.pF[2][:], [R.pF[2]], [R.Xp[nxt]])
                    yield
                    if not lastlev:
                        mm(R.pF[3][:], R.Xp[cur][:], R.XpT[cur][:], [R.XpT[cur], R.Xp[cur]], [R.pF[3]])
                        acopy(R.XpT[nxt][:], R.pF[3][:], [R.pF[3]], [R.XpT[nxt]])
                        yield
                    mm(R.pF[1][:], R.Xp[nxt][:], R.TT[:], [R.Xp[nxt], R.TT], [R.pF[1]])
                    vtt(R.TT[:], R.TT[:], R.pF[1][:], ALU.add, [R.TT, R.pF[1]], [R.TT])
                    yield
                    cur = nxt
                vts(R.Ru[:], v_tok[:, hh, :], dsc[:, 8 + hh:9 + hh], ALU.mult, [v_tok, dsc], [R.Ru])
                vts(R.Rw[:], k_tok[:, hh, :], dsc[:, 32 + hh:33 + hh], ALU.mult, [k_tok, dsc], [R.Rw])
                mm(R.pU[:, 0:128], R.TT[:], R.Ru[:], [R.TT, R.Ru], [R.pU])
                mm(R.pU[:, 128:256], R.Rw[:], R.TT[:], [R.TT, R.Rw], [R.pU])
                acopy(R.u_sb[:], R.pU[:, 0:128], [R.pU], [R.u_sb])
                yield
                acopy(R.wT_b[:], R.pU[:, 128:256], [R.pU], [R.wT_b])
                yield
                mm(R.pF[2][:], knT[:, hh, :], qkv_b[:, hh, :], [knT, qkv_b], [R.pF[2]])
                vtt(R.qkTm[:], R.pF[2][:], R.decT[:], ALU.mult, [R.pF[2], R.decT], [R.qkTm])
                yield
                if samp:
                    dma('pool', sS[:], s_dl[l, :, hh].rearrange("b k v -> k b v"), w=[sS])
                    acopy(sS_b[:], sS[:], [sS], [sS_b])
                    yield
                p33 = R.pP[:, 256:512].rearrange("p (q t) -> p q t", q=2)
                for b in range(nb):
                    Sb = sS_b[:, b, :] if samp else Sd_b[l][:, hh, :]
                    Sbt = sS_b if samp else Sd_b[l]
                    mm(p33[:, 0, b * bs:(b + 1) * bs], Sb, R.wT_b[:, b * bs:(b + 1) * bs], [Sbt, R.wT_b], [R.pP])
                    mm(p33[:, 1, b * bs:(b + 1) * bs], Sb, qkv_b[:, hh, b * bs:(b + 1) * bs], [Sbt, qkv_b], [R.pP])
                acopy(R.wq_sb[:], p33, [R.pP], [R.wq_sb])
                yield
                tr(R.pF[0][:], R.wq_sb[:, 0, :], identf, [R.wq_sb, cst], [R.pF[0]])
                tr(R.pF[1][:], R.wq_sb[:, 1, :], identf, [R.wq_sb, cst], [R.pF[1]])
                vtt(R.vnew[:], R.u_sb[:], R.pF[0][:], ALU.subtract, [R.u_sb, R.pF[0]], [R.vnew])
                yield
                act(R.t1s[:], R.pF[1][:], AF.Identity, [R.pF[1], dsc], [R.t1s], scale=dsc[:, 36 + hh:37 + hh])
                yield
                mm(R.pF[2][:], R.qkTm[:], R.vnew[:], [R.qkTm, R.vnew], [R.pF[2]])
                vstt(oA[:, hh, :], R.pF[2][:], dsc[:, hh:hh + 1], R.t1s[:], ALU.mult, ALU.add, [R.pF[2], dsc, R.t1s], [oA])
                yield
                vts(R.kdec[:], k_tok[:, hh, :], dsc[:, 24 + hh:25 + hh], ALU.mult, [k_tok, dsc], [R.kdec])
                if samp:
                    for b in range(nb):
                        pd = R.pF[b % 2]
                        km = kdm[b % 2]
                        vts(km[:], R.kdec[:], self_[:, b:b + 1], ALU.mult, [R.kdec, cst], [km])
                        mm(pd[:], km[:], R.vnew[:], [km, R.vnew], [pd])
                        vstt(sS[:, b, :], sS[:, b, :], glb[:, hh, b:b + 1], pd[:], ALU.mult, ALU.add, [sS, glb, pd], [sS])
                        yield
                    dma('pool', o_sdl[l, :, hh].rearrange("b k v -> k b v"), sS[:], r=[sS])
                else:
                    mm(R.pF[0][:], R.kdec[:], R.vnew[:], [R.kdec, R.vnew], [R.pF[0]])
                    vstt(Sd[l][:, hh, :], Sd[l][:, hh, :], glb[:, hh, 0:1], R.pF[0][:], ALU.mult, ALU.add, [Sd[l], glb, R.pF[0]], [Sd[l]])
                    yield
                    acopy(Sd_b[l][:, hh, :], Sd[l][:, hh, :], [Sd[l]], [Sd_b[l]])
                    yield
                    if last:
                        dma('pool', o_pdl[l, hh], Sd[l][:, hh, :], r=[Sd[l]])
            def run_heads(gens):
                gens = list(gens)
                while gens:
                    for g_ in list(gens):
                        try:
                            next(g_)
                        except StopIteration:
                            gens.remove(g_)
            if samp:
                for hh in range(4):
                    run_heads([head_body(hh, RS0)])
            else:
                run_heads([head_body(0, RS0), head_body(1, RS1)])
                run_heads([head_body(2, RS0), head_body(3, RS1)])
            V(lambda e: e.memset(dsc[:, 44:48], 0.0), [], [dsc])
            for hh in range(4):
                act(dtmp[:], oA[:, hh, :], AF.Square, [oA], [dtmp, dsc], accum=dsc[:, 44 + hh:45 + hh])
            rsqrt_small(dsc[:, 44:48], dsc[:, 44:48], [dsc], [dsc], 1.0 / 128, EPS)
            for hh in range(4):
                vstt(oA[:, hh, :], oA[:, hh, :], dsc[:, 44 + hh:45 + hh], bcv(l, 'dnn'), ALU.mult, ALU.mult, [oA, dsc, bcs[l]], [oA])
            vtt(oab[:], oA[:], gate_s[:].rearrange("p (h d) -> p h d", h=4), ALU.mult, [oA, gate_s], [oab])
            for hh in range(4):
                tr(pT[:, hh, :], oab[:, hh, :], identb, [oab, cstb], [pT])
            acopy(ocatT[:, 0:4, :], pT[:, 0:4, :], [pT], [ocatT])
        def s5(l, samp, last):
            kind = 'S' if samp else 'P'
            nb = 16 if samp else 1
            bs = 128 // nb
            cTb = C('cT' + kind, True)
            Lmv = Lm[l]
            if samp:
                for r in range(2):
                    dma('pool', x0s[r][:], s_ss[l, r], w=[x0s[r]])
                for (dst, a_, b_, op) in ((cinS[0], 0, 1, ALU.subtract), (cinS[1], 1, 0, ALU.add)):
                    vtt(xe[0][:], x0s[0][:], lbar[l][a_][:].unsqueeze(2).to_broadcast([128, 8, 16]), ALU.mult, [x0s[0], lbar[l][a_]], [xe[0]])
                    yield
                    vtt(xe[1][:], x0s[1][:], lbar[l][b_][:].unsqueeze(2).to_broadcast([128, 8, 16]), ALU.mult, [x0s[1], lbar[l][b_]], [xe[1]])
                    yield
                    vtt(dst[:], xe[0][:], xe[1][:], op, [xe[0], xe[1]], [dst])
                    yield
                cin = cinS
            else:
                cin = cinP[l]
            for cc in range(2):
                hs = slice(cc * 512, (cc + 1) * 512)
                mm(pM[0][:], uT[:, cc, :], bft[l][0][:, cc, :], [uT, bft[l][0]], [pM[0]])
                mm(pM[1][:], uT[:, cc, :], bft[l][1][:, cc, :], [uT, bft[l][1]], [pM[1]])
                vtt(tA[:, 0:512], pM[0][:], Lmv[0][:, hs], ALU.mult, [pM[0], Lmv[0]], [tA])
                yield
                vtt(tB_[:, 0:512], pM[1][:], Lmv[1][:, hs], ALU.mult, [pM[1], Lmv[1]], [tB_])
                yield
                vtt(Wre[:, hs], tA[:, 0:512], tB_[:, 0:512], ALU.subtract, [tA, tB_], [Wre])
                yield
                vtt(tA[:, 0:512], pM[1][:], Lmv[0][:, hs], ALU.mult, [pM[1], Lmv[0]], [tA])
                yield
                vtt(tB_[:, 0:512], pM[0][:], Lmv[1][:, hs], ALU.mult, [pM[0], Lmv[1]], [tB_])
                yield
                vtt(Wim[:, hs], tA[:, 0:512], tB_[:, 0:512], ALU.add, [tA, tB_], [Wim])
                yield
                for jj in range(4):
                    j = 4 * cc + jj
                    mm(pM[2][:, jj * 128:(jj + 1) * 128], Wre[:, j * 128:(j + 1) * 128], cTb, [Wre, cstb], [pM[2]])
                    mm(pM[4][:, jj * 128:(jj + 1) * 128], Wim[:, j * 128:(j + 1) * 128], cTb, [Wim, cstb], [pM[4]])
                js = slice(4 * cc, 4 * cc + 4)

                def v4(ap):
                    return ap.rearrange("p (j b t) -> p j b t", j=4, b=nb)

                def v4b(ap3):
                    return ap3.rearrange("p j (b t) -> p j b t", b=nb)
                ar, ai = tC, tD
                vtt(v4(ar[:, 0:512]), v4(pM[2][:]), cin[0][:, js, :].unsqueeze(3).to_broadcast([128, 4, nb, bs]), ALU.add, [pM[2], cin[0]], [ar])
                yield
                vtt(v4(ai[:, 0:512]), v4(pM[4][:]), cin[1][:, js, :].unsqueeze(3).to_broadcast([128, 4, nb, bs]), ALU.add, [pM[4], cin[1]], [ai])
                yield
                if samp:
                    Lr = LpT[l][0][:, js, 0:8].unsqueeze(2).to_broadcast([128, 4, 16, 8])
                    Li = LpT[l][1][:, js, 0:8].unsqueeze(2).to_broadcast([128, 4, 16, 8])
                else:
                    Lr = v4b(LpT[l][0][:, js, :])
                    Li = v4b(LpT[l][1][:, js, :])
                vtt(v4(tA[:, 0:512]), v4(ar[:, 0:512]), Lr, ALU.mult, [ar, LpT[l][0]], [tA])
                yield
                vtt(v4(tB_[:, 0:512]), v4(ai[:, 0:512]), Li, ALU.mult, [ai, LpT[l][1]], [tB_])
                yield
                vtt(v4b(xre[:, js, :]), v4(tA[:, 0:512]), v4(tB_[:, 0:512]), ALU.subtract, [tA, tB_], [xre])
                yield
                vtt(v4(tA[:, 0:512]), v4(ar[:, 0:512]), Li, ALU.mult, [ar, LpT[l][1]], [tA])
                yield
                vtt(v4(tB_[:, 0:512]), v4(ai[:, 0:512]), Lr, ALU.mult, [ai, LpT[l][0]], [tB_])
                yield
                vtt(v4b(xim[:, js, :]), v4(tA[:, 0:512]), v4(tB_[:, 0:512]), ALU.add, [tA, tB_], [xim])
                yield
            acopy(xbre[:], xre[:], [xre], [xbre])
            yield
            acopy(xbim[:], xim[:], [xim], [xbim])
            yield
            xr4 = xre[:].rearrange("p j (b t) -> p j b t", b=nb)
            xi4 = xim[:].rearrange("p j (b t) -> p j b t", b=nb)
            vcopy(xe[0][:, :, 0:nb], xr4[:, :, :, bs - 1], [xre], [xe[0]])
            yield
            vcopy(xe[1][:, :, 0:nb], xi4[:, :, :, bs - 1], [xim], [xe[1]])
            yield
            if samp:
                for r in range(2):
                    dma('pool', o_sss[l, r], xe[r][:], r=[xe[r]])
            else:
                for r in range(2):
                    vcopy(xep[r][:], xe[r][:, :, 0], [xe[r]], [xep[r]])
                    yield
                if last:
                    for r in range(2):
                        dma('pool', o_pss[l, r], xep[r][:], r=[xep[r]])
                vtt(tA[:, 0:8], xep[0][:], lbar[l][0][:], ALU.mult, [xep[0], lbar[l][0]], [tA])
                yield
                vtt(tB_[:, 0:8], xep[1][:], lbar[l][1][:], ALU.mult, [xep[1], lbar[l][1]], [tB_])
                yield
                vtt(cinP[l][0][:, :, 0], tA[:, 0:8], tB_[:, 0:8], ALU.subtract, [tA, tB_], [cinP[l][0]])
                yield
                vtt(tA[:, 0:8], xep[0][:], lbar[l][1][:], ALU.mult, [xep[0], lbar[l][1]], [tA])
                yield
                vtt(tB_[:, 0:8], xep[1][:], lbar[l][0][:], ALU.mult, [xep[1], lbar[l][0]], [tB_])
                yield
                vtt(cinP[l][1][:, :, 0], tA[:, 0:8], tB_[:, 0:8], ALU.add, [tA, tB_], [cinP[l][1]])
                yield
            py = pM[0]
            for cc in range(2):
                mm(py[:, cc * 128:(cc + 1) * 128], uT[:, cc, :], diagD[l][:, cc, :], [uT, diagD[l]], [py], start=True, stop=False)
                for jj in range(4):
                    j = 4 * cc + jj
                    mm(py[:, j * 32:(j + 1) * 32], xbre[:, j, :], cft[l][0][:, j, :], [xbre, cft[l][0]], [py], start=False, stop=False)
                    mm(py[:, j * 32:(j + 1) * 32], xbim[:, j, :], cft[l][1][:, j, :], [xbim, cft[l][1]], [py], start=False, stop=(jj == 3))
            acopy(yb0[:], py[:, 0:256], [py], [yb0])
            yield
            vtt(yb[:], yb0[:], yb0[:], ALU.mult, [yb0], [yb])
            yield
            vts(yb[:], yb[:], 0.044715, ALU.mult, [yb], [yb], 1.0, ALU.add)
            yield
            vtt(yb[:], yb[:], yb0[:], ALU.mult, [yb, yb0], [yb])
            yield
            act(yb[:], yb[:], AF.Tanh, [yb], [yb], scale=0.7978845608028654)
            yield
            vstt(yb[:], yb[:], 1.0, yb0[:], ALU.add, ALU.mult, [yb, yb0], [yb])
            yield
            vts(yb[:], yb[:], 0.5, ALU.mult, [yb], [yb])
            yield
            for cc in range(2):
                tr(pF[2 + cc][:], yb[:, cc * 128:(cc + 1) * 128], identf, [yb, cst], [pF[2 + cc]])
                acopy(ybT[:, cc, :], pF[2 + cc][:], [pF[2 + cc]], [ybT])
                yield
            vcopy(ybTb[:], ybT[:], [ybT], [ybTb])
            yield
            for c2 in range(2):
                for cc in range(2):
                    mm(pF[2 + c2][:], glw[l][:, cc, c2 * 128:(c2 + 1) * 128], ybTb[:, cc, :], [glw[l], ybTb], [pF[2 + c2]], start=(cc == 0), stop=(cc == 1))
                act(sgT[:], pF[2 + c2][:], AF.Sigmoid, [pF[2 + c2], ppt[l]], [sgT], bias=pp(l, 'glub', c2, c2 + 1))
                yield
                vtt(ocatT[:, 4 + c2, :], ybT[:, c2, :], sgT[:], ALU.mult, [ybT, sgT], [ocatT])
                yield

        def hgrn(l, samp, last):
            kind = 'S' if samp else 'H'
            nb = 16 if samp else 4
            bs = 128 // nb
            cTf, cTb = C('cT' + kind), C('cT' + kind, True)
            smf = C('sm' + kind)
            self_ = C('sel' + kind)
            pTf = pT[:].rearrange("p c t -> p (c t)").bitcast(F32)
            pTfb = pT
            vtt(fto[:], fto[:], omlb[l][:], ALU.mult, [fto, omlb[l]], [fto])
            yield
            vtt(fto[:], fto[:], lbb[l][:], ALU.add, [fto, lbb[l]], [fto])
            yield
            act(logf[:], fto[:], AF.Ln, [fto], [logf])
            yield
            vts(omf[:], fto[:], -1.0, ALU.mult, [fto], [omf], 1.0, ALU.add)
            yield
            for cc in range(2):
                vts(fT[:, cc, :], fT[:, cc, :], omlf[l][:, cc:cc + 1], ALU.mult, [fT, omlf[l], lbf[l]], [fT], lbf[l][:, cc:cc + 1], ALU.add)
                yield
            vts(fT[:], fT[:], -1.0, ALU.mult, [fT], [fT], 1.0, ALU.add)
            yield
            mm(pM[5][:, 0:256], cTf, logf[:], [cst, logf], [pM[5]])
            mm(pM[5][:, 256:512], smf, logf[:], [cst, logf], [pM[5]])
            pbT = pTf[:, 0:256].rearrange("p (c t) -> p c t", c=2)
            for cc in range(2):
                mm(pbT[:, cc, :], logf[:, cc * 128:(cc + 1) * 128], cTf, [logf, cst], [pTfb])
                mm(pTf[:, 256 + 16 * cc:256 + 16 * cc + nb], logf[:, cc * 128:(cc + 1) * 128], self_[:, 0:nb], [logf, cst], [pTfb])
            act(e1[:], pbT, AF.Exp, [pTfb], [e1])
            yield
            vtt(qtT[:], hqTs[:], e1[:], ALU.mult, [hqTs, e1], [qtT])
            yield
            act(e1[:], pbT, AF.Exp, [pTfb], [e1], scale=-1.0)
            yield
            vtt(ktT[:], fT[:], e1[:], ALU.mult, [fT, e1], [ktT])
            yield
            act(glh[:, :, 0:nb], pTf[:, 256:288].rearrange("p (c b) -> p c b", c=2)[:, :, 0:nb], AF.Exp, [pTfb], [glh])
            yield
            acopy(b_sb[:], pM[5][:, 0:256], [pM[5]], [b_sb])
            yield
            vtt(b_sb[:], pM[5][:, 256:512], b_sb[:], ALU.subtract, [pM[5], b_sb], [b_sb])
            yield
            act(b_sb[:], b_sb[:], AF.Exp, [b_sb], [b_sb])
            yield
            vtt(khat[:], omf[:], b_sb[:], ALU.mult, [omf, b_sb], [khat])
            yield
            pa = pTf[:, 0:512].rearrange("p (h t) -> p h t", h=4)
            for hh in range(4):
                hl, hc = hh % 2, hh // 2
                mm(pa[:, hh, :], ktT[hl * 64:(hl + 1) * 64, hc, :], qtT[hl * 64:(hl + 1) * 64, hc, :], [ktT, qtT], [pTfb], sync=True)
            vtt(aTm[:], pa, cTf.unsqueeze(1).to_broadcast([128, 4, 128]), ALU.mult, [pTfb, cst], [aTm])
            yield
            po = pM[3][0:64, :].rearrange("p (h t) -> p h t", h=4)
            for hh in range(4):
                mm(po[:, hh, :], v_h[:, hh * 64:(hh + 1) * 64], aTm[:, hh, :], [v_h, aTm], [pM[3]], start=(hh == 0), stop=False, skip=True, sync=True)
            if samp:
                for hc_ in range(2):
                    dma('pool', Shs[:, hc_], s_hg[l][:, 2 * hc_:2 * hc_ + 2].rearrange("b hl k v -> (hl k) b v"), w=[Shs])
                acopy(Shs_b[:], Shs[:], [Shs], [Shs_b])
                yield
            for j in range(nb):
                for hh in range(4):
                    hl, hc = hh % 2, hh // 2
                    if samp:
                        Sb, Sbt = Shs_b[hl * 64:(hl + 1) * 64, hc, j, :], Shs_b
                    else:
                        Sb, Sbt = Sh_b[l][hl * 64:(hl + 1) * 64, hc, :], Sh_b[l]
                    mm(po[:, hh, j * bs:(j + 1) * bs], Sb, qtT[hl * 64:(hl + 1) * 64, hc, j * bs:(j + 1) * bs], [Sbt, qtT], [pM[3]], start=False, stop=True, skip=True, sync=True)
                kh = khm[j % 2]
                vts(kh[:], khat[:], self_[:, j:j + 1], ALU.mult, [khat, cst], [kh])
                yield
                for cc in range(2):
                    pd = pF[cc]
                    mm(pd[:], kh[:, cc * 128:(cc + 1) * 128], v_h[:, cc * 128:(cc + 1) * 128], [kh, v_h], [pd], sync=(cc == 0))
                    for hl in range(2):
                        ps_ = slice(hl * 64, (hl + 1) * 64)
                        if samp:
                            vstt(Shs[ps_, cc, j, :], Shs[ps_, cc, j, :], glh[ps_, cc, j:j + 1], pd[ps_, hl * 64:(hl + 1) * 64], ALU.mult, ALU.add, [Shs, glh, pd], [Shs])
                            yield
                        else:
                            vstt(Sh[l][ps_, cc, :], Sh[l][ps_, cc, :], glh[ps_, cc, j:j + 1], pd[ps_, hl * 64:(hl + 1) * 64], ALU.mult, ALU.add, [Sh[l], glh, pd], [Sh[l]])
                            yield
                if not samp:
                    acopy(Sh_b[l][:], Sh[l][:], [Sh[l]], [Sh_b[l]])
                    yield
            if samp:
                dma('pool', o_shg[l], Shs[:], r=[Shs])
            elif last:
                dma('pool', o_phg[l], Sh[l][:], r=[Sh[l]])
            acopy(oT_sb[:], po, [pM[3]], [oT_sb])
            yield
            poc = pM[5]
            for hh in range(4):
                tr(poc[:, hh * 64:(hh + 1) * 64], oT_sb[:, hh, :], identf[0:64, 0:64], [oT_sb, cst], [poc])
            V(lambda e: e.memset(dsc[:, 48:52], 0.0), [], [dsc])
            for hh in range(4):
                act(dtmp[:, 0:64], poc[:, hh * 64:(hh + 1) * 64], AF.Square, [poc], [dtmp, dsc], accum=dsc[:, 48 + hh:49 + hh])
                yield
            rsqrt_small(dsc[:, 48:52], dsc[:, 48:52], [dsc], [dsc], 1.0 / 64, EPS)
            for hh in range(4):
                vstt(oc[:, hh * 64:(hh + 1) * 64], poc[:, hh * 64:(hh + 1) * 64], dsc[:, 48 + hh:49 + hh], bcv(l, 'hgn'), ALU.mult, ALU.mult, [poc, dsc, bcs[l]], [oc])
                yield
            vtt(ocb[:], oc[:], hg_s[:], ALU.mult, [oc, hg_s], [ocb])
            yield
            for cc in range(2):
                tr(pT[:, cc, :], ocb[:, cc * 128:(cc + 1) * 128], identb, [ocb, cstb], [pT])
            acopy(ocatT[:, 6:8, :], pT[:, 0:2, :], [pT], [ocatT])
            yield

        def out_proj(l):
            for half in range(2):
                slot, wv = wload(('out', l, half))
                pm = pM[4 + half]
                for kc in range(8):
                    mm(pm[:], ocatT[:, kc, :], wv[:, kc, :], [ocatT, slot], [pm], start=(kc == 0), stop=(kc == 7))
                hs = slice(half * 512, (half + 1) * 512)
                vtt(h[:, hs], h[:, hs], pm[:], ALU.add, [h, pm], [h])

        def ffn(l, samp, last):
            norm_T(pp(l, 'nffn'), ppt[l])
            if samp:
                dma('pool', scf[:], s_cf[l], w=[scf])
            def views(blk):
                xpf, acc, acc2, sa = FB[blk % 2]
                o_f = PP['fcw'][0] + blk * 12
                cw = ppt[l][:, o_f:o_f + 12].rearrange("p (q j) -> p q j", q=4)
                cwh = cwb[l][:, blk * 12:blk * 12 + 12].rearrange("p (q j) -> p q j", q=4)
                if not samp:
                    xpf3 = xpf[:, :, 0:130]
                    xs = [xpf3[:, :, j:j + 128] for j in range(3)]
                    ws = [cwh[:, :, j:j + 1].to_broadcast([128, 4, 128]) for j in range(3)]
                    av, a2v = acc[:], acc2[:]
                    cin_v, new_v, cout_v = xpf3[:, :, 0:2], xpf3[:, :, 2:130], xpf3[:, :, 128:130]
                else:
                    xpf4 = xpf[:].rearrange("p q (b t) -> p q b t", b=16)
                    xs = [xpf4[:, :, :, j:j + 8] for j in range(3)]
                    ws = [cwh[:, :, j:j + 1].unsqueeze(3).to_broadcast([128, 4, 16, 8]) for j in range(3)]
                    av = acc[:].rearrange("p q (b t) -> p q b t", b=16)
                    a2v = acc2[:].rearrange("p q (b t) -> p q b t", b=16)
                    cin_v, new_v, cout_v = xpf4[:, :, :, 0:2], xpf4[:, :, :, 2:10], xpf4[:, :, :, 8:10]
                return xpf, acc, acc2, sa, cw, xs, ws, av, a2v, cin_v, new_v, cout_v

            def front(blk):
                xpf, acc, acc2, sa, cw, xs, ws, av, a2v, cin_v, new_v, cout_v = views(blk)
                slot, wv = wload(('up', l, blk))
                pm = pM[blk % 2]
                pm3 = pm[:].rearrange("p (q t) -> p q t", q=4)
                for q in range(4):
                    for kc in range(8):
                        mm(pm3[:, q, :], wv[:, kc, q * 128:(q + 1) * 128], hnT[:, kc, :], [slot, hnT], [pm], start=(kc == 0), stop=(kc == 7))
                if not samp:
                    acopy(cin_v, cfc[l][:, blk, :, :], [cfc[l]], [xpf])
                    acopy(new_v, pm3, [pm], [xpf])
                else:
                    acopy(cin_v, scf[:, blk], [scf], [xpf])
                    acopy(new_v, pm[:].rearrange("p (q b t) -> p q b t", q=4, b=16), [pm], [xpf])
                for q in range(4):
                    act(a2v[:, q], xs[1][:, q], AF.Identity, [xpf, ppt[l]], [acc2], scale=cw[:, q, 1:2])

            def conv(blk):
                xpf, acc, acc2, sa, cw, xs, ws, av, a2v, cin_v, new_v, cout_v = views(blk)
                vtt(av, xs[0], ws[0], ALU.mult, [xpf, cwb[l]], [acc])
                vtt(av, av, a2v, ALU.add, [acc, acc2], [acc])
                vtt(a2v, xs[2], ws[2], ALU.mult, [xpf, cwb[l]], [acc2])
                vtt(av, av, a2v, ALU.add, [acc, acc2], [acc])
                if not samp:
                    acopy(cfc[l][:, blk, :, :], cout_v, [xpf], [cfc[l]])
                else:
                    acopy(scf[:, blk], cout_v, [xpf], [scf])

            def back_silu(blk):
                xpf, acc, acc2, sa = FB[blk % 2]
                act(sa[:], acc[:, 0:2, :], AF.Silu, [acc], [sa])

            def back_mult(blk):
                xpf, acc, acc2, sa = FB[blk % 2]
                vtt(gT[:, 2 * blk:2 * blk + 2, :], sa[:], acc[:, 2:4, :], ALU.mult, [sa, acc], [gT])
            front(0)
            conv(0)
            for blk in range(11):
                if blk + 1 < 11:
                    front(blk + 1)
                back_silu(blk)
                if blk + 1 < 11:
                    conv(blk + 1)
                back_mult(blk)
            if samp:
                dma('pool', o_scf[l], scf[:], r=[scf])
            elif last:
                dma('pool', o_pcf[l], cfc[l][:], r=[cfc[l]])
            for half in range(2):
                pm = pM[2 + half]
                for q4 in range(4):
                    slot, wv = wload(('down', l, half, q4))
                    n_c = 6 if q4 < 3 else 4
                    c0 = q4 * 6
                    for c in range(n_c):
                        mm(pm[:], gT[:, c0 + c, :], wv[:, c, :], [gT, slot], [pm], start=(c0 + c == 0), stop=(c0 + c == 21))
                hs = slice(half * 512, (half + 1) * 512)
                vtt(h[:, hs], h[:, hs], pm[:], ALU.add, [h, pm], [h])

        def ple(l, si):
            norm_T(pp(l, 'nple'), ppt[l])
            dma('pool', pf[:], pin[l, si], w=[pf])
            vcopy(pb[:], pf[:], [pf], [pb])
            for half in range(2):
                slot, wv = wload(('pg', l, half))
                pm = pM[half]
                for kc in range(8):
                    mm(pm[:], hnT[:, kc, :], wv[:, kc, :], [hnT, slot], [pm], start=(kc == 0), stop=(kc == 7))
                act(gsbh[half][:], pm[:], AF.Sigmoid, [pm], [gsbh[half]])
            for cc in range(2):
                tr(pT[:, cc, :], pb[:, cc * 128:(cc + 1) * 128], identb, [pb, cstb], [pT])
            vcopy(ppT[:], pT[:, 0:2, :], [pT], [ppT])
            for half in range(2):
                slot, wv = wload(('pp', l, half))
                pm = pM[2 + half]
                for cc in range(2):
                    mm(pm[:], ppT[:, cc, :], wv[:, cc, :], [ppT, slot], [pm], start=(cc == 0), stop=(cc == 1))
                hs = slice(half * 512, (half + 1) * 512)
                vtt(tmp1[:], gsbh[half][:], pm[:], ALU.mult, [gsbh[half], pm], [tmp1])
                vtt(h[:, hs], h[:, hs], tmp1[:], ALU.add, [h, tmp1], [h])

        for si in range(NSUB):
            samp = (si == NSUB - 1)
            last = (si == NSUB - 2)
            if samp:
                for l in range(2):
                    build_tables(l, True)
            dma('pool', h[:], xin[si], w=[h])
            for l in range(2):
                in_proj(l, samp)
                gens_ = [in_proj_rest(l, samp), delta(l, samp, last)]
                while gens_:
                    for g_ in list(gens_):
                        try:
                            next(g_)
                        except StopIteration:
                            gens_.remove(g_)
                gens_ = [s5(l, samp, last), hgrn(l, samp, last)]
                while gens_:
                    for g_ in list(gens_):
                        try:
                            next(g_)
                        except StopIteration:
                            gens_.remove(g_)
                if KSTAGE >= 5:
                    out_proj(l)
                ffn(l, samp, last)
                ple(l, si)
            norm_stats()
            for half in range(2):
                hs = slice(half * 512, (half + 1) * 512)
                vstt(yoh[half][:], h[:, hs], st[:, 3:4], nfin[:, hs], ALU.mult, ALU.mult, [h, st, nfin], [yoh[half]])
                dma('pool', y_d[si][:, hs], yoh[half][:], r=[yoh[half]])
        S.finish()
        with nc.Block() as block:
            S.emit(nc, block)
    return nc


def _fm(v, nchunk):
    return np.ascontiguousarray(v.reshape(nchunk, 128).T)


_NC_CACHE = {}


def _prepare(inputs):
    f = {k: np.asarray(v, dtype=np.float32) for k, v in inputs.items()}
    n = 8
    pps, bcs = [], []
    for l in range(2):
        pp = np.zeros((128, NPP), np.float32)

        def put(name, arr):
            o, w = PP[name]
            pp[:, o:o + w] = arr.reshape(128, w)
        put('nmix', _fm(f['norm_mix'][l], 8))
        put('nffn', _fm(f['norm_ffn'][l], 8))
        put('nple', _fm(f['norm_ple'][l], 8))
        put('dcw', f['dn_conv_w'][l].reshape(4, 12, 128).transpose(2, 1, 0))
        fw = f['ffn_conv_w'][l].reshape(3, 2, 11, 2, 128)
        put('fcw', fw.transpose(4, 2, 1, 3, 0))
        put('lre', _fm(f['ssm_lam_re'][l].reshape(-1), 8))
        put('lim', _fm(f['ssm_lam_im'][l].reshape(-1), 8))
        put('lst', _fm(np.repeat(f['ssm_log_step'][l], 64), 8))
        put('ssd', _fm(f['ssm_d'][l], 2))
        put('glub', _fm(f['ssm_glu_b'][l], 2))
        put('hl0', _fm(f['hg_lower'][0], 2))
        put('hl1', _fm(f['hg_lower'][1], 2))
        pps.append(pp)
        bc = np.zeros((NBC,), np.float32)

        def putb(name, arr):
            o, w = BC[name]
            bc[o:o + w] = arr.reshape(w)
        putb('dnn', f['dn_norm'][l])
        putb('hgn', f['hg_norm'][l])
        putb('alog', f['dn_a_log'][l])
        putb('dtb', f['dn_dt_bias'][l])
        putb('lre', f['ssm_lam_re'][l])
        putb('lim', f['ssm_lam_im'][l])
        putb('lst', np.repeat(f['ssm_log_step'][l], 64))
        putb('hl0', f['hg_lower'][0])
        putb('hl1', f['hg_lower'][1])
        putb('nfin', f['norm_final'])
        bcs.append(bc)
    pp_all = np.stack(pps)
    bc_all = np.stack(bcs)
    bfull = np.zeros((2, 2, 128, 2, 512), np.float32)
    cfull = np.zeros((2, 2, 128, 8, 32), np.float32)
    for l in range(2):
        for r, (bn, cn) in enumerate((('ssm_b_re', 'ssm_c_re'), ('ssm_b_im', 'ssm_c_im'))):
            for g in range(16):
                cc, gg = divmod(g, 8)
                j, gl = divmod(g, 2)
                bfull[l, r, gg * 16:(gg + 1) * 16, cc, (j % 4) * 128 + gl * 64:(j % 4) * 128 + gl * 64 + 64] = f[bn][l, g].T
                cfull[l, r, gl * 64:(gl + 1) * 64, j, gl * 16:(gl + 1) * 16] = f[cn][l, g].T
    in_maps = []
    for c in range(n):
        xs = np.concatenate([f['x_prompt'][c].reshape(16, 128, 1024),
                             f['x_sample'][c * 16:(c + 1) * 16].reshape(1, 128, 1024)], axis=0)
        ps_ = np.concatenate([f['p_prompt'][:, c].reshape(2, 16, 128, 256),
                              f['p_sample'][:, c * 16:(c + 1) * 16].reshape(2, 1, 128, 256)], axis=1)
        sl = slice(c * 16, (c + 1) * 16)
        s_cq = f['state_conv_qkv'][:, sl].reshape(2, 16, 3, 12, 128).transpose(0, 4, 3, 1, 2)
        s_ss = np.stack([f['state_ssm_re'][:, sl], f['state_ssm_im'][:, sl]], axis=1).reshape(2, 2, 16, 8, 128).transpose(0, 1, 4, 3, 2)
        s_cf = f['state_conv_ffn'][:, sl].reshape(2, 16, 2, 2, 11, 2, 128).transpose(0, 6, 4, 3, 5, 1, 2).reshape(2, 128, 11, 4, 16, 2)
        in_maps.append({
            "xin": np.ascontiguousarray(xs), "pin": np.ascontiguousarray(ps_), "consts": CONSTS, "pp": pp_all, "bc": bc_all,
            "w_in": f['w_in'], "w_out": f['w_out'], "w_up": f['ffn_w_up'], "w_down": f['ffn_w_down'],
            "w_pg": f['ple_w_gate'], "w_pp": f['ple_w_proj'], "w_glu": f['ssm_glu_w'], "bfull": bfull, "cfull": cfull,
            "s_cq": np.ascontiguousarray(s_cq), "s_dl": np.ascontiguousarray(f['state_delta'][:, sl]),
            "s_ss": np.ascontiguousarray(s_ss), "s_hg": np.ascontiguousarray(f['state_hgrn'][:, sl]),
            "s_cf": np.ascontiguousarray(s_cf),
        })
    return in_maps


def kernel(**inputs):
    n = 8
    in_maps = _prepare(inputs)
    if 'nc' not in _NC_CACHE:
        _NC_CACHE['nc'] = build()
    res = run_bass_kernel_spmd(_NC_CACHE['nc'], in_maps, core_ids=list(range(n))).results
    yp = np.stack([r["y"][:16].reshape(2048, 1024) for r in res])
    ys = np.concatenate([r["y"][16].reshape(16, 8, 1024) for r in res], axis=0)

    def cat(fn, axis=1):
        return np.ascontiguousarray(np.concatenate([fn(r) for r in res], axis=axis))
    p_cq = cat(lambda r: r["o_pcq"].transpose(0, 3, 2, 1).reshape(2, 1, 3, 1536))
    p_dl = cat(lambda r: r["o_pdl"].reshape(2, 1, 4, 128, 128))
    p_sr = cat(lambda r: r["o_pss"][:, 0].transpose(0, 2, 1).reshape(2, 1, 16, 64))
    p_si = cat(lambda r: r["o_pss"][:, 1].transpose(0, 2, 1).reshape(2, 1, 16, 64))
    p_hg = cat(lambda r: r["o_phg"].reshape(2, 2, 64, 2, 64).transpose(0, 3, 1, 2, 4).reshape(2, 1, 4, 64, 64))
    p_cf = cat(lambda r: r["o_pcf"].reshape(2, 128, 11, 2, 2, 2).transpose(0, 5, 3, 2, 4, 1).reshape(2, 1, 2, 5632))
    s_cq = cat(lambda r: r["o_scq"].transpose(0, 3, 4, 2, 1).reshape(2, 16, 3, 1536))
    s_dl = cat(lambda r: r["o_sdl"])
    s_sr = cat(lambda r: r["o_sss"][:, 0].transpose(0, 3, 2, 1).reshape(2, 16, 16, 64))
    s_si = cat(lambda r: r["o_sss"][:, 1].transpose(0, 3, 2, 1).reshape(2, 16, 16, 64))
    s_hg = cat(lambda r: r["o_shg"].reshape(2, 2, 64, 2, 16, 64).transpose(0, 4, 3, 1, 2, 5).reshape(2, 16, 4, 64, 64))
    s_cf = cat(lambda r: r["o_scf"].reshape(2, 128, 11, 2, 2, 16, 2).transpose(0, 5, 6, 3, 2, 4, 1).reshape(2, 16, 2, 5632))
    return (yp, ys, p_cq, p_dl, p_sr, p_si, p_hg, p_cf, s_cq, s_dl, s_sr, s_si, s_hg, s_cf)
```
